# Optimizing a Trainium2 kernel written in Bass

```python
import jax, jax.numpy as jnp
from jax import lax
import numpy as np

D_MODEL = 2048
BATCH = 16
SEQ = 256
DEPTH = 1
DEC_BATCH = 2
DEC_SEQ = 1024
PAST_LEN = 256

GRID_W = 64
HEAD_DIM = 128
A_HEADS = 8
A_KV_HEADS = 2
A_GROUP = A_HEADS // A_KV_HEADS
A_WINDOW = 128
BLOCK = 128
B_HEADS = 8
NA_ROWS_MAX = 8
NA_COLS = 16
D_FF = 4 * D_MODEL
ROPE_THETA = 10000.0
EPS = 1e-6
NEG = -1e30
Q_BLOCK = 128
A_Q = A_HEADS * HEAD_DIM
A_KV = A_KV_HEADS * HEAD_DIM
B_QKV = B_HEADS * HEAD_DIM
SPLIT_POINTS = (A_Q, A_Q + A_KV, A_Q + 2 * A_KV, A_Q + 2 * A_KV + B_QKV,
                A_Q + 2 * A_KV + 2 * B_QKV, A_Q + 2 * A_KV + 3 * B_QKV,
                A_Q + 2 * A_KV + 3 * B_QKV + D_MODEL)
IN_WIDTH = A_Q + 2 * A_KV + 3 * B_QKV + 2 * D_MODEL
SCALE = HEAD_DIM ** -0.5

kernel_name = "hybrid_dit_window_gqa_natten_prefix_step"


def rms_norm(x, w):
    xf = x.astype(jnp.float32)
    y = xf * lax.rsqrt(jnp.mean(xf * xf, axis=-1, keepdims=True) + EPS)
    return (y * w.astype(jnp.float32)).astype(x.dtype)


def ada_mod(cvec, w_ada, b_ada):
    m = jax.nn.silu(cvec) @ w_ada + b_ada
    return jnp.split(m, 6, axis=-1)


def softmax_sink(logits, sink_col):
    if sink_col is None:
        return jax.nn.softmax(logits, axis=-1)
    full = jnp.concatenate([logits, jnp.broadcast_to(sink_col, logits.shape[:-1] + (1,)).astype(jnp.float32)], axis=-1)
    return jax.nn.softmax(full, axis=-1)[..., :-1]


def axial_rope(x, T):
    n_freq = HEAD_DIM // 4
    pos = jnp.arange(T)
    row = (pos // GRID_W).astype(jnp.float32)
    col = (pos % GRID_W).astype(jnp.float32)
    inv = ROPE_THETA ** (-jnp.arange(n_freq, dtype=jnp.float32) / n_freq)
    shape = (T,) + (1,) * (x.ndim - 3) + (n_freq,)
    xf = x.astype(jnp.float32)
    xr, xc = jnp.split(xf, 2, axis=-1)

    def rot(xh, ang):
        cos = jnp.cos(ang).reshape(shape)
        sin = jnp.sin(ang).reshape(shape)
        x1, x2 = jnp.split(xh, 2, axis=-1)
        return jnp.concatenate([x1 * cos - x2 * sin, x1 * sin + x2 * cos], axis=-1)

    out = jnp.concatenate([rot(xr, row[:, None] * inv), rot(xc, col[:, None] * inv)], axis=-1)
    return out.astype(x.dtype)


def project_heads(h, w_in, q_norm_a, k_norm_a, q_norm_b, k_norm_b):
    B, T, _ = h.shape
    p = h @ w_in
    qa, ka, va, qb, kb, vb, ga, gb = jnp.split(p, SPLIT_POINTS, axis=-1)
    qa = rms_norm(qa.reshape(B, T, A_KV_HEADS, A_GROUP, HEAD_DIM), q_norm_a)
    ka = rms_norm(ka.reshape(B, T, A_KV_HEADS, HEAD_DIM), k_norm_a)
    va = va.reshape(B, T, A_KV_HEADS, HEAD_DIM)
    qb = rms_norm(qb.reshape(B, T, B_HEADS, HEAD_DIM), q_norm_b)
    kb = rms_norm(kb.reshape(B, T, B_HEADS, HEAD_DIM), k_norm_b)
    vb = vb.reshape(B, T, B_HEADS, HEAD_DIM)
    return qa, ka, va, qb, kb, vb, ga, gb


def merge_branches(oa, ob, ga, gb, w_br_a, w_br_b, w_out):
    B, T = oa.shape[:2]
    ya = oa.reshape(B, T, A_Q) @ w_br_a
    yb = ob.reshape(B, T, B_QKV) @ w_br_b
    return (jax.nn.sigmoid(ga) * ya + jax.nn.sigmoid(gb) * yb) @ w_out


def sq_relu_mlp(h, w_up, w_down):
    return jnp.square(jax.nn.relu(h @ w_up)) @ w_down


def context_attention(q, k, v, sink):
    B, S, Hk, G, D = q.shape
    nb = S // Q_BLOCK
    qb = jnp.moveaxis(q.reshape(B, nb, Q_BLOCK, Hk, G, D), 1, 0)
    sink_col = None if sink is None else sink[None, :, :, None, None]

    def one_block(qblk):
        lg = jnp.einsum("bqhgd,bkhd->bhgqk", qblk, k).astype(jnp.float32) * SCALE
        pr = softmax_sink(lg, sink_col)
        return jnp.einsum("bhgqk,bkhd->bqhgd", pr.astype(v.dtype), v)

    out = lax.map(one_block, qb)
    return jnp.moveaxis(out, 0, 1).reshape(B, S, Hk, G, D)


def latent_window_attention(q, k, v, kc, vc, sink):
    B, T, Hk, G, D = q.shape
    nb = T // BLOCK
    qb = q.reshape(B, nb, BLOCK, Hk, G, D)

    def band(t):
        tp = jnp.pad(t, ((0, 0), (BLOCK, BLOCK), (0, 0), (0, 0)))
        tb = tp.reshape(B, nb + 2, BLOCK, Hk, D)
        return jnp.concatenate([tb[:, :-2], tb[:, 1:-1], tb[:, 2:]], axis=2)

    kband, vband = band(k), band(v)
    qpos = jnp.arange(nb)[:, None] * BLOCK + jnp.arange(BLOCK)[None, :]
    kpos = (jnp.arange(nb)[:, None] - 1) * BLOCK + jnp.arange(3 * BLOCK)[None, :]
    valid = ((kpos[:, None, :] >= 0) & (kpos[:, None, :] < T)
             & (jnp.abs(qpos[:, :, None] - kpos[:, None, :]) <= A_WINDOW))
    lg_loc = jnp.einsum("bnqhgd,bnkhd->bnhgqk", qb, kband).astype(jnp.float32) * SCALE
    lg_loc = jnp.where(valid[None, :, None, None], lg_loc, NEG)
    lg_ctx = jnp.einsum("bnqhgd,bchd->bnhgqc", qb, kc).astype(jnp.float32) * SCALE
    pr = softmax_sink(jnp.concatenate([lg_loc, lg_ctx], axis=-1), sink[None, None, :, :, None, None])
    p_loc = pr[..., :3 * BLOCK].astype(v.dtype)
    p_ctx = pr[..., 3 * BLOCK:].astype(v.dtype)
    out = (jnp.einsum("bnhgqk,bnkhd->bnqhgd", p_loc, vband)
           + jnp.einsum("bnhgqc,bchd->bnqhgd", p_ctx, vc))
    return out.reshape(B, T, Hk, G, D)


def latent_neighbourhood_attention(q, k, v, kc, vc, rpb):
    B, T, H, D = q.shape
    rows = T // GRID_W
    kr = min(NA_ROWS_MAX, rows)
    r = jnp.arange(rows)
    col = jnp.arange(GRID_W)
    rstart = jnp.clip(r - kr // 2, 0, rows - kr)
    row_idx = rstart[:, None] + jnp.arange(kr)[None, :]
    cstart = jnp.clip(col - NA_COLS // 2, 0, GRID_W - NA_COLS)
    K = kr * GRID_W
    qg = q.reshape(B, rows, GRID_W, H, D)
    kg = k.reshape(B, rows, GRID_W, H, D)[:, row_idx].reshape(B, rows, K, H, D)
    vg = v.reshape(B, rows, GRID_W, H, D)[:, row_idx].reshape(B, rows, K, H, D)
    key_row = jnp.broadcast_to(row_idx[:, :, None], (rows, kr, GRID_W)).reshape(rows, K)
    key_col = jnp.tile(col, kr)
    valid = (key_col[None, :] >= cstart[:, None]) & (key_col[None, :] < cstart[:, None] + NA_COLS)
    dr = key_row[:, None, :] - r[:, None, None] + NA_ROWS_MAX - 1
    dc = jnp.clip(key_col[None, :] - col[:, None] + NA_COLS - 1, 0, 2 * NA_COLS - 2)
    bias = rpb[:, dr, dc].astype(jnp.float32)
    lg_loc = jnp.einsum("brwhd,brkhd->bhrwk", qg, kg).astype(jnp.float32) * SCALE + bias[None]
    lg_loc = jnp.where(valid[None, None, None], lg_loc, NEG)
    lg_ctx = jnp.einsum("brwhd,bchd->bhrwc", qg, kc).astype(jnp.float32) * SCALE
    pr = jax.nn.softmax(jnp.concatenate([lg_loc, lg_ctx], axis=-1), axis=-1)
    p_loc = pr[..., :K].astype(v.dtype)
    p_ctx = pr[..., K:].astype(v.dtype)
    out = (jnp.einsum("bhrwk,brkhd->brwhd", p_loc, vg)
           + jnp.einsum("bhrwc,bchd->brwhd", p_ctx, vc))
    return out.reshape(B, T, H, D)


def setup_inputs(seed: int = 0) -> dict:
    key = jax.random.key(seed)
    ks = jax.random.split(key, 24)
    f32 = jnp.float32

    def nrm(k, shape, scale):
        return jax.random.normal(k, shape, f32) * scale

    return {
        "x_prompt": nrm(ks[0], (BATCH, SEQ, D_MODEL), 1.0),
        "x_sample": nrm(ks[1], (DEC_BATCH, DEC_SEQ, D_MODEL), 1.0),
        "cache_a_k": nrm(ks[2], (DEC_BATCH, DEPTH, PAST_LEN, A_KV_HEADS, HEAD_DIM), 1.0),
        "cache_a_v": nrm(ks[3], (DEC_BATCH, DEPTH, PAST_LEN, A_KV_HEADS, HEAD_DIM), 1.0),
        "cache_b_k": nrm(ks[4], (DEC_BATCH, DEPTH, PAST_LEN, B_HEADS, HEAD_DIM), 1.0),
        "cache_b_v": nrm(ks[5], (DEC_BATCH, DEPTH, PAST_LEN, B_HEADS, HEAD_DIM), 1.0),
        "c": nrm(ks[6], (DEC_BATCH, D_MODEL), 1.0),
        "c_ctx": nrm(ks[7], (D_MODEL,), 1.0),
        "norm1_w": 1.0 + nrm(ks[8], (DEPTH, D_MODEL), 0.01),
        "norm2_w": 1.0 + nrm(ks[9], (DEPTH, D_MODEL), 0.01),
        "w_ada": nrm(ks[10], (DEPTH, D_MODEL, 6 * D_MODEL), D_MODEL ** -0.5),
        "b_ada": nrm(ks[11], (DEPTH, 6 * D_MODEL), 0.01),
        "w_in": nrm(ks[12], (DEPTH, D_MODEL, IN_WIDTH), D_MODEL ** -0.5),
        "q_norm_a": 1.0 + nrm(ks[13], (DEPTH, HEAD_DIM), 0.01),
        "k_norm_a": 1.0 + nrm(ks[14], (DEPTH, HEAD_DIM), 0.01),
        "q_norm_b": 1.0 + nrm(ks[15], (DEPTH, HEAD_DIM), 0.01),
        "k_norm_b": 1.0 + nrm(ks[16], (DEPTH, HEAD_DIM), 0.01),
        "sink_a": nrm(ks[17], (DEPTH, A_HEADS), 0.5),
        "rpb_b": nrm(ks[18], (DEPTH, B_HEADS, 2 * NA_ROWS_MAX - 1, 2 * NA_COLS - 1), 0.1),
        "w_br_a": nrm(ks[19], (DEPTH, A_Q, D_MODEL), A_Q ** -0.5),
        "w_br_b": nrm(ks[20], (DEPTH, B_QKV, D_MODEL), B_QKV ** -0.5),
        "w_out": nrm(ks[21], (DEPTH, D_MODEL, D_MODEL), D_MODEL ** -0.5),
        "w_up": nrm(ks[22], (DEPTH, D_MODEL, D_FF), D_MODEL ** -0.5),
        "w_down": nrm(ks[23], (DEPTH, D_FF, D_MODEL), D_FF ** -0.5),
    }


def reference(x_prompt, x_sample, cache_a_k, cache_a_v, cache_b_k, cache_b_v, c, c_ctx,
              norm1_w, norm2_w, w_ada, b_ada, w_in, q_norm_a, k_norm_a, q_norm_b, k_norm_b,
              sink_a, rpb_b, w_br_a, w_br_b, w_out, w_up, w_down):
    xp = x_prompt
    new_ak, new_av, new_bk, new_bv = [], [], [], []
    for l in range(DEPTH):
        sh1, sc1, g1, sh2, sc2, g2 = ada_mod(c_ctx[None, None, :], w_ada[l], b_ada[l])
        h = rms_norm(xp, norm1_w[l]) * (1 + sc1) + sh1
        qa, ka, va, qb, kb, vb, ga, gb = project_heads(h, w_in[l], q_norm_a[l], k_norm_a[l],
                                                        q_norm_b[l], k_norm_b[l])
        sink = sink_a[l].reshape(A_KV_HEADS, A_GROUP)
        oa = context_attention(qa, ka, va, sink)
        ob = context_attention(qb[:, :, :, None, :], kb, vb, None)
        xp = xp + g1 * merge_branches(oa, ob, ga, gb, w_br_a[l], w_br_b[l], w_out[l])
        h2 = rms_norm(xp, norm2_w[l]) * (1 + sc2) + sh2
        xp = xp + g2 * sq_relu_mlp(h2, w_up[l], w_down[l])
        new_ak.append(ka)
        new_av.append(va)
        new_bk.append(kb)
        new_bv.append(vb)
    y_prompt = xp
    new_a_k = jnp.stack(new_ak, axis=1)
    new_a_v = jnp.stack(new_av, axis=1)
    new_b_k = jnp.stack(new_bk, axis=1)
    new_b_v = jnp.stack(new_bv, axis=1)

    xs = x_sample
    T = xs.shape[1]
    for l in range(DEPTH):
        sh1, sc1, g1, sh2, sc2, g2 = ada_mod(c[:, None, :], w_ada[l], b_ada[l])
        h = rms_norm(xs, norm1_w[l]) * (1 + sc1) + sh1
        qa, ka, va, qb, kb, vb, ga, gb = project_heads(h, w_in[l], q_norm_a[l], k_norm_a[l],
                                                        q_norm_b[l], k_norm_b[l])
        qa = axial_rope(qa, T)
        ka = axial_rope(ka, T)
        sink = sink_a[l].reshape(A_KV_HEADS, A_GROUP)
        oa = latent_window_attention(qa, ka, va, cache_a_k[:, l], cache_a_v[:, l], sink)
        ob = latent_neighbourhood_attention(qb, kb, vb, cache_b_k[:, l], cache_b_v[:, l], rpb_b[l])
        xs = xs + g1 * merge_branches(oa, ob, ga, gb, w_br_a[l], w_br_b[l], w_out[l])
        h2 = rms_norm(xs, norm2_w[l]) * (1 + sc2) + sh2
        xs = xs + g2 * sq_relu_mlp(h2, w_up[l], w_down[l])
    y_sample = xs
    return (y_prompt, y_sample, new_a_k, new_a_v, new_b_k, new_b_v)
```

```python
import os
import numpy as np
import concourse.bass as bass
import concourse.mybir as mybir
from concourse.bass_utils import run_bass_kernel_spmd

F32 = mybir.dt.float32
BF16 = mybir.dt.bfloat16
ALU = mybir.AluOpType
AF = mybir.ActivationFunctionType
AX = mybir.AxisListType

NCORES = 8
D = 2048
KC = 16
NMAIN = 768
NHALO = 512
NTOK = NMAIN + NHALO
EPS = 1e-6
NEGM = -30000.0
IN_W = 8704
DFF = 8192
CH_P = (0, 512)
CH_S = (512, 256)
CH_H = (768, 512)

PP_N1 = 0
PP_N2 = 16
PP_BADA = 32
PP_QNA = 128
PP_KNA = 129
PP_QNB = 130
PP_KNB = 131
PP_SINK = 132
PP_SEL = 140
PP_W = 142


class Prog:
    CE = ("pe", "act", "dve", "pool")
    ALL = ("pe", "act", "dve", "pool", "sp")

    def __init__(self, nc, esems, dma_sems):
        self.nc = nc
        self.q = {e: [] for e in self.ALL}
        self.cnt = {e: 0 for e in self.CE}
        self.esem = esems
        self.free_dsems = list(dma_sems)
        self.dsem = {}
        self.seen = {e: {} for e in self.ALL}
        self.lastw = {}
        self.readers = {}
        self.n_wait = 0

    def _need(self, eng, ev, waits):
        if ev is None:
            return
        semkey, handle, val, _ = ev
        if self.seen[eng].get(semkey, 0) >= val:
            return
        self.seen[eng][semkey] = val
        waits.append((handle, val))
        self.n_wait += 1

    def _deps(self, eng, reads, writes, waits):
        for k in reads:
            ev = self.lastw.get(k)
            if ev is not None and not (ev[3] == "pe" and eng == "pe"):
                self._need(eng, ev, waits)
        for k in writes:
            ev = self.lastw.get(k)
            if ev is not None and not (ev[3] == "pe" and eng == "pe"):
                self._need(eng, ev, waits)
            for ev in self.readers.get(k, ()):
                if not (ev[3] == "pe" and eng == "pe"):
                    self._need(eng, ev, waits)

    def _commit(self, ev, reads, writes):
        for k in reads:
            lst = self.readers.setdefault(k, [])
            lst[:] = [e for e in lst if e[0] != ev[0]]
            lst.append(ev)
        for k in writes:
            self.lastw[k] = ev
            self.readers[k] = []

    def op(self, eng, fn, reads=(), writes=(), signal=True):
        waits = []
        self._deps(eng, reads, writes, waits)
        if signal:
            self.cnt[eng] += 1
            val = self.cnt[eng]
        else:
            val = self.cnt[eng] + 1
        ev = ("E" + eng, self.esem[eng], val, eng)
        self._commit(ev, reads, writes)
        self.q[eng].append((waits, fn, (self.esem[eng], 1) if signal else None))

    def dma(self, eng, out, in_, sem, reads=(), writes=(), **kw):
        waits = []
        self._deps(eng, reads, writes, waits)
        if sem not in self.dsem:
            self.dsem[sem] = [self.free_dsems.pop(), 0]
        rec = self.dsem[sem]
        rec[1] += 16
        ev = ("D" + sem, rec[0], rec[1], "dma")
        self._commit(ev, reads, writes)
        self.q[eng].append((waits, (lambda e, o=out, i=in_, k=kw: e.dma_start(out=o, in_=i, **k)), (rec[0], 16)))

    def custom(self, eng, fn, sem, inc, reads=(), writes=()):
        waits = []
        self._deps(eng, reads, writes, waits)
        if sem not in self.dsem:
            self.dsem[sem] = [self.free_dsems.pop(), 0]
        rec = self.dsem[sem]
        rec[1] += inc
        ev = ("D" + sem, rec[0], rec[1], "dma")
        self._commit(ev, reads, writes)
        self.q[eng].append((waits, fn, (rec[0], inc)))

    def wait_all(self, eng, keys):
        waits = []
        for k in keys:
            self._need(eng, self.lastw.get(k), waits)
            for ev in self.readers.get(k, ()):
                self._need(eng, ev, waits)
        if waits:
            self.q[eng].append((waits, None, None))

    def alias(self, new_keys, old_keys):
        evs = []
        for k in old_keys:
            if self.lastw.get(k) is not None:
                evs.append(self.lastw[k])
            evs.extend(self.readers.get(k, ()))
        for k in new_keys:
            self.lastw[k] = None
            self.readers[k] = list(evs)

    def emit(self, eng_name, eng):
        for waits, fn, inc in self.q[eng_name]:
            for h, v in waits:
                eng.wait_ge(h, v)
            if fn is None:
                continue
            ins = fn(eng)
            if inc is not None:
                ins.then_inc(inc[0], inc[1])


def I(method, *a, **k):
    return lambda e: getattr(e, method)(*a, **k)


class Rot:
    def __init__(self, items):
        self.items = list(items)
        self.i = 0

    def next(self):
        v = self.items[self.i % len(self.items)]
        self.i += 1
        return v


def build_program(debug=None):
    nc = bass.Bass("TRN2", target_bir_lowering=False)

    def din(name, shape, dt=F32):
        return nc.dram_tensor(name, list(shape), dt, kind="ExternalInput").ap()

    def dout(name, shape, dt=F32):
        return nc.dram_tensor(name, list(shape), dt, kind="ExternalOutput").ap()

    x_main = din("x_main", [NMAIN, D])
    x_halo = din("x_halo", [NHALO, D])
    cT_d = din("cT", [128, 32])
    pp_d = din("pp", [128, PP_W])
    ident_d = din("ident", [128, 128])
    perm_d = din("perm", [128, 128])
    cos_d = din("cosT", [128, NMAIN])
    sin_d = din("sinT", [128, NMAIN])
    maskA_d = din("maskA", [NMAIN, 512])
    biasB_d = din("biasB", [NMAIN, 2048])
    cak_d = din("cak", [256, 256])
    cav_d = din("cav", [256, 256])
    cbk_d = din("cbk", [256, 1024])
    cbv_d = din("cbv", [256, 1024])
    w_ada = din("w_ada", [D, 6 * D])
    w_in = din("w_in", [D, IN_W])
    w_bra = din("w_br_a", [1024, D])
    w_brb = din("w_br_b", [1024, D])
    w_out = din("w_out", [D, D])
    w_up = din("w_up", [D, DFF])
    w_down = din("w_down", [DFF, D])

    y_main = dout("y_main", [NMAIN, D])
    nak_o = dout("nak", [512, 256])
    nav_o = dout("nav", [512, 256])
    nbk_o = dout("nbk", [512, 1024])
    nbv_o = dout("nbv", [512, 1024])
    dbg_o = None
    if debug is not None:
        dbg_o = dout("dbg", debug["shape"], BF16 if debug.get("dtype") == "bf16" else F32)

    import contextlib
    es = contextlib.ExitStack()
    with es:
        ARENA_W = 53120
        arena = es.enter_context(nc.sbuf_tensor("arena", [128, ARENA_W], F32))
        ps = es.enter_context(nc.psum_tensor("ps", [128, 8, 512], F32))
        sem_names = ["pe", "act", "dve", "pool"]
        esems = {n: es.enter_context(nc.semaphore("s_" + n)) for n in sem_names}
        dsems = [es.enter_context(nc.semaphore("d%d" % i)) for i in range(70)]
        P = Prog(nc, esems, dsems)

        class Arena:
            def __init__(self):
                self.off = 0

            def take(self, nbytes):
                o = self.off
                self.off += (nbytes + 63) // 64 * 64
                assert self.off <= ARENA_W * 4, ("arena overflow", self.off)
                return o

        def view(off_b, shape, dt):
            esz = 2 if dt == BF16 else 4
            n = int(np.prod(shape))
            assert off_b % 4 == 0
            w0 = off_b // 4
            nw = (n * esz + 3) // 4
            ap = arena[:, w0:w0 + nw]
            if dt == BF16:
                ap = ap.bitcast(BF16)
            if len(shape) == 2:
                ap = ap.rearrange("p (a b) -> p a b", a=shape[0])
            elif len(shape) == 3:
                ap = ap.rearrange("p (a b c) -> p a b c", a=shape[0], b=shape[1])
            return ap

        A = Arena()
        RING_N = 6
        ring_off = [A.take(8192) for _ in range(RING_N)]
        o_pp = A.take(PP_W * 4)
        o_identf = A.take(512)
        o_identb = A.take(256)
        o_permf = A.take(512)
        o_onesb = A.take(256)
        o_modT = A.take(96 * 2 * 4)
        o_ab = A.take(4 * 2 * 16 * 4)
        o_small = A.take(256)
        o_esink = A.take(32)
        o_cT = A.take(128)
        o_scT = A.take(64)

        pp = view(o_pp, [PP_W], F32)
        identf = view(o_identf, [128], F32)
        identb = view(o_identb, [128], BF16)
        permf = view(o_permf, [128], F32)
        onesb = view(o_onesb, [128], BF16)
        modT = view(o_modT, [96, 2], F32)
        abv = view(o_ab, [4, 2, 16], F32)
        small = view(o_small, [64], F32)
        esink = view(o_esink, [8], F32)
        epsc = small[:, 20:21]
        epsc128 = small[:, 21:22]
        zeroc = small[:, 22:23]

        R1 = A.take(67584)
        R2 = A.take(65536)
        mT_off = A.take(KC * NMAIN * 2)
        mTv = view(mT_off, [KC, NMAIN], BF16)
        hT = view(R2, [KC, NTOK], BF16)
        oT = view(R2 + 40960, [2, 8, NMAIN], BF16)
        Gt = view(R2 + 65536 - 16384, [2, 2048], F32)
        h2T = view(R2, [KC, NMAIN], BF16)
        uT = view(R2 + 24576, [KC, NMAIN], BF16)
        xnn = [view(R2 + 24576 + i * 4096, [2048], BF16) for i in range(6)]
        xs_ = [view(R1 + i * 8192, [2048], F32) for i in range(3)]
        xn_ = [view(R1 + 24576 + i * 4096, [2048], BF16) for i in range(10)]
        o = R1
        qT = view(o, [4, NMAIN], BF16); o += 4 * NMAIN * 2
        kT = view(o, [4, NTOK], BF16); o += 4 * NTOK * 2
        vS = view(o, [10, 512], BF16); o += 10 * 512 * 2
        NPT = 8
        pT = [view(o + i * 1024, [512], BF16) for i in range(NPT)]; o += NPT * 1024
        m = o
        maskA = view(m, [6, 512], BF16)
        cosT = view(m + 6144, [NMAIN], F32)
        sinT = view(m + 9216, [NMAIN], F32)
        biasB = [view(m + i * 6144, [6, 512], BF16) for i in range(2)]
        kc_T = view(m + 12288, [4, 256], BF16)
        vc_S = view(m + 14336, [2, 512], BF16)
        kc_S = view(m + 16384, [2, 512], BF16)
        f32s = [view(m + 18432 + i * 2048, [512], F32) for i in range(3)]
        o += 24576
        ost = [view(o + i * 2048, [512], F32) for i in range(2)]; o += 2 * 2048
        sqb = [view(o + i * 1024, [512], BF16) for i in range(2)]; o += 2 * 1024
        rl_ = [view(o + i * 1024, [256], F32) for i in range(2)]; o += 2 * 1024
        assert o - R1 <= 67584, o - R1
        R1_ATT_KEYS = (["qT%d" % i for i in range(4)] + ["kT%d" % i for i in range(4)] + ["vS", "kcT", "vcS", "kcS", "f32s0", "f32s1", "f32s2", "ost0", "ost1", "sqb0", "sqb1",
                        "rl0", "rl1", "biasB0", "biasB1", "maskA", "cosT", "sinT"] + ["pT%d" % i for i in range(NPT)])
        xres = view(R1, [6, 2048], F32)
        pb_scr = R1 + 49152
        sgs = [view(pb_scr + i * 1536, [384], F32) for i in range(2)]
        tms = [view(pb_scr + 3072 + i * 1536, [384], F32) for i in range(2)]
        tmp512 = [view(pb_scr + 6144 + i * 2048, [512], F32) for i in range(2)]
        onesf = view(pb_scr + 10240, [128], F32)
        diag = [view(pb_scr + 10752 + i * 512, [128], F32) for i in range(2)]
        tA = view(pb_scr + 11776, [4, NMAIN], BF16)
        rsb = [view(pb_scr + i * 1024, [384], BF16) for i in range(2)]
        tmp512b = [view(pb_scr + 4096 + i * 2048, [512], F32) for i in range(2)]

        def PS(bank, n=512, off=0):
            return ps[:, bank, off:off + n]

        def PSB(bank, n, off=0):
            return ps[:, bank, :].bitcast(BF16)[:, off:off + n]

        rot_main = Rot([0, 1, 2, 3])
        rot_a1 = Rot([4, 5])
        rot_a2 = Rot([6, 7])

        def bk(b):
            return "ps%d" % b

        def set_mode(m):
            if m == "proj":
                rot_main.items, rot_a1.items, rot_a2.items = [0, 1, 2, 3, 4, 5], [6], [7]
            else:
                rot_main.items, rot_a1.items, rot_a2.items = [0, 1, 2, 3], [4, 5], [6, 7]

        ring_state = {"n": 0}

        class WT:
            def __init__(self, slots):
                self.slots = slots

            def k(self, k):
                return view(ring_off[self.slots[k // 8]], [8, 512], BF16)[:, k % 8, :]

            def key(self, k):
                return "ring%d" % self.slots[k // 8]

        def load_entry(src):
            n = ring_state["n"]
            ring_state["n"] += 1
            slot = n % RING_N
            key = "ring%d" % slot
            P.dma("pool", view(ring_off[slot], [8, 512], BF16), src, sem=key, writes=[key])
            return slot

        def wsrc(w, r0, nr, c0, ncol):
            return w[r0:r0 + nr, c0:c0 + ncol].rearrange("(k p) n -> p k n", p=128)

        def load_w16(w, c0, r0=0):
            n = ring_state["n"]
            slot = n % RING_N
            if slot % 2 == 0:
                ring_state["n"] += 2
                k0, k1 = "ring%d" % slot, "ring%d" % (slot + 1)
                dst = view(ring_off[slot], [16, 512], BF16)
                P.dma("pool", dst, wsrc(w, r0, 2048, c0, 512), sem=k0, writes=[k0, k1])
                return WT([slot, slot + 1])
            return WT([load_entry(wsrc(w, r0, 1024, c0, 512)), load_entry(wsrc(w, r0 + 1024, 1024, c0, 512))])

        def load_w8(w, c0):
            return WT([load_entry(wsrc(w, 0, 1024, c0, 512))])

        class _Stop(Exception):
            pass

        def stop_at(name, ap, keys):
            if debug is not None and debug["at"] == name:
                P.dma("sp", dbg_o, ap, sem="dbg", reads=keys)
                P.wait_all("sp", keys)
                raise _Stop()

        chains = []

        def tick():
            for ch in list(chains):
                step = ch.pop(0)
                step()
                if not ch:
                    chains.remove(ch)

        def drain():
            while chains:
                tick()

        def body():
            P.dma("sp", pp, pp_d, sem="pp", writes=["pp"])
            P.dma("sp", identf, ident_d, sem="identf", writes=["identf"])
            P.dma("sp", permf, perm_d, sem="permf", writes=["permf"])
            P.dma("pool", identb, ident_d, sem="identb", writes=["identb"])
            P.op("dve", I("memset", onesb, 1.0), writes=["onesb"])
            P.op("dve", I("memset", epsc, EPS), writes=["epsc"])
            P.op("dve", I("memset", epsc128, 128.0 * EPS), writes=["epsc"])
            P.op("dve", I("memset", zeroc, 0.0), writes=["epsc"])
            cTf = view(o_cT, [32], F32)
            scTflat = view(o_scT, [32], BF16)
            scTb = scTflat.rearrange("p (k v) -> p k v", v=2)
            P.dma("sp", cTf, cT_d, sem="cT", writes=["cTf"])
            P.op("act", I("activation", out=scTflat, in_=cTf, func=AF.Silu), reads=["cTf"], writes=["scT"])
            P.op("act", I("activation", out=esink, in_=pp[:, PP_SINK:PP_SINK + 8], func=AF.Exp), reads=["pp"], writes=["esink"])

            def norm_group(tiles, src_fn, vec, kindA, kindB, dstT, dst_key, xbufs, xkeys, tok0, stat0):
                norm_stats(tiles, xbufs, xkeys, stat0)
                norm_tr(len(tiles), vec, kindA, kindB, dstT, dst_key, xbufs, xkeys, tok0)

            def norm_stats(tiles, xbufs, xkeys, stat0):
                for i, (xin, xkey, loader) in enumerate(tiles):
                    if loader is not None:
                        loader()
                    xnb, xnk = xbufs[i], xkeys[i]
                    si = (stat0 + i) % 10
                    ssq = small[:, 24 + si:25 + si]
                    rst = small[:, 36 + si:37 + si]
                    sk = "ssq%d" % si
                    P.op("act", I("activation", out=xnb, in_=xin, func=AF.Square, accum_out=ssq), reads=[xkey], writes=[xnk, sk])
                    P.op("act", I("activation", out=rst, in_=ssq, func=AF.Sqrt, scale=1.0 / D, bias=epsc), reads=[sk, "epsc"], writes=[sk + "r"])
                    P.op("dve", I("reciprocal", out=rst, in_=rst), reads=[sk + "r"], writes=[sk + "r"])
                    P.op("dve", I("tensor_scalar", out=xnb, in0=xin, scalar1=rst, scalar2=None, op0=ALU.mult), reads=[xkey, sk + "r"], writes=[xnk])

            def norm_tr(nt, vec, kindA, kindB, dstT, dst_key, xbufs, xkeys, tok0):
                for c in range(KC):
                    b = rot_main.next()
                    for i in range(nt):
                        P.op("pe", I("transpose", PSB(b, 128, i * 128), xbufs[i][:, c * 128:(c + 1) * 128], identb),
                             reads=[xkeys[i], "identb"], writes=[bk(b)], signal=(i == nt - 1))
                    dst = dstT[:, c, tok0:tok0 + nt * 128]
                    if c % 2 == 0:
                        P.op("act", I("activation", out=dst, in_=PSB(b, nt * 128), func=AF.Identity,
                                      scale=abv[:, kindA, vec, c:c + 1], bias=abv[:, kindB, vec, c:c + 1]),
                             reads=[bk(b), "ab", "ab2"], writes=[dst_key + str(c)])
                    else:
                        P.op("dve", I("tensor_scalar", out=dst, in0=PSB(b, nt * 128), scalar1=abv[:, kindA, vec, c:c + 1],
                                      scalar2=abv[:, kindB, vec, c:c + 1], op0=ALU.mult, op1=ALU.add),
                             reads=[bk(b), "ab", "ab2"], writes=[dst_key + str(c)])

            xm_t = x_main.rearrange("(t p) d -> t p d", p=128)
            xh_t = x_halo.rearrange("(t p) d -> t p d", p=128)
            norm1_groups = []
            xi = 0
            for tl, vec in [([0, 1, 2, 3], 0), ([4, 5], 1), ([6, 7, 8, 9], 1)]:
                tiles, xb, xk = [], [], []
                for ti in tl:
                    bi = xi % 3
                    xi += 1
                    src = xm_t[ti] if ti < 6 else xh_t[ti - 6]
                    ld = (lambda bi=bi, src=src: P.dma("sp", xs_[bi], src, sem="xs%d" % bi, writes=["xs%d" % bi]))
                    tiles.append((xs_[bi], "xs%d" % bi, ld))
                    xb.append(xn_[ti])
                    xk.append("xn%d" % ti)
                norm_stats(tiles, xb, xk, tl[0])
                norm1_groups.append((tl, vec, xb, xk))

            ada_pending = list(range(24))

            def ada_some(n):
                for _ in range(min(n, len(ada_pending))):
                    t = ada_pending.pop(0)
                    wt = load_w16(w_ada, t * 512)
                    b = rot_a2.next()
                    for cc in range(4):
                        for k in range(KC):
                            P.op("pe", I("matmul", ps[:, b, cc * 2:cc * 2 + 2], lhsT=wt.k(k)[:, cc * 128:(cc + 1) * 128],
                                         rhs=scTb[:, k, :], start=(k == 0), stop=(k == KC - 1)),
                                 reads=[wt.key(k), "scT"], writes=[bk(b)], signal=(k == KC - 1 or k == 7))
                    for v in range(2):
                        P.op("dve", I("tensor_tensor", out=modT[:, t * 4:(t + 1) * 4, v],
                                      in0=ps[:, b, 0:8].rearrange("p (c v) -> p c v", v=2)[:, :, v],
                                      in1=pp[:, PP_BADA + t * 4:PP_BADA + t * 4 + 4], op=ALU.add),
                             reads=[bk(b), "pp"], writes=["modT"])

            ada_some(8)
            for v in range(2):
                P.op("dve", I("scalar_tensor_tensor", out=abv[:, 0, v, :], in0=modT[:, 16:32, v], scalar=1.0,
                              in1=pp[:, PP_N1:PP_N1 + 16], op0=ALU.add, op1=ALU.mult), reads=["modT", "pp"], writes=["ab"])
                P.op("dve", I("tensor_copy", out=abv[:, 1, v, :], in_=modT[:, 0:16, v]), reads=["modT"], writes=["ab"])
            SQ128 = float(np.sqrt(128.0))
            P.op("dve", I("tensor_copy", out=small[:, 0:1], in_=pp[:, PP_QNA:PP_QNA + 1]), reads=["pp"], writes=["small"])
            P.op("dve", I("tensor_scalar", out=small[:, 1:2], in0=pp[:, PP_KNA:PP_KNA + 1], scalar1=SQ128, scalar2=None, op0=ALU.mult), reads=["pp"], writes=["small"])
            P.op("dve", I("tensor_copy", out=small[:, 2:3], in_=pp[:, PP_QNB:PP_QNB + 1]), reads=["pp"], writes=["small"])
            P.op("dve", I("tensor_scalar", out=small[:, 3:4], in0=pp[:, PP_KNB:PP_KNB + 1], scalar1=SQ128, scalar2=None, op0=ALU.mult), reads=["pp"], writes=["small"])

            def ada_finish():
                ada_some(len(ada_pending))
                for v in range(2):
                    P.op("dve", I("scalar_tensor_tensor", out=abv[:, 2, v, :], in0=modT[:, 64:80, v], scalar=1.0,
                                  in1=pp[:, PP_N2:PP_N2 + 16], op0=ALU.add, op1=ALU.mult), reads=["modT", "pp"], writes=["ab2"])
                    P.op("dve", I("tensor_copy", out=abv[:, 3, v, :], in_=modT[:, 48:64, v]), reads=["modT"], writes=["ab2"])
            stop_at("ada", modT, ["modT", "ab", "small"])

            tkv = load_w16(w_in, 1024)
            for (tl, vec, xb, xk) in norm1_groups:
                norm_tr(len(tl), vec, 0, 1, hT, "hT", xb, xk, tl[0] * 128)
            stop_at("hT", hT, ["hT%d" % i for i in range(16)])

            P.alias(R1_ATT_KEYS, ["xs0", "xs1", "xs2"] + ["xn%d" % i for i in range(10)])
            P.dma("pool", maskA, maskA_d.rearrange("(c p) n -> p c n", p=128), sem="maskA", writes=["maskA"])
            P.dma("sp", cosT, cos_d, sem="cosT", writes=["cosT"])
            P.dma("sp", sinT, sin_d, sem="sinT", writes=["sinT"])

            cstate = {"i": 0, "ost": 0}

            def proj_fm(wt, cc, chunks, make_chain):
                for (s0, sz) in chunks:
                    b = rot_main.next()
                    for k in range(KC):
                        P.op("pe", I("matmul", PS(b, sz), lhsT=wt.k(k)[:, cc * 128:(cc + 1) * 128], rhs=hT[:, k, s0:s0 + sz],
                                     start=(k == 0), stop=(k == KC - 1)),
                             reads=[wt.key(k), "hT%d" % k], writes=[bk(b)], signal=(k == KC - 1 or k == 7))
                    tick()
                    make_chain(b, s0, sz)

            def norm_steps(b, sz, wcol, out_ap, out_key):
                i = cstate["i"] % 2
                cstate["i"] += 1
                sq = sqb[i][:, 0:sz]
                sqk = "sqb%d" % i
                P.op("act", I("activation", out=sq, in_=PS(b, sz), func=AF.Square), reads=[bk(b)], writes=[sqk])

                def step1():
                    b2 = rot_a1.next()
                    P.op("pe", I("matmul", PS(b2, sz), lhsT=onesb, rhs=sq, start=True, stop=True), reads=[sqk, "onesb"], writes=[bk(b2)])
                    rr = f32s[2][:, 0:sz]
                    P.op("act", I("activation", out=rr, in_=PS(b2, sz), func=AF.Ln, scale=1.0, bias=epsc128), reads=[bk(b2), "epsc"], writes=["f32s2"])
                    P.op("act", I("activation", out=rr, in_=rr, func=AF.Exp, scale=-0.5), reads=["f32s2"], writes=["f32s2"])
                    P.op("dve", I("scalar_tensor_tensor", out=out_ap, in0=PS(b, sz), scalar=wcol, in1=rr, op0=ALU.mult, op1=ALU.mult),
                         reads=[bk(b), "f32s2", "small"], writes=[out_key])
                return step1

            def rope_step(src_ap, src_key, e0, sz, out_ap, out_key):
                def step():
                    b3 = rot_a2.next()
                    P.op("pe", I("matmul", PS(b3, sz), lhsT=permf, rhs=src_ap, start=True, stop=True), reads=[src_key, "permf"], writes=[bk(b3)])
                    t1 = f32s[2][:, 0:sz]
                    P.op("dve", I("tensor_tensor", out=t1, in0=PS(b3, sz), in1=sinT[:, e0:e0 + sz], op=ALU.mult), reads=[bk(b3), "sinT"], writes=["f32s2"])
                    P.op("dve", I("tensor_tensor", out=src_ap, in0=src_ap, in1=cosT[:, e0:e0 + sz], op=ALU.mult), reads=[src_key, "cosT"], writes=[src_key])
                    P.op("dve", I("tensor_tensor", out=out_ap, in0=src_ap, in1=t1, op=ALU.add), reads=[src_key, "f32s2"], writes=[out_key])
                return step

            def kout_step(kn_ap, kn_key, dst_dram, head_col, kt_dst, kt_key):
                def step():
                    b4 = rot_a2.next()
                    for t in range(4):
                        P.op("pe", I("transpose", PS(b4, 128, t * 128), kn_ap[:, t * 128:(t + 1) * 128], identf),
                             reads=[kn_key, "identf"], writes=[bk(b4)], signal=(t == 3))
                    i = cstate["ost"] % 2
                    cstate["ost"] += 1
                    st = ost[i]
                    P.op("act", I("activation", out=st, in_=PS(b4, 512), func=AF.Copy), reads=[bk(b4)], writes=["ost%d" % i])
                    P.dma("sp", dst_dram.rearrange("(t p) n -> p t n", p=128)[:, :, head_col * 128:(head_col + 1) * 128],
                          st.rearrange("p (t d) -> p t d", t=4), sem="ost%d" % i, reads=["ost%d" % i])
                    P.op("act", I("activation", out=kt_dst, in_=kn_ap, func=AF.Copy), reads=[kn_key], writes=[kt_key])
                return step

            tmp_i = [0]

            def _nop():
                pass

            def proj_q(wt, cc, hslot, wcol, do_rope):
                def mk(b, s0, sz):
                    dst = qT[:, hslot, s0:s0 + sz]
                    if do_rope and s0 == CH_S[0]:
                        fi = tmp_i[0] % 2
                        tmp_i[0] += 1
                        tmp = f32s[fi][:, 0:sz]
                        chains.append([norm_steps(b, sz, wcol, tmp, "f32s%d" % fi), _nop, rope_step(tmp, "f32s%d" % fi, 0, sz, dst, "qT%d" % hslot)])
                    else:
                        chains.append([norm_steps(b, sz, wcol, dst, "qT%d" % hslot)])
                proj_fm(wt, cc, [CH_P, CH_S], mk)

            def proj_k(wt, cc, hslot, wcol, do_rope, dst_dram, head_col):
                def mk(b, s0, sz):
                    fi = tmp_i[0] % 2
                    tmp_i[0] += 1
                    tmp = f32s[fi][:, 0:sz]
                    tk = "f32s%d" % fi
                    dst = kT[:, hslot, s0:s0 + sz]
                    st1 = norm_steps(b, sz, wcol, tmp, tk)
                    if s0 == 0:
                        chains.append([st1, _nop, kout_step(tmp, tk, dst_dram, head_col, dst, "kT%d" % hslot)])
                    elif do_rope:
                        chains.append([st1, _nop, rope_step(tmp, tk, s0 - 512, sz, dst, "kT%d" % hslot)])
                    else:
                        def cp(tmp=tmp, tk=tk, dst=dst):
                            P.op("act", I("activation", out=dst, in_=tmp, func=AF.Copy), reads=[tk], writes=["kT%d" % hslot])
                        chains.append([st1, _nop, cp])
                proj_fm(wt, cc, [CH_P, CH_S, CH_H], mk)

            def proj_v(wt, c0, ncols, vcol0, dst_dram, dcol0):
                for ti in range(10):
                    b = rot_main.next()
                    for k in range(KC):
                        P.op("pe", I("matmul", PS(b, ncols), lhsT=hT[:, k, ti * 128:(ti + 1) * 128], rhs=wt.k(k)[:, c0:c0 + ncols],
                                     start=(k == 0), stop=(k == KC - 1)),
                             reads=[wt.key(k), "hT%d" % k], writes=[bk(b)], signal=(k == KC - 1 or k == 7))
                    tick()
                    if ti < 4:
                        i = cstate["ost"] % 2
                        cstate["ost"] += 1
                        st = ost[i][:, 0:ncols]
                        P.op("act", I("activation", out=st, in_=PS(b, ncols), func=AF.Copy), reads=[bk(b)], writes=["ost%d" % i])
                        P.op("dve", I("tensor_copy", out=vS[:, ti, vcol0:vcol0 + ncols], in_=st), reads=["ost%d" % i], writes=["vS"])
                        P.dma("sp", dst_dram[ti * 128:(ti + 1) * 128, dcol0:dcol0 + ncols], st, sem="ost%d" % i, reads=["ost%d" % i])
                    else:
                        P.op("dve", I("tensor_copy", out=vS[:, ti, vcol0:vcol0 + ncols], in_=PS(b, ncols)), reads=[bk(b)], writes=["vS"])

            pt_i = [0]

            def att_item(pair_heads, kslots, vcols, hs_slots, mixer, sink_cols, sample_bias, grp):
                same_kv = (kslots[0] == kslots[1]) and (vcols[0] == vcols[1])
                q0 = grp * 256
                if grp < 2:
                    chunks = [("tok", grp * 256 + c * 128, grp * 2 + c, None) for c in range(2)]
                else:
                    chunks = [("tok", 512 + c * 128, 4 + c, c) for c in range(6)] + [("ctx", c * 128, c, None) for c in range(2)]
                pts = []

                def scores():
                    for (kind, k0, vt, bc) in chunks:
                        b = rot_main.next()
                        first = True
                        if bc is not None:
                            bap, bkey = sample_bias(bc)
                            P.op("pe", I("matmul", PS(b, 512), lhsT=identb, rhs=bap, start=True, stop=False),
                                 reads=[bkey, "identb"], writes=[bk(b)], signal=False)
                            first = False
                        if same_kv:
                            lk, lkey = (kT[:, kslots[0], k0:k0 + 128], "kT%d" % kslots[0]) if kind == "tok" else (kc_T[:, kslots[0], k0:k0 + 128], "kcT")
                            P.op("pe", I("matmul", PS(b, 512).rearrange("p (a q) -> p a q", a=2), lhsT=lk,
                                         rhs=qT[:, hs_slots[0]:hs_slots[0] + 2, q0:q0 + 256], start=first, stop=True),
                                 reads=[lkey, "qT%d" % hs_slots[0], "qT%d" % (hs_slots[0] + 1)], writes=[bk(b)])
                        else:
                            for i in range(2):
                                lk, lkey = (kT[:, kslots[i], k0:k0 + 128], "kT%d" % kslots[i]) if kind == "tok" else (kc_T[:, kslots[i], k0:k0 + 128], "kcT")
                                P.op("pe", I("matmul", PS(b, 256, i * 256), lhsT=lk, rhs=qT[:, hs_slots[i], q0:q0 + 256],
                                             start=first, stop=(first or i == 1)),
                                     reads=[lkey, "qT%d" % hs_slots[i]], writes=[bk(b)], signal=(i == 1))
                        pi = pt_i[0] % NPT
                        pt_i[0] += 1
                        P.op("act", I("activation", out=pT[pi], in_=PS(b, 512), func=AF.Exp), reads=[bk(b)], writes=["pT%d" % pi])
                        pts.append((pi, kind, vt))

                def finish():
                    bo = rot_a1.next()
                    bl = rot_a2.next()
                    n = len(pts)
                    for i in range(2):
                        if same_kv and i == 1:
                            break
                        for ci, (pi, kind, vt) in enumerate(pts):
                            lv, vkey = (vS[:, vt, vcols[i]:vcols[i] + 128], "vS") if kind == "tok" else (vc_S[:, vt, vcols[i]:vcols[i] + 128], "vcS")
                            if same_kv:
                                P.op("pe", I("matmul", PS(bo, 512), lhsT=lv, rhs=pT[pi], start=(ci == 0), stop=(ci == n - 1)),
                                     reads=[vkey, "pT%d" % pi], writes=[bk(bo)], signal=(ci == n - 1))
                            else:
                                P.op("pe", I("matmul", PS(bo, 256, i * 256), lhsT=lv, rhs=pT[pi][:, i * 256:(i + 1) * 256],
                                             start=(ci == 0), stop=(ci == n - 1)),
                                     reads=[vkey, "pT%d" % pi], writes=[bk(bo)], signal=(ci == n - 1))
                    for ci, (pi, kind, vt) in enumerate(pts):
                        P.op("pe", I("matmul", PS(bl, 512), lhsT=onesb, rhs=pT[pi], start=(ci == 0), stop=(ci == n - 1)),
                             reads=["onesb", "pT%d" % pi], writes=[bk(bl)], signal=(ci == n - 1))
                    for i in range(2):
                        rl = rl_[i]
                        sc = esink[:, sink_cols[i]:sink_cols[i] + 1] if sink_cols is not None else zeroc
                        P.op("act", I("activation", out=rl, in_=PS(bl, 256, i * 256), func=AF.Ln, scale=1.0, bias=sc),
                             reads=[bk(bl), "esink", "epsc"], writes=["rl%d" % i])
                        P.op("act", I("activation", out=rl, in_=rl, func=AF.Exp, scale=-1.0), reads=["rl%d" % i], writes=["rl%d" % i])
                        P.op("dve", I("tensor_tensor", out=oT[:, mixer, pair_heads[i], q0:q0 + 256], in0=PS(bo, 256, i * 256), in1=rl, op=ALU.mult),
                             reads=[bk(bo), "rl%d" % i], writes=["oT"])
                return scores, finish, len(chunks)

            def run_items(items, hooks=()):
                prev = None
                prev_n = 0
                hook_at = {}
                for hi_, h in enumerate(hooks):
                    hook_at[(hi_ + 1) * len(items) // (len(hooks) + 1)] = h
                for ii_, (sc, fin, n) in enumerate(items):
                    if ii_ in hook_at:
                        hook_at[ii_]()
                    if prev is not None and prev_n + n > NPT:
                        prev()
                        prev = None
                    sc()
                    if prev is not None:
                        prev()
                    prev = fin
                    prev_n = n
                if prev is not None:
                    prev()

            def load_ctx(k_d, v_d, c0, ncols, nheads):
                P.dma("pool", kc_S[:, :, 0:ncols], k_d[:, c0:c0 + ncols].rearrange("(c p) n -> p c n", p=128), sem="kcS", writes=["kcS"])
                P.dma("pool", vc_S[:, :, 0:ncols], v_d[:, c0:c0 + ncols].rearrange("(c p) n -> p c n", p=128), sem="vcS", writes=["vcS"])
                for h in range(nheads):
                    b = rot_a2.next()
                    for c in range(2):
                        P.op("pe", I("transpose", PSB(b, 128, c * 128), kc_S[:, c, h * 128:(h + 1) * 128], identb),
                             reads=["kcS", "identb"], writes=[bk(b)], signal=(c == 1))
                    P.op("dve", I("tensor_copy", out=kc_T[:, h, :], in_=PSB(b, 256)), reads=[bk(b)], writes=["kcT"])

            Wq = small[:, 0:1]
            Wk = small[:, 1:2]
            Wqb = small[:, 2:3]
            Wkb = small[:, 3:4]
            set_mode("proj")
            load_ctx(cak_d, cav_d, 0, 256, 2)
            for h in range(2):
                proj_k(tkv, h, h, Wk, True, nak_o, h)
            proj_v(tkv, 256, 256, 0, nav_o, 0)
            ada_some(1)
            for rnd in range(2):
                set_mode("proj")
                tq = load_w16(w_in, rnd * 512)
                for cc in range(4):
                    proj_q(tq, cc, cc, Wq, True)
                drain()
                ada_some(1)
                set_mode("att")
                items = []
                for pr in range(2):
                    h0 = rnd * 4 + pr * 2
                    kvh = h0 // 4
                    for grp in range(3):
                        items.append(att_item([h0, h0 + 1], [kvh, kvh], [kvh * 128, kvh * 128], [pr * 2, pr * 2 + 1], 0,
                                              [h0, h0 + 1], lambda c: (maskA[:, c, :], "maskA"), grp))
                run_items(items, hooks=[lambda: ada_some(1), lambda: ada_some(1)])
            stop_at("oA", oT[:, 0], ["oT"])

            P.alias(["biasB0", "biasB1"], ["maskA", "cosT", "sinT"])
            bB_t = biasB_d.rearrange("(c p) n -> p c n", p=128)
            for rnd in range(2):
                set_mode("proj")
                tq = load_w16(w_in, 1536 + rnd * 512)
                load_ctx(cbk_d, cbv_d, rnd * 512, 512, 4)
                for cc in range(4):
                    proj_q(tq, cc, cc, Wqb, False)
                ada_some(1)
                tk = load_w16(w_in, 2560 + rnd * 512)
                for cc in range(4):
                    proj_k(tk, cc, cc, Wkb, False, nbk_o, rnd * 4 + cc)
                ada_some(1)
                tv = load_w16(w_in, 3584 + rnd * 512)
                proj_v(tv, 0, 512, 0, nbv_o, rnd * 512)
                drain()
                ada_some(1)
                set_mode("att")
                items = []
                for pr in range(2):
                    gp = rnd * 2 + pr
                    bi = gp % 2
                    P.dma("pool", biasB[bi], bB_t[:, :, gp * 512:(gp + 1) * 512], sem="biasB%d" % bi, writes=["biasB%d" % bi])
                    h0 = rnd * 4 + pr * 2
                    for grp in range(3):
                        items.append(att_item([h0, h0 + 1], [pr * 2, pr * 2 + 1], [pr * 256, pr * 256 + 128], [pr * 2, pr * 2 + 1], 1,
                                              None, lambda c, bi=bi: (biasB[bi][:, c, :], "biasB%d" % bi), grp))
                run_items(items, hooks=[lambda: ada_some(1), lambda: ada_some(1)])
            stop_at("oB", oT[:, 1], ["oT"])

            xkeys = ["x%d" % t for t in range(6)]
            newk = xkeys + ["sg0", "sg1", "tm0", "tm1", "t512_0", "t512_1", "onesf", "diag0", "diag1", "tA"]
            P.alias(newk, R1_ATT_KEYS)
            for t in range(6):
                P.dma("sp", xres[:, t, :], xm_t[t], sem="x%d" % t, writes=["x%d" % t])

            HALF = [(0, 384), (384, 384)]
            for cg in range(4):
                for mix in range(2):
                    wg = load_w16(w_in, (4608 if mix == 0 else 6656) + cg * 512)
                    wb = load_w8(w_bra if mix == 0 else w_brb, cg * 512)
                    for cc in range(4):
                        ch = cg * 4 + cc
                        for hi, (s0, sz) in enumerate(HALF):
                            bg = rot_main.next()
                            for k in range(KC):
                                P.op("pe", I("matmul", PS(bg, sz), lhsT=wg.k(k)[:, cc * 128:(cc + 1) * 128], rhs=hT[:, k, s0:s0 + sz],
                                             start=(k == 0), stop=(k == KC - 1)),
                                     reads=[wg.key(k), "hT%d" % k], writes=[bk(bg)], signal=(k == KC - 1 or k == 7))
                            by = (rot_a1 if hi == 0 else rot_a2).next()
                            for k in range(8):
                                P.op("pe", I("matmul", PS(by, sz), lhsT=wb.k(k)[:, cc * 128:(cc + 1) * 128], rhs=oT[:, mix, k, s0:s0 + sz],
                                             start=(k == 0), stop=(k == 7)),
                                     reads=[wb.key(k), "oT"], writes=[bk(by)], signal=(k == 7))
                            sg = sgs[hi][:, 0:sz]
                            P.op("act", I("activation", out=sg, in_=PS(bg, sz), func=AF.Sigmoid), reads=[bk(bg)], writes=["sg%d" % hi])
                            if mix == 0:
                                P.op("dve", I("tensor_tensor", out=tA[:, cc, s0:s0 + sz], in0=PS(by, sz), in1=sg, op=ALU.mult),
                                     reads=[bk(by), "sg%d" % hi], writes=["tA"])
                            else:
                                tm = tms[hi][:, 0:sz]
                                P.op("dve", I("tensor_tensor", out=tm, in0=PS(by, sz), in1=sg, op=ALU.mult),
                                     reads=[bk(by), "sg%d" % hi], writes=["tm%d" % hi])
                                P.op("dve", I("tensor_tensor", out=mTv[:, ch, s0:s0 + sz], in0=tm, in1=tA[:, cc, s0:s0 + sz], op=ALU.add),
                                     reads=["tm%d" % hi, "tA"], writes=["mT"])
            ada_finish()
            stop_at("mT", mTv, ["mT"])

            P.op("dve", I("memset", onesf, 1.0), writes=["onesf"])

            def build_G(mod_base, alias_from):
                P.alias(["G"], alias_from)
                di = 0
                for v in range(2):
                    for g4 in range(4):
                        b = rot_a1.next()
                        for j in range(4):
                            c = g4 * 4 + j
                            d_ = diag[di % 2]
                            dk = "diag%d" % (di % 2)
                            di += 1
                            P.op("dve", I("tensor_scalar", out=d_, in0=identf, scalar1=modT[:, mod_base + c, v:v + 1], scalar2=None, op0=ALU.mult),
                                 reads=["identf", "modT"], writes=[dk])
                            P.op("pe", I("matmul", PS(b, 128, j * 128), lhsT=onesf, rhs=d_, start=True, stop=True),
                                 reads=["onesf", dk], writes=[bk(b)], signal=True)
                        P.op("act", I("activation", out=Gt[:, v, g4 * 512:(g4 + 1) * 512], in_=PS(b, 512), func=AF.Copy),
                             reads=[bk(b)], writes=["G"])

            build_G(32, ["oT"])

            for cg in range(4):
                wt = load_w16(w_out, cg * 512)
                for t in range(6):
                    v = 0 if t < 4 else 1
                    b = rot_main.next()
                    for k in range(KC):
                        P.op("pe", I("matmul", PS(b, 512), lhsT=mTv[:, k, t * 128:(t + 1) * 128], rhs=wt.k(k),
                                     start=(k == 0), stop=(k == KC - 1)),
                             reads=[wt.key(k), "mT"], writes=[bk(b)], signal=(k == KC - 1 or k == 7))
                    i = (cg * 6 + t) % 2
                    tp = tmp512[i]
                    P.op("dve", I("tensor_tensor", out=tp, in0=PS(b, 512), in1=Gt[:, v, cg * 512:(cg + 1) * 512], op=ALU.mult),
                         reads=[bk(b), "G"], writes=["t512_%d" % i])
                    P.op("dve", I("tensor_tensor", out=xres[:, t, cg * 512:(cg + 1) * 512], in0=xres[:, t, cg * 512:(cg + 1) * 512], in1=tp, op=ALU.add),
                         reads=["t512_%d" % i, "x%d" % t], writes=["x%d" % t])
            stop_at("x1", xres, ["x%d" % t for t in range(6)])

            xnk2 = ["xnn%d" % i for i in range(6)]
            P.alias(["h2T%d" % i for i in range(16)] + xnk2, ["hT%d" % i for i in range(16)] + ["oT"])
            xni = 0
            for tl, vec in [([0, 1, 2, 3], 0), ([4, 5], 1)]:
                tiles, xb, xk = [], [], []
                for ti in tl:
                    tiles.append((xres[:, ti, :], "x%d" % ti, None))
                    xb.append(xnn[xni % 6])
                    xk.append(xnk2[xni % 6])
                    xni += 1
                norm_group(tiles, None, vec, 2, 3, h2T, "h2T", xb, xk, tl[0] * 128, tl[0])
            stop_at("h2T", h2T, ["h2T%d" % i for i in range(16)])
            build_G(80, ["G"])

            P.alias(["uT"], xnk2)
            P.alias(["rs0", "rs1", "tb0", "tb1"], ["sg0", "sg1", "tm0", "tm1", "t512_0", "t512_1"])
            ri = 0
            for fg in range(4):
                for t4 in range(4):
                    wt = load_w16(w_up, fg * 2048 + t4 * 512)
                    for cc in range(4):
                        fc = t4 * 4 + cc
                        for (s0, sz) in HALF:
                            b = rot_main.next()
                            for k in range(KC):
                                P.op("pe", I("matmul", PS(b, sz), lhsT=wt.k(k)[:, cc * 128:(cc + 1) * 128], rhs=h2T[:, k, s0:s0 + sz],
                                             start=(k == 0), stop=(k == KC - 1)),
                                     reads=[wt.key(k), "h2T%d" % k], writes=[bk(b)], signal=(k == KC - 1 or k == 7))
                            i = ri % 2
                            ri += 1
                            rs = rsb[i][:, 0:sz]
                            P.op("act", I("activation", out=rs, in_=PS(b, sz), func=AF.Relu), reads=[bk(b)], writes=["rs%d" % i])
                            P.op("dve", I("tensor_tensor", out=uT[:, fc, s0:s0 + sz], in0=rs, in1=rs, op=ALU.mult), reads=["rs%d" % i], writes=["uT"])
                for cg in range(4):
                    wt = load_w16(w_down, cg * 512, r0=fg * 2048)
                    for t in range(6):
                        v = 0 if t < 4 else 1
                        b = rot_a1.next() if (t % 2 == 0) else rot_a2.next()
                        for k in range(KC):
                            P.op("pe", I("matmul", PS(b, 512), lhsT=uT[:, k, t * 128:(t + 1) * 128], rhs=wt.k(k),
                                         start=(k == 0), stop=(k == KC - 1)),
                                 reads=[wt.key(k), "uT"], writes=[bk(b)], signal=(k == KC - 1 or k == 7))
                        i = (cg * 6 + t) % 2
                        tp = tmp512b[i]
                        P.op("dve", I("tensor_tensor", out=tp, in0=PS(b, 512), in1=Gt[:, v, cg * 512:(cg + 1) * 512], op=ALU.mult),
                             reads=[bk(b), "G"], writes=["tb%d" % i])
                        P.op("dve", I("tensor_tensor", out=xres[:, t, cg * 512:(cg + 1) * 512], in0=xres[:, t, cg * 512:(cg + 1) * 512], in1=tp, op=ALU.add),
                             reads=["tb%d" % i, "x%d" % t], writes=["x%d" % t])

            ym_t = y_main.rearrange("(t p) d -> t p d", p=128)
            for t in range(6):
                P.dma("sp", ym_t[t], xres[:, t, :], sem="x%d" % t, reads=["x%d" % t])
            P.wait_all("sp", ["x%d" % t for t in range(6)] + ["ost0", "ost1"])

        try:
            body()
        except _Stop:
            pass
        fin = []
        for e_ in Prog.CE:
            if P.cnt[e_] > 0:
                fin.append((P.esem[e_], P.cnt[e_]))
        P.q["sp"].append((fin, None, None))

        with nc.Block() as block:
            @block.sync
            def _(e):
                P.emit("sp", e)

            @block.gpsimd
            def _(e):
                P.emit("pool", e)

            @block.tensor
            def _(e):
                P.emit("pe", e)

            @block.scalar
            def _(e):
                P.emit("act", e)

            @block.vector
            def _(e):
                P.emit("dve", e)
        build_program.stats = dict(n={k: len(v) for k, v in P.q.items()}, waits=P.n_wait, arena=A.off, cnt=dict(P.cnt))
    return nc


GRID_W = 64
ROWS = 16


def _core_geometry(j):
    b = j // 4
    qq = j % 4
    ws = 0 if qq < 2 else 4
    own_rows = list(range(4 * qq, 4 * qq + 4))
    halo_rows = [r for r in range(ws, ws + 12) if r not in own_rows]
    rows = own_rows + halo_rows
    pos = np.concatenate([np.arange(r * GRID_W, (r + 1) * GRID_W) for r in rows])
    return b, qq, pos


def _static_tables(j, rpb):
    b, qq, pos = _core_geometry(j)
    row = (pos // GRID_W).astype(np.int64)
    col = (pos % GRID_W).astype(np.int64)
    n_freq = 32
    inv = (10000.0 ** (-np.arange(n_freq, dtype=np.float32) / n_freq)).astype(np.float32)
    ang_r = row[:, None].astype(np.float32) * inv[None, :]
    ang_c = col[:, None].astype(np.float32) * inv[None, :]
    cosT = np.zeros((128, 768), np.float32)
    sinT = np.zeros((128, 768), np.float32)
    cosT[0:32] = np.cos(ang_r).T
    cosT[32:64] = np.cos(ang_r).T
    cosT[64:96] = np.cos(ang_c).T
    cosT[96:128] = np.cos(ang_c).T
    sinT[0:32] = -np.sin(ang_r).T
    sinT[32:64] = np.sin(ang_r).T
    sinT[64:96] = -np.sin(ang_c).T
    sinT[96:128] = np.sin(ang_c).T
    qpos = pos[:256]
    valid = np.abs(qpos[None, :] - pos[:, None]) <= 128
    mA = np.where(valid, 0.0, NEGM).astype(np.float32)
    maskA = np.concatenate([mA, mA], axis=1)
    qr = row[:256]
    qc = col[:256]
    rstart = np.clip(qr - 4, 0, ROWS - 8)
    cstart = np.clip(qc - 8, 0, GRID_W - 16)
    kr = row[:, None]
    kcc = col[:, None]
    vr = (kr >= rstart[None, :]) & (kr < rstart[None, :] + 8)
    vc = (kcc >= cstart[None, :]) & (kcc < cstart[None, :] + 16)
    valid = vr & vc
    dr = np.clip(kr - qr[None, :] + 7, 0, 14)
    dc = np.clip(kcc - qc[None, :] + 15, 0, 30)
    bias = rpb[:, dr, dc]
    bias = np.where(valid[None], bias, np.float32(NEGM)).astype(np.float32)
    biasB = np.ascontiguousarray(np.transpose(bias, (1, 0, 2))).reshape(768, 2048)
    return cosT, sinT, maskA, biasB


def _perm():
    p = np.zeros((128, 128), np.float32)
    for d in range(128):
        s = d + 32 if (d % 64) < 32 else d - 32
        p[s, d] = 1.0
    return p


_NC_CACHE = {}


def kernel(x_prompt, x_sample, cache_a_k, cache_a_v, cache_b_k, cache_b_v, c, c_ctx,
           norm1_w, norm2_w, w_ada, b_ada, w_in, q_norm_a, k_norm_a, q_norm_b, k_norm_b,
           sink_a, rpb_b, w_br_a, w_br_b, w_out, w_up, w_down, _debug=None, _cores=None):
    f = lambda a: np.ascontiguousarray(np.asarray(a, dtype=np.float32))
    x_prompt, x_sample = f(x_prompt), f(x_sample)
    c, c_ctx = f(c), f(c_ctx)
    cores = list(range(NCORES)) if _cores is None else _cores
    key = "dbg" if _debug is not None else "main"
    if key not in _NC_CACHE:
        _NC_CACHE[key] = build_program(_debug)
    nc = _NC_CACHE[key]

    ident = np.eye(128, dtype=np.float32)
    perm = _perm()
    shared = dict(
        ident=ident, perm=perm,
        w_ada=f(w_ada)[0], w_in=f(w_in)[0], w_br_a=f(w_br_a)[0], w_br_b=f(w_br_b)[0],
        w_out=f(w_out)[0], w_up=f(w_up)[0], w_down=f(w_down)[0],
    )
    rpb = f(rpb_b)[0]
    in_maps = []
    for idx_core, j in enumerate(cores):
        b, qq, pos = _core_geometry(j)
        xm = np.concatenate([x_prompt[2 * j], x_prompt[2 * j + 1], x_sample[b][pos[:256]]], axis=0)
        xh = x_sample[b][pos[256:]]
        cvec = np.stack([c_ctx, c[b]], axis=0)
        cT = np.ascontiguousarray(cvec.reshape(2, 16, 128).transpose(2, 1, 0)).reshape(128, 32)
        pp = np.zeros((128, PP_W), np.float32)
        pp[:, PP_N1:PP_N1 + 16] = f(norm1_w)[0].reshape(16, 128).T
        pp[:, PP_N2:PP_N2 + 16] = f(norm2_w)[0].reshape(16, 128).T
        pp[:, PP_BADA:PP_BADA + 96] = f(b_ada)[0].reshape(96, 128).T
        pp[:, PP_QNA] = f(q_norm_a)[0]
        pp[:, PP_KNA] = f(k_norm_a)[0]
        pp[:, PP_QNB] = f(q_norm_b)[0]
        pp[:, PP_KNB] = f(k_norm_b)[0]
        pp[:, PP_SINK:PP_SINK + 8] = f(sink_a)[0][None, :]
        cosT, sinT, maskA, biasB = _static_tables(j, rpb)
        m = dict(shared)
        m.update(
            x_main=np.ascontiguousarray(xm), x_halo=np.ascontiguousarray(xh), cT=cT, pp=pp,
            cosT=cosT, sinT=sinT, maskA=maskA, biasB=biasB,
            cak=f(cache_a_k)[b, 0].reshape(256, 256), cav=f(cache_a_v)[b, 0].reshape(256, 256),
            cbk=f(cache_b_k)[b, 0].reshape(256, 1024), cbv=f(cache_b_v)[b, 0].reshape(256, 1024),
        )
        in_maps.append(m)

    res = run_bass_kernel_spmd(nc, in_maps, core_ids=list(range(len(cores))))
    if _debug is not None:
        return res

    y_prompt = np.zeros((16, 256, D), np.float32)
    y_sample = np.zeros((2, 1024, D), np.float32)
    nak = np.zeros((16, 1, 256, 2, 128), np.float32)
    nav = np.zeros((16, 1, 256, 2, 128), np.float32)
    nbk = np.zeros((16, 1, 256, 8, 128), np.float32)
    nbv = np.zeros((16, 1, 256, 8, 128), np.float32)
    for idx, j in enumerate(cores):
        r = res.results[idx]
        b, qq, pos = _core_geometry(j)
        ym = r["y_main"]
        y_prompt[2 * j] = ym[0:256]
        y_prompt[2 * j + 1] = ym[256:512]
        y_sample[b][pos[:256]] = ym[512:768]
        for s in range(2):
            nak[2 * j + s, 0] = r["nak"][s * 256:(s + 1) * 256].reshape(256, 2, 128)
            nav[2 * j + s, 0] = r["nav"][s * 256:(s + 1) * 256].reshape(256, 2, 128)
            nbk[2 * j + s, 0] = r["nbk"][s * 256:(s + 1) * 256].reshape(256, 8, 128)
            nbv[2 * j + s, 0] = r["nbv"][s * 256:(s + 1) * 256].reshape(256, 8, 128)
    return (y_prompt, y_sample, nak, nav, nbk, nbv)
```

```python
import os
import numpy as np
import concourse.bass as bass
import concourse.mybir as mybir
from concourse.bass_utils import run_bass_kernel_spmd

F32 = mybir.dt.float32
BF16 = mybir.dt.bfloat16
ALU = mybir.AluOpType
AF = mybir.ActivationFunctionType
AX = mybir.AxisListType

NCORES = 8
D = 2048
KC = 16
NMAIN = 768
NHALO = 512
NTOK = NMAIN + NHALO
EPS = 1e-6
NEGM = -30000.0
IN_W = 8704
DFF = 8192
CH_P = (0, 512)
CH_S = (512, 256)
CH_H = (768, 512)

PP_N1 = 0
PP_N2 = 16
PP_BADA = 32
PP_QNA = 128
PP_KNA = 129
PP_QNB = 130
PP_KNB = 131
PP_SINK = 132
PP_SEL = 140
PP_W = 142


class Prog:
    CE = ("pe", "act", "dve", "pool")
    ALL = ("pe", "act", "dve", "pool", "sp")

    def __init__(self, nc, esems, dma_sems):
        self.nc = nc
        self.q = {e: [] for e in self.ALL}
        self.cnt = {e: 0 for e in self.CE}
        self.esem = esems
        self.free_dsems = list(dma_sems)
        self.dsem = {}
        self.seen = {e: {} for e in self.ALL}
        self.lastw = {}
        self.readers = {}
        self.n_wait = 0

    def _need(self, eng, ev, waits):
        if ev is None:
            return
        semkey, handle, val, _ = ev
        if self.seen[eng].get(semkey, 0) >= val:
            return
        self.seen[eng][semkey] = val
        waits.append((handle, val))
        self.n_wait += 1

    def _deps(self, eng, reads, writes, waits):
        for k in reads:
            ev = self.lastw.get(k)
            if ev is not None and not (ev[3] == "pe" and eng == "pe"):
                self._need(eng, ev, waits)
        for k in writes:
            ev = self.lastw.get(k)
            if ev is not None and not (ev[3] == "pe" and eng == "pe"):
                self._need(eng, ev, waits)
            for ev in self.readers.get(k, ()):
                if not (ev[3] == "pe" and eng == "pe"):
                    self._need(eng, ev, waits)

    def _commit(self, ev, reads, writes):
        for k in reads:
            lst = self.readers.setdefault(k, [])
            lst[:] = [e for e in lst if e[0] != ev[0]]
            lst.append(ev)
        for k in writes:
            self.lastw[k] = ev
            self.readers[k] = []

    def op(self, eng, fn, reads=(), writes=(), signal=True):
        waits = []
        self._deps(eng, reads, writes, waits)
        if signal:
            self.cnt[eng] += 1
            val = self.cnt[eng]
        else:
            val = self.cnt[eng] + 1
        ev = ("E" + eng, self.esem[eng], val, eng)
        self._commit(ev, reads, writes)
        self.q[eng].append((waits, fn, (self.esem[eng], 1) if signal else None))

    def dma(self, eng, out, in_, sem, reads=(), writes=(), **kw):
        waits = []
        self._deps(eng, reads, writes, waits)
        if sem not in self.dsem:
            self.dsem[sem] = [self.free_dsems.pop(), 0]
        rec = self.dsem[sem]
        rec[1] += 16
        ev = ("D" + sem, rec[0], rec[1], "dma")
        self._commit(ev, reads, writes)
        self.q[eng].append((waits, (lambda e, o=out, i=in_, k=kw: e.dma_start(out=o, in_=i, **k)), (rec[0], 16)))

    def custom(self, eng, fn, sem, inc, reads=(), writes=()):
        waits = []
        self._deps(eng, reads, writes, waits)
        if sem not in self.dsem:
            self.dsem[sem] = [self.free_dsems.pop(), 0]
        rec = self.dsem[sem]
        rec[1] += inc
        ev = ("D" + sem, rec[0], rec[1], "dma")
        self._commit(ev, reads, writes)
        self.q[eng].append((waits, fn, (rec[0], inc)))

    def wait_all(self, eng, keys):
        waits = []
        for k in keys:
            self._need(eng, self.lastw.get(k), waits)
            for ev in self.readers.get(k, ()):
                self._need(eng, ev, waits)
        if waits:
            self.q[eng].append((waits, None, None))

    def alias(self, new_keys, old_keys):
        evs = []
        for k in old_keys:
            if self.lastw.get(k) is not None:
                evs.append(self.lastw[k])
            evs.extend(self.readers.get(k, ()))
        for k in new_keys:
            self.lastw[k] = None
            self.readers[k] = list(evs)

    def emit(self, eng_name, eng):
        for waits, fn, inc in self.q[eng_name]:
            for h, v in waits:
                eng.wait_ge(h, v)
            if fn is None:
                continue
            ins = fn(eng)
            if inc is not None:
                ins.then_inc(inc[0], inc[1])


def I(method, *a, **k):
    return lambda e: getattr(e, method)(*a, **k)


class Rot:
    def __init__(self, items):
        self.items = list(items)
        self.i = 0

    def next(self):
        v = self.items[self.i % len(self.items)]
        self.i += 1
        return v


def build_program(debug=None):
    nc = bass.Bass("TRN2", target_bir_lowering=False)

    def din(name, shape, dt=F32):
        return nc.dram_tensor(name, list(shape), dt, kind="ExternalInput").ap()

    def dout(name, shape, dt=F32):
        return nc.dram_tensor(name, list(shape), dt, kind="ExternalOutput").ap()

    x_main = din("x_main", [NMAIN, D])
    x_halo = din("x_halo", [NHALO, D])
    cT_d = din("cT", [128, 32])
    pp_d = din("pp", [128, PP_W])
    ident_d = din("ident", [128, 128])
    perm_d = din("perm", [128, 128])
    cos_d = din("cosT", [128, NMAIN])
    sin_d = din("sinT", [128, NMAIN])
    maskA_d = din("maskA", [NMAIN, 512])
    biasB_d = din("biasB", [NMAIN, 2048])
    cak_d = din("cak", [256, 256])
    cav_d = din("cav", [256, 256])
    cbk_d = din("cbk", [256, 1024])
    cbv_d = din("cbv", [256, 1024])
    w_ada = din("w_ada", [D, 6 * D])
    w_in = din("w_in", [D, IN_W])
    w_bra = din("w_br_a", [1024, D])
    w_brb = din("w_br_b", [1024, D])
    w_out = din("w_out", [D, D])
    w_up = din("w_up", [D, DFF])
    w_down = din("w_down", [DFF, D])

    y_main = dout("y_main", [NMAIN, D])
    nak_o = dout("nak", [512, 256])
    nav_o = dout("nav", [512, 256])
    nbk_o = dout("nbk", [512, 1024])
    nbv_o = dout("nbv", [512, 1024])
    dbg_o = None
    if debug is not None:
        dbg_o = dout("dbg", debug["shape"], BF16 if debug.get("dtype") == "bf16" else F32)

    import contextlib
    es = contextlib.ExitStack()
    with es:
        ARENA_W = 53120
        arena = es.enter_context(nc.sbuf_tensor("arena", [128, ARENA_W], F32))
        ps = es.enter_context(nc.psum_tensor("ps", [128, 8, 512], F32))
        sem_names = ["pe", "act", "dve", "pool"]
        esems = {n: es.enter_context(nc.semaphore("s_" + n)) for n in sem_names}
        dsems = [es.enter_context(nc.semaphore("d%d" % i)) for i in range(70)]
        P = Prog(nc, esems, dsems)

        class Arena:
            def __init__(self):
                self.off = 0

            def take(self, nbytes):
                o = self.off
                self.off += (nbytes + 63) // 64 * 64
                assert self.off <= ARENA_W * 4, ("arena overflow", self.off)
                return o

        def view(off_b, shape, dt):
            esz = 2 if dt == BF16 else 4
            n = int(np.prod(shape))
            assert off_b % 4 == 0
            w0 = off_b // 4
            nw = (n * esz + 3) // 4
            ap = arena[:, w0:w0 + nw]
            if dt == BF16:
                ap = ap.bitcast(BF16)
            if len(shape) == 2:
                ap = ap.rearrange("p (a b) -> p a b", a=shape[0])
            elif len(shape) == 3:
                ap = ap.rearrange("p (a b c) -> p a b c", a=shape[0], b=shape[1])
            return ap

        A = Arena()
        RING_N = 6
        ring_off = [A.take(8192) for _ in range(RING_N)]
        o_pp = A.take(PP_W * 4)
        o_identf = A.take(512)
        o_identb = A.take(256)
        o_permf = A.take(512)
        o_onesb = A.take(256)
        o_modT = A.take(96 * 2 * 4)
        o_ab = A.take(4 * 2 * 16 * 4)
        o_small = A.take(256)
        o_esink = A.take(32)
        o_cT = A.take(128)
        o_scT = A.take(64)

        pp = view(o_pp, [PP_W], F32)
        identf = view(o_identf, [128], F32)
        identb = view(o_identb, [128], BF16)
        permf = view(o_permf, [128], F32)
        onesb = view(o_onesb, [128], BF16)
        modT = view(o_modT, [96, 2], F32)
        abv = view(o_ab, [4, 2, 16], F32)
        small = view(o_small, [64], F32)
        esink = view(o_esink, [8], F32)
        epsc = small[:, 20:21]
        epsc128 = small[:, 21:22]
        zeroc = small[:, 22:23]

        R1 = A.take(67584)
        R2 = A.take(65536)
        mT_off = A.take(KC * NMAIN * 2)
        mTv = view(mT_off, [KC, NMAIN], BF16)
        hT = view(R2, [KC, NTOK], BF16)
        oT = view(R2 + 40960, [2, 8, NMAIN], BF16)
        Gt = view(R2 + 65536 - 16384, [2, 2048], F32)
        h2T = view(R2, [KC, NMAIN], BF16)
        uT = view(R2 + 24576, [KC, NMAIN], BF16)
        xnn = [view(R2 + 24576 + i * 4096, [2048], BF16) for i in range(6)]
        xs_ = [view(R1 + i * 8192, [2048], F32) for i in range(3)]
        xn_ = [view(R1 + 24576 + i * 4096, [2048], BF16) for i in range(10)]
        o = R1
        qT = view(o, [4, NMAIN], BF16); o += 4 * NMAIN * 2
        kT = view(o, [4, NTOK], BF16); o += 4 * NTOK * 2
        vS = view(o, [10, 512], BF16); o += 10 * 512 * 2
        NPT = 8
        pT = [view(o + i * 1024, [512], BF16) for i in range(NPT)]; o += NPT * 1024
        m = o
        maskA = view(m, [6, 512], BF16)
        cosT = view(m + 6144, [NMAIN], F32)
        sinT = view(m + 9216, [NMAIN], F32)
        biasB = [view(m + i * 6144, [6, 512], BF16) for i in range(2)]
        kc_T = view(m + 12288, [4, 256], BF16)
        vc_S = view(m + 14336, [2, 512], BF16)
        kc_S = view(m + 16384, [2, 512], BF16)
        f32s = [view(m + 18432 + i * 2048, [512], F32) for i in range(3)]
        o += 24576
        ost = [view(o + i * 2048, [512], F32) for i in range(2)]; o += 2 * 2048
        sqb = [view(o + i * 1024, [512], BF16) for i in range(2)]; o += 2 * 1024
        rl_ = [view(o + i * 1024, [256], F32) for i in range(2)]; o += 2 * 1024
        assert o - R1 <= 67584, o - R1
        R1_ATT_KEYS = (["qT%d" % i for i in range(4)] + ["kT%d" % i for i in range(4)] + ["vS", "kcT", "vcS", "kcS", "f32s0", "f32s1", "f32s2", "ost0", "ost1", "sqb0", "sqb1",
                        "rl0", "rl1", "biasB0", "biasB1", "maskA", "cosT", "sinT"] + ["pT%d" % i for i in range(NPT)])
        xres = view(R1, [6, 2048], F32)
        pb_scr = R1 + 49152
        sgs = [view(pb_scr + i * 1536, [384], F32) for i in range(2)]
        tms = [view(pb_scr + 3072 + i * 1536, [384], F32) for i in range(2)]
        tmp512 = [view(pb_scr + 6144 + i * 2048, [512], F32) for i in range(2)]
        onesf = view(pb_scr + 10240, [128], F32)
        diag = [view(pb_scr + 10752 + i * 512, [128], F32) for i in range(2)]
        tA = view(pb_scr + 11776, [4, NMAIN], BF16)
        rsb = [view(pb_scr + i * 1024, [384], BF16) for i in range(2)]
        tmp512b = [view(pb_scr + 4096 + i * 2048, [512], F32) for i in range(2)]

        def PS(bank, n=512, off=0):
            return ps[:, bank, off:off + n]

        def PSB(bank, n, off=0):
            return ps[:, bank, :].bitcast(BF16)[:, off:off + n]

        rot_main = Rot([0, 1, 2, 3])
        rot_a1 = Rot([4, 5])
        rot_a2 = Rot([6, 7])

        def bk(b):
            return "ps%d" % b

        def set_mode(m):
            if m == "proj":
                rot_main.items, rot_a1.items, rot_a2.items = [0, 1, 2, 3, 4, 5], [6], [7]
            else:
                rot_main.items, rot_a1.items, rot_a2.items = [0, 1, 2, 3], [4, 5], [6, 7]

        ring_state = {"n": 0}

        class WT:
            def __init__(self, slots):
                self.slots = slots

            def k(self, k):
                return view(ring_off[self.slots[k // 8]], [8, 512], BF16)[:, k % 8, :]

            def key(self, k):
                return "ring%d" % self.slots[k // 8]

        def load_entry(src):
            n = ring_state["n"]
            ring_state["n"] += 1
            slot = n % RING_N
            key = "ring%d" % slot
            P.dma("pool", view(ring_off[slot], [8, 512], BF16), src, sem=key, writes=[key])
            return slot

        def wsrc(w, r0, nr, c0, ncol):
            return w[r0:r0 + nr, c0:c0 + ncol].rearrange("(k p) n -> p k n", p=128)

        def load_w16(w, c0, r0=0):
            n = ring_state["n"]
            slot = n % RING_N
            if slot % 2 == 0:
                ring_state["n"] += 2
                k0, k1 = "ring%d" % slot, "ring%d" % (slot + 1)
                dst = view(ring_off[slot], [16, 512], BF16)
                P.dma("pool", dst, wsrc(w, r0, 2048, c0, 512), sem=k0, writes=[k0, k1])
                return WT([slot, slot + 1])
            return WT([load_entry(wsrc(w, r0, 1024, c0, 512)), load_entry(wsrc(w, r0 + 1024, 1024, c0, 512))])

        def load_w8(w, c0):
            return WT([load_entry(wsrc(w, 0, 1024, c0, 512))])

        class _Stop(Exception):
            pass

        def stop_at(name, ap, keys):
            if debug is not None and debug["at"] == name:
                P.dma("sp", dbg_o, ap, sem="dbg", reads=keys)
                P.wait_all("sp", keys)
                raise _Stop()

        chains = []

        def tick():
            for ch in list(chains):
                step = ch.pop(0)
                step()
                if not ch:
                    chains.remove(ch)

        def drain():
            while chains:
                tick()

        def body():
            P.dma("sp", pp, pp_d, sem="pp", writes=["pp"])
            P.dma("sp", identf, ident_d, sem="identf", writes=["identf"])
            P.dma("sp", permf, perm_d, sem="permf", writes=["permf"])
            P.dma("pool", identb, ident_d, sem="identb", writes=["identb"])
            P.op("dve", I("memset", onesb, 1.0), writes=["onesb"])
            P.op("dve", I("memset", epsc, EPS), writes=["epsc"])
            P.op("dve", I("memset", epsc128, 128.0 * EPS), writes=["epsc"])
            P.op("dve", I("memset", zeroc, 0.0), writes=["epsc"])
            cTf = view(o_cT, [32], F32)
            scTflat = view(o_scT, [32], BF16)
            scTb = scTflat.rearrange("p (k v) -> p k v", v=2)
            P.dma("sp", cTf, cT_d, sem="cT", writes=["cTf"])
            P.op("act", I("activation", out=scTflat, in_=cTf, func=AF.Silu), reads=["cTf"], writes=["scT"])
            P.op("act", I("activation", out=esink, in_=pp[:, PP_SINK:PP_SINK + 8], func=AF.Exp), reads=["pp"], writes=["esink"])

            def norm_group(tiles, src_fn, vec, kindA, kindB, dstT, dst_key, xbufs, xkeys, tok0, stat0):
                norm_stats(tiles, xbufs, xkeys, stat0)
                norm_tr(len(tiles), vec, kindA, kindB, dstT, dst_key, xbufs, xkeys, tok0)

            def norm_stats(tiles, xbufs, xkeys, stat0):
                for i, (xin, xkey, loader) in enumerate(tiles):
                    if loader is not None:
                        loader()
                    xnb, xnk = xbufs[i], xkeys[i]
                    si = (stat0 + i) % 10
                    ssq = small[:, 24 + si:25 + si]
                    rst = small[:, 36 + si:37 + si]
                    sk = "ssq%d" % si
                    P.op("act", I("activation", out=xnb, in_=xin, func=AF.Square, accum_out=ssq), reads=[xkey], writes=[xnk, sk])
                    P.op("act", I("activation", out=rst, in_=ssq, func=AF.Sqrt, scale=1.0 / D, bias=epsc), reads=[sk, "epsc"], writes=[sk + "r"])
                    P.op("dve", I("reciprocal", out=rst, in_=rst), reads=[sk + "r"], writes=[sk + "r"])
                    P.op("dve", I("tensor_scalar", out=xnb, in0=xin, scalar1=rst, scalar2=None, op0=ALU.mult), reads=[xkey, sk + "r"], writes=[xnk])

            def norm_tr(nt, vec, kindA, kindB, dstT, dst_key, xbufs, xkeys, tok0):
                for c in range(KC):
                    b = rot_main.next()
                    for i in range(nt):
                        P.op("pe", I("transpose", PSB(b, 128, i * 128), xbufs[i][:, c * 128:(c + 1) * 128], identb),
                             reads=[xkeys[i], "identb"], writes=[bk(b)], signal=(i == nt - 1))
                    dst = dstT[:, c, tok0:tok0 + nt * 128]
                    if c % 2 == 0:
                        P.op("act", I("activation", out=dst, in_=PSB(b, nt * 128), func=AF.Identity,
                                      scale=abv[:, kindA, vec, c:c + 1], bias=abv[:, kindB, vec, c:c + 1]),
                             reads=[bk(b), "ab", "ab2"], writes=[dst_key + str(c)])
                    else:
                        P.op("dve", I("tensor_scalar", out=dst, in0=PSB(b, nt * 128), scalar1=abv[:, kindA, vec, c:c + 1],
                                      scalar2=abv[:, kindB, vec, c:c + 1], op0=ALU.mult, op1=ALU.add),
                             reads=[bk(b), "ab", "ab2"], writes=[dst_key + str(c)])

            xm_t = x_main.rearrange("(t p) d -> t p d", p=128)
            xh_t = x_halo.rearrange("(t p) d -> t p d", p=128)
            norm1_groups = []
            xi = 0
            for tl, vec in [([0, 1, 2, 3], 0), ([4, 5], 1), ([6, 7, 8, 9], 1)]:
                tiles, xb, xk = [], [], []
                for ti in tl:
                    bi = xi % 3
                    xi += 1
                    src = xm_t[ti] if ti < 6 else xh_t[ti - 6]
                    ld = (lambda bi=bi, src=src: P.dma("sp", xs_[bi], src, sem="xs%d" % bi, writes=["xs%d" % bi]))
                    tiles.append((xs_[bi], "xs%d" % bi, ld))
                    xb.append(xn_[ti])
                    xk.append("xn%d" % ti)
                norm_stats(tiles, xb, xk, tl[0])
                norm1_groups.append((tl, vec, xb, xk))

            ada_pending = list(range(24))

            def ada_some(n):
                for _ in range(min(n, len(ada_pending))):
                    t = ada_pending.pop(0)
                    wt = load_w16(w_ada, t * 512)
                    b = rot_a2.next()
                    for cc in range(4):
                        for k in range(KC):
                            P.op("pe", I("matmul", ps[:, b, cc * 2:cc * 2 + 2], lhsT=wt.k(k)[:, cc * 128:(cc + 1) * 128],
                                         rhs=scTb[:, k, :], start=(k == 0), stop=(k == KC - 1)),
                                 reads=[wt.key(k), "scT"], writes=[bk(b)], signal=(k == KC - 1 or k == 7))
                    for v in range(2):
                        P.op("dve", I("tensor_tensor", out=modT[:, t * 4:(t + 1) * 4, v],
                                      in0=ps[:, b, 0:8].rearrange("p (c v) -> p c v", v=2)[:, :, v],
                                      in1=pp[:, PP_BADA + t * 4:PP_BADA + t * 4 + 4], op=ALU.add),
                             reads=[bk(b), "pp"], writes=["modT"])

            ada_some(8)
            for v in range(2):
                P.op("dve", I("scalar_tensor_tensor", out=abv[:, 0, v, :], in0=modT[:, 16:32, v], scalar=1.0,
                              in1=pp[:, PP_N1:PP_N1 + 16], op0=ALU.add, op1=ALU.mult), reads=["modT", "pp"], writes=["ab"])
                P.op("dve", I("tensor_copy", out=abv[:, 1, v, :], in_=modT[:, 0:16, v]), reads=["modT"], writes=["ab"])
            SQ128 = float(np.sqrt(128.0))
            P.op("dve", I("tensor_copy", out=small[:, 0:1], in_=pp[:, PP_QNA:PP_QNA + 1]), reads=["pp"], writes=["small"])
            P.op("dve", I("tensor_scalar", out=small[:, 1:2], in0=pp[:, PP_KNA:PP_KNA + 1], scalar1=SQ128, scalar2=None, op0=ALU.mult), reads=["pp"], writes=["small"])
            P.op("dve", I("tensor_copy", out=small[:, 2:3], in_=pp[:, PP_QNB:PP_QNB + 1]), reads=["pp"], writes=["small"])
            P.op("dve", I("tensor_scalar", out=small[:, 3:4], in0=pp[:, PP_KNB:PP_KNB + 1], scalar1=SQ128, scalar2=None, op0=ALU.mult), reads=["pp"], writes=["small"])

            def ada_finish():
                ada_some(len(ada_pending))
                for v in range(2):
                    P.op("dve", I("scalar_tensor_tensor", out=abv[:, 2, v, :], in0=modT[:, 64:80, v], scalar=1.0,
                                  in1=pp[:, PP_N2:PP_N2 + 16], op0=ALU.add, op1=ALU.mult), reads=["modT", "pp"], writes=["ab2"])
                    P.op("dve", I("tensor_copy", out=abv[:, 3, v, :], in_=modT[:, 48:64, v]), reads=["modT"], writes=["ab2"])
            stop_at("ada", modT, ["modT", "ab", "small"])

            tkv = load_w16(w_in, 1024)
            for (tl, vec, xb, xk) in norm1_groups:
                norm_tr(len(tl), vec, 0, 1, hT, "hT", xb, xk, tl[0] * 128)
            stop_at("hT", hT, ["hT%d" % i for i in range(16)])

            P.alias(R1_ATT_KEYS, ["xs0", "xs1", "xs2"] + ["xn%d" % i for i in range(10)])
            P.dma("pool", maskA, maskA_d.rearrange("(c p) n -> p c n", p=128), sem="maskA", writes=["maskA"])
            P.dma("sp", cosT, cos_d, sem="cosT", writes=["cosT"])
            P.dma("sp", sinT, sin_d, sem="sinT", writes=["sinT"])

            cstate = {"i": 0, "ost": 0}

            def proj_fm(wt, cc, chunks, make_chain):
                for (s0, sz) in chunks:
                    b = rot_main.next()
                    for k in range(KC):
                        P.op("pe", I("matmul", PS(b, sz), lhsT=wt.k(k)[:, cc * 128:(cc + 1) * 128], rhs=hT[:, k, s0:s0 + sz],
                                     start=(k == 0), stop=(k == KC - 1)),
                             reads=[wt.key(k), "hT%d" % k], writes=[bk(b)], signal=(k == KC - 1 or k == 7))
                    tick()
                    make_chain(b, s0, sz)

            def norm_steps(b, sz, wcol, out_ap, out_key):
                i = cstate["i"] % 2
                cstate["i"] += 1
                sq = sqb[i][:, 0:sz]
                sqk = "sqb%d" % i
                P.op("act", I("activation", out=sq, in_=PS(b, sz), func=AF.Square), reads=[bk(b)], writes=[sqk])

                def step1():
                    b2 = rot_a1.next()
                    P.op("pe", I("matmul", PS(b2, sz), lhsT=onesb, rhs=sq, start=True, stop=True), reads=[sqk, "onesb"], writes=[bk(b2)])
                    rr = f32s[2][:, 0:sz]
                    P.op("act", I("activation", out=rr, in_=PS(b2, sz), func=AF.Ln, scale=1.0, bias=epsc128), reads=[bk(b2), "epsc"], writes=["f32s2"])
                    P.op("act", I("activation", out=rr, in_=rr, func=AF.Exp, scale=-0.5), reads=["f32s2"], writes=["f32s2"])
                    P.op("dve", I("scalar_tensor_tensor", out=out_ap, in0=PS(b, sz), scalar=wcol, in1=rr, op0=ALU.mult, op1=ALU.mult),
                         reads=[bk(b), "f32s2", "small"], writes=[out_key])
                return step1

            def rope_step(src_ap, src_key, e0, sz, out_ap, out_key):
                def step():
                    b3 = rot_a2.next()
                    P.op("pe", I("matmul", PS(b3, sz), lhsT=permf, rhs=src_ap, start=True, stop=True), reads=[src_key, "permf"], writes=[bk(b3)])
                    t1 = f32s[2][:, 0:sz]
                    P.op("dve", I("tensor_tensor", out=t1, in0=PS(b3, sz), in1=sinT[:, e0:e0 + sz], op=ALU.mult), reads=[bk(b3), "sinT"], writes=["f32s2"])
                    P.op("dve", I("tensor_tensor", out=src_ap, in0=src_ap, in1=cosT[:, e0:e0 + sz], op=ALU.mult), reads=[src_key, "cosT"], writes=[src_key])
                    P.op("dve", I("tensor_tensor", out=out_ap, in0=src_ap, in1=t1, op=ALU.add), reads=[src_key, "f32s2"], writes=[out_key])
                return step

            def kout_step(kn_ap, kn_key, dst_dram, head_col, kt_dst, kt_key):
                def step():
                    b4 = rot_a2.next()
                    for t in range(4):
                        P.op("pe", I("transpose", PS(b4, 128, t * 128), kn_ap[:, t * 128:(t + 1) * 128], identf),
                             reads=[kn_key, "identf"], writes=[bk(b4)], signal=(t == 3))
                    i = cstate["ost"] % 2
                    cstate["ost"] += 1
                    st = ost[i]
                    P.op("act", I("activation", out=st, in_=PS(b4, 512), func=AF.Copy), reads=[bk(b4)], writes=["ost%d" % i])
                    P.dma("sp", dst_dram.rearrange("(t p) n -> p t n", p=128)[:, :, head_col * 128:(head_col + 1) * 128],
                          st.rearrange("p (t d) -> p t d", t=4), sem="ost%d" % i, reads=["ost%d" % i])
                    P.op("act", I("activation", out=kt_dst, in_=kn_ap, func=AF.Copy), reads=[kn_key], writes=[kt_key])
                return step

            tmp_i = [0]

            def _nop():
                pass

            def proj_q(wt, cc, hslot, wcol, do_rope):
                def mk(b, s0, sz):
                    dst = qT[:, hslot, s0:s0 + sz]
                    if do_rope and s0 == CH_S[0]:
                        fi = tmp_i[0] % 2
                        tmp_i[0] += 1
                        tmp = f32s[fi][:, 0:sz]
                        chains.append([norm_steps(b, sz, wcol, tmp, "f32s%d" % fi), _nop, rope_step(tmp, "f32s%d" % fi, 0, sz, dst, "qT%d" % hslot)])
                    else:
                        chains.append([norm_steps(b, sz, wcol, dst, "qT%d" % hslot)])
                proj_fm(wt, cc, [CH_P, CH_S], mk)

            def proj_k(wt, cc, hslot, wcol, do_rope, dst_dram, head_col):
                def mk(b, s0, sz):
                    fi = tmp_i[0] % 2
                    tmp_i[0] += 1
                    tmp = f32s[fi][:, 0:sz]
                    tk = "f32s%d" % fi
                    dst = kT[:, hslot, s0:s0 + sz]
                    st1 = norm_steps(b, sz, wcol, tmp, tk)
                    if s0 == 0:
                        chains.append([st1, _nop, kout_step(tmp, tk, dst_dram, head_col, dst, "kT%d" % hslot)])
                    elif do_rope:
                        chains.append([st1, _nop, rope_step(tmp, tk, s0 - 512, sz, dst, "kT%d" % hslot)])
                    else:
                        def cp(tmp=tmp, tk=tk, dst=dst):
                            P.op("act", I("activation", out=dst, in_=tmp, func=AF.Copy), reads=[tk], writes=["kT%d" % hslot])
                        chains.append([st1, _nop, cp])
                proj_fm(wt, cc, [CH_P, CH_S, CH_H], mk)

            def proj_v(wt, c0, ncols, vcol0, dst_dram, dcol0):
                for ti in range(10):
                    b = rot_main.next()
                    for k in range(KC):
                        P.op("pe", I("matmul", PS(b, ncols), lhsT=hT[:, k, ti * 128:(ti + 1) * 128], rhs=wt.k(k)[:, c0:c0 + ncols],
                                     start=(k == 0), stop=(k == KC - 1)),
                             reads=[wt.key(k), "hT%d" % k], writes=[bk(b)], signal=(k == KC - 1 or k == 7))
                    tick()
                    if ti < 4:
                        i = cstate["ost"] % 2
                        cstate["ost"] += 1
                        st = ost[i][:, 0:ncols]
                        P.op("act", I("activation", out=st, in_=PS(b, ncols), func=AF.Copy), reads=[bk(b)], writes=["ost%d" % i])
                        P.op("dve", I("tensor_copy", out=vS[:, ti, vcol0:vcol0 + ncols], in_=st), reads=["ost%d" % i], writes=["vS"])
                        P.dma("sp", dst_dram[ti * 128:(ti + 1) * 128, dcol0:dcol0 + ncols], st, sem="ost%d" % i, reads=["ost%d" % i])
                    else:
                        P.op("dve", I("tensor_copy", out=vS[:, ti, vcol0:vcol0 + ncols], in_=PS(b, ncols)), reads=[bk(b)], writes=["vS"])

            pt_i = [0]

            def att_item(pair_heads, kslots, vcols, hs_slots, mixer, sink_cols, sample_bias, grp):
                same_kv = (kslots[0] == kslots[1]) and (vcols[0] == vcols[1])
                q0 = grp * 256
                if grp < 2:
                    chunks = [("tok", grp * 256 + c * 128, grp * 2 + c, None) for c in range(2)]
                else:
                    chunks = [("tok", 512 + c * 128, 4 + c, c) for c in range(6)] + [("ctx", c * 128, c, None) for c in range(2)]
                pts = []

                def scores():
                    for (kind, k0, vt, bc) in chunks:
                        b = rot_main.next()
                        first = True
                        if bc is not None:
                            bap, bkey = sample_bias(bc)
                            P.op("pe", I("matmul", PS(b, 512), lhsT=identb, rhs=bap, start=True, stop=False),
                                 reads=[bkey, "identb"], writes=[bk(b)], signal=False)
                            first = False
                        if same_kv:
                            lk, lkey = (kT[:, kslots[0], k0:k0 + 128], "kT%d" % kslots[0]) if kind == "tok" else (kc_T[:, kslots[0], k0:k0 + 128], "kcT")
                            P.op("pe", I("matmul", PS(b, 512).rearrange("p (a q) -> p a q", a=2), lhsT=lk,
                                         rhs=qT[:, hs_slots[0]:hs_slots[0] + 2, q0:q0 + 256], start=first, stop=True),
                                 reads=[lkey, "qT%d" % hs_slots[0], "qT%d" % (hs_slots[0] + 1)], writes=[bk(b)])
                        else:
                            for i in range(2):
                                lk, lkey = (kT[:, kslots[i], k0:k0 + 128], "kT%d" % kslots[i]) if kind == "tok" else (kc_T[:, kslots[i], k0:k0 + 128], "kcT")
                                P.op("pe", I("matmul", PS(b, 256, i * 256), lhsT=lk, rhs=qT[:, hs_slots[i], q0:q0 + 256],
                                             start=first, stop=(first or i == 1)),
                                     reads=[lkey, "qT%d" % hs_slots[i]], writes=[bk(b)], signal=(i == 1))
                        pi = pt_i[0] % NPT
                        pt_i[0] += 1
                        P.op("act", I("activation", out=pT[pi], in_=PS(b, 512), func=AF.Exp), reads=[bk(b)], writes=["pT%d" % pi])
                        pts.append((pi, kind, vt))

                def finish():
                    bo = rot_a1.next()
                    bl = rot_a2.next()
                    n = len(pts)
                    for i in range(2):
                        if same_kv and i == 1:
                            break
                        for ci, (pi, kind, vt) in enumerate(pts):
                            lv, vkey = (vS[:, vt, vcols[i]:vcols[i] + 128], "vS") if kind == "tok" else (vc_S[:, vt, vcols[i]:vcols[i] + 128], "vcS")
                            if same_kv:
                                P.op("pe", I("matmul", PS(bo, 512), lhsT=lv, rhs=pT[pi], start=(ci == 0), stop=(ci == n - 1)),
                                     reads=[vkey, "pT%d" % pi], writes=[bk(bo)], signal=(ci == n - 1))
                            else:
                                P.op("pe", I("matmul", PS(bo, 256, i * 256), lhsT=lv, rhs=pT[pi][:, i * 256:(i + 1) * 256],
                                             start=(ci == 0), stop=(ci == n - 1)),
                                     reads=[vkey, "pT%d" % pi], writes=[bk(bo)], signal=(ci == n - 1))
                    for ci, (pi, kind, vt) in enumerate(pts):
                        P.op("pe", I("matmul", PS(bl, 512), lhsT=onesb, rhs=pT[pi], start=(ci == 0), stop=(ci == n - 1)),
                             reads=["onesb", "pT%d" % pi], writes=[bk(bl)], signal=(ci == n - 1))
                    for i in range(2):
                        rl = rl_[i]
                        sc = esink[:, sink_cols[i]:sink_cols[i] + 1] if sink_cols is not None else zeroc
                        P.op("act", I("activation", out=rl, in_=PS(bl, 256, i * 256), func=AF.Ln, scale=1.0, bias=sc),
                             reads=[bk(bl), "esink", "epsc"], writes=["rl%d" % i])
                        P.op("act", I("activation", out=rl, in_=rl, func=AF.Exp, scale=-1.0), reads=["rl%d" % i], writes=["rl%d" % i])
                        P.op("dve", I("tensor_tensor", out=oT[:, mixer, pair_heads[i], q0:q0 + 256], in0=PS(bo, 256, i * 256), in1=rl, op=ALU.mult),
                             reads=[bk(bo), "rl%d" % i], writes=["oT"])
                return scores, finish, len(chunks)

            def run_items(items, hooks=()):
                prev = None
                prev_n = 0
                hook_at = {}
                for hi_, h in enumerate(hooks):
                    hook_at[(hi_ + 1) * len(items) // (len(hooks) + 1)] = h
                for ii_, (sc, fin, n) in enumerate(items):
                    if ii_ in hook_at:
                        hook_at[ii_]()
                    if prev is not None and prev_n + n > NPT:
                        prev()
                        prev = None
                    sc()
                    if prev is not None:
                        prev()
                    prev = fin
                    prev_n = n
                if prev is not None:
                    prev()

            def load_ctx(k_d, v_d, c0, ncols, nheads):
                P.dma("pool", kc_S[:, :, 0:ncols], k_d[:, c0:c0 + ncols].rearrange("(c p) n -> p c n", p=128), sem="kcS", writes=["kcS"])
                P.dma("pool", vc_S[:, :, 0:ncols], v_d[:, c0:c0 + ncols].rearrange("(c p) n -> p c n", p=128), sem="vcS", writes=["vcS"])
                for h in range(nheads):
                    b = rot_a2.next()
                    for c in range(2):
                        P.op("pe", I("transpose", PSB(b, 128, c * 128), kc_S[:, c, h * 128:(h + 1) * 128], identb),
                             reads=["kcS", "identb"], writes=[bk(b)], signal=(c == 1))
                    P.op("dve", I("tensor_copy", out=kc_T[:, h, :], in_=PSB(b, 256)), reads=[bk(b)], writes=["kcT"])

            Wq = small[:, 0:1]
            Wk = small[:, 1:2]
            Wqb = small[:, 2:3]
            Wkb = small[:, 3:4]
            set_mode("proj")
            load_ctx(cak_d, cav_d, 0, 256, 2)
            for h in range(2):
                proj_k(tkv, h, h, Wk, True, nak_o, h)
            proj_v(tkv, 256, 256, 0, nav_o, 0)
            ada_some(1)
            for rnd in range(2):
                set_mode("proj")
                tq = load_w16(w_in, rnd * 512)
                for cc in range(4):
                    proj_q(tq, cc, cc, Wq, True)
                drain()
                ada_some(1)
                set_mode("att")
                items = []
                for pr in range(2):
                    h0 = rnd * 4 + pr * 2
                    kvh = h0 // 4
                    for grp in range(3):
                        items.append(att_item([h0, h0 + 1], [kvh, kvh], [kvh * 128, kvh * 128], [pr * 2, pr * 2 + 1], 0,
                                              [h0, h0 + 1], lambda c: (maskA[:, c, :], "maskA"), grp))
                run_items(items, hooks=[lambda: ada_some(1), lambda: ada_some(1)])
            stop_at("oA", oT[:, 0], ["oT"])

            P.alias(["biasB0", "biasB1"], ["maskA", "cosT", "sinT"])
            bB_t = biasB_d.rearrange("(c p) n -> p c n", p=128)
            for rnd in range(2):
                set_mode("proj")
                tq = load_w16(w_in, 1536 + rnd * 512)
                load_ctx(cbk_d, cbv_d, rnd * 512, 512, 4)
                for cc in range(4):
                    proj_q(tq, cc, cc, Wqb, False)
                ada_some(1)
                tk = load_w16(w_in, 2560 + rnd * 512)
                for cc in range(4):
                    proj_k(tk, cc, cc, Wkb, False, nbk_o, rnd * 4 + cc)
                ada_some(1)
                tv = load_w16(w_in, 3584 + rnd * 512)
                proj_v(tv, 0, 512, 0, nbv_o, rnd * 512)
                drain()
                ada_some(1)
                set_mode("att")
                items = []
                for pr in range(2):
                    gp = rnd * 2 + pr
                    bi = gp % 2
                    P.dma("pool", biasB[bi], bB_t[:, :, gp * 512:(gp + 1) * 512], sem="biasB%d" % bi, writes=["biasB%d" % bi])
                    h0 = rnd * 4 + pr * 2
                    for grp in range(3):
                        items.append(att_item([h0, h0 + 1], [pr * 2, pr * 2 + 1], [pr * 256, pr * 256 + 128], [pr * 2, pr * 2 + 1], 1,
                                              None, lambda c, bi=bi: (biasB[bi][:, c, :], "biasB%d" % bi), grp))
                run_items(items, hooks=[lambda: ada_some(1), lambda: ada_some(1)])
            stop_at("oB", oT[:, 1], ["oT"])

            xkeys = ["x%d" % t for t in range(6)]
            newk = xkeys + ["sg0", "sg1", "tm0", "tm1", "t512_0", "t512_1", "onesf", "diag0", "diag1", "tA"]
            P.alias(newk, R1_ATT_KEYS)
            for t in range(6):
                P.dma("sp", xres[:, t, :], xm_t[t], sem="x%d" % t, writes=["x%d" % t])

            HALF = [(0, 384), (384, 384)]
            for cg in range(4):
                for mix in range(2):
                    wg = load_w16(w_in, (4608 if mix == 0 else 6656) + cg * 512)
                    wb = load_w8(w_bra if mix == 0 else w_brb, cg * 512)
                    for cc in range(4):
                        ch = cg * 4 + cc
                        for hi, (s0, sz) in enumerate(HALF):
                            bg = rot_main.next()
                            for k in range(KC):
                                P.op("pe", I("matmul", PS(bg, sz), lhsT=wg.k(k)[:, cc * 128:(cc + 1) * 128], rhs=hT[:, k, s0:s0 + sz],
                                             start=(k == 0), stop=(k == KC - 1)),
                                     reads=[wg.key(k), "hT%d" % k], writes=[bk(bg)], signal=(k == KC - 1 or k == 7))
                            by = (rot_a1 if hi == 0 else rot_a2).next()
                            for k in range(8):
                                P.op("pe", I("matmul", PS(by, sz), lhsT=wb.k(k)[:, cc * 128:(cc + 1) * 128], rhs=oT[:, mix, k, s0:s0 + sz],
                                             start=(k == 0), stop=(k == 7)),
                                     reads=[wb.key(k), "oT"], writes=[bk(by)], signal=(k == 7))
                            sg = sgs[hi][:, 0:sz]
                            P.op("act", I("activation", out=sg, in_=PS(bg, sz), func=AF.Sigmoid), reads=[bk(bg)], writes=["sg%d" % hi])
                            if mix == 0:
                                P.op("dve", I("tensor_tensor", out=tA[:, cc, s0:s0 + sz], in0=PS(by, sz), in1=sg, op=ALU.mult),
                                     reads=[bk(by), "sg%d" % hi], writes=["tA"])
                            else:
                                tm = tms[hi][:, 0:sz]
                                P.op("dve", I("tensor_tensor", out=tm, in0=PS(by, sz), in1=sg, op=ALU.mult),
                                     reads=[bk(by), "sg%d" % hi], writes=["tm%d" % hi])
                                P.op("dve", I("tensor_tensor", out=mTv[:, ch, s0:s0 + sz], in0=tm, in1=tA[:, cc, s0:s0 + sz], op=ALU.add),
                                     reads=["tm%d" % hi, "tA"], writes=["mT"])
            ada_finish()
            stop_at("mT", mTv, ["mT"])

            P.op("dve", I("memset", onesf, 1.0), writes=["onesf"])

            diag4 = [view(pb_scr + 11776 + i * 2048, [4, 128], F32) for i in range(2)]

            def build_G(mod_base, alias_from):
                P.alias(["G"], alias_from)
                di = 0
                for v in range(2):
                    for g4 in range(4):
                        b = rot_a1.next()
                        d4 = diag4[di % 2]
                        dk = "dg4_%d" % (di % 2)
                        di += 1
                        c0 = mod_base + g4 * 4
                        for j in range(4):
                            P.op("dve", I("tensor_scalar", out=d4[:, j, :], in0=identf, scalar1=modT[:, c0 + j, v:v + 1], scalar2=None, op0=ALU.mult),
                                 reads=["identf", "modT"], writes=[dk + "_%d" % j])
                        for j in range(4):
                            P.op("pe", I("matmul", PS(b, 128, j * 128), lhsT=onesf, rhs=d4[:, j, :], start=True, stop=True),
                                 reads=["onesf", dk + "_%d" % j], writes=[bk(b)], signal=(j == 3))
                        P.op("act", I("activation", out=Gt[:, v, g4 * 512:(g4 + 1) * 512], in_=PS(b, 512), func=AF.Copy),
                             reads=[bk(b)], writes=["G"])

            P.alias(["dg4_%d_%d" % (i, j) for i in range(2) for j in range(4)], ["tA"])
            build_G(32, ["oT"])

            for cg in range(4):
                wt = load_w16(w_out, cg * 512)
                for t in range(6):
                    v = 0 if t < 4 else 1
                    b = rot_main.next()
                    for k in range(KC):
                        P.op("pe", I("matmul", PS(b, 512), lhsT=mTv[:, k, t * 128:(t + 1) * 128], rhs=wt.k(k),
                                     start=(k == 0), stop=(k == KC - 1)),
                             reads=[wt.key(k), "mT"], writes=[bk(b)], signal=(k == KC - 1 or k == 7))
                    i = (cg * 6 + t) % 2
                    tp = tmp512[i]
                    P.op("dve", I("tensor_tensor", out=tp, in0=PS(b, 512), in1=Gt[:, v, cg * 512:(cg + 1) * 512], op=ALU.mult),
                         reads=[bk(b), "G"], writes=["t512_%d" % i])
                    P.op("dve", I("tensor_tensor", out=xres[:, t, cg * 512:(cg + 1) * 512], in0=xres[:, t, cg * 512:(cg + 1) * 512], in1=tp, op=ALU.add),
                         reads=["t512_%d" % i, "x%d" % t], writes=["x%d" % t])
            stop_at("x1", xres, ["x%d" % t for t in range(6)])

            xnk2 = ["xnn%d" % i for i in range(6)]
            P.alias(["h2T%d" % i for i in range(16)] + xnk2, ["hT%d" % i for i in range(16)] + ["oT"])
            xni = 0
            for tl, vec in [([0, 1, 2, 3], 0), ([4, 5], 1)]:
                tiles, xb, xk = [], [], []
                for ti in tl:
                    tiles.append((xres[:, ti, :], "x%d" % ti, None))
                    xb.append(xnn[xni % 6])
                    xk.append(xnk2[xni % 6])
                    xni += 1
                norm_group(tiles, None, vec, 2, 3, h2T, "h2T", xb, xk, tl[0] * 128, tl[0])
            stop_at("h2T", h2T, ["h2T%d" % i for i in range(16)])
            build_G(80, ["G"])

            P.alias(["uT"], xnk2)
            P.alias(["rs0", "rs1", "tb0", "tb1"], ["sg0", "sg1", "tm0", "tm1", "t512_0", "t512_1"])
            ri = 0
            for fg in range(4):
                for t4 in range(4):
                    wt = load_w16(w_up, fg * 2048 + t4 * 512)
                    for cc in range(4):
                        fc = t4 * 4 + cc
                        for (s0, sz) in HALF:
                            b = rot_main.next()
                            for k in range(KC):
                                P.op("pe", I("matmul", PS(b, sz), lhsT=wt.k(k)[:, cc * 128:(cc + 1) * 128], rhs=h2T[:, k, s0:s0 + sz],
                                             start=(k == 0), stop=(k == KC - 1)),
                                     reads=[wt.key(k), "h2T%d" % k], writes=[bk(b)], signal=(k == KC - 1 or k == 7))
                            i = ri % 2
                            ri += 1
                            rs = rsb[i][:, 0:sz]
                            P.op("act", I("activation", out=rs, in_=PS(b, sz), func=AF.Relu), reads=[bk(b)], writes=["rs%d" % i])
                            P.op("dve", I("tensor_tensor", out=uT[:, fc, s0:s0 + sz], in0=rs, in1=rs, op=ALU.mult), reads=["rs%d" % i], writes=["uT"])
                for cg in range(4):
                    wt = load_w16(w_down, cg * 512, r0=fg * 2048)
                    for t in range(6):
                        v = 0 if t < 4 else 1
                        b = rot_a1.next() if (t % 2 == 0) else rot_a2.next()
                        for k in range(KC):
                            P.op("pe", I("matmul", PS(b, 512), lhsT=uT[:, k, t * 128:(t + 1) * 128], rhs=wt.k(k),
                                         start=(k == 0), stop=(k == KC - 1)),
                                 reads=[wt.key(k), "uT"], writes=[bk(b)], signal=(k == KC - 1 or k == 7))
                        i = (cg * 6 + t) % 2
                        tp = tmp512b[i]
                        P.op("dve", I("tensor_tensor", out=tp, in0=PS(b, 512), in1=Gt[:, v, cg * 512:(cg + 1) * 512], op=ALU.mult),
                             reads=[bk(b), "G"], writes=["tb%d" % i])
                        P.op("dve", I("tensor_tensor", out=xres[:, t, cg * 512:(cg + 1) * 512], in0=xres[:, t, cg * 512:(cg + 1) * 512], in1=tp, op=ALU.add),
                             reads=["tb%d" % i, "x%d" % t], writes=["x%d" % t])

            ym_t = y_main.rearrange("(t p) d -> t p d", p=128)
            for t in range(6):
                P.dma("sp", ym_t[t], xres[:, t, :], sem="x%d" % t, reads=["x%d" % t])
            P.wait_all("sp", ["x%d" % t for t in range(6)] + ["ost0", "ost1"])

        try:
            body()
        except _Stop:
            pass
        fin = []
        for e_ in Prog.CE:
            if P.cnt[e_] > 0:
                fin.append((P.esem[e_], P.cnt[e_]))
        P.q["sp"].append((fin, None, None))

        with nc.Block() as block:
            @block.sync
            def _(e):
                P.emit("sp", e)

            @block.gpsimd
            def _(e):
                P.emit("pool", e)

            @block.tensor
            def _(e):
                P.emit("pe", e)

            @block.scalar
            def _(e):
                P.emit("act", e)

            @block.vector
            def _(e):
                P.emit("dve", e)
        build_program.stats = dict(n={k: len(v) for k, v in P.q.items()}, waits=P.n_wait, arena=A.off, cnt=dict(P.cnt))
    return nc


GRID_W = 64
ROWS = 16


def _core_geometry(j):
    b = j // 4
    qq = j % 4
    ws = 0 if qq < 2 else 4
    own_rows = list(range(4 * qq, 4 * qq + 4))
    halo_rows = [r for r in range(ws, ws + 12) if r not in own_rows]
    rows = own_rows + halo_rows
    pos = np.concatenate([np.arange(r * GRID_W, (r + 1) * GRID_W) for r in rows])
    return b, qq, pos


def _static_tables(j, rpb):
    b, qq, pos = _core_geometry(j)
    row = (pos // GRID_W).astype(np.int64)
    col = (pos % GRID_W).astype(np.int64)
    n_freq = 32
    inv = (10000.0 ** (-np.arange(n_freq, dtype=np.float32) / n_freq)).astype(np.float32)
    ang_r = row[:, None].astype(np.float32) * inv[None, :]
    ang_c = col[:, None].astype(np.float32) * inv[None, :]
    cosT = np.zeros((128, 768), np.float32)
    sinT = np.zeros((128, 768), np.float32)
    cosT[0:32] = np.cos(ang_r).T
    cosT[32:64] = np.cos(ang_r).T
    cosT[64:96] = np.cos(ang_c).T
    cosT[96:128] = np.cos(ang_c).T
    sinT[0:32] = -np.sin(ang_r).T
    sinT[32:64] = np.sin(ang_r).T
    sinT[64:96] = -np.sin(ang_c).T
    sinT[96:128] = np.sin(ang_c).T
    qpos = pos[:256]
    valid = np.abs(qpos[None, :] - pos[:, None]) <= 128
    mA = np.where(valid, 0.0, NEGM).astype(np.float32)
    maskA = np.concatenate([mA, mA], axis=1)
    qr = row[:256]
    qc = col[:256]
    rstart = np.clip(qr - 4, 0, ROWS - 8)
    cstart = np.clip(qc - 8, 0, GRID_W - 16)
    kr = row[:, None]
    kcc = col[:, None]
    vr = (kr >= rstart[None, :]) & (kr < rstart[None, :] + 8)
    vc = (kcc >= cstart[None, :]) & (kcc < cstart[None, :] + 16)
    valid = vr & vc
    dr = np.clip(kr - qr[None, :] + 7, 0, 14)
    dc = np.clip(kcc - qc[None, :] + 15, 0, 30)
    bias = rpb[:, dr, dc]
    bias = np.where(valid[None], bias, np.float32(NEGM)).astype(np.float32)
    biasB = np.ascontiguousarray(np.transpose(bias, (1, 0, 2))).reshape(768, 2048)
    return cosT, sinT, maskA, biasB


def _perm():
    p = np.zeros((128, 128), np.float32)
    for d in range(128):
        s = d + 32 if (d % 64) < 32 else d - 32
        p[s, d] = 1.0
    return p


_NC_CACHE = {}


def kernel(x_prompt, x_sample, cache_a_k, cache_a_v, cache_b_k, cache_b_v, c, c_ctx,
           norm1_w, norm2_w, w_ada, b_ada, w_in, q_norm_a, k_norm_a, q_norm_b, k_norm_b,
           sink_a, rpb_b, w_br_a, w_br_b, w_out, w_up, w_down, _debug=None, _cores=None):
    f = lambda a: np.ascontiguousarray(np.asarray(a, dtype=np.float32))
    x_prompt, x_sample = f(x_prompt), f(x_sample)
    c, c_ctx = f(c), f(c_ctx)
    cores = list(range(NCORES)) if _cores is None else _cores
    key = "dbg" if _debug is not None else "main"
    if key not in _NC_CACHE:
        _NC_CACHE[key] = build_program(_debug)
    nc = _NC_CACHE[key]

    ident = np.eye(128, dtype=np.float32)
    perm = _perm()
    shared = dict(
        ident=ident, perm=perm,
        w_ada=f(w_ada)[0], w_in=f(w_in)[0], w_br_a=f(w_br_a)[0], w_br_b=f(w_br_b)[0],
        w_out=f(w_out)[0], w_up=f(w_up)[0], w_down=f(w_down)[0],
    )
    rpb = f(rpb_b)[0]
    in_maps = []
    for idx_core, j in enumerate(cores):
        b, qq, pos = _core_geometry(j)
        xm = np.concatenate([x_prompt[2 * j], x_prompt[2 * j + 1], x_sample[b][pos[:256]]], axis=0)
        xh = x_sample[b][pos[256:]]
        cvec = np.stack([c_ctx, c[b]], axis=0)
        cT = np.ascontiguousarray(cvec.reshape(2, 16, 128).transpose(2, 1, 0)).reshape(128, 32)
        pp = np.zeros((128, PP_W), np.float32)
        pp[:, PP_N1:PP_N1 + 16] = f(norm1_w)[0].reshape(16, 128).T
        pp[:, PP_N2:PP_N2 + 16] = f(norm2_w)[0].reshape(16, 128).T
        pp[:, PP_BADA:PP_BADA + 96] = f(b_ada)[0].reshape(96, 128).T
        pp[:, PP_QNA] = f(q_norm_a)[0]
        pp[:, PP_KNA] = f(k_norm_a)[0]
        pp[:, PP_QNB] = f(q_norm_b)[0]
        pp[:, PP_KNB] = f(k_norm_b)[0]
        pp[:, PP_SINK:PP_SINK + 8] = f(sink_a)[0][None, :]
        cosT, sinT, maskA, biasB = _static_tables(j, rpb)
        m = dict(shared)
        m.update(
            x_main=np.ascontiguousarray(xm), x_halo=np.ascontiguousarray(xh), cT=cT, pp=pp,
            cosT=cosT, sinT=sinT, maskA=maskA, biasB=biasB,
            cak=f(cache_a_k)[b, 0].reshape(256, 256), cav=f(cache_a_v)[b, 0].reshape(256, 256),
            cbk=f(cache_b_k)[b, 0].reshape(256, 1024), cbv=f(cache_b_v)[b, 0].reshape(256, 1024),
        )
        in_maps.append(m)

    res = run_bass_kernel_spmd(nc, in_maps, core_ids=list(range(len(cores))))
    if _debug is not None:
        return res

    y_prompt = np.zeros((16, 256, D), np.float32)
    y_sample = np.zeros((2, 1024, D), np.float32)
    nak = np.zeros((16, 1, 256, 2, 128), np.float32)
    nav = np.zeros((16, 1, 256, 2, 128), np.float32)
    nbk = np.zeros((16, 1, 256, 8, 128), np.float32)
    nbv = np.zeros((16, 1, 256, 8, 128), np.float32)
    for idx, j in enumerate(cores):
        r = res.results[idx]
        b, qq, pos = _core_geometry(j)
        ym = r["y_main"]
        y_prompt[2 * j] = ym[0:256]
        y_prompt[2 * j + 1] = ym[256:512]
        y_sample[b][pos[:256]] = ym[512:768]
        for s in range(2):
            nak[2 * j + s, 0] = r["nak"][s * 256:(s + 1) * 256].reshape(256, 2, 128)
            nav[2 * j + s, 0] = r["nav"][s * 256:(s + 1) * 256].reshape(256, 2, 128)
            nbk[2 * j + s, 0] = r["nbk"][s * 256:(s + 1) * 256].reshape(256, 8, 128)
            nbv[2 * j + s, 0] = r["nbv"][s * 256:(s + 1) * 256].reshape(256, 8, 128)
    return (y_prompt, y_sample, nak, nav, nbk, nbv)
```

```python
import os
import numpy as np
import concourse.bass as bass
import concourse.mybir as mybir
from concourse.bass_utils import run_bass_kernel_spmd

F32 = mybir.dt.float32
BF16 = mybir.dt.bfloat16
ALU = mybir.AluOpType
AF = mybir.ActivationFunctionType
AX = mybir.AxisListType

NCORES = 8
D = 2048
KC = 16
NMAIN = 768
NHALO = 512
NTOK = NMAIN + NHALO
EPS = 1e-6
NEGM = -30000.0
IN_W = 8704
DFF = 8192
CH_P = (0, 512)
CH_S = (512, 256)
CH_H = (768, 512)

PP_N1 = 0
PP_N2 = 16
PP_BADA = 32
PP_QNA = 128
PP_KNA = 129
PP_QNB = 130
PP_KNB = 131
PP_SINK = 132
PP_SEL = 140
PP_W = 142


class Prog:
    CE = ("pe", "act", "dve", "pool")
    ALL = ("pe", "act", "dve", "pool", "sp")

    def __init__(self, nc, esems, dma_sems):
        self.nc = nc
        self.q = {e: [] for e in self.ALL}
        self.cnt = {e: 0 for e in self.CE}
        self.esem = esems
        self.free_dsems = list(dma_sems)
        self.dsem = {}
        self.seen = {e: {} for e in self.ALL}
        self.lastw = {}
        self.readers = {}
        self.n_wait = 0

    def _need(self, eng, ev, waits):
        if ev is None:
            return
        semkey, handle, val, _ = ev
        if self.seen[eng].get(semkey, 0) >= val:
            return
        self.seen[eng][semkey] = val
        waits.append((handle, val))
        self.n_wait += 1

    def _deps(self, eng, reads, writes, waits):
        for k in reads:
            ev = self.lastw.get(k)
            if ev is not None and not (ev[3] == "pe" and eng == "pe"):
                self._need(eng, ev, waits)
        for k in writes:
            ev = self.lastw.get(k)
            if ev is not None and not (ev[3] == "pe" and eng == "pe"):
                self._need(eng, ev, waits)
            for ev in self.readers.get(k, ()):
                if not (ev[3] == "pe" and eng == "pe"):
                    self._need(eng, ev, waits)

    def _commit(self, ev, reads, writes):
        for k in reads:
            lst = self.readers.setdefault(k, [])
            lst[:] = [e for e in lst if e[0] != ev[0]]
            lst.append(ev)
        for k in writes:
            self.lastw[k] = ev
            self.readers[k] = []

    def op(self, eng, fn, reads=(), writes=(), signal=True):
        waits = []
        self._deps(eng, reads, writes, waits)
        if signal:
            self.cnt[eng] += 1
            val = self.cnt[eng]
        else:
            val = self.cnt[eng] + 1
        ev = ("E" + eng, self.esem[eng], val, eng)
        self._commit(ev, reads, writes)
        self.q[eng].append((waits, fn, (self.esem[eng], 1) if signal else None))

    def dma(self, eng, out, in_, sem, reads=(), writes=(), **kw):
        waits = []
        self._deps(eng, reads, writes, waits)
        if sem not in self.dsem:
            self.dsem[sem] = [self.free_dsems.pop(), 0]
        rec = self.dsem[sem]
        rec[1] += 16
        ev = ("D" + sem, rec[0], rec[1], "dma")
        self._commit(ev, reads, writes)
        self.q[eng].append((waits, (lambda e, o=out, i=in_, k=kw: e.dma_start(out=o, in_=i, **k)), (rec[0], 16)))

    def custom(self, eng, fn, sem, inc, reads=(), writes=()):
        waits = []
        self._deps(eng, reads, writes, waits)
        if sem not in self.dsem:
            self.dsem[sem] = [self.free_dsems.pop(), 0]
        rec = self.dsem[sem]
        rec[1] += inc
        ev = ("D" + sem, rec[0], rec[1], "dma")
        self._commit(ev, reads, writes)
        self.q[eng].append((waits, fn, (rec[0], inc)))

    def wait_all(self, eng, keys):
        waits = []
        for k in keys:
            self._need(eng, self.lastw.get(k), waits)
            for ev in self.readers.get(k, ()):
                self._need(eng, ev, waits)
        if waits:
            self.q[eng].append((waits, None, None))

    def alias(self, new_keys, old_keys):
        evs = []
        for k in old_keys:
            if self.lastw.get(k) is not None:
                evs.append(self.lastw[k])
            evs.extend(self.readers.get(k, ()))
        for k in new_keys:
            self.lastw[k] = None
            self.readers[k] = list(evs)

    def emit(self, eng_name, eng):
        for waits, fn, inc in self.q[eng_name]:
            for h, v in waits:
                eng.wait_ge(h, v)
            if fn is None:
                continue
            ins = fn(eng)
            if inc is not None:
                ins.then_inc(inc[0], inc[1])


def I(method, *a, **k):
    return lambda e: getattr(e, method)(*a, **k)


class Rot:
    def __init__(self, items):
        self.items = list(items)
        self.i = 0

    def next(self):
        v = self.items[self.i % len(self.items)]
        self.i += 1
        return v


def build_program(debug=None):
    nc = bass.Bass("TRN2", target_bir_lowering=False)

    def din(name, shape, dt=F32):
        return nc.dram_tensor(name, list(shape), dt, kind="ExternalInput").ap()

    def dout(name, shape, dt=F32):
        return nc.dram_tensor(name, list(shape), dt, kind="ExternalOutput").ap()

    x_main = din("x_main", [NMAIN, D])
    x_halo = din("x_halo", [NHALO, D])
    cT_d = din("cT", [128, 32])
    pp_d = din("pp", [128, PP_W])
    ident_d = din("ident", [128, 128])
    perm_d = din("perm", [128, 128])
    cos_d = din("cosT", [128, NMAIN])
    sin_d = din("sinT", [128, NMAIN])
    maskA_d = din("maskA", [NMAIN, 512])
    biasB_d = din("biasB", [NMAIN, 2048])
    cak_d = din("cak", [256, 256])
    cav_d = din("cav", [256, 256])
    cbk_d = din("cbk", [256, 1024])
    cbv_d = din("cbv", [256, 1024])
    w_ada = din("w_ada", [D, 6 * D])
    w_in = din("w_in", [D, IN_W])
    w_bra = din("w_br_a", [1024, D])
    w_brb = din("w_br_b", [1024, D])
    w_out = din("w_out", [D, D])
    w_up = din("w_up", [D, DFF])
    w_down = din("w_down", [DFF, D])

    y_main = dout("y_main", [NMAIN, D])
    nak_o = dout("nak", [512, 256])
    nav_o = dout("nav", [512, 256])
    nbk_o = dout("nbk", [512, 1024])
    nbv_o = dout("nbv", [512, 1024])
    dbg_o = None
    if debug is not None:
        dbg_o = dout("dbg", debug["shape"], BF16 if debug.get("dtype") == "bf16" else F32)

    import contextlib
    es = contextlib.ExitStack()
    with es:
        ARENA_W = 53120
        arena = es.enter_context(nc.sbuf_tensor("arena", [128, ARENA_W], F32))
        ps = es.enter_context(nc.psum_tensor("ps", [128, 8, 512], F32))
        sem_names = ["pe", "act", "dve", "pool"]
        esems = {n: es.enter_context(nc.semaphore("s_" + n)) for n in sem_names}
        dsems = [es.enter_context(nc.semaphore("d%d" % i)) for i in range(70)]
        P = Prog(nc, esems, dsems)

        class Arena:
            def __init__(self):
                self.off = 0

            def take(self, nbytes):
                o = self.off
                self.off += (nbytes + 63) // 64 * 64
                assert self.off <= ARENA_W * 4, ("arena overflow", self.off)
                return o

        def view(off_b, shape, dt):
            esz = 2 if dt == BF16 else 4
            n = int(np.prod(shape))
            assert off_b % 4 == 0
            w0 = off_b // 4
            nw = (n * esz + 3) // 4
            ap = arena[:, w0:w0 + nw]
            if dt == BF16:
                ap = ap.bitcast(BF16)
            if len(shape) == 2:
                ap = ap.rearrange("p (a b) -> p a b", a=shape[0])
            elif len(shape) == 3:
                ap = ap.rearrange("p (a b c) -> p a b c", a=shape[0], b=shape[1])
            return ap

        A = Arena()
        RING_N = 6
        ring_off = [A.take(8192) for _ in range(RING_N)]
        o_pp = A.take(PP_W * 4)
        o_identf = A.take(512)
        o_identb = A.take(256)
        o_permf = A.take(512)
        o_onesb = A.take(256)
        o_modT = A.take(96 * 2 * 4)
        o_ab = A.take(4 * 2 * 16 * 4)
        o_small = A.take(256)
        o_esink = A.take(32)
        o_cT = A.take(128)
        o_scT = A.take(64)

        pp = view(o_pp, [PP_W], F32)
        identf = view(o_identf, [128], F32)
        identb = view(o_identb, [128], BF16)
        permf = view(o_permf, [128], F32)
        onesb = view(o_onesb, [128], BF16)
        modT = view(o_modT, [96, 2], F32)
        abv = view(o_ab, [4, 2, 16], F32)
        small = view(o_small, [64], F32)
        esink = view(o_esink, [8], F32)
        epsc = small[:, 20:21]
        epsc128 = small[:, 21:22]
        zeroc = small[:, 22:23]

        R1 = A.take(67584)
        R2 = A.take(65536)
        mT_off = A.take(KC * NMAIN * 2)
        mTv = view(mT_off, [KC, NMAIN], BF16)
        hT = view(R2, [KC, NTOK], BF16)
        oT = view(R2 + 40960, [2, 8, NMAIN], BF16)
        Gt = view(R2 + 65536 - 16384, [2, 2048], F32)
        h2T = view(R2, [KC, NMAIN], BF16)
        uT = view(R2 + 24576, [KC, NMAIN], BF16)
        xnn = [view(R2 + 24576 + i * 4096, [2048], BF16) for i in range(6)]
        xs_ = [view(R1 + i * 8192, [2048], F32) for i in range(3)]
        xn_ = [view(R1 + 24576 + i * 4096, [2048], BF16) for i in range(10)]
        o = R1
        qT = view(o, [4, NMAIN], BF16); o += 4 * NMAIN * 2
        kT = view(o, [4, NTOK], BF16); o += 4 * NTOK * 2
        vS = view(o, [10, 512], BF16); o += 10 * 512 * 2
        NPT = 8
        pT = [view(o + i * 1024, [512], BF16) for i in range(NPT)]; o += NPT * 1024
        m = o
        maskA = view(m, [6, 512], BF16)
        cosT = view(m + 6144, [NMAIN], F32)
        sinT = view(m + 9216, [NMAIN], F32)
        biasB = [view(m + i * 6144, [6, 512], BF16) for i in range(2)]
        kc_T = view(m + 12288, [4, 256], BF16)
        vc_S = view(m + 14336, [2, 512], BF16)
        kc_S = view(m + 16384, [2, 512], BF16)
        f32s = [view(m + 18432 + i * 2048, [512], F32) for i in range(3)]
        o += 24576
        ost = [view(o + i * 2048, [512], F32) for i in range(2)]; o += 2 * 2048
        sqb = [view(o + i * 1024, [512], BF16) for i in range(2)]; o += 2 * 1024
        rl_ = [view(o + i * 1024, [256], F32) for i in range(2)]; o += 2 * 1024
        assert o - R1 <= 67584, o - R1
        R1_ATT_KEYS = (["qT%d" % i for i in range(4)] + ["kT%d" % i for i in range(4)] + ["vS", "kcT", "vcS", "kcS", "f32s0", "f32s1", "f32s2", "ost0", "ost1", "sqb0", "sqb1",
                        "rl0", "rl1", "biasB0", "biasB1", "maskA", "cosT", "sinT"] + ["pT%d" % i for i in range(NPT)])
        xres = view(R1, [6, 2048], F32)
        pb_scr = R1 + 49152
        sgs = [view(pb_scr + i * 1536, [384], F32) for i in range(2)]
        tms = [view(pb_scr + 3072 + i * 1536, [384], F32) for i in range(2)]
        tmp512 = [view(pb_scr + 6144 + i * 2048, [512], F32) for i in range(2)]
        onesf = view(pb_scr + 10240, [128], F32)
        diag = [view(pb_scr + 10752 + i * 512, [128], F32) for i in range(2)]
        tA = view(pb_scr + 11776, [4, NMAIN], BF16)
        rsb = [view(pb_scr + i * 1024, [384], BF16) for i in range(2)]
        tmp512b = [view(pb_scr + 4096 + i * 2048, [512], F32) for i in range(2)]

        def PS(bank, n=512, off=0):
            return ps[:, bank, off:off + n]

        def PSB(bank, n, off=0):
            return ps[:, bank, :].bitcast(BF16)[:, off:off + n]

        rot_main = Rot([0, 1, 2, 3])
        rot_a1 = Rot([4, 5])
        rot_a2 = Rot([6, 7])

        def bk(b):
            return "ps%d" % b

        def set_mode(m):
            if m == "proj":
                rot_main.items, rot_a1.items, rot_a2.items = [0, 1, 2, 3, 4, 5], [6], [7]
            else:
                rot_main.items, rot_a1.items, rot_a2.items = [0, 1, 2, 3], [4, 5], [6, 7]

        ring_state = {"n": 0}

        class WT:
            def __init__(self, slots):
                self.slots = slots

            def k(self, k):
                return view(ring_off[self.slots[k // 8]], [8, 512], BF16)[:, k % 8, :]

            def key(self, k):
                return "ring%d" % self.slots[k // 8]

        def load_entry(src):
            n = ring_state["n"]
            ring_state["n"] += 1
            slot = n % RING_N
            key = "ring%d" % slot
            P.dma("pool", view(ring_off[slot], [8, 512], BF16), src, sem=key, writes=[key])
            return slot

        def wsrc(w, r0, nr, c0, ncol):
            return w[r0:r0 + nr, c0:c0 + ncol].rearrange("(k p) n -> p k n", p=128)

        def load_w16(w, c0, r0=0):
            n = ring_state["n"]
            slot = n % RING_N
            if slot % 2 == 0:
                ring_state["n"] += 2
                k0, k1 = "ring%d" % slot, "ring%d" % (slot + 1)
                dst = view(ring_off[slot], [16, 512], BF16)
                P.dma("pool", dst, wsrc(w, r0, 2048, c0, 512), sem=k0, writes=[k0, k1])
                return WT([slot, slot + 1])
            return WT([load_entry(wsrc(w, r0, 1024, c0, 512)), load_entry(wsrc(w, r0 + 1024, 1024, c0, 512))])

        def load_w8(w, c0):
            return WT([load_entry(wsrc(w, 0, 1024, c0, 512))])

        class _Stop(Exception):
            pass

        def stop_at(name, ap, keys):
            if debug is not None and debug["at"] == name:
                P.dma("sp", dbg_o, ap, sem="dbg", reads=keys)
                P.wait_all("sp", keys)
                raise _Stop()

        chains = []

        def tick():
            for ch in list(chains):
                step = ch.pop(0)
                step()
                if not ch:
                    chains.remove(ch)

        def drain():
            while chains:
                tick()

        def body():
            P.dma("sp", pp, pp_d, sem="pp", writes=["pp"])
            P.dma("sp", identf, ident_d, sem="identf", writes=["identf"])
            P.dma("sp", permf, perm_d, sem="permf", writes=["permf"])
            P.dma("pool", identb, ident_d, sem="identb", writes=["identb"])
            P.op("dve", I("memset", onesb, 1.0), writes=["onesb"])
            P.op("dve", I("memset", epsc, EPS), writes=["epsc"])
            P.op("dve", I("memset", epsc128, 128.0 * EPS), writes=["epsc"])
            P.op("dve", I("memset", zeroc, 0.0), writes=["epsc"])
            cTf = view(o_cT, [32], F32)
            scTflat = view(o_scT, [32], BF16)
            scTb = scTflat.rearrange("p (k v) -> p k v", v=2)
            P.dma("sp", cTf, cT_d, sem="cT", writes=["cTf"])
            P.op("act", I("activation", out=scTflat, in_=cTf, func=AF.Silu), reads=["cTf"], writes=["scT"])
            P.op("act", I("activation", out=esink, in_=pp[:, PP_SINK:PP_SINK + 8], func=AF.Exp), reads=["pp"], writes=["esink"])

            def norm_group(tiles, src_fn, vec, kindA, kindB, dstT, dst_key, xbufs, xkeys, tok0, stat0):
                norm_stats(tiles, xbufs, xkeys, stat0)
                norm_tr(len(tiles), vec, kindA, kindB, dstT, dst_key, xbufs, xkeys, tok0)

            def norm_stats(tiles, xbufs, xkeys, stat0):
                for i, (xin, xkey, loader) in enumerate(tiles):
                    if loader is not None:
                        loader()
                    xnb, xnk = xbufs[i], xkeys[i]
                    si = (stat0 + i) % 10
                    ssq = small[:, 24 + si:25 + si]
                    rst = small[:, 36 + si:37 + si]
                    sk = "ssq%d" % si
                    P.op("act", I("activation", out=xnb, in_=xin, func=AF.Square, accum_out=ssq), reads=[xkey], writes=[xnk, sk])
                    P.op("act", I("activation", out=rst, in_=ssq, func=AF.Sqrt, scale=1.0 / D, bias=epsc), reads=[sk, "epsc"], writes=[sk + "r"])
                    P.op("dve", I("reciprocal", out=rst, in_=rst), reads=[sk + "r"], writes=[sk + "r"])
                    P.op("dve", I("tensor_scalar", out=xnb, in0=xin, scalar1=rst, scalar2=None, op0=ALU.mult), reads=[xkey, sk + "r"], writes=[xnk])

            def norm_tr(nt, vec, kindA, kindB, dstT, dst_key, xbufs, xkeys, tok0, chunks=None):
                for c in (range(KC) if chunks is None else chunks):
                    b = rot_main.next()
                    for i in range(nt):
                        P.op("pe", I("transpose", PSB(b, 128, i * 128), xbufs[i][:, c * 128:(c + 1) * 128], identb),
                             reads=[xkeys[i], "identb"], writes=[bk(b)], signal=(i == nt - 1))
                    dst = dstT[:, c, tok0:tok0 + nt * 128]
                    if c % 2 == 0:
                        P.op("act", I("activation", out=dst, in_=PSB(b, nt * 128), func=AF.Identity,
                                      scale=abv[:, kindA, vec, c:c + 1], bias=abv[:, kindB, vec, c:c + 1]),
                             reads=[bk(b), "ab", "ab2"], writes=[dst_key + str(c)])
                    else:
                        P.op("dve", I("tensor_scalar", out=dst, in0=PSB(b, nt * 128), scalar1=abv[:, kindA, vec, c:c + 1],
                                      scalar2=abv[:, kindB, vec, c:c + 1], op0=ALU.mult, op1=ALU.add),
                             reads=[bk(b), "ab", "ab2"], writes=[dst_key + str(c)])

            xm_t = x_main.rearrange("(t p) d -> t p d", p=128)
            xh_t = x_halo.rearrange("(t p) d -> t p d", p=128)
            norm1_groups = []
            xi = 0
            for tl, vec in [([0, 1, 2, 3], 0), ([4, 5], 1), ([6, 7, 8, 9], 1)]:
                tiles, xb, xk = [], [], []
                for ti in tl:
                    bi = xi % 3
                    xi += 1
                    src = xm_t[ti] if ti < 6 else xh_t[ti - 6]
                    ld = (lambda bi=bi, src=src: P.dma("sp", xs_[bi], src, sem="xs%d" % bi, writes=["xs%d" % bi]))
                    tiles.append((xs_[bi], "xs%d" % bi, ld))
                    xb.append(xn_[ti])
                    xk.append("xn%d" % ti)
                norm_stats(tiles, xb, xk, tl[0])
                norm1_groups.append((tl, vec, xb, xk))

            ada_pending = [0, 4, 1, 5, 2, 6, 3, 7] + list(range(8, 24))

            def ada_some(n):
                for _ in range(min(n, len(ada_pending))):
                    t = ada_pending.pop(0)
                    wt = load_w16(w_ada, t * 512)
                    b = rot_a2.next()
                    for cc in range(4):
                        for k in range(KC):
                            P.op("pe", I("matmul", ps[:, b, cc * 2:cc * 2 + 2], lhsT=wt.k(k)[:, cc * 128:(cc + 1) * 128],
                                         rhs=scTb[:, k, :], start=(k == 0), stop=(k == KC - 1)),
                                 reads=[wt.key(k), "scT"], writes=[bk(b)], signal=(k == KC - 1 or k == 7))
                    for v in range(2):
                        P.op("dve", I("tensor_tensor", out=modT[:, t * 4:(t + 1) * 4, v],
                                      in0=ps[:, b, 0:8].rearrange("p (c v) -> p c v", v=2)[:, :, v],
                                      in1=pp[:, PP_BADA + t * 4:PP_BADA + t * 4 + 4], op=ALU.add),
                             reads=[bk(b), "pp"], writes=["modT"])

            for qi in range(4):
                ada_some(2)
                c4 = slice(4 * qi, 4 * qi + 4)
                for v in range(2):
                    P.op("dve", I("scalar_tensor_tensor", out=abv[:, 0, v, c4], in0=modT[:, 16 + 4 * qi:20 + 4 * qi, v], scalar=1.0,
                                  in1=pp[:, PP_N1 + 4 * qi:PP_N1 + 4 * qi + 4], op0=ALU.add, op1=ALU.mult), reads=["modT", "pp"], writes=["ab"])
                    P.op("dve", I("tensor_copy", out=abv[:, 1, v, c4], in_=modT[:, 4 * qi:4 * qi + 4, v]), reads=["modT"], writes=["ab"])
                for (tl, vec, xb, xk) in norm1_groups:
                    norm_tr(len(tl), vec, 0, 1, hT, "hT", xb, xk, tl[0] * 128, chunks=range(4 * qi, 4 * qi + 4))
            SQ128 = float(np.sqrt(128.0))
            P.op("dve", I("tensor_copy", out=small[:, 0:1], in_=pp[:, PP_QNA:PP_QNA + 1]), reads=["pp"], writes=["small"])
            P.op("dve", I("tensor_scalar", out=small[:, 1:2], in0=pp[:, PP_KNA:PP_KNA + 1], scalar1=SQ128, scalar2=None, op0=ALU.mult), reads=["pp"], writes=["small"])
            P.op("dve", I("tensor_copy", out=small[:, 2:3], in_=pp[:, PP_QNB:PP_QNB + 1]), reads=["pp"], writes=["small"])
            P.op("dve", I("tensor_scalar", out=small[:, 3:4], in0=pp[:, PP_KNB:PP_KNB + 1], scalar1=SQ128, scalar2=None, op0=ALU.mult), reads=["pp"], writes=["small"])

            def ada_finish():
                ada_some(len(ada_pending))
                for v in range(2):
                    P.op("dve", I("scalar_tensor_tensor", out=abv[:, 2, v, :], in0=modT[:, 64:80, v], scalar=1.0,
                                  in1=pp[:, PP_N2:PP_N2 + 16], op0=ALU.add, op1=ALU.mult), reads=["modT", "pp"], writes=["ab2"])
                    P.op("dve", I("tensor_copy", out=abv[:, 3, v, :], in_=modT[:, 48:64, v]), reads=["modT"], writes=["ab2"])
            stop_at("ada", modT, ["modT", "ab", "small"])

            tkv = load_w16(w_in, 1024)
            stop_at("hT", hT, ["hT%d" % i for i in range(16)])

            P.alias(R1_ATT_KEYS, ["xs0", "xs1", "xs2"] + ["xn%d" % i for i in range(10)])
            P.dma("pool", maskA, maskA_d.rearrange("(c p) n -> p c n", p=128), sem="maskA", writes=["maskA"])
            P.dma("sp", cosT, cos_d, sem="cosT", writes=["cosT"])
            P.dma("sp", sinT, sin_d, sem="sinT", writes=["sinT"])

            cstate = {"i": 0, "ost": 0}

            def proj_fm(wt, cc, chunks, make_chain):
                for (s0, sz) in chunks:
                    b = rot_main.next()
                    for k in range(KC):
                        P.op("pe", I("matmul", PS(b, sz), lhsT=wt.k(k)[:, cc * 128:(cc + 1) * 128], rhs=hT[:, k, s0:s0 + sz],
                                     start=(k == 0), stop=(k == KC - 1)),
                             reads=[wt.key(k), "hT%d" % k], writes=[bk(b)], signal=(k == KC - 1 or k == 7))
                    tick()
                    make_chain(b, s0, sz)

            def norm_steps(b, sz, wcol, out_ap, out_key):
                i = cstate["i"] % 2
                cstate["i"] += 1
                sq = sqb[i][:, 0:sz]
                sqk = "sqb%d" % i
                P.op("act", I("activation", out=sq, in_=PS(b, sz), func=AF.Square), reads=[bk(b)], writes=[sqk])

                def step1():
                    b2 = rot_a1.next()
                    P.op("pe", I("matmul", PS(b2, sz), lhsT=onesb, rhs=sq, start=True, stop=True), reads=[sqk, "onesb"], writes=[bk(b2)])
                    rr = f32s[2][:, 0:sz]
                    P.op("act", I("activation", out=rr, in_=PS(b2, sz), func=AF.Ln, scale=1.0, bias=epsc128), reads=[bk(b2), "epsc"], writes=["f32s2"])
                    P.op("act", I("activation", out=rr, in_=rr, func=AF.Exp, scale=-0.5), reads=["f32s2"], writes=["f32s2"])
                    P.op("dve", I("scalar_tensor_tensor", out=out_ap, in0=PS(b, sz), scalar=wcol, in1=rr, op0=ALU.mult, op1=ALU.mult),
                         reads=[bk(b), "f32s2", "small"], writes=[out_key])
                return step1

            def rope_step(src_ap, src_key, e0, sz, out_ap, out_key):
                def step():
                    b3 = rot_a2.next()
                    P.op("pe", I("matmul", PS(b3, sz), lhsT=permf, rhs=src_ap, start=True, stop=True), reads=[src_key, "permf"], writes=[bk(b3)])
                    t1 = f32s[2][:, 0:sz]
                    P.op("dve", I("tensor_tensor", out=t1, in0=PS(b3, sz), in1=sinT[:, e0:e0 + sz], op=ALU.mult), reads=[bk(b3), "sinT"], writes=["f32s2"])
                    P.op("dve", I("tensor_tensor", out=src_ap, in0=src_ap, in1=cosT[:, e0:e0 + sz], op=ALU.mult), reads=[src_key, "cosT"], writes=[src_key])
                    P.op("dve", I("tensor_tensor", out=out_ap, in0=src_ap, in1=t1, op=ALU.add), reads=[src_key, "f32s2"], writes=[out_key])
                return step

            def kout_step(kn_ap, kn_key, dst_dram, head_col, kt_dst, kt_key):
                def step():
                    b4 = rot_a2.next()
                    for t in range(4):
                        P.op("pe", I("transpose", PS(b4, 128, t * 128), kn_ap[:, t * 128:(t + 1) * 128], identf),
                             reads=[kn_key, "identf"], writes=[bk(b4)], signal=(t == 3))
                    i = cstate["ost"] % 2
                    cstate["ost"] += 1
                    st = ost[i]
                    P.op("act", I("activation", out=st, in_=PS(b4, 512), func=AF.Copy), reads=[bk(b4)], writes=["ost%d" % i])
                    P.dma("sp", dst_dram.rearrange("(t p) n -> p t n", p=128)[:, :, head_col * 128:(head_col + 1) * 128],
                          st.rearrange("p (t d) -> p t d", t=4), sem="ost%d" % i, reads=["ost%d" % i])
                    P.op("act", I("activation", out=kt_dst, in_=kn_ap, func=AF.Copy), reads=[kn_key], writes=[kt_key])
                return step

            tmp_i = [0]

            def _nop():
                pass

            def proj_q(wt, cc, hslot, wcol, do_rope):
                def mk(b, s0, sz):
                    dst = qT[:, hslot, s0:s0 + sz]
                    if do_rope and s0 == CH_S[0]:
                        fi = tmp_i[0] % 2
                        tmp_i[0] += 1
                        tmp = f32s[fi][:, 0:sz]
                        chains.append([norm_steps(b, sz, wcol, tmp, "f32s%d" % fi), _nop, rope_step(tmp, "f32s%d" % fi, 0, sz, dst, "qT%d" % hslot)])
                    else:
                        chains.append([norm_steps(b, sz, wcol, dst, "qT%d" % hslot)])
                proj_fm(wt, cc, [CH_P, CH_S], mk)

            def proj_k(wt, cc, hslot, wcol, do_rope, dst_dram, head_col):
                def mk(b, s0, sz):
                    fi = tmp_i[0] % 2
                    tmp_i[0] += 1
                    tmp = f32s[fi][:, 0:sz]
                    tk = "f32s%d" % fi
                    dst = kT[:, hslot, s0:s0 + sz]
                    st1 = norm_steps(b, sz, wcol, tmp, tk)
                    if s0 == 0:
                        chains.append([st1, _nop, kout_step(tmp, tk, dst_dram, head_col, dst, "kT%d" % hslot)])
                    elif do_rope:
                        chains.append([st1, _nop, rope_step(tmp, tk, s0 - 512, sz, dst, "kT%d" % hslot)])
                    else:
                        def cp(tmp=tmp, tk=tk, dst=dst):
                            P.op("act", I("activation", out=dst, in_=tmp, func=AF.Copy), reads=[tk], writes=["kT%d" % hslot])
                        chains.append([st1, _nop, cp])
                proj_fm(wt, cc, [CH_P, CH_S, CH_H], mk)

            def proj_v(wt, c0, ncols, vcol0, dst_dram, dcol0):
                for ti in range(10):
                    b = rot_main.next()
                    for k in range(KC):
                        P.op("pe", I("matmul", PS(b, ncols), lhsT=hT[:, k, ti * 128:(ti + 1) * 128], rhs=wt.k(k)[:, c0:c0 + ncols],
                                     start=(k == 0), stop=(k == KC - 1)),
                             reads=[wt.key(k), "hT%d" % k], writes=[bk(b)], signal=(k == KC - 1 or k == 7))
                    tick()
                    if ti < 4:
                        i = cstate["ost"] % 2
                        cstate["ost"] += 1
                        st = ost[i][:, 0:ncols]
                        P.op("act", I("activation", out=st, in_=PS(b, ncols), func=AF.Copy), reads=[bk(b)], writes=["ost%d" % i])
                        P.op("dve", I("tensor_copy", out=vS[:, ti, vcol0:vcol0 + ncols], in_=st), reads=["ost%d" % i], writes=["vS"])
                        P.dma("sp", dst_dram[ti * 128:(ti + 1) * 128, dcol0:dcol0 + ncols], st, sem="ost%d" % i, reads=["ost%d" % i])
                    else:
                        P.op("dve", I("tensor_copy", out=vS[:, ti, vcol0:vcol0 + ncols], in_=PS(b, ncols)), reads=[bk(b)], writes=["vS"])

            pt_i = [0]

            def att_item(pair_heads, kslots, vcols, hs_slots, mixer, sink_cols, sample_bias, grp):
                same_kv = (kslots[0] == kslots[1]) and (vcols[0] == vcols[1])
                q0 = grp * 256
                if grp < 2:
                    chunks = [("tok", grp * 256 + c * 128, grp * 2 + c, None) for c in range(2)]
                else:
                    chunks = [("tok", 512 + c * 128, 4 + c, c) for c in range(6)] + [("ctx", c * 128, c, None) for c in range(2)]
                pts = []

                def scores():
                    for (kind, k0, vt, bc) in chunks:
                        b = rot_main.next()
                        first = True
                        if bc is not None:
                            bap, bkey = sample_bias(bc)
                            P.op("pe", I("matmul", PS(b, 512), lhsT=identb, rhs=bap, start=True, stop=False),
                                 reads=[bkey, "identb"], writes=[bk(b)], signal=False)
                            first = False
                        if same_kv:
                            lk, lkey = (kT[:, kslots[0], k0:k0 + 128], "kT%d" % kslots[0]) if kind == "tok" else (kc_T[:, kslots[0], k0:k0 + 128], "kcT")
                            P.op("pe", I("matmul", PS(b, 512).rearrange("p (a q) -> p a q", a=2), lhsT=lk,
                                         rhs=qT[:, hs_slots[0]:hs_slots[0] + 2, q0:q0 + 256], start=first, stop=True),
                                 reads=[lkey, "qT%d" % hs_slots[0], "qT%d" % (hs_slots[0] + 1)], writes=[bk(b)])
                        else:
                            for i in range(2):
                                lk, lkey = (kT[:, kslots[i], k0:k0 + 128], "kT%d" % kslots[i]) if kind == "tok" else (kc_T[:, kslots[i], k0:k0 + 128], "kcT")
                                P.op("pe", I("matmul", PS(b, 256, i * 256), lhsT=lk, rhs=qT[:, hs_slots[i], q0:q0 + 256],
                                             start=first, stop=(first or i == 1)),
                                     reads=[lkey, "qT%d" % hs_slots[i]], writes=[bk(b)], signal=(i == 1))
                        pi = pt_i[0] % NPT
                        pt_i[0] += 1
                        P.op("act", I("activation", out=pT[pi], in_=PS(b, 512), func=AF.Exp), reads=[bk(b)], writes=["pT%d" % pi])
                        pts.append((pi, kind, vt))

                def finish():
                    bo = rot_a1.next()
                    bl = rot_a2.next()
                    n = len(pts)
                    for i in range(2):
                        if same_kv and i == 1:
                            break
                        for ci, (pi, kind, vt) in enumerate(pts):
                            lv, vkey = (vS[:, vt, vcols[i]:vcols[i] + 128], "vS") if kind == "tok" else (vc_S[:, vt, vcols[i]:vcols[i] + 128], "vcS")
                            if same_kv:
                                P.op("pe", I("matmul", PS(bo, 512), lhsT=lv, rhs=pT[pi], start=(ci == 0), stop=(ci == n - 1)),
                                     reads=[vkey, "pT%d" % pi], writes=[bk(bo)], signal=(ci == n - 1))
                            else:
                                P.op("pe", I("matmul", PS(bo, 256, i * 256), lhsT=lv, rhs=pT[pi][:, i * 256:(i + 1) * 256],
                                             start=(ci == 0), stop=(ci == n - 1)),
                                     reads=[vkey, "pT%d" % pi], writes=[bk(bo)], signal=(ci == n - 1))
                    for ci, (pi, kind, vt) in enumerate(pts):
                        P.op("pe", I("matmul", PS(bl, 512), lhsT=onesb, rhs=pT[pi], start=(ci == 0), stop=(ci == n - 1)),
                             reads=["onesb", "pT%d" % pi], writes=[bk(bl)], signal=(ci == n - 1))
                    for i in range(2):
                        rl = rl_[i]
                        sc = esink[:, sink_cols[i]:sink_cols[i] + 1] if sink_cols is not None else zeroc
                        P.op("act", I("activation", out=rl, in_=PS(bl, 256, i * 256), func=AF.Ln, scale=1.0, bias=sc),
                             reads=[bk(bl), "esink", "epsc"], writes=["rl%d" % i])
                        P.op("act", I("activation", out=rl, in_=rl, func=AF.Exp, scale=-1.0), reads=["rl%d" % i], writes=["rl%d" % i])
                        P.op("dve", I("tensor_tensor", out=oT[:, mixer, pair_heads[i], q0:q0 + 256], in0=PS(bo, 256, i * 256), in1=rl, op=ALU.mult),
                             reads=[bk(bo), "rl%d" % i], writes=["oT"])
                return scores, finish, len(chunks)

            def run_items(items, hooks=()):
                prev = None
                prev_n = 0
                hook_at = {}
                for hi_, h in enumerate(hooks):
                    hook_at[(hi_ + 1) * len(items) // (len(hooks) + 1)] = h
                for ii_, (sc, fin, n) in enumerate(items):
                    if ii_ in hook_at:
                        hook_at[ii_]()
                    if prev is not None and prev_n + n > NPT:
                        prev()
                        prev = None
                    sc()
                    if prev is not None:
                        prev()
                    prev = fin
                    prev_n = n
                if prev is not None:
                    prev()

            def load_ctx(k_d, v_d, c0, ncols, nheads):
                P.dma("pool", kc_S[:, :, 0:ncols], k_d[:, c0:c0 + ncols].rearrange("(c p) n -> p c n", p=128), sem="kcS", writes=["kcS"])
                P.dma("pool", vc_S[:, :, 0:ncols], v_d[:, c0:c0 + ncols].rearrange("(c p) n -> p c n", p=128), sem="vcS", writes=["vcS"])
                for h in range(nheads):
                    b = rot_a2.next()
                    for c in range(2):
                        P.op("pe", I("transpose", PSB(b, 128, c * 128), kc_S[:, c, h * 128:(h + 1) * 128], identb),
                             reads=["kcS", "identb"], writes=[bk(b)], signal=(c == 1))
                    P.op("dve", I("tensor_copy", out=kc_T[:, h, :], in_=PSB(b, 256)), reads=[bk(b)], writes=["kcT"])

            Wq = small[:, 0:1]
            Wk = small[:, 1:2]
            Wqb = small[:, 2:3]
            Wkb = small[:, 3:4]
            set_mode("proj")
            load_ctx(cak_d, cav_d, 0, 256, 2)
            for h in range(2):
                proj_k(tkv, h, h, Wk, True, nak_o, h)
            proj_v(tkv, 256, 256, 0, nav_o, 0)
            ada_some(1)
            for rnd in range(2):
                set_mode("proj")
                tq = load_w16(w_in, rnd * 512)
                for cc in range(4):
                    proj_q(tq, cc, cc, Wq, True)
                drain()
                ada_some(1)
                set_mode("att")
                items = []
                for pr in range(2):
                    h0 = rnd * 4 + pr * 2
                    kvh = h0 // 4
                    for grp in range(3):
                        items.append(att_item([h0, h0 + 1], [kvh, kvh], [kvh * 128, kvh * 128], [pr * 2, pr * 2 + 1], 0,
                                              [h0, h0 + 1], lambda c: (maskA[:, c, :], "maskA"), grp))
                run_items(items, hooks=[lambda: ada_some(1), lambda: ada_some(1)])
            stop_at("oA", oT[:, 0], ["oT"])

            P.alias(["biasB0", "biasB1"], ["maskA", "cosT", "sinT"])
            bB_t = biasB_d.rearrange("(c p) n -> p c n", p=128)
            for rnd in range(2):
                set_mode("proj")
                tq = load_w16(w_in, 1536 + rnd * 512)
                load_ctx(cbk_d, cbv_d, rnd * 512, 512, 4)
                for cc in range(4):
                    proj_q(tq, cc, cc, Wqb, False)
                ada_some(1)
                tk = load_w16(w_in, 2560 + rnd * 512)
                for cc in range(4):
                    proj_k(tk, cc, cc, Wkb, False, nbk_o, rnd * 4 + cc)
                ada_some(1)
                tv = load_w16(w_in, 3584 + rnd * 512)
                proj_v(tv, 0, 512, 0, nbv_o, rnd * 512)
                drain()
                ada_some(1)
                set_mode("att")
                items = []
                for pr in range(2):
                    gp = rnd * 2 + pr
                    bi = gp % 2
                    P.dma("pool", biasB[bi], bB_t[:, :, gp * 512:(gp + 1) * 512], sem="biasB%d" % bi, writes=["biasB%d" % bi])
                    h0 = rnd * 4 + pr * 2
                    for grp in range(3):
                        items.append(att_item([h0, h0 + 1], [pr * 2, pr * 2 + 1], [pr * 256, pr * 256 + 128], [pr * 2, pr * 2 + 1], 1,
                                              None, lambda c, bi=bi: (biasB[bi][:, c, :], "biasB%d" % bi), grp))
                run_items(items, hooks=[lambda: ada_some(1), lambda: ada_some(1)])
            stop_at("oB", oT[:, 1], ["oT"])

            xkeys = ["x%d" % t for t in range(6)]
            newk = xkeys + ["sg0", "sg1", "tm0", "tm1", "t512_0", "t512_1", "onesf", "diag0", "diag1", "tA"]
            P.alias(newk, R1_ATT_KEYS)
            for t in range(6):
                P.dma("sp", xres[:, t, :], xm_t[t], sem="x%d" % t, writes=["x%d" % t])

            HALF = [(0, 384), (384, 384)]
            for cg in range(4):
                for mix in range(2):
                    wg = load_w16(w_in, (4608 if mix == 0 else 6656) + cg * 512)
                    wb = load_w8(w_bra if mix == 0 else w_brb, cg * 512)
                    for cc in range(4):
                        ch = cg * 4 + cc
                        for hi, (s0, sz) in enumerate(HALF):
                            bg = rot_main.next()
                            for k in range(KC):
                                P.op("pe", I("matmul", PS(bg, sz), lhsT=wg.k(k)[:, cc * 128:(cc + 1) * 128], rhs=hT[:, k, s0:s0 + sz],
                                             start=(k == 0), stop=(k == KC - 1)),
                                     reads=[wg.key(k), "hT%d" % k], writes=[bk(bg)], signal=(k == KC - 1 or k == 7))
                            by = (rot_a1 if hi == 0 else rot_a2).next()
                            for k in range(8):
                                P.op("pe", I("matmul", PS(by, sz), lhsT=wb.k(k)[:, cc * 128:(cc + 1) * 128], rhs=oT[:, mix, k, s0:s0 + sz],
                                             start=(k == 0), stop=(k == 7)),
                                     reads=[wb.key(k), "oT"], writes=[bk(by)], signal=(k == 7))
                            sg = sgs[hi][:, 0:sz]
                            P.op("act", I("activation", out=sg, in_=PS(bg, sz), func=AF.Sigmoid), reads=[bk(bg)], writes=["sg%d" % hi])
                            if mix == 0:
                                P.op("dve", I("tensor_tensor", out=tA[:, cc, s0:s0 + sz], in0=PS(by, sz), in1=sg, op=ALU.mult),
                                     reads=[bk(by), "sg%d" % hi], writes=["tA"])
                            else:
                                tm = tms[hi][:, 0:sz]
                                P.op("dve", I("tensor_tensor", out=tm, in0=PS(by, sz), in1=sg, op=ALU.mult),
                                     reads=[bk(by), "sg%d" % hi], writes=["tm%d" % hi])
                                P.op("dve", I("tensor_tensor", out=mTv[:, ch, s0:s0 + sz], in0=tm, in1=tA[:, cc, s0:s0 + sz], op=ALU.add),
                                     reads=["tm%d" % hi, "tA"], writes=["mT"])
            ada_finish()
            stop_at("mT", mTv, ["mT"])

            P.op("dve", I("memset", onesf, 1.0), writes=["onesf"])

            diag4 = [view(pb_scr + 11776 + i * 2048, [4, 128], F32) for i in range(2)]

            def build_G(mod_base, alias_from):
                P.alias(["G"], alias_from)
                di = 0
                for v in range(2):
                    for g4 in range(4):
                        b = rot_a1.next()
                        d4 = diag4[di % 2]
                        dk = "dg4_%d" % (di % 2)
                        di += 1
                        c0 = mod_base + g4 * 4
                        for j in range(4):
                            P.op("dve", I("tensor_scalar", out=d4[:, j, :], in0=identf, scalar1=modT[:, c0 + j, v:v + 1], scalar2=None, op0=ALU.mult),
                                 reads=["identf", "modT"], writes=[dk + "_%d" % j])
                        for j in range(4):
                            P.op("pe", I("matmul", PS(b, 128, j * 128), lhsT=onesf, rhs=d4[:, j, :], start=True, stop=True),
                                 reads=["onesf", dk + "_%d" % j], writes=[bk(b)], signal=(j == 3))
                        P.op("act", I("activation", out=Gt[:, v, g4 * 512:(g4 + 1) * 512], in_=PS(b, 512), func=AF.Copy),
                             reads=[bk(b)], writes=["G"])

            P.alias(["dg4_%d_%d" % (i, j) for i in range(2) for j in range(4)], ["tA"])
            build_G(32, ["oT"])

            for cg in range(4):
                wt = load_w16(w_out, cg * 512)
                for t in range(6):
                    v = 0 if t < 4 else 1
                    b = rot_main.next()
                    for k in range(KC):
                        P.op("pe", I("matmul", PS(b, 512), lhsT=mTv[:, k, t * 128:(t + 1) * 128], rhs=wt.k(k),
                                     start=(k == 0), stop=(k == KC - 1)),
                             reads=[wt.key(k), "mT"], writes=[bk(b)], signal=(k == KC - 1 or k == 7))
                    i = (cg * 6 + t) % 2
                    tp = tmp512[i]
                    P.op("dve", I("tensor_tensor", out=tp, in0=PS(b, 512), in1=Gt[:, v, cg * 512:(cg + 1) * 512], op=ALU.mult),
                         reads=[bk(b), "G"], writes=["t512_%d" % i])
                    P.op("dve", I("tensor_tensor", out=xres[:, t, cg * 512:(cg + 1) * 512], in0=xres[:, t, cg * 512:(cg + 1) * 512], in1=tp, op=ALU.add),
                         reads=["t512_%d" % i, "x%d" % t], writes=["x%d" % t])
            stop_at("x1", xres, ["x%d" % t for t in range(6)])

            xnk2 = ["xnn%d" % i for i in range(6)]
            P.alias(["h2T%d" % i for i in range(16)] + xnk2, ["hT%d" % i for i in range(16)] + ["oT"])
            xni = 0
            for tl, vec in [([0, 1, 2, 3], 0), ([4, 5], 1)]:
                tiles, xb, xk = [], [], []
                for ti in tl:
                    tiles.append((xres[:, ti, :], "x%d" % ti, None))
                    xb.append(xnn[xni % 6])
                    xk.append(xnk2[xni % 6])
                    xni += 1
                norm_group(tiles, None, vec, 2, 3, h2T, "h2T", xb, xk, tl[0] * 128, tl[0])
            stop_at("h2T", h2T, ["h2T%d" % i for i in range(16)])
            build_G(80, ["G"])

            P.alias(["uT"], xnk2)
            P.alias(["rs0", "rs1", "tb0", "tb1"], ["sg0", "sg1", "tm0", "tm1", "t512_0", "t512_1"])
            ri = 0
            for fg in range(4):
                for t4 in range(4):
                    wt = load_w16(w_up, fg * 2048 + t4 * 512)
                    for cc in range(4):
                        fc = t4 * 4 + cc
                        for (s0, sz) in HALF:
                            b = rot_main.next()
                            for k in range(KC):
                                P.op("pe", I("matmul", PS(b, sz), lhsT=wt.k(k)[:, cc * 128:(cc + 1) * 128], rhs=h2T[:, k, s0:s0 + sz],
                                             start=(k == 0), stop=(k == KC - 1)),
                                     reads=[wt.key(k), "h2T%d" % k], writes=[bk(b)], signal=(k == KC - 1 or k == 7))
                            i = ri % 2
                            ri += 1
                            rs = rsb[i][:, 0:sz]
                            P.op("act", I("activation", out=rs, in_=PS(b, sz), func=AF.Relu), reads=[bk(b)], writes=["rs%d" % i])
                            P.op("dve", I("tensor_tensor", out=uT[:, fc, s0:s0 + sz], in0=rs, in1=rs, op=ALU.mult), reads=["rs%d" % i], writes=["uT"])
                for cg in range(4):
                    wt = load_w16(w_down, cg * 512, r0=fg * 2048)
                    for t in range(6):
                        v = 0 if t < 4 else 1
                        b = rot_a1.next() if (t % 2 == 0) else rot_a2.next()
                        for k in range(KC):
                            P.op("pe", I("matmul", PS(b, 512), lhsT=uT[:, k, t * 128:(t + 1) * 128], rhs=wt.k(k),
                                         start=(k == 0), stop=(k == KC - 1)),
                                 reads=[wt.key(k), "uT"], writes=[bk(b)], signal=(k == KC - 1 or k == 7))
                        i = (cg * 6 + t) % 2
                        tp = tmp512b[i]
                        P.op("dve", I("tensor_tensor", out=tp, in0=PS(b, 512), in1=Gt[:, v, cg * 512:(cg + 1) * 512], op=ALU.mult),
                             reads=[bk(b), "G"], writes=["tb%d" % i])
                        P.op("dve", I("tensor_tensor", out=xres[:, t, cg * 512:(cg + 1) * 512], in0=xres[:, t, cg * 512:(cg + 1) * 512], in1=tp, op=ALU.add),
                             reads=["tb%d" % i, "x%d" % t], writes=["x%d" % t])

            ym_t = y_main.rearrange("(t p) d -> t p d", p=128)
            for t in range(6):
                P.dma("sp", ym_t[t], xres[:, t, :], sem="x%d" % t, reads=["x%d" % t])
            P.wait_all("sp", ["x%d" % t for t in range(6)] + ["ost0", "ost1"])

        try:
            body()
        except _Stop:
            pass
        fin = []
        for e_ in Prog.CE:
            if P.cnt[e_] > 0:
                fin.append((P.esem[e_], P.cnt[e_]))
        P.q["sp"].append((fin, None, None))

        with nc.Block() as block:
            @block.sync
            def _(e):
                P.emit("sp", e)

            @block.gpsimd
            def _(e):
                P.emit("pool", e)

            @block.tensor
            def _(e):
                P.emit("pe", e)

            @block.scalar
            def _(e):
                P.emit("act", e)

            @block.vector
            def _(e):
                P.emit("dve", e)
        build_program.stats = dict(n={k: len(v) for k, v in P.q.items()}, waits=P.n_wait, arena=A.off, cnt=dict(P.cnt))
    return nc


GRID_W = 64
ROWS = 16


def _core_geometry(j):
    b = j // 4
    qq = j % 4
    ws = 0 if qq < 2 else 4
    own_rows = list(range(4 * qq, 4 * qq + 4))
    halo_rows = [r for r in range(ws, ws + 12) if r not in own_rows]
    rows = own_rows + halo_rows
    pos = np.concatenate([np.arange(r * GRID_W, (r + 1) * GRID_W) for r in rows])
    return b, qq, pos


def _static_tables(j, rpb):
    b, qq, pos = _core_geometry(j)
    row = (pos // GRID_W).astype(np.int64)
    col = (pos % GRID_W).astype(np.int64)
    n_freq = 32
    inv = (10000.0 ** (-np.arange(n_freq, dtype=np.float32) / n_freq)).astype(np.float32)
    ang_r = row[:, None].astype(np.float32) * inv[None, :]
    ang_c = col[:, None].astype(np.float32) * inv[None, :]
    cosT = np.zeros((128, 768), np.float32)
    sinT = np.zeros((128, 768), np.float32)
    cosT[0:32] = np.cos(ang_r).T
    cosT[32:64] = np.cos(ang_r).T
    cosT[64:96] = np.cos(ang_c).T
    cosT[96:128] = np.cos(ang_c).T
    sinT[0:32] = -np.sin(ang_r).T
    sinT[32:64] = np.sin(ang_r).T
    sinT[64:96] = -np.sin(ang_c).T
    sinT[96:128] = np.sin(ang_c).T
    qpos = pos[:256]
    valid = np.abs(qpos[None, :] - pos[:, None]) <= 128
    mA = np.where(valid, 0.0, NEGM).astype(np.float32)
    maskA = np.concatenate([mA, mA], axis=1)
    qr = row[:256]
    qc = col[:256]
    rstart = np.clip(qr - 4, 0, ROWS - 8)
    cstart = np.clip(qc - 8, 0, GRID_W - 16)
    kr = row[:, None]
    kcc = col[:, None]
    vr = (kr >= rstart[None, :]) & (kr < rstart[None, :] + 8)
    vc = (kcc >= cstart[None, :]) & (kcc < cstart[None, :] + 16)
    valid = vr & vc
    dr = np.clip(kr - qr[None, :] + 7, 0, 14)
    dc = np.clip(kcc - qc[None, :] + 15, 0, 30)
    bias = rpb[:, dr, dc]
    bias = np.where(valid[None], bias, np.float32(NEGM)).astype(np.float32)
    biasB = np.ascontiguousarray(np.transpose(bias, (1, 0, 2))).reshape(768, 2048)
    return cosT, sinT, maskA, biasB


def _perm():
    p = np.zeros((128, 128), np.float32)
    for d in range(128):
        s = d + 32 if (d % 64) < 32 else d - 32
        p[s, d] = 1.0
    return p


_NC_CACHE = {}


def kernel(x_prompt, x_sample, cache_a_k, cache_a_v, cache_b_k, cache_b_v, c, c_ctx,
           norm1_w, norm2_w, w_ada, b_ada, w_in, q_norm_a, k_norm_a, q_norm_b, k_norm_b,
           sink_a, rpb_b, w_br_a, w_br_b, w_out, w_up, w_down, _debug=None, _cores=None):
    f = lambda a: np.ascontiguousarray(np.asarray(a, dtype=np.float32))
    x_prompt, x_sample = f(x_prompt), f(x_sample)
    c, c_ctx = f(c), f(c_ctx)
    cores = list(range(NCORES)) if _cores is None else _cores
    key = "dbg" if _debug is not None else "main"
    if key not in _NC_CACHE:
        _NC_CACHE[key] = build_program(_debug)
    nc = _NC_CACHE[key]

    ident = np.eye(128, dtype=np.float32)
    perm = _perm()
    shared = dict(
        ident=ident, perm=perm,
        w_ada=f(w_ada)[0], w_in=f(w_in)[0], w_br_a=f(w_br_a)[0], w_br_b=f(w_br_b)[0],
        w_out=f(w_out)[0], w_up=f(w_up)[0], w_down=f(w_down)[0],
    )
    rpb = f(rpb_b)[0]
    in_maps = []
    for idx_core, j in enumerate(cores):
        b, qq, pos = _core_geometry(j)
        xm = np.concatenate([x_prompt[2 * j], x_prompt[2 * j + 1], x_sample[b][pos[:256]]], axis=0)
        xh = x_sample[b][pos[256:]]
        cvec = np.stack([c_ctx, c[b]], axis=0)
        cT = np.ascontiguousarray(cvec.reshape(2, 16, 128).transpose(2, 1, 0)).reshape(128, 32)
        pp = np.zeros((128, PP_W), np.float32)
        pp[:, PP_N1:PP_N1 + 16] = f(norm1_w)[0].reshape(16, 128).T
        pp[:, PP_N2:PP_N2 + 16] = f(norm2_w)[0].reshape(16, 128).T
        pp[:, PP_BADA:PP_BADA + 96] = f(b_ada)[0].reshape(96, 128).T
        pp[:, PP_QNA] = f(q_norm_a)[0]
        pp[:, PP_KNA] = f(k_norm_a)[0]
        pp[:, PP_QNB] = f(q_norm_b)[0]
        pp[:, PP_KNB] = f(k_norm_b)[0]
        pp[:, PP_SINK:PP_SINK + 8] = f(sink_a)[0][None, :]
        cosT, sinT, maskA, biasB = _static_tables(j, rpb)
        m = dict(shared)
        m.update(
            x_main=np.ascontiguousarray(xm), x_halo=np.ascontiguousarray(xh), cT=cT, pp=pp,
            cosT=cosT, sinT=sinT, maskA=maskA, biasB=biasB,
            cak=f(cache_a_k)[b, 0].reshape(256, 256), cav=f(cache_a_v)[b, 0].reshape(256, 256),
            cbk=f(cache_b_k)[b, 0].reshape(256, 1024), cbv=f(cache_b_v)[b, 0].reshape(256, 1024),
        )
        in_maps.append(m)

    res = run_bass_kernel_spmd(nc, in_maps, core_ids=list(range(len(cores))))
    if _debug is not None:
        return res

    y_prompt = np.zeros((16, 256, D), np.float32)
    y_sample = np.zeros((2, 1024, D), np.float32)
    nak = np.zeros((16, 1, 256, 2, 128), np.float32)
    nav = np.zeros((16, 1, 256, 2, 128), np.float32)
    nbk = np.zeros((16, 1, 256, 8, 128), np.float32)
    nbv = np.zeros((16, 1, 256, 8, 128), np.float32)
    for idx, j in enumerate(cores):
        r = res.results[idx]
        b, qq, pos = _core_geometry(j)
        ym = r["y_main"]
        y_prompt[2 * j] = ym[0:256]
        y_prompt[2 * j + 1] = ym[256:512]
        y_sample[b][pos[:256]] = ym[512:768]
        for s in range(2):
            nak[2 * j + s, 0] = r["nak"][s * 256:(s + 1) * 256].reshape(256, 2, 128)
            nav[2 * j + s, 0] = r["nav"][s * 256:(s + 1) * 256].reshape(256, 2, 128)
            nbk[2 * j + s, 0] = r["nbk"][s * 256:(s + 1) * 256].reshape(256, 8, 128)
            nbv[2 * j + s, 0] = r["nbv"][s * 256:(s + 1) * 256].reshape(256, 8, 128)
    return (y_prompt, y_sample, nak, nav, nbk, nbv)
```

```python
import os
import numpy as np
import concourse.bass as bass
import concourse.mybir as mybir
from concourse.bass_utils import run_bass_kernel_spmd

F32 = mybir.dt.float32
BF16 = mybir.dt.bfloat16
ALU = mybir.AluOpType
AF = mybir.ActivationFunctionType
AX = mybir.AxisListType

NCORES = 8
D = 2048
KC = 16
NMAIN = 768
NHALO = 512
NTOK = NMAIN + NHALO
EPS = 1e-6
NEGM = -30000.0
IN_W = 8704
DFF = 8192
CH_P = (0, 512)
CH_S = (512, 256)
CH_H = (768, 512)

PP_N1 = 0
PP_N2 = 16
PP_BADA = 32
PP_QNA = 128
PP_KNA = 129
PP_QNB = 130
PP_KNB = 131
PP_SINK = 132
PP_SEL = 140
PP_W = 142


class Prog:
    CE = ("pe", "act", "dve", "pool")
    ALL = ("pe", "act", "dve", "pool", "sp")

    def __init__(self, nc, esems, dma_sems):
        self.nc = nc
        self.q = {e: [] for e in self.ALL}
        self.cnt = {e: 0 for e in self.CE}
        self.esem = esems
        self.free_dsems = list(dma_sems)
        self.dsem = {}
        self.seen = {e: {} for e in self.ALL}
        self.lastw = {}
        self.readers = {}
        self.n_wait = 0

    def _need(self, eng, ev, waits):
        if ev is None:
            return
        semkey, handle, val, _ = ev
        if self.seen[eng].get(semkey, 0) >= val:
            return
        self.seen[eng][semkey] = val
        waits.append((handle, val))
        self.n_wait += 1

    def _deps(self, eng, reads, writes, waits):
        for k in reads:
            ev = self.lastw.get(k)
            if ev is not None and not (ev[3] == "pe" and eng == "pe"):
                self._need(eng, ev, waits)
        for k in writes:
            ev = self.lastw.get(k)
            if ev is not None and not (ev[3] == "pe" and eng == "pe"):
                self._need(eng, ev, waits)
            for ev in self.readers.get(k, ()):
                if not (ev[3] == "pe" and eng == "pe"):
                    self._need(eng, ev, waits)

    def _commit(self, ev, reads, writes):
        for k in reads:
            lst = self.readers.setdefault(k, [])
            lst[:] = [e for e in lst if e[0] != ev[0]]
            lst.append(ev)
        for k in writes:
            self.lastw[k] = ev
            self.readers[k] = []

    def op(self, eng, fn, reads=(), writes=(), signal=True):
        waits = []
        self._deps(eng, reads, writes, waits)
        if signal:
            self.cnt[eng] += 1
            val = self.cnt[eng]
        else:
            val = self.cnt[eng] + 1
        ev = ("E" + eng, self.esem[eng], val, eng)
        self._commit(ev, reads, writes)
        self.q[eng].append((waits, fn, (self.esem[eng], 1) if signal else None))

    def dma(self, eng, out, in_, sem, reads=(), writes=(), **kw):
        waits = []
        self._deps(eng, reads, writes, waits)
        if sem not in self.dsem:
            self.dsem[sem] = [self.free_dsems.pop(), 0]
        rec = self.dsem[sem]
        rec[1] += 16
        ev = ("D" + sem, rec[0], rec[1], "dma")
        self._commit(ev, reads, writes)
        self.q[eng].append((waits, (lambda e, o=out, i=in_, k=kw: e.dma_start(out=o, in_=i, **k)), (rec[0], 16)))

    def custom(self, eng, fn, sem, inc, reads=(), writes=()):
        waits = []
        self._deps(eng, reads, writes, waits)
        if sem not in self.dsem:
            self.dsem[sem] = [self.free_dsems.pop(), 0]
        rec = self.dsem[sem]
        rec[1] += inc
        ev = ("D" + sem, rec[0], rec[1], "dma")
        self._commit(ev, reads, writes)
        self.q[eng].append((waits, fn, (rec[0], inc)))

    def wait_all(self, eng, keys):
        waits = []
        for k in keys:
            self._need(eng, self.lastw.get(k), waits)
            for ev in self.readers.get(k, ()):
                self._need(eng, ev, waits)
        if waits:
            self.q[eng].append((waits, None, None))

    def alias(self, new_keys, old_keys):
        evs = []
        for k in old_keys:
            if self.lastw.get(k) is not None:
                evs.append(self.lastw[k])
            evs.extend(self.readers.get(k, ()))
        for k in new_keys:
            self.lastw[k] = None
            self.readers[k] = list(evs)

    def emit(self, eng_name, eng):
        for waits, fn, inc in self.q[eng_name]:
            for h, v in waits:
                eng.wait_ge(h, v)
            if fn is None:
                continue
            ins = fn(eng)
            if inc is not None:
                ins.then_inc(inc[0], inc[1])


def I(method, *a, **k):
    return lambda e: getattr(e, method)(*a, **k)


class Rot:
    def __init__(self, items):
        self.items = list(items)
        self.i = 0

    def next(self):
        v = self.items[self.i % len(self.items)]
        self.i += 1
        return v


def build_program(debug=None):
    nc = bass.Bass("TRN2", target_bir_lowering=False)

    def din(name, shape, dt=F32):
        return nc.dram_tensor(name, list(shape), dt, kind="ExternalInput").ap()

    def dout(name, shape, dt=F32):
        return nc.dram_tensor(name, list(shape), dt, kind="ExternalOutput").ap()

    x_main = din("x_main", [NMAIN, D])
    x_halo = din("x_halo", [NHALO, D])
    cT_d = din("cT", [128, 32])
    pp_d = din("pp", [128, PP_W])
    ident_d = din("ident", [128, 128])
    perm_d = din("perm", [128, 128])
    cos_d = din("cosT", [128, NMAIN])
    sin_d = din("sinT", [128, NMAIN])
    maskA_d = din("maskA", [NMAIN, 512])
    biasB_d = din("biasB", [NMAIN, 2048])
    cak_d = din("cak", [256, 256])
    cav_d = din("cav", [256, 256])
    cbk_d = din("cbk", [256, 1024])
    cbv_d = din("cbv", [256, 1024])
    w_ada = din("w_ada", [D, 6 * D])
    w_in = din("w_in", [D, IN_W])
    w_bra = din("w_br_a", [1024, D])
    w_brb = din("w_br_b", [1024, D])
    w_out = din("w_out", [D, D])
    w_up = din("w_up", [D, DFF])
    w_down = din("w_down", [DFF, D])

    y_main = dout("y_main", [NMAIN, D])
    nak_o = dout("nak", [512, 256])
    nav_o = dout("nav", [512, 256])
    nbk_o = dout("nbk", [512, 1024])
    nbv_o = dout("nbv", [512, 1024])
    dbg_o = None
    if debug is not None:
        dbg_o = dout("dbg", debug["shape"], BF16 if debug.get("dtype") == "bf16" else F32)

    import contextlib
    es = contextlib.ExitStack()
    with es:
        ARENA_W = 53200
        arena = es.enter_context(nc.sbuf_tensor("arena", [128, ARENA_W], F32))
        ps = es.enter_context(nc.psum_tensor("ps", [128, 8, 512], F32))
        sem_names = ["pe", "act", "dve", "pool"]
        esems = {n: es.enter_context(nc.semaphore("s_" + n)) for n in sem_names}
        dsems = [es.enter_context(nc.semaphore("d%d" % i)) for i in range(70)]
        P = Prog(nc, esems, dsems)

        class Arena:
            def __init__(self):
                self.off = 0

            def take(self, nbytes):
                o = self.off
                self.off += (nbytes + 63) // 64 * 64
                assert self.off <= ARENA_W * 4, ("arena overflow", self.off)
                return o

        def view(off_b, shape, dt):
            esz = 2 if dt == BF16 else 4
            n = int(np.prod(shape))
            assert off_b % 4 == 0
            w0 = off_b // 4
            nw = (n * esz + 3) // 4
            ap = arena[:, w0:w0 + nw]
            if dt == BF16:
                ap = ap.bitcast(BF16)
            if len(shape) == 2:
                ap = ap.rearrange("p (a b) -> p a b", a=shape[0])
            elif len(shape) == 3:
                ap = ap.rearrange("p (a b c) -> p a b c", a=shape[0], b=shape[1])
            return ap

        A = Arena()
        RING_N = 6
        ring_off = [A.take(8192) for _ in range(RING_N)]
        o_pp = A.take(PP_W * 4)
        o_identf = A.take(512)
        o_identb = A.take(256)
        o_permf = A.take(512)
        o_onesb = A.take(256)
        o_modT = A.take(96 * 2 * 4)
        o_ab = A.take(4 * 2 * 16 * 4)
        o_small = A.take(256)
        o_esink = A.take(32)
        o_cT = A.take(128)
        o_scT = A.take(64)

        pp = view(o_pp, [PP_W], F32)
        identf = view(o_identf, [128], F32)
        identb = view(o_identb, [128], BF16)
        permf = view(o_permf, [128], F32)
        onesb = view(o_onesb, [128], BF16)
        modT = view(o_modT, [96, 2], F32)
        abv = view(o_ab, [4, 2, 16], F32)
        small = view(o_small, [64], F32)
        esink = view(o_esink, [8], F32)
        epsc = small[:, 20:21]
        epsc128 = small[:, 21:22]
        zeroc = small[:, 22:23]

        R1 = A.take(69632)
        R2 = A.take(65536)
        mT_off = A.take(KC * NMAIN * 2)
        mTv = view(mT_off, [KC, NMAIN], BF16)
        hT = view(R2, [KC, NTOK], BF16)
        oT = view(R2 + 40960, [2, 8, NMAIN], BF16)
        Gt = view(R2 + 65536 - 16384, [2, 2048], F32)
        h2T = view(R2, [KC, NMAIN], BF16)
        uT = view(R2 + 24576, [KC, NMAIN], BF16)
        xnn = [view(R2 + 24576 + i * 4096, [2048], BF16) for i in range(6)]
        xs_ = [view(R1 + i * 8192, [2048], F32) for i in range(3)]
        xn_ = [view(R1 + 24576 + i * 4096, [2048], BF16) for i in range(10)]
        o = R1
        qT = view(o, [4, NMAIN], BF16); o += 4 * NMAIN * 2
        kT = view(o, [4, NTOK], BF16); o += 4 * NTOK * 2
        vS = view(o, [10, 512], BF16); o += 10 * 512 * 2
        NPT = 10
        pT = [view(o + i * 1024, [512], BF16) for i in range(NPT)]; o += NPT * 1024
        m = o
        maskA = view(m, [6, 512], BF16)
        cosT = view(m + 6144, [NMAIN], F32)
        sinT = view(m + 9216, [NMAIN], F32)
        biasB = [view(m + i * 6144, [6, 512], BF16) for i in range(2)]
        kc_T = view(m + 12288, [4, 256], BF16)
        vc_S = view(m + 14336, [2, 512], BF16)
        kc_S = view(m + 16384, [2, 512], BF16)
        f32s = [view(m + 18432 + i * 2048, [512], F32) for i in range(3)]
        o += 24576
        ost = [view(o + i * 2048, [512], F32) for i in range(2)]; o += 2 * 2048
        sqb = [view(o + i * 1024, [512], BF16) for i in range(2)]; o += 2 * 1024
        rl_ = [view(o + i * 1024, [256], F32) for i in range(2)]; o += 2 * 1024
        assert o - R1 <= 69632, o - R1
        R1_ATT_KEYS = (["qT%d" % i for i in range(4)] + ["kT%d" % i for i in range(4)] + ["vS", "kcT", "vcS", "kcS", "f32s0", "f32s1", "f32s2", "ost0", "ost1", "sqb0", "sqb1",
                        "rl0", "rl1", "biasB0", "biasB1", "maskA", "cosT", "sinT"] + ["pT%d" % i for i in range(NPT)])
        xres = view(R1, [6, 2048], F32)
        pb_scr = R1 + 49152
        sgs = [view(pb_scr + i * 1536, [384], F32) for i in range(2)]
        tms = [view(pb_scr + 3072 + i * 1536, [384], F32) for i in range(2)]
        tmp512 = [view(pb_scr + 6144 + i * 2048, [512], F32) for i in range(2)]
        onesf = view(pb_scr + 10240, [128], F32)
        diag = [view(pb_scr + 10752 + i * 512, [128], F32) for i in range(2)]
        tA = view(pb_scr + 11776, [4, NMAIN], BF16)
        rsb = [view(pb_scr + i * 1024, [384], BF16) for i in range(2)]
        tmp512b = [view(pb_scr + 4096 + i * 2048, [512], F32) for i in range(2)]

        def PS(bank, n=512, off=0):
            return ps[:, bank, off:off + n]

        def PSB(bank, n, off=0):
            return ps[:, bank, :].bitcast(BF16)[:, off:off + n]

        rot_main = Rot([0, 1, 2, 3])
        rot_a1 = Rot([4, 5])
        rot_a2 = Rot([6, 7])

        def bk(b):
            return "ps%d" % b

        def set_mode(m):
            if m == "proj":
                rot_main.items, rot_a1.items, rot_a2.items = [0, 1, 2, 3, 4, 5], [6], [7]
            else:
                rot_main.items, rot_a1.items, rot_a2.items = [0, 1, 2, 3], [4, 5], [6, 7]

        ring_state = {"n": 0}

        class WT:
            def __init__(self, slots):
                self.slots = slots

            def k(self, k):
                return view(ring_off[self.slots[k // 8]], [8, 512], BF16)[:, k % 8, :]

            def key(self, k):
                return "ring%d" % self.slots[k // 8]

        def load_entry(src):
            n = ring_state["n"]
            ring_state["n"] += 1
            slot = n % RING_N
            key = "ring%d" % slot
            P.dma("pool", view(ring_off[slot], [8, 512], BF16), src, sem=key, writes=[key])
            return slot

        def wsrc(w, r0, nr, c0, ncol):
            return w[r0:r0 + nr, c0:c0 + ncol].rearrange("(k p) n -> p k n", p=128)

        def load_w16(w, c0, r0=0):
            n = ring_state["n"]
            slot = n % RING_N
            if slot % 2 == 0:
                ring_state["n"] += 2
                k0, k1 = "ring%d" % slot, "ring%d" % (slot + 1)
                dst = view(ring_off[slot], [16, 512], BF16)
                P.dma("pool", dst, wsrc(w, r0, 2048, c0, 512), sem=k0, writes=[k0, k1])
                return WT([slot, slot + 1])
            return WT([load_entry(wsrc(w, r0, 1024, c0, 512)), load_entry(wsrc(w, r0 + 1024, 1024, c0, 512))])

        def load_w8(w, c0):
            return WT([load_entry(wsrc(w, 0, 1024, c0, 512))])

        class _Stop(Exception):
            pass

        def stop_at(name, ap, keys):
            if debug is not None and debug["at"] == name:
                P.dma("sp", dbg_o, ap, sem="dbg", reads=keys)
                P.wait_all("sp", keys)
                raise _Stop()

        chains = []

        def tick():
            for ch in list(chains):
                step = ch.pop(0)
                step()
                if not ch:
                    chains.remove(ch)

        def drain():
            while chains:
                tick()

        def body():
            P.dma("sp", pp, pp_d, sem="pp", writes=["pp"])
            P.dma("sp", identf, ident_d, sem="identf", writes=["identf"])
            P.dma("sp", permf, perm_d, sem="permf", writes=["permf"])
            P.dma("pool", identb, ident_d, sem="identb", writes=["identb"])
            P.op("dve", I("memset", onesb, 1.0), writes=["onesb"])
            P.op("dve", I("memset", epsc, EPS), writes=["epsc"])
            P.op("dve", I("memset", epsc128, 128.0 * EPS), writes=["epsc"])
            P.op("dve", I("memset", zeroc, 0.0), writes=["epsc"])
            cTf = view(o_cT, [32], F32)
            scTflat = view(o_scT, [32], BF16)
            scTb = scTflat.rearrange("p (k v) -> p k v", v=2)
            P.dma("sp", cTf, cT_d, sem="cT", writes=["cTf"])
            P.op("act", I("activation", out=scTflat, in_=cTf, func=AF.Silu), reads=["cTf"], writes=["scT"])
            P.op("act", I("activation", out=esink, in_=pp[:, PP_SINK:PP_SINK + 8], func=AF.Exp), reads=["pp"], writes=["esink"])

            def norm_group(tiles, src_fn, vec, kindA, kindB, dstT, dst_key, xbufs, xkeys, tok0, stat0):
                norm_stats(tiles, xbufs, xkeys, stat0)
                norm_tr(len(tiles), vec, kindA, kindB, dstT, dst_key, xbufs, xkeys, tok0)

            def norm_stats(tiles, xbufs, xkeys, stat0):
                for i, (xin, xkey, loader) in enumerate(tiles):
                    if loader is not None:
                        loader()
                    xnb, xnk = xbufs[i], xkeys[i]
                    si = (stat0 + i) % 10
                    ssq = small[:, 24 + si:25 + si]
                    rst = small[:, 36 + si:37 + si]
                    sk = "ssq%d" % si
                    P.op("act", I("activation", out=xnb, in_=xin, func=AF.Square, accum_out=ssq), reads=[xkey], writes=[xnk, sk])
                    P.op("act", I("activation", out=rst, in_=ssq, func=AF.Sqrt, scale=1.0 / D, bias=epsc), reads=[sk, "epsc"], writes=[sk + "r"])
                    P.op("dve", I("reciprocal", out=rst, in_=rst), reads=[sk + "r"], writes=[sk + "r"])
                    P.op("dve", I("tensor_scalar", out=xnb, in0=xin, scalar1=rst, scalar2=None, op0=ALU.mult), reads=[xkey, sk + "r"], writes=[xnk])

            def norm_tr(nt, vec, kindA, kindB, dstT, dst_key, xbufs, xkeys, tok0, chunks=None):
                for c in (range(KC) if chunks is None else chunks):
                    b = rot_main.next()
                    for i in range(nt):
                        P.op("pe", I("transpose", PSB(b, 128, i * 128), xbufs[i][:, c * 128:(c + 1) * 128], identb),
                             reads=[xkeys[i], "identb"], writes=[bk(b)], signal=(i == nt - 1))
                    dst = dstT[:, c, tok0:tok0 + nt * 128]
                    if c % 2 == 0:
                        P.op("act", I("activation", out=dst, in_=PSB(b, nt * 128), func=AF.Identity,
                                      scale=abv[:, kindA, vec, c:c + 1], bias=abv[:, kindB, vec, c:c + 1]),
                             reads=[bk(b), "ab", "ab2"], writes=[dst_key + str(c)])
                    else:
                        P.op("dve", I("tensor_scalar", out=dst, in0=PSB(b, nt * 128), scalar1=abv[:, kindA, vec, c:c + 1],
                                      scalar2=abv[:, kindB, vec, c:c + 1], op0=ALU.mult, op1=ALU.add),
                             reads=[bk(b), "ab", "ab2"], writes=[dst_key + str(c)])

            xm_t = x_main.rearrange("(t p) d -> t p d", p=128)
            xh_t = x_halo.rearrange("(t p) d -> t p d", p=128)
            norm1_groups = []
            xi = 0
            for tl, vec in [([0, 1, 2, 3], 0), ([4, 5], 1), ([6, 7, 8, 9], 1)]:
                tiles, xb, xk = [], [], []
                for ti in tl:
                    bi = xi % 3
                    xi += 1
                    src = xm_t[ti] if ti < 6 else xh_t[ti - 6]
                    ld = (lambda bi=bi, src=src: P.dma("sp", xs_[bi], src, sem="xs%d" % bi, writes=["xs%d" % bi]))
                    tiles.append((xs_[bi], "xs%d" % bi, ld))
                    xb.append(xn_[ti])
                    xk.append("xn%d" % ti)
                norm_stats(tiles, xb, xk, tl[0])
                norm1_groups.append((tl, vec, xb, xk))

            ada_pending = [0, 4, 1, 5, 2, 6, 3, 7] + list(range(8, 24))

            def ada_some(n):
                for _ in range(min(n, len(ada_pending))):
                    t = ada_pending.pop(0)
                    wt = load_w16(w_ada, t * 512)
                    b = rot_a2.next()
                    for cc in range(4):
                        for k in range(KC):
                            P.op("pe", I("matmul", ps[:, b, cc * 2:cc * 2 + 2], lhsT=wt.k(k)[:, cc * 128:(cc + 1) * 128],
                                         rhs=scTb[:, k, :], start=(k == 0), stop=(k == KC - 1)),
                                 reads=[wt.key(k), "scT"], writes=[bk(b)], signal=(k == KC - 1 or k == 7))
                    for v in range(2):
                        P.op("dve", I("tensor_tensor", out=modT[:, t * 4:(t + 1) * 4, v],
                                      in0=ps[:, b, 0:8].rearrange("p (c v) -> p c v", v=2)[:, :, v],
                                      in1=pp[:, PP_BADA + t * 4:PP_BADA + t * 4 + 4], op=ALU.add),
                             reads=[bk(b), "pp"], writes=["modT"])

            for qi in range(4):
                ada_some(2)
                c4 = slice(4 * qi, 4 * qi + 4)
                for v in range(2):
                    P.op("dve", I("scalar_tensor_tensor", out=abv[:, 0, v, c4], in0=modT[:, 16 + 4 * qi:20 + 4 * qi, v], scalar=1.0,
                                  in1=pp[:, PP_N1 + 4 * qi:PP_N1 + 4 * qi + 4], op0=ALU.add, op1=ALU.mult), reads=["modT", "pp"], writes=["ab"])
                    P.op("dve", I("tensor_copy", out=abv[:, 1, v, c4], in_=modT[:, 4 * qi:4 * qi + 4, v]), reads=["modT"], writes=["ab"])
                for (tl, vec, xb, xk) in norm1_groups:
                    norm_tr(len(tl), vec, 0, 1, hT, "hT", xb, xk, tl[0] * 128, chunks=range(4 * qi, 4 * qi + 4))
            SQ128 = float(np.sqrt(128.0))
            P.op("dve", I("tensor_copy", out=small[:, 0:1], in_=pp[:, PP_QNA:PP_QNA + 1]), reads=["pp"], writes=["small"])
            P.op("dve", I("tensor_scalar", out=small[:, 1:2], in0=pp[:, PP_KNA:PP_KNA + 1], scalar1=SQ128, scalar2=None, op0=ALU.mult), reads=["pp"], writes=["small"])
            P.op("dve", I("tensor_copy", out=small[:, 2:3], in_=pp[:, PP_QNB:PP_QNB + 1]), reads=["pp"], writes=["small"])
            P.op("dve", I("tensor_scalar", out=small[:, 3:4], in0=pp[:, PP_KNB:PP_KNB + 1], scalar1=SQ128, scalar2=None, op0=ALU.mult), reads=["pp"], writes=["small"])

            def ada_finish():
                ada_some(len(ada_pending))
                for v in range(2):
                    P.op("dve", I("scalar_tensor_tensor", out=abv[:, 2, v, :], in0=modT[:, 64:80, v], scalar=1.0,
                                  in1=pp[:, PP_N2:PP_N2 + 16], op0=ALU.add, op1=ALU.mult), reads=["modT", "pp"], writes=["ab2"])
                    P.op("dve", I("tensor_copy", out=abv[:, 3, v, :], in_=modT[:, 48:64, v]), reads=["modT"], writes=["ab2"])
            stop_at("ada", modT, ["modT", "ab", "small"])

            tkv = load_w16(w_in, 1024)
            stop_at("hT", hT, ["hT%d" % i for i in range(16)])

            P.alias(R1_ATT_KEYS, ["xs0", "xs1", "xs2"] + ["xn%d" % i for i in range(10)])
            P.dma("pool", maskA, maskA_d.rearrange("(c p) n -> p c n", p=128), sem="maskA", writes=["maskA"])
            P.dma("sp", cosT, cos_d, sem="cosT", writes=["cosT"])
            P.dma("sp", sinT, sin_d, sem="sinT", writes=["sinT"])

            cstate = {"i": 0, "ost": 0}

            def proj_fm(wt, cc, chunks, make_chain):
                for (s0, sz) in chunks:
                    b = rot_main.next()
                    for k in range(KC):
                        P.op("pe", I("matmul", PS(b, sz), lhsT=wt.k(k)[:, cc * 128:(cc + 1) * 128], rhs=hT[:, k, s0:s0 + sz],
                                     start=(k == 0), stop=(k == KC - 1)),
                             reads=[wt.key(k), "hT%d" % k], writes=[bk(b)], signal=(k == KC - 1 or k == 7))
                    tick()
                    make_chain(b, s0, sz)

            def norm_steps(b, sz, wcol, out_ap, out_key):
                i = cstate["i"] % 2
                cstate["i"] += 1
                sq = sqb[i][:, 0:sz]
                sqk = "sqb%d" % i
                P.op("act", I("activation", out=sq, in_=PS(b, sz), func=AF.Square), reads=[bk(b)], writes=[sqk])

                def step1():
                    b2 = rot_a1.next()
                    P.op("pe", I("matmul", PS(b2, sz), lhsT=onesb, rhs=sq, start=True, stop=True), reads=[sqk, "onesb"], writes=[bk(b2)])
                    rr = f32s[2][:, 0:sz]
                    P.op("act", I("activation", out=rr, in_=PS(b2, sz), func=AF.Ln, scale=1.0, bias=epsc128), reads=[bk(b2), "epsc"], writes=["f32s2"])
                    P.op("act", I("activation", out=rr, in_=rr, func=AF.Exp, scale=-0.5), reads=["f32s2"], writes=["f32s2"])
                    P.op("dve", I("scalar_tensor_tensor", out=out_ap, in0=PS(b, sz), scalar=wcol, in1=rr, op0=ALU.mult, op1=ALU.mult),
                         reads=[bk(b), "f32s2", "small"], writes=[out_key])
                return step1

            def rope_step(src_ap, src_key, e0, sz, out_ap, out_key):
                def step():
                    b3 = rot_a2.next()
                    P.op("pe", I("matmul", PS(b3, sz), lhsT=permf, rhs=src_ap, start=True, stop=True), reads=[src_key, "permf"], writes=[bk(b3)])
                    t1 = f32s[2][:, 0:sz]
                    P.op("dve", I("tensor_tensor", out=t1, in0=PS(b3, sz), in1=sinT[:, e0:e0 + sz], op=ALU.mult), reads=[bk(b3), "sinT"], writes=["f32s2"])
                    P.op("dve", I("tensor_tensor", out=src_ap, in0=src_ap, in1=cosT[:, e0:e0 + sz], op=ALU.mult), reads=[src_key, "cosT"], writes=[src_key])
                    P.op("dve", I("tensor_tensor", out=out_ap, in0=src_ap, in1=t1, op=ALU.add), reads=[src_key, "f32s2"], writes=[out_key])
                return step

            def kout_step(kn_ap, kn_key, dst_dram, head_col, kt_dst, kt_key):
                def step():
                    b4 = rot_a2.next()
                    for t in range(4):
                        P.op("pe", I("transpose", PS(b4, 128, t * 128), kn_ap[:, t * 128:(t + 1) * 128], identf),
                             reads=[kn_key, "identf"], writes=[bk(b4)], signal=(t == 3))
                    i = cstate["ost"] % 2
                    cstate["ost"] += 1
                    st = ost[i]
                    P.op("act", I("activation", out=st, in_=PS(b4, 512), func=AF.Copy), reads=[bk(b4)], writes=["ost%d" % i])
                    P.dma("sp", dst_dram.rearrange("(t p) n -> p t n", p=128)[:, :, head_col * 128:(head_col + 1) * 128],
                          st.rearrange("p (t d) -> p t d", t=4), sem="ost%d" % i, reads=["ost%d" % i])
                    P.op("act", I("activation", out=kt_dst, in_=kn_ap, func=AF.Copy), reads=[kn_key], writes=[kt_key])
                return step

            tmp_i = [0]

            def _nop():
                pass

            def proj_q(wt, cc, hslot, wcol, do_rope):
                def mk(b, s0, sz):
                    dst = qT[:, hslot, s0:s0 + sz]
                    if do_rope and s0 == CH_S[0]:
                        fi = tmp_i[0] % 2
                        tmp_i[0] += 1
                        tmp = f32s[fi][:, 0:sz]
                        chains.append([norm_steps(b, sz, wcol, tmp, "f32s%d" % fi), _nop, rope_step(tmp, "f32s%d" % fi, 0, sz, dst, "qT%d" % hslot)])
                    else:
                        chains.append([norm_steps(b, sz, wcol, dst, "qT%d" % hslot)])
                proj_fm(wt, cc, [CH_P, CH_S], mk)

            def proj_k(wt, cc, hslot, wcol, do_rope, dst_dram, head_col):
                def mk(b, s0, sz):
                    fi = tmp_i[0] % 2
                    tmp_i[0] += 1
                    tmp = f32s[fi][:, 0:sz]
                    tk = "f32s%d" % fi
                    dst = kT[:, hslot, s0:s0 + sz]
                    st1 = norm_steps(b, sz, wcol, tmp, tk)
                    if s0 == 0:
                        chains.append([st1, _nop, kout_step(tmp, tk, dst_dram, head_col, dst, "kT%d" % hslot)])
                    elif do_rope:
                        chains.append([st1, _nop, rope_step(tmp, tk, s0 - 512, sz, dst, "kT%d" % hslot)])
                    else:
                        def cp(tmp=tmp, tk=tk, dst=dst):
                            P.op("act", I("activation", out=dst, in_=tmp, func=AF.Copy), reads=[tk], writes=["kT%d" % hslot])
                        chains.append([st1, _nop, cp])
                proj_fm(wt, cc, [CH_P, CH_S, CH_H], mk)

            def proj_v(wt, c0, ncols, vcol0, dst_dram, dcol0):
                for ti in range(10):
                    b = rot_main.next()
                    for k in range(KC):
                        P.op("pe", I("matmul", PS(b, ncols), lhsT=hT[:, k, ti * 128:(ti + 1) * 128], rhs=wt.k(k)[:, c0:c0 + ncols],
                                     start=(k == 0), stop=(k == KC - 1)),
                             reads=[wt.key(k), "hT%d" % k], writes=[bk(b)], signal=(k == KC - 1 or k == 7))
                    tick()
                    if ti < 4:
                        i = cstate["ost"] % 2
                        cstate["ost"] += 1
                        st = ost[i][:, 0:ncols]
                        P.op("act", I("activation", out=st, in_=PS(b, ncols), func=AF.Copy), reads=[bk(b)], writes=["ost%d" % i])
                        P.op("dve", I("tensor_copy", out=vS[:, ti, vcol0:vcol0 + ncols], in_=st), reads=["ost%d" % i], writes=["vS"])
                        P.dma("sp", dst_dram[ti * 128:(ti + 1) * 128, dcol0:dcol0 + ncols], st, sem="ost%d" % i, reads=["ost%d" % i])
                    else:
                        P.op("dve", I("tensor_copy", out=vS[:, ti, vcol0:vcol0 + ncols], in_=PS(b, ncols)), reads=[bk(b)], writes=["vS"])

            pt_i = [0]

            def att_item(pair_heads, kslots, vcols, hs_slots, mixer, sink_cols, sample_bias, grp):
                same_kv = (kslots[0] == kslots[1]) and (vcols[0] == vcols[1])
                q0 = grp * 256
                if grp < 2:
                    chunks = [("tok", grp * 256 + c * 128, grp * 2 + c, None) for c in range(2)]
                else:
                    chunks = [("tok", 512 + c * 128, 4 + c, c) for c in range(6)] + [("ctx", c * 128, c, None) for c in range(2)]
                pts = []

                def scores():
                    for (kind, k0, vt, bc) in chunks:
                        b = rot_main.next()
                        first = True
                        if bc is not None:
                            bap, bkey = sample_bias(bc)
                            P.op("pe", I("matmul", PS(b, 512), lhsT=identb, rhs=bap, start=True, stop=False),
                                 reads=[bkey, "identb"], writes=[bk(b)], signal=False)
                            first = False
                        if same_kv:
                            lk, lkey = (kT[:, kslots[0], k0:k0 + 128], "kT%d" % kslots[0]) if kind == "tok" else (kc_T[:, kslots[0], k0:k0 + 128], "kcT")
                            P.op("pe", I("matmul", PS(b, 512).rearrange("p (a q) -> p a q", a=2), lhsT=lk,
                                         rhs=qT[:, hs_slots[0]:hs_slots[0] + 2, q0:q0 + 256], start=first, stop=True),
                                 reads=[lkey, "qT%d" % hs_slots[0], "qT%d" % (hs_slots[0] + 1)], writes=[bk(b)])
                        else:
                            for i in range(2):
                                lk, lkey = (kT[:, kslots[i], k0:k0 + 128], "kT%d" % kslots[i]) if kind == "tok" else (kc_T[:, kslots[i], k0:k0 + 128], "kcT")
                                P.op("pe", I("matmul", PS(b, 256, i * 256), lhsT=lk, rhs=qT[:, hs_slots[i], q0:q0 + 256],
                                             start=first, stop=(first or i == 1)),
                                     reads=[lkey, "qT%d" % hs_slots[i]], writes=[bk(b)], signal=(i == 1))
                        pi = pt_i[0] % NPT
                        pt_i[0] += 1
                        P.op("act", I("activation", out=pT[pi], in_=PS(b, 512), func=AF.Exp), reads=[bk(b)], writes=["pT%d" % pi])
                        pts.append((pi, kind, vt))

                def finish():
                    bo = rot_a1.next()
                    bl = rot_a2.next()
                    n = len(pts)
                    for i in range(2):
                        if same_kv and i == 1:
                            break
                        for ci, (pi, kind, vt) in enumerate(pts):
                            lv, vkey = (vS[:, vt, vcols[i]:vcols[i] + 128], "vS") if kind == "tok" else (vc_S[:, vt, vcols[i]:vcols[i] + 128], "vcS")
                            if same_kv:
                                P.op("pe", I("matmul", PS(bo, 512), lhsT=lv, rhs=pT[pi], start=(ci == 0), stop=(ci == n - 1)),
                                     reads=[vkey, "pT%d" % pi], writes=[bk(bo)], signal=(ci == n - 1))
                            else:
                                P.op("pe", I("matmul", PS(bo, 256, i * 256), lhsT=lv, rhs=pT[pi][:, i * 256:(i + 1) * 256],
                                             start=(ci == 0), stop=(ci == n - 1)),
                                     reads=[vkey, "pT%d" % pi], writes=[bk(bo)], signal=(ci == n - 1))
                    for ci, (pi, kind, vt) in enumerate(pts):
                        P.op("pe", I("matmul", PS(bl, 512), lhsT=onesb, rhs=pT[pi], start=(ci == 0), stop=(ci == n - 1)),
                             reads=["onesb", "pT%d" % pi], writes=[bk(bl)], signal=(ci == n - 1))
                    for i in range(2):
                        rl = rl_[i]
                        sc = esink[:, sink_cols[i]:sink_cols[i] + 1] if sink_cols is not None else zeroc
                        P.op("act", I("activation", out=rl, in_=PS(bl, 256, i * 256), func=AF.Ln, scale=1.0, bias=sc),
                             reads=[bk(bl), "esink", "epsc"], writes=["rl%d" % i])
                        P.op("act", I("activation", out=rl, in_=rl, func=AF.Exp, scale=-1.0), reads=["rl%d" % i], writes=["rl%d" % i])
                        P.op("dve", I("tensor_tensor", out=oT[:, mixer, pair_heads[i], q0:q0 + 256], in0=PS(bo, 256, i * 256), in1=rl, op=ALU.mult),
                             reads=[bk(bo), "rl%d" % i], writes=["oT"])
                return scores, finish, len(chunks)

            def run_items(items, hooks=()):
                prev = None
                prev_n = 0
                hook_at = {}
                for hi_, h in enumerate(hooks):
                    hook_at[(hi_ + 1) * len(items) // (len(hooks) + 1)] = h
                for ii_, (sc, fin, n) in enumerate(items):
                    if ii_ in hook_at:
                        hook_at[ii_]()
                    if prev is not None and prev_n + n > NPT:
                        prev()
                        prev = None
                    sc()
                    if prev is not None:
                        prev()
                    prev = fin
                    prev_n = n
                if prev is not None:
                    prev()

            def load_ctx(k_d, v_d, c0, ncols, nheads):
                P.dma("pool", kc_S[:, :, 0:ncols], k_d[:, c0:c0 + ncols].rearrange("(c p) n -> p c n", p=128), sem="kcS", writes=["kcS"])
                P.dma("pool", vc_S[:, :, 0:ncols], v_d[:, c0:c0 + ncols].rearrange("(c p) n -> p c n", p=128), sem="vcS", writes=["vcS"])
                for h in range(nheads):
                    b = rot_a2.next()
                    for c in range(2):
                        P.op("pe", I("transpose", PSB(b, 128, c * 128), kc_S[:, c, h * 128:(h + 1) * 128], identb),
                             reads=["kcS", "identb"], writes=[bk(b)], signal=(c == 1))
                    P.op("dve", I("tensor_copy", out=kc_T[:, h, :], in_=PSB(b, 256)), reads=[bk(b)], writes=["kcT"])

            Wq = small[:, 0:1]
            Wk = small[:, 1:2]
            Wqb = small[:, 2:3]
            Wkb = small[:, 3:4]
            set_mode("proj")
            load_ctx(cak_d, cav_d, 0, 256, 2)
            for h in range(2):
                proj_k(tkv, h, h, Wk, True, nak_o, h)
            proj_v(tkv, 256, 256, 0, nav_o, 0)
            ada_some(1)
            for rnd in range(2):
                set_mode("proj")
                tq = load_w16(w_in, rnd * 512)
                for cc in range(4):
                    proj_q(tq, cc, cc, Wq, True)
                drain()
                ada_some(1)
                set_mode("att")
                items = []
                for pr in range(2):
                    h0 = rnd * 4 + pr * 2
                    kvh = h0 // 4
                    for grp in range(3):
                        items.append(att_item([h0, h0 + 1], [kvh, kvh], [kvh * 128, kvh * 128], [pr * 2, pr * 2 + 1], 0,
                                              [h0, h0 + 1], lambda c: (maskA[:, c, :], "maskA"), grp))
                run_items(items, hooks=[lambda: ada_some(1), lambda: ada_some(1)])
            stop_at("oA", oT[:, 0], ["oT"])

            P.alias(["biasB0", "biasB1"], ["maskA", "cosT", "sinT"])
            bB_t = biasB_d.rearrange("(c p) n -> p c n", p=128)
            for rnd in range(2):
                set_mode("proj")
                tq = load_w16(w_in, 1536 + rnd * 512)
                load_ctx(cbk_d, cbv_d, rnd * 512, 512, 4)
                for cc in range(4):
                    proj_q(tq, cc, cc, Wqb, False)
                ada_some(1)
                tk = load_w16(w_in, 2560 + rnd * 512)
                for cc in range(4):
                    proj_k(tk, cc, cc, Wkb, False, nbk_o, rnd * 4 + cc)
                ada_some(1)
                tv = load_w16(w_in, 3584 + rnd * 512)
                proj_v(tv, 0, 512, 0, nbv_o, rnd * 512)
                drain()
                ada_some(1)
                set_mode("att")
                items = []
                for pr in range(2):
                    gp = rnd * 2 + pr
                    bi = gp % 2
                    P.dma("pool", biasB[bi], bB_t[:, :, gp * 512:(gp + 1) * 512], sem="biasB%d" % bi, writes=["biasB%d" % bi])
                    h0 = rnd * 4 + pr * 2
                    for grp in range(3):
                        items.append(att_item([h0, h0 + 1], [pr * 2, pr * 2 + 1], [pr * 256, pr * 256 + 128], [pr * 2, pr * 2 + 1], 1,
                                              None, lambda c, bi=bi: (biasB[bi][:, c, :], "biasB%d" % bi), grp))
                run_items(items, hooks=[lambda: ada_some(1), lambda: ada_some(1)])
            stop_at("oB", oT[:, 1], ["oT"])

            xkeys = ["x%d" % t for t in range(6)]
            newk = xkeys + ["sg0", "sg1", "tm0", "tm1", "t512_0", "t512_1", "onesf", "diag0", "diag1", "tA"]
            P.alias(newk, R1_ATT_KEYS)
            for t in range(6):
                P.dma("sp", xres[:, t, :], xm_t[t], sem="x%d" % t, writes=["x%d" % t])

            HALF = [(0, 384), (384, 384)]
            for cg in range(4):
                for mix in range(2):
                    wg = load_w16(w_in, (4608 if mix == 0 else 6656) + cg * 512)
                    wb = load_w8(w_bra if mix == 0 else w_brb, cg * 512)
                    for cc in range(4):
                        ch = cg * 4 + cc
                        for hi, (s0, sz) in enumerate(HALF):
                            bg = rot_main.next()
                            for k in range(KC):
                                P.op("pe", I("matmul", PS(bg, sz), lhsT=wg.k(k)[:, cc * 128:(cc + 1) * 128], rhs=hT[:, k, s0:s0 + sz],
                                             start=(k == 0), stop=(k == KC - 1)),
                                     reads=[wg.key(k), "hT%d" % k], writes=[bk(bg)], signal=(k == KC - 1 or k == 7))
                            by = (rot_a1 if hi == 0 else rot_a2).next()
                            for k in range(8):
                                P.op("pe", I("matmul", PS(by, sz), lhsT=wb.k(k)[:, cc * 128:(cc + 1) * 128], rhs=oT[:, mix, k, s0:s0 + sz],
                                             start=(k == 0), stop=(k == 7)),
                                     reads=[wb.key(k), "oT"], writes=[bk(by)], signal=(k == 7))
                            sg = sgs[hi][:, 0:sz]
                            P.op("act", I("activation", out=sg, in_=PS(bg, sz), func=AF.Sigmoid), reads=[bk(bg)], writes=["sg%d" % hi])
                            if mix == 0:
                                P.op("dve", I("tensor_tensor", out=tA[:, cc, s0:s0 + sz], in0=PS(by, sz), in1=sg, op=ALU.mult),
                                     reads=[bk(by), "sg%d" % hi], writes=["tA"])
                            else:
                                tm = tms[hi][:, 0:sz]
                                P.op("dve", I("tensor_tensor", out=tm, in0=PS(by, sz), in1=sg, op=ALU.mult),
                                     reads=[bk(by), "sg%d" % hi], writes=["tm%d" % hi])
                                P.op("dve", I("tensor_tensor", out=mTv[:, ch, s0:s0 + sz], in0=tm, in1=tA[:, cc, s0:s0 + sz], op=ALU.add),
                                     reads=["tm%d" % hi, "tA"], writes=["mT"])
            ada_finish()
            stop_at("mT", mTv, ["mT"])

            P.op("dve", I("memset", onesf, 1.0), writes=["onesf"])

            diag4 = [view(pb_scr + 11776 + i * 2048, [4, 128], F32) for i in range(2)]

            def build_G(mod_base, alias_from):
                P.alias(["G"], alias_from)
                di = 0
                for v in range(2):
                    for g4 in range(4):
                        b = rot_a1.next()
                        d4 = diag4[di % 2]
                        dk = "dg4_%d" % (di % 2)
                        di += 1
                        c0 = mod_base + g4 * 4
                        for j in range(4):
                            P.op("dve", I("tensor_scalar", out=d4[:, j, :], in0=identf, scalar1=modT[:, c0 + j, v:v + 1], scalar2=None, op0=ALU.mult),
                                 reads=["identf", "modT"], writes=[dk + "_%d" % j])
                        for j in range(4):
                            P.op("pe", I("matmul", PS(b, 128, j * 128), lhsT=onesf, rhs=d4[:, j, :], start=True, stop=True),
                                 reads=["onesf", dk + "_%d" % j], writes=[bk(b)], signal=(j == 3))
                        P.op("act", I("activation", out=Gt[:, v, g4 * 512:(g4 + 1) * 512], in_=PS(b, 512), func=AF.Copy),
                             reads=[bk(b)], writes=["G"])

            P.alias(["dg4_%d_%d" % (i, j) for i in range(2) for j in range(4)], ["tA"])
            build_G(32, ["oT"])

            for cg in range(4):
                wt = load_w16(w_out, cg * 512)
                for t in range(6):
                    v = 0 if t < 4 else 1
                    b = rot_main.next()
                    for k in range(KC):
                        P.op("pe", I("matmul", PS(b, 512), lhsT=mTv[:, k, t * 128:(t + 1) * 128], rhs=wt.k(k),
                                     start=(k == 0), stop=(k == KC - 1)),
                             reads=[wt.key(k), "mT"], writes=[bk(b)], signal=(k == KC - 1 or k == 7))
                    i = (cg * 6 + t) % 2
                    tp = tmp512[i]
                    P.op("dve", I("tensor_tensor", out=tp, in0=PS(b, 512), in1=Gt[:, v, cg * 512:(cg + 1) * 512], op=ALU.mult),
                         reads=[bk(b), "G"], writes=["t512_%d" % i])
                    P.op("dve", I("tensor_tensor", out=xres[:, t, cg * 512:(cg + 1) * 512], in0=xres[:, t, cg * 512:(cg + 1) * 512], in1=tp, op=ALU.add),
                         reads=["t512_%d" % i, "x%d" % t], writes=["x%d" % t])
            stop_at("x1", xres, ["x%d" % t for t in range(6)])

            xnk2 = ["xnn%d" % i for i in range(6)]
            P.alias(["h2T%d" % i for i in range(16)] + xnk2, ["hT%d" % i for i in range(16)] + ["oT"])
            xni = 0
            for tl, vec in [([0, 1, 2, 3], 0), ([4, 5], 1)]:
                tiles, xb, xk = [], [], []
                for ti in tl:
                    tiles.append((xres[:, ti, :], "x%d" % ti, None))
                    xb.append(xnn[xni % 6])
                    xk.append(xnk2[xni % 6])
                    xni += 1
                norm_group(tiles, None, vec, 2, 3, h2T, "h2T", xb, xk, tl[0] * 128, tl[0])
            stop_at("h2T", h2T, ["h2T%d" % i for i in range(16)])
            build_G(80, ["G"])

            P.alias(["uT"], xnk2)
            P.alias(["rs0", "rs1", "tb0", "tb1"], ["sg0", "sg1", "tm0", "tm1", "t512_0", "t512_1"])
            ri = 0
            for fg in range(4):
                for t4 in range(4):
                    wt = load_w16(w_up, fg * 2048 + t4 * 512)
                    for cc in range(4):
                        fc = t4 * 4 + cc
                        for (s0, sz) in HALF:
                            b = rot_main.next()
                            for k in range(KC):
                                P.op("pe", I("matmul", PS(b, sz), lhsT=wt.k(k)[:, cc * 128:(cc + 1) * 128], rhs=h2T[:, k, s0:s0 + sz],
                                             start=(k == 0), stop=(k == KC - 1)),
                                     reads=[wt.key(k), "h2T%d" % k], writes=[bk(b)], signal=(k == KC - 1 or k == 7))
                            i = ri % 2
                            ri += 1
                            rs = rsb[i][:, 0:sz]
                            P.op("act", I("activation", out=rs, in_=PS(b, sz), func=AF.Relu), reads=[bk(b)], writes=["rs%d" % i])
                            P.op("dve", I("tensor_tensor", out=uT[:, fc, s0:s0 + sz], in0=rs, in1=rs, op=ALU.mult), reads=["rs%d" % i], writes=["uT"])
                for cg in range(4):
                    wt = load_w16(w_down, cg * 512, r0=fg * 2048)
                    for t in range(6):
                        v = 0 if t < 4 else 1
                        b = rot_a1.next() if (t % 2 == 0) else rot_a2.next()
                        for k in range(KC):
                            P.op("pe", I("matmul", PS(b, 512), lhsT=uT[:, k, t * 128:(t + 1) * 128], rhs=wt.k(k),
                                         start=(k == 0), stop=(k == KC - 1)),
                                 reads=[wt.key(k), "uT"], writes=[bk(b)], signal=(k == KC - 1 or k == 7))
                        i = (cg * 6 + t) % 2
                        tp = tmp512b[i]
                        P.op("dve", I("tensor_tensor", out=tp, in0=PS(b, 512), in1=Gt[:, v, cg * 512:(cg + 1) * 512], op=ALU.mult),
                             reads=[bk(b), "G"], writes=["tb%d" % i])
                        P.op("dve", I("tensor_tensor", out=xres[:, t, cg * 512:(cg + 1) * 512], in0=xres[:, t, cg * 512:(cg + 1) * 512], in1=tp, op=ALU.add),
                             reads=["tb%d" % i, "x%d" % t], writes=["x%d" % t])

            ym_t = y_main.rearrange("(t p) d -> t p d", p=128)
            for t in range(6):
                P.dma("sp", ym_t[t], xres[:, t, :], sem="x%d" % t, reads=["x%d" % t])
            P.wait_all("sp", ["x%d" % t for t in range(6)] + ["ost0", "ost1"])

        try:
            body()
        except _Stop:
            pass
        fin = []
        for e_ in Prog.CE:
            if P.cnt[e_] > 0:
                fin.append((P.esem[e_], P.cnt[e_]))
        P.q["sp"].append((fin, None, None))

        with nc.Block() as block:
            @block.sync
            def _(e):
                P.emit("sp", e)

            @block.gpsimd
            def _(e):
                P.emit("pool", e)

            @block.tensor
            def _(e):
                P.emit("pe", e)

            @block.scalar
            def _(e):
                P.emit("act", e)

            @block.vector
            def _(e):
                P.emit("dve", e)
        build_program.stats = dict(n={k: len(v) for k, v in P.q.items()}, waits=P.n_wait, arena=A.off, cnt=dict(P.cnt))
    return nc


GRID_W = 64
ROWS = 16


def _core_geometry(j):
    b = j // 4
    qq = j % 4
    ws = 0 if qq < 2 else 4
    own_rows = list(range(4 * qq, 4 * qq + 4))
    halo_rows = [r for r in range(ws, ws + 12) if r not in own_rows]
    rows = own_rows + halo_rows
    pos = np.concatenate([np.arange(r * GRID_W, (r + 1) * GRID_W) for r in rows])
    return b, qq, pos


def _static_tables(j, rpb):
    b, qq, pos = _core_geometry(j)
    row = (pos // GRID_W).astype(np.int64)
    col = (pos % GRID_W).astype(np.int64)
    n_freq = 32
    inv = (10000.0 ** (-np.arange(n_freq, dtype=np.float32) / n_freq)).astype(np.float32)
    ang_r = row[:, None].astype(np.float32) * inv[None, :]
    ang_c = col[:, None].astype(np.float32) * inv[None, :]
    cosT = np.zeros((128, 768), np.float32)
    sinT = np.zeros((128, 768), np.float32)
    cosT[0:32] = np.cos(ang_r).T
    cosT[32:64] = np.cos(ang_r).T
    cosT[64:96] = np.cos(ang_c).T
    cosT[96:128] = np.cos(ang_c).T
    sinT[0:32] = -np.sin(ang_r).T
    sinT[32:64] = np.sin(ang_r).T
    sinT[64:96] = -np.sin(ang_c).T
    sinT[96:128] = np.sin(ang_c).T
    qpos = pos[:256]
    valid = np.abs(qpos[None, :] - pos[:, None]) <= 128
    mA = np.where(valid, 0.0, NEGM).astype(np.float32)
    maskA = np.concatenate([mA, mA], axis=1)
    qr = row[:256]
    qc = col[:256]
    rstart = np.clip(qr - 4, 0, ROWS - 8)
    cstart = np.clip(qc - 8, 0, GRID_W - 16)
    kr = row[:, None]
    kcc = col[:, None]
    vr = (kr >= rstart[None, :]) & (kr < rstart[None, :] + 8)
    vc = (kcc >= cstart[None, :]) & (kcc < cstart[None, :] + 16)
    valid = vr & vc
    dr = np.clip(kr - qr[None, :] + 7, 0, 14)
    dc = np.clip(kcc - qc[None, :] + 15, 0, 30)
    bias = rpb[:, dr, dc]
    bias = np.where(valid[None], bias, np.float32(NEGM)).astype(np.float32)
    biasB = np.ascontiguousarray(np.transpose(bias, (1, 0, 2))).reshape(768, 2048)
    return cosT, sinT, maskA, biasB


def _perm():
    p = np.zeros((128, 128), np.float32)
    for d in range(128):
        s = d + 32 if (d % 64) < 32 else d - 32
        p[s, d] = 1.0
    return p


_NC_CACHE = {}


def kernel(x_prompt, x_sample, cache_a_k, cache_a_v, cache_b_k, cache_b_v, c, c_ctx,
           norm1_w, norm2_w, w_ada, b_ada, w_in, q_norm_a, k_norm_a, q_norm_b, k_norm_b,
           sink_a, rpb_b, w_br_a, w_br_b, w_out, w_up, w_down, _debug=None, _cores=None):
    f = lambda a: np.ascontiguousarray(np.asarray(a, dtype=np.float32))
    x_prompt, x_sample = f(x_prompt), f(x_sample)
    c, c_ctx = f(c), f(c_ctx)
    cores = list(range(NCORES)) if _cores is None else _cores
    key = "dbg" if _debug is not None else "main"
    if key not in _NC_CACHE:
        _NC_CACHE[key] = build_program(_debug)
    nc = _NC_CACHE[key]

    ident = np.eye(128, dtype=np.float32)
    perm = _perm()
    shared = dict(
        ident=ident, perm=perm,
        w_ada=f(w_ada)[0], w_in=f(w_in)[0], w_br_a=f(w_br_a)[0], w_br_b=f(w_br_b)[0],
        w_out=f(w_out)[0], w_up=f(w_up)[0], w_down=f(w_down)[0],
    )
    rpb = f(rpb_b)[0]
    in_maps = []
    for idx_core, j in enumerate(cores):
        b, qq, pos = _core_geometry(j)
        xm = np.concatenate([x_prompt[2 * j], x_prompt[2 * j + 1], x_sample[b][pos[:256]]], axis=0)
        xh = x_sample[b][pos[256:]]
        cvec = np.stack([c_ctx, c[b]], axis=0)
        cT = np.ascontiguousarray(cvec.reshape(2, 16, 128).transpose(2, 1, 0)).reshape(128, 32)
        pp = np.zeros((128, PP_W), np.float32)
        pp[:, PP_N1:PP_N1 + 16] = f(norm1_w)[0].reshape(16, 128).T
        pp[:, PP_N2:PP_N2 + 16] = f(norm2_w)[0].reshape(16, 128).T
        pp[:, PP_BADA:PP_BADA + 96] = f(b_ada)[0].reshape(96, 128).T
        pp[:, PP_QNA] = f(q_norm_a)[0]
        pp[:, PP_KNA] = f(k_norm_a)[0]
        pp[:, PP_QNB] = f(q_norm_b)[0]
        pp[:, PP_KNB] = f(k_norm_b)[0]
        pp[:, PP_SINK:PP_SINK + 8] = f(sink_a)[0][None, :]
        cosT, sinT, maskA, biasB = _static_tables(j, rpb)
        m = dict(shared)
        m.update(
            x_main=np.ascontiguousarray(xm), x_halo=np.ascontiguousarray(xh), cT=cT, pp=pp,
            cosT=cosT, sinT=sinT, maskA=maskA, biasB=biasB,
            cak=f(cache_a_k)[b, 0].reshape(256, 256), cav=f(cache_a_v)[b, 0].reshape(256, 256),
            cbk=f(cache_b_k)[b, 0].reshape(256, 1024), cbv=f(cache_b_v)[b, 0].reshape(256, 1024),
        )
        in_maps.append(m)

    res = run_bass_kernel_spmd(nc, in_maps, core_ids=list(range(len(cores))))
    if _debug is not None:
        return res

    y_prompt = np.zeros((16, 256, D), np.float32)
    y_sample = np.zeros((2, 1024, D), np.float32)
    nak = np.zeros((16, 1, 256, 2, 128), np.float32)
    nav = np.zeros((16, 1, 256, 2, 128), np.float32)
    nbk = np.zeros((16, 1, 256, 8, 128), np.float32)
    nbv = np.zeros((16, 1, 256, 8, 128), np.float32)
    for idx, j in enumerate(cores):
        r = res.results[idx]
        b, qq, pos = _core_geometry(j)
        ym = r["y_main"]
        y_prompt[2 * j] = ym[0:256]
        y_prompt[2 * j + 1] = ym[256:512]
        y_sample[b][pos[:256]] = ym[512:768]
        for s in range(2):
            nak[2 * j + s, 0] = r["nak"][s * 256:(s + 1) * 256].reshape(256, 2, 128)
            nav[2 * j + s, 0] = r["nav"][s * 256:(s + 1) * 256].reshape(256, 2, 128)
            nbk[2 * j + s, 0] = r["nbk"][s * 256:(s + 1) * 256].reshape(256, 8, 128)
            nbv[2 * j + s, 0] = r["nbv"][s * 256:(s + 1) * 256].reshape(256, 8, 128)
    return (y_prompt, y_sample, nak, nav, nbk, nbv)
```

```python
import os
import numpy as np
import concourse.bass as bass
import concourse.mybir as mybir
from concourse.bass_utils import run_bass_kernel_spmd

F32 = mybir.dt.float32
BF16 = mybir.dt.bfloat16
ALU = mybir.AluOpType
AF = mybir.ActivationFunctionType
AX = mybir.AxisListType

NCORES = 8
D = 2048
KC = 16
NMAIN = 768
NHALO = 512
NTOK = NMAIN + NHALO
EPS = 1e-6
NEGM = -30000.0
IN_W = 8704
DFF = 8192
CH_P = (0, 512)
CH_S = (512, 256)
CH_H = (768, 512)

PP_N1 = 0
PP_N2 = 16
PP_BADA = 32
PP_QNA = 128
PP_KNA = 129
PP_QNB = 130
PP_KNB = 131
PP_SINK = 132
PP_SEL = 140
PP_W = 142


class Prog:
    CE = ("pe", "act", "dve", "pool")
    ALL = ("pe", "act", "dve", "pool", "sp")

    def __init__(self, nc, esems, dma_sems):
        self.nc = nc
        self.q = {e: [] for e in self.ALL}
        self.cnt = {e: 0 for e in self.CE}
        self.esem = esems
        self.free_dsems = list(dma_sems)
        self.dsem = {}
        self.seen = {e: {} for e in self.ALL}
        self.lastw = {}
        self.readers = {}
        self.n_wait = 0

    def _need(self, eng, ev, waits):
        if ev is None:
            return
        semkey, handle, val, _ = ev
        if self.seen[eng].get(semkey, 0) >= val:
            return
        self.seen[eng][semkey] = val
        waits.append((handle, val))
        self.n_wait += 1

    def _deps(self, eng, reads, writes, waits):
        for k in reads:
            ev = self.lastw.get(k)
            if ev is not None and not (ev[3] == "pe" and eng == "pe"):
                self._need(eng, ev, waits)
        for k in writes:
            ev = self.lastw.get(k)
            if ev is not None and not (ev[3] == "pe" and eng == "pe"):
                self._need(eng, ev, waits)
            for ev in self.readers.get(k, ()):
                if not (ev[3] == "pe" and eng == "pe"):
                    self._need(eng, ev, waits)

    def _commit(self, ev, reads, writes):
        for k in reads:
            lst = self.readers.setdefault(k, [])
            lst[:] = [e for e in lst if e[0] != ev[0]]
            lst.append(ev)
        for k in writes:
            self.lastw[k] = ev
            self.readers[k] = []

    def op(self, eng, fn, reads=(), writes=(), signal=True):
        waits = []
        self._deps(eng, reads, writes, waits)
        if signal:
            self.cnt[eng] += 1
            val = self.cnt[eng]
        else:
            val = self.cnt[eng] + 1
        ev = ("E" + eng, self.esem[eng], val, eng)
        self._commit(ev, reads, writes)
        self.q[eng].append((waits, fn, (self.esem[eng], 1) if signal else None))

    def dma(self, eng, out, in_, sem, reads=(), writes=(), **kw):
        waits = []
        self._deps(eng, reads, writes, waits)
        if sem not in self.dsem:
            self.dsem[sem] = [self.free_dsems.pop(), 0]
        rec = self.dsem[sem]
        rec[1] += 16
        ev = ("D" + sem, rec[0], rec[1], "dma")
        self._commit(ev, reads, writes)
        self.q[eng].append((waits, (lambda e, o=out, i=in_, k=kw: e.dma_start(out=o, in_=i, **k)), (rec[0], 16)))

    def custom(self, eng, fn, sem, inc, reads=(), writes=()):
        waits = []
        self._deps(eng, reads, writes, waits)
        if sem not in self.dsem:
            self.dsem[sem] = [self.free_dsems.pop(), 0]
        rec = self.dsem[sem]
        rec[1] += inc
        ev = ("D" + sem, rec[0], rec[1], "dma")
        self._commit(ev, reads, writes)
        self.q[eng].append((waits, fn, (rec[0], inc)))

    def wait_all(self, eng, keys):
        best = {}
        for k in keys:
            for ev in [self.lastw.get(k)] + list(self.readers.get(k, ())):
                if ev is not None and (ev[0] not in best or best[ev[0]][2] < ev[2]):
                    best[ev[0]] = ev
        waits = []
        for ev in best.values():
            self._need(eng, ev, waits)
        if waits:
            self.q[eng].append((waits, None, None))

    def alias(self, new_keys, old_keys):
        evs = []
        for k in old_keys:
            if self.lastw.get(k) is not None:
                evs.append(self.lastw[k])
            evs.extend(self.readers.get(k, ()))
        for k in new_keys:
            self.lastw[k] = None
            self.readers[k] = list(evs)

    def emit(self, eng_name, eng):
        for waits, fn, inc in self.q[eng_name]:
            for h, v in waits:
                eng.wait_ge(h, v)
            if fn is None:
                continue
            ins = fn(eng)
            if inc is not None:
                ins.then_inc(inc[0], inc[1])


def I(method, *a, **k):
    return lambda e: getattr(e, method)(*a, **k)


class Rot:
    def __init__(self, items):
        self.items = list(items)
        self.i = 0

    def next(self):
        v = self.items[self.i % len(self.items)]
        self.i += 1
        return v


def build_program(debug=None):
    nc = bass.Bass("TRN2", target_bir_lowering=False)

    def din(name, shape, dt=F32):
        return nc.dram_tensor(name, list(shape), dt, kind="ExternalInput").ap()

    def dout(name, shape, dt=F32):
        return nc.dram_tensor(name, list(shape), dt, kind="ExternalOutput").ap()

    x_main = din("x_main", [NMAIN, D])
    x_halo = din("x_halo", [NHALO, D])
    cT_d = din("cT", [128, 32])
    pp_d = din("pp", [128, PP_W])
    ident_d = din("ident", [128, 128])
    perm_d = din("perm", [128, 128])
    cos_d = din("cosT", [128, NMAIN])
    sin_d = din("sinT", [128, NMAIN])
    maskA_d = din("maskA", [NMAIN, 512])
    biasB_d = din("biasB", [NMAIN, 2048])
    cak_d = din("cak", [256, 256])
    cav_d = din("cav", [256, 256])
    cbk_d = din("cbk", [256, 1024])
    cbv_d = din("cbv", [256, 1024])
    w_ada = din("w_ada", [D, 6 * D])
    w_in = din("w_in", [D, IN_W])
    w_bra = din("w_br_a", [1024, D])
    w_brb = din("w_br_b", [1024, D])
    w_out = din("w_out", [D, D])
    w_up = din("w_up", [D, DFF])
    w_down = din("w_down", [DFF, D])

    y_main = dout("y_main", [NMAIN, D])
    nak_o = dout("nak", [512, 256])
    nav_o = dout("nav", [512, 256])
    nbk_o = dout("nbk", [512, 1024])
    nbv_o = dout("nbv", [512, 1024])
    dbg_o = None
    if debug is not None:
        dbg_o = dout("dbg", debug["shape"], BF16 if debug.get("dtype") == "bf16" else F32)

    import contextlib
    es = contextlib.ExitStack()
    with es:
        ARENA_W = 53200
        arena = es.enter_context(nc.sbuf_tensor("arena", [128, ARENA_W], F32))
        ps = es.enter_context(nc.psum_tensor("ps", [128, 8, 512], F32))
        sem_names = ["pe", "act", "dve", "pool"]
        esems = {n: es.enter_context(nc.semaphore("s_" + n)) for n in sem_names}
        dsems = [es.enter_context(nc.semaphore("d%d" % i)) for i in range(70)]
        P = Prog(nc, esems, dsems)

        class Arena:
            def __init__(self):
                self.off = 0

            def take(self, nbytes):
                o = self.off
                self.off += (nbytes + 63) // 64 * 64
                assert self.off <= ARENA_W * 4, ("arena overflow", self.off)
                return o

        def view(off_b, shape, dt):
            esz = 2 if dt == BF16 else 4
            n = int(np.prod(shape))
            assert off_b % 4 == 0
            w0 = off_b // 4
            nw = (n * esz + 3) // 4
            ap = arena[:, w0:w0 + nw]
            if dt == BF16:
                ap = ap.bitcast(BF16)
            if len(shape) == 2:
                ap = ap.rearrange("p (a b) -> p a b", a=shape[0])
            elif len(shape) == 3:
                ap = ap.rearrange("p (a b c) -> p a b c", a=shape[0], b=shape[1])
            return ap

        A = Arena()
        RING_N = 6
        ring_off = [A.take(8192) for _ in range(RING_N)]
        o_pp = A.take(PP_W * 4)
        o_identf = A.take(512)
        o_identb = A.take(256)
        o_permf = A.take(512)
        o_onesb = A.take(256)
        o_modT = A.take(96 * 2 * 4)
        o_ab = A.take(4 * 2 * 16 * 4)
        o_small = A.take(256)
        o_esink = A.take(32)
        o_cT = A.take(128)
        o_scT = A.take(64)

        pp = view(o_pp, [PP_W], F32)
        identf = view(o_identf, [128], F32)
        identb = view(o_identb, [128], BF16)
        permf = view(o_permf, [128], F32)
        onesb = view(o_onesb, [128], BF16)
        modT = view(o_modT, [96, 2], F32)
        abv = view(o_ab, [4, 2, 16], F32)
        small = view(o_small, [64], F32)
        esink = view(o_esink, [8], F32)
        epsc = small[:, 20:21]
        epsc128 = small[:, 21:22]
        zeroc = small[:, 22:23]

        R1 = A.take(69632)
        R2 = A.take(65536)
        mT_off = A.take(KC * NMAIN * 2)
        mTv = view(mT_off, [KC, NMAIN], BF16)
        hT = view(R2, [KC, NTOK], BF16)
        oT = view(R2 + 40960, [2, 8, NMAIN], BF16)
        Gt = view(R2 + 65536 - 16384, [2, 2048], F32)
        h2T = view(R2, [KC, NMAIN], BF16)
        uT = view(R2 + 24576, [KC, NMAIN], BF16)
        xnn = [view(R2 + 24576 + i * 4096, [2048], BF16) for i in range(6)]
        xs_ = [view(R1 + i * 8192, [2048], F32) for i in range(3)]
        xn_ = [view(R1 + 24576 + i * 4096, [2048], BF16) for i in range(10)]
        o = R1
        qT = view(o, [4, NMAIN], BF16); o += 4 * NMAIN * 2
        kT = view(o, [4, NTOK], BF16); o += 4 * NTOK * 2
        vS = view(o, [10, 512], BF16); o += 10 * 512 * 2
        NPT = 10
        pT = [view(o + i * 1024, [512], BF16) for i in range(NPT)]; o += NPT * 1024
        m = o
        maskA = view(m, [6, 512], BF16)
        cosT = view(m + 6144, [NMAIN], F32)
        sinT = view(m + 9216, [NMAIN], F32)
        biasB = [view(m + i * 6144, [6, 512], BF16) for i in range(2)]
        kc_T = view(m + 12288, [4, 256], BF16)
        vc_S = view(m + 14336, [2, 512], BF16)
        kc_S = view(m + 16384, [2, 512], BF16)
        f32s = [view(m + 18432 + i * 2048, [512], F32) for i in range(3)]
        o += 24576
        ost = [view(o + i * 2048, [512], F32) for i in range(2)]; o += 2 * 2048
        sqb = [view(o + i * 1024, [512], BF16) for i in range(2)]; o += 2 * 1024
        rl_ = [view(o + i * 1024, [256], F32) for i in range(2)]; o += 2 * 1024
        assert o - R1 <= 69632, o - R1
        R1_ATT_KEYS = (["qT%d" % i for i in range(4)] + ["kT%d" % i for i in range(4)] + ["vS", "kcT", "vcS", "kcS", "f32s0", "f32s1", "f32s2", "ost0", "ost1", "sqb0", "sqb1",
                        "rl0", "rl1", "biasB0", "biasB1", "maskA", "cosT", "sinT"] + ["pT%d" % i for i in range(NPT)])
        xres = view(R1, [6, 2048], F32)
        pb_scr = R1 + 49152
        sgs = [view(pb_scr + i * 1536, [384], F32) for i in range(2)]
        tms = [view(pb_scr + 3072 + i * 1536, [384], F32) for i in range(2)]
        tmp512 = [view(pb_scr + 6144 + i * 2048, [512], F32) for i in range(2)]
        onesf = view(pb_scr + 10240, [128], F32)
        diag = [view(pb_scr + 10752 + i * 512, [128], F32) for i in range(2)]
        tA = view(pb_scr + 11776, [4, NMAIN], BF16)
        rsb = [view(pb_scr + i * 1024, [384], BF16) for i in range(2)]
        tmp512b = [view(pb_scr + 4096 + i * 2048, [512], F32) for i in range(2)]

        def PS(bank, n=512, off=0):
            return ps[:, bank, off:off + n]

        def PSB(bank, n, off=0):
            return ps[:, bank, :].bitcast(BF16)[:, off:off + n]

        rot_main = Rot([0, 1, 2, 3])
        rot_a1 = Rot([4, 5])
        rot_a2 = Rot([6, 7])

        def bk(b):
            return "ps%d" % b

        def set_mode(m):
            if m == "proj":
                rot_main.items, rot_a1.items, rot_a2.items = [0, 1, 2, 3, 4, 5], [6], [7]
            else:
                rot_main.items, rot_a1.items, rot_a2.items = [0, 1, 2, 3], [4, 5], [6, 7]

        ring_state = {"n": 0}

        class WT:
            def __init__(self, slots):
                self.slots = slots

            def k(self, k):
                return view(ring_off[self.slots[k // 8]], [8, 512], BF16)[:, k % 8, :]

            def key(self, k):
                return "ring%d" % self.slots[k // 8]

        def load_entry(src):
            n = ring_state["n"]
            ring_state["n"] += 1
            slot = n % RING_N
            key = "ring%d" % slot
            P.dma("pool", view(ring_off[slot], [8, 512], BF16), src, sem=key, writes=[key])
            return slot

        def wsrc(w, r0, nr, c0, ncol):
            return w[r0:r0 + nr, c0:c0 + ncol].rearrange("(k p) n -> p k n", p=128)

        def load_w16(w, c0, r0=0):
            n = ring_state["n"]
            slot = n % RING_N
            if slot % 2 == 0:
                ring_state["n"] += 2
                k0, k1 = "ring%d" % slot, "ring%d" % (slot + 1)
                dst = view(ring_off[slot], [16, 512], BF16)
                P.dma("pool", dst, wsrc(w, r0, 2048, c0, 512), sem=k0, writes=[k0, k1])
                return WT([slot, slot + 1])
            return WT([load_entry(wsrc(w, r0, 1024, c0, 512)), load_entry(wsrc(w, r0 + 1024, 1024, c0, 512))])

        def load_w8(w, c0):
            return WT([load_entry(wsrc(w, 0, 1024, c0, 512))])

        class _Stop(Exception):
            pass

        def stop_at(name, ap, keys):
            if debug is not None and debug["at"] == name:
                P.dma("sp", dbg_o, ap, sem="dbg", reads=keys)
                P.wait_all("sp", keys)
                raise _Stop()

        chains = []

        def tick():
            for ch in list(chains):
                step = ch.pop(0)
                step()
                if not ch:
                    chains.remove(ch)

        def drain():
            while chains:
                tick()

        def body():
            P.dma("sp", pp, pp_d, sem="pp", writes=["pp"])
            P.dma("sp", identf, ident_d, sem="identf", writes=["identf"])
            P.dma("sp", permf, perm_d, sem="permf", writes=["permf"])
            P.dma("pool", identb, ident_d, sem="identb", writes=["identb"])
            P.op("dve", I("memset", onesb, 1.0), writes=["onesb"])
            P.op("dve", I("memset", epsc, EPS), writes=["epsc"])
            P.op("dve", I("memset", epsc128, 128.0 * EPS), writes=["epsc"])
            P.op("dve", I("memset", zeroc, 0.0), writes=["epsc"])
            cTf = view(o_cT, [32], F32)
            scTflat = view(o_scT, [32], BF16)
            scTb = scTflat.rearrange("p (k v) -> p k v", v=2)
            P.dma("sp", cTf, cT_d, sem="cT", writes=["cTf"])
            P.op("act", I("activation", out=scTflat, in_=cTf, func=AF.Silu), reads=["cTf"], writes=["scT"])
            P.op("act", I("activation", out=esink, in_=pp[:, PP_SINK:PP_SINK + 8], func=AF.Exp), reads=["pp"], writes=["esink"])

            def norm_group(tiles, src_fn, vec, kindA, kindB, dstT, dst_key, xbufs, xkeys, tok0, stat0):
                norm_stats(tiles, xbufs, xkeys, stat0)
                norm_tr(len(tiles), vec, kindA, kindB, dstT, dst_key, xbufs, xkeys, tok0)

            def norm_stats(tiles, xbufs, xkeys, stat0):
                for i, (xin, xkey, loader) in enumerate(tiles):
                    if loader is not None:
                        loader()
                    xnb, xnk = xbufs[i], xkeys[i]
                    si = (stat0 + i) % 10
                    ssq = small[:, 24 + si:25 + si]
                    rst = small[:, 36 + si:37 + si]
                    sk = "ssq%d" % si
                    P.op("act", I("activation", out=xnb, in_=xin, func=AF.Square, accum_out=ssq), reads=[xkey], writes=[xnk, sk])
                    P.op("act", I("activation", out=rst, in_=ssq, func=AF.Sqrt, scale=1.0 / D, bias=epsc), reads=[sk, "epsc"], writes=[sk + "r"])
                    P.op("dve", I("reciprocal", out=rst, in_=rst), reads=[sk + "r"], writes=[sk + "r"])
                    P.op("dve", I("tensor_scalar", out=xnb, in0=xin, scalar1=rst, scalar2=None, op0=ALU.mult), reads=[xkey, sk + "r"], writes=[xnk])

            def norm_tr(nt, vec, kindA, kindB, dstT, dst_key, xbufs, xkeys, tok0, chunks=None):
                for c in (range(KC) if chunks is None else chunks):
                    b = rot_main.next()
                    for i in range(nt):
                        P.op("pe", I("transpose", PSB(b, 128, i * 128), xbufs[i][:, c * 128:(c + 1) * 128], identb),
                             reads=[xkeys[i], "identb"], writes=[bk(b)], signal=(i == nt - 1))
                    dst = dstT[:, c, tok0:tok0 + nt * 128]
                    if c % 2 == 0:
                        P.op("act", I("activation", out=dst, in_=PSB(b, nt * 128), func=AF.Identity,
                                      scale=abv[:, kindA, vec, c:c + 1], bias=abv[:, kindB, vec, c:c + 1]),
                             reads=[bk(b), "ab", "ab2"], writes=[dst_key + str(c)])
                    else:
                        P.op("dve", I("tensor_scalar", out=dst, in0=PSB(b, nt * 128), scalar1=abv[:, kindA, vec, c:c + 1],
                                      scalar2=abv[:, kindB, vec, c:c + 1], op0=ALU.mult, op1=ALU.add),
                             reads=[bk(b), "ab", "ab2"], writes=[dst_key + str(c)])

            xm_t = x_main.rearrange("(t p) d -> t p d", p=128)
            xh_t = x_halo.rearrange("(t p) d -> t p d", p=128)
            norm1_groups = []
            xi = 0
            for tl, vec in [([0, 1, 2, 3], 0), ([4, 5], 1), ([6, 7, 8, 9], 1)]:
                tiles, xb, xk = [], [], []
                for ti in tl:
                    bi = xi % 3
                    xi += 1
                    src = xm_t[ti] if ti < 6 else xh_t[ti - 6]
                    ld = (lambda bi=bi, src=src: P.dma("sp", xs_[bi], src, sem="xs%d" % bi, writes=["xs%d" % bi]))
                    tiles.append((xs_[bi], "xs%d" % bi, ld))
                    xb.append(xn_[ti])
                    xk.append("xn%d" % ti)
                norm_stats(tiles, xb, xk, tl[0])
                norm1_groups.append((tl, vec, xb, xk))

            ada_pending = [0, 4, 1, 5, 2, 6, 3, 7] + list(range(8, 24))

            def ada_some(n):
                for _ in range(min(n, len(ada_pending))):
                    t = ada_pending.pop(0)
                    wt = load_w16(w_ada, t * 512)
                    b = rot_a2.next()
                    for cc in range(4):
                        for k in range(KC):
                            P.op("pe", I("matmul", ps[:, b, cc * 2:cc * 2 + 2], lhsT=wt.k(k)[:, cc * 128:(cc + 1) * 128],
                                         rhs=scTb[:, k, :], start=(k == 0), stop=(k == KC - 1)),
                                 reads=[wt.key(k), "scT"], writes=[bk(b)], signal=(k == KC - 1 or k == 7))
                    for v in range(2):
                        P.op("dve", I("tensor_tensor", out=modT[:, t * 4:(t + 1) * 4, v],
                                      in0=ps[:, b, 0:8].rearrange("p (c v) -> p c v", v=2)[:, :, v],
                                      in1=pp[:, PP_BADA + t * 4:PP_BADA + t * 4 + 4], op=ALU.add),
                             reads=[bk(b), "pp"], writes=["modT"])

            for qi in range(4):
                ada_some(2)
                c4 = slice(4 * qi, 4 * qi + 4)
                for v in range(2):
                    P.op("dve", I("scalar_tensor_tensor", out=abv[:, 0, v, c4], in0=modT[:, 16 + 4 * qi:20 + 4 * qi, v], scalar=1.0,
                                  in1=pp[:, PP_N1 + 4 * qi:PP_N1 + 4 * qi + 4], op0=ALU.add, op1=ALU.mult), reads=["modT", "pp"], writes=["ab"])
                    P.op("dve", I("tensor_copy", out=abv[:, 1, v, c4], in_=modT[:, 4 * qi:4 * qi + 4, v]), reads=["modT"], writes=["ab"])
                for (tl, vec, xb, xk) in norm1_groups:
                    norm_tr(len(tl), vec, 0, 1, hT, "hT", xb, xk, tl[0] * 128, chunks=range(4 * qi, 4 * qi + 4))
            SQ128 = float(np.sqrt(128.0))
            P.op("dve", I("tensor_copy", out=small[:, 0:1], in_=pp[:, PP_QNA:PP_QNA + 1]), reads=["pp"], writes=["small"])
            P.op("dve", I("tensor_scalar", out=small[:, 1:2], in0=pp[:, PP_KNA:PP_KNA + 1], scalar1=SQ128, scalar2=None, op0=ALU.mult), reads=["pp"], writes=["small"])
            P.op("dve", I("tensor_copy", out=small[:, 2:3], in_=pp[:, PP_QNB:PP_QNB + 1]), reads=["pp"], writes=["small"])
            P.op("dve", I("tensor_scalar", out=small[:, 3:4], in0=pp[:, PP_KNB:PP_KNB + 1], scalar1=SQ128, scalar2=None, op0=ALU.mult), reads=["pp"], writes=["small"])

            def ada_finish():
                ada_some(len(ada_pending))
                for v in range(2):
                    P.op("dve", I("scalar_tensor_tensor", out=abv[:, 2, v, :], in0=modT[:, 64:80, v], scalar=1.0,
                                  in1=pp[:, PP_N2:PP_N2 + 16], op0=ALU.add, op1=ALU.mult), reads=["modT", "pp"], writes=["ab2"])
                    P.op("dve", I("tensor_copy", out=abv[:, 3, v, :], in_=modT[:, 48:64, v]), reads=["modT"], writes=["ab2"])
            stop_at("ada", modT, ["modT", "ab", "small"])

            tkv = load_w16(w_in, 1024)
            stop_at("hT", hT, ["hT%d" % i for i in range(16)])

            P.alias(R1_ATT_KEYS, ["xs0", "xs1", "xs2"] + ["xn%d" % i for i in range(10)])
            P.dma("pool", maskA, maskA_d.rearrange("(c p) n -> p c n", p=128), sem="maskA", writes=["maskA"])
            P.dma("sp", cosT, cos_d, sem="cosT", writes=["cosT"])
            P.dma("sp", sinT, sin_d, sem="sinT", writes=["sinT"])

            cstate = {"i": 0, "ost": 0}

            def proj_fm(wt, cc, chunks, make_chain):
                for (s0, sz) in chunks:
                    b = rot_main.next()
                    for k in range(KC):
                        P.op("pe", I("matmul", PS(b, sz), lhsT=wt.k(k)[:, cc * 128:(cc + 1) * 128], rhs=hT[:, k, s0:s0 + sz],
                                     start=(k == 0), stop=(k == KC - 1)),
                             reads=[wt.key(k), "hT%d" % k], writes=[bk(b)], signal=(k == KC - 1 or k == 7))
                    tick()
                    make_chain(b, s0, sz)

            def norm_steps(b, sz, wcol, out_ap, out_key):
                i = cstate["i"] % 2
                cstate["i"] += 1
                sq = sqb[i][:, 0:sz]
                sqk = "sqb%d" % i
                P.op("act", I("activation", out=sq, in_=PS(b, sz), func=AF.Square), reads=[bk(b)], writes=[sqk])

                def step1():
                    b2 = rot_a1.next()
                    P.op("pe", I("matmul", PS(b2, sz), lhsT=onesb, rhs=sq, start=True, stop=True), reads=[sqk, "onesb"], writes=[bk(b2)])
                    rr = f32s[2][:, 0:sz]
                    P.op("act", I("activation", out=rr, in_=PS(b2, sz), func=AF.Ln, scale=1.0, bias=epsc128), reads=[bk(b2), "epsc"], writes=["f32s2"])
                    P.op("act", I("activation", out=rr, in_=rr, func=AF.Exp, scale=-0.5), reads=["f32s2"], writes=["f32s2"])
                    P.op("dve", I("scalar_tensor_tensor", out=out_ap, in0=PS(b, sz), scalar=wcol, in1=rr, op0=ALU.mult, op1=ALU.mult),
                         reads=[bk(b), "f32s2", "small"], writes=[out_key])
                return step1

            def rope_step(src_ap, src_key, e0, sz, out_ap, out_key):
                def step():
                    b3 = rot_a2.next()
                    P.op("pe", I("matmul", PS(b3, sz), lhsT=permf, rhs=src_ap, start=True, stop=True), reads=[src_key, "permf"], writes=[bk(b3)])
                    t1 = f32s[2][:, 0:sz]
                    P.op("dve", I("tensor_tensor", out=t1, in0=PS(b3, sz), in1=sinT[:, e0:e0 + sz], op=ALU.mult), reads=[bk(b3), "sinT"], writes=["f32s2"])
                    P.op("dve", I("tensor_tensor", out=src_ap, in0=src_ap, in1=cosT[:, e0:e0 + sz], op=ALU.mult), reads=[src_key, "cosT"], writes=[src_key])
                    P.op("dve", I("tensor_tensor", out=out_ap, in0=src_ap, in1=t1, op=ALU.add), reads=[src_key, "f32s2"], writes=[out_key])
                return step

            def kout_step(kn_ap, kn_key, dst_dram, head_col, kt_dst, kt_key):
                def step():
                    b4 = rot_a2.next()
                    for t in range(4):
                        P.op("pe", I("transpose", PS(b4, 128, t * 128), kn_ap[:, t * 128:(t + 1) * 128], identf),
                             reads=[kn_key, "identf"], writes=[bk(b4)], signal=(t == 3))
                    i = cstate["ost"] % 2
                    cstate["ost"] += 1
                    st = ost[i]
                    P.op("act", I("activation", out=st, in_=PS(b4, 512), func=AF.Copy), reads=[bk(b4)], writes=["ost%d" % i])
                    P.dma("sp", dst_dram.rearrange("(t p) n -> p t n", p=128)[:, :, head_col * 128:(head_col + 1) * 128],
                          st.rearrange("p (t d) -> p t d", t=4), sem="ost%d" % i, reads=["ost%d" % i])
                    P.op("act", I("activation", out=kt_dst, in_=kn_ap, func=AF.Copy), reads=[kn_key], writes=[kt_key])
                return step

            tmp_i = [0]

            def _nop():
                pass

            def proj_q(wt, cc, hslot, wcol, do_rope):
                def mk(b, s0, sz):
                    dst = qT[:, hslot, s0:s0 + sz]
                    if do_rope and s0 == CH_S[0]:
                        fi = tmp_i[0] % 2
                        tmp_i[0] += 1
                        tmp = f32s[fi][:, 0:sz]
                        chains.append([norm_steps(b, sz, wcol, tmp, "f32s%d" % fi), _nop, rope_step(tmp, "f32s%d" % fi, 0, sz, dst, "qT%d" % hslot)])
                    else:
                        chains.append([norm_steps(b, sz, wcol, dst, "qT%d" % hslot)])
                proj_fm(wt, cc, [CH_P, CH_S], mk)

            def proj_k(wt, cc, hslot, wcol, do_rope, dst_dram, head_col):
                def mk(b, s0, sz):
                    fi = tmp_i[0] % 2
                    tmp_i[0] += 1
                    tmp = f32s[fi][:, 0:sz]
                    tk = "f32s%d" % fi
                    dst = kT[:, hslot, s0:s0 + sz]
                    st1 = norm_steps(b, sz, wcol, tmp, tk)
                    if s0 == 0:
                        chains.append([st1, _nop, kout_step(tmp, tk, dst_dram, head_col, dst, "kT%d" % hslot)])
                    elif do_rope:
                        chains.append([st1, _nop, rope_step(tmp, tk, s0 - 512, sz, dst, "kT%d" % hslot)])
                    else:
                        def cp(tmp=tmp, tk=tk, dst=dst):
                            P.op("act", I("activation", out=dst, in_=tmp, func=AF.Copy), reads=[tk], writes=["kT%d" % hslot])
                        chains.append([st1, _nop, cp])
                proj_fm(wt, cc, [CH_P, CH_S, CH_H], mk)

            def proj_v(wt, c0, ncols, vcol0, dst_dram, dcol0):
                for ti in range(10):
                    b = rot_main.next()
                    for k in range(KC):
                        P.op("pe", I("matmul", PS(b, ncols), lhsT=hT[:, k, ti * 128:(ti + 1) * 128], rhs=wt.k(k)[:, c0:c0 + ncols],
                                     start=(k == 0), stop=(k == KC - 1)),
                             reads=[wt.key(k), "hT%d" % k], writes=[bk(b)], signal=(k == KC - 1 or k == 7))
                    tick()
                    if ti < 4:
                        i = cstate["ost"] % 2
                        cstate["ost"] += 1
                        st = ost[i][:, 0:ncols]
                        P.op("act", I("activation", out=st, in_=PS(b, ncols), func=AF.Copy), reads=[bk(b)], writes=["ost%d" % i])
                        P.op("dve", I("tensor_copy", out=vS[:, ti, vcol0:vcol0 + ncols], in_=st), reads=["ost%d" % i], writes=["vS"])
                        P.dma("sp", dst_dram[ti * 128:(ti + 1) * 128, dcol0:dcol0 + ncols], st, sem="ost%d" % i, reads=["ost%d" % i])
                    else:
                        P.op("dve", I("tensor_copy", out=vS[:, ti, vcol0:vcol0 + ncols], in_=PS(b, ncols)), reads=[bk(b)], writes=["vS"])

            pt_i = [0]

            def att_item(pair_heads, kslots, vcols, hs_slots, mixer, sink_cols, sample_bias, grp):
                same_kv = (kslots[0] == kslots[1]) and (vcols[0] == vcols[1])
                q0 = grp * 256
                if grp < 2:
                    chunks = [("tok", grp * 256 + c * 128, grp * 2 + c, None) for c in range(2)]
                else:
                    chunks = [("tok", 512 + c * 128, 4 + c, c) for c in range(6)] + [("ctx", c * 128, c, None) for c in range(2)]
                pts = []

                def scores():
                    for (kind, k0, vt, bc) in chunks:
                        b = rot_main.next()
                        first = True
                        if bc is not None:
                            bap, bkey = sample_bias(bc)
                            P.op("pe", I("matmul", PS(b, 512), lhsT=identb, rhs=bap, start=True, stop=False),
                                 reads=[bkey, "identb"], writes=[bk(b)], signal=False)
                            first = False
                        if same_kv:
                            lk, lkey = (kT[:, kslots[0], k0:k0 + 128], "kT%d" % kslots[0]) if kind == "tok" else (kc_T[:, kslots[0], k0:k0 + 128], "kcT")
                            P.op("pe", I("matmul", PS(b, 512).rearrange("p (a q) -> p a q", a=2), lhsT=lk,
                                         rhs=qT[:, hs_slots[0]:hs_slots[0] + 2, q0:q0 + 256], start=first, stop=True),
                                 reads=[lkey, "qT%d" % hs_slots[0], "qT%d" % (hs_slots[0] + 1)], writes=[bk(b)])
                        else:
                            for i in range(2):
                                lk, lkey = (kT[:, kslots[i], k0:k0 + 128], "kT%d" % kslots[i]) if kind == "tok" else (kc_T[:, kslots[i], k0:k0 + 128], "kcT")
                                P.op("pe", I("matmul", PS(b, 256, i * 256), lhsT=lk, rhs=qT[:, hs_slots[i], q0:q0 + 256],
                                             start=first, stop=(first or i == 1)),
                                     reads=[lkey, "qT%d" % hs_slots[i]], writes=[bk(b)], signal=(i == 1))
                        pi = pt_i[0] % NPT
                        pt_i[0] += 1
                        P.op("act", I("activation", out=pT[pi], in_=PS(b, 512), func=AF.Exp), reads=[bk(b)], writes=["pT%d" % pi])
                        pts.append((pi, kind, vt))

                def finish():
                    bo = rot_a1.next()
                    bl = rot_a2.next()
                    n = len(pts)
                    for i in range(2):
                        if same_kv and i == 1:
                            break
                        for ci, (pi, kind, vt) in enumerate(pts):
                            lv, vkey = (vS[:, vt, vcols[i]:vcols[i] + 128], "vS") if kind == "tok" else (vc_S[:, vt, vcols[i]:vcols[i] + 128], "vcS")
                            if same_kv:
                                P.op("pe", I("matmul", PS(bo, 512), lhsT=lv, rhs=pT[pi], start=(ci == 0), stop=(ci == n - 1)),
                                     reads=[vkey, "pT%d" % pi], writes=[bk(bo)], signal=(ci == n - 1))
                            else:
                                P.op("pe", I("matmul", PS(bo, 256, i * 256), lhsT=lv, rhs=pT[pi][:, i * 256:(i + 1) * 256],
                                             start=(ci == 0), stop=(ci == n - 1)),
                                     reads=[vkey, "pT%d" % pi], writes=[bk(bo)], signal=(ci == n - 1))
                    for ci, (pi, kind, vt) in enumerate(pts):
                        P.op("pe", I("matmul", PS(bl, 512), lhsT=onesb, rhs=pT[pi], start=(ci == 0), stop=(ci == n - 1)),
                             reads=["onesb", "pT%d" % pi], writes=[bk(bl)], signal=(ci == n - 1))
                    for i in range(2):
                        rl = rl_[i]
                        sc = esink[:, sink_cols[i]:sink_cols[i] + 1] if sink_cols is not None else zeroc
                        P.op("act", I("activation", out=rl, in_=PS(bl, 256, i * 256), func=AF.Ln, scale=1.0, bias=sc),
                             reads=[bk(bl), "esink", "epsc"], writes=["rl%d" % i])
                        P.op("act", I("activation", out=rl, in_=rl, func=AF.Exp, scale=-1.0), reads=["rl%d" % i], writes=["rl%d" % i])
                        P.op("dve", I("tensor_tensor", out=oT[:, mixer, pair_heads[i], q0:q0 + 256], in0=PS(bo, 256, i * 256), in1=rl, op=ALU.mult),
                             reads=[bk(bo), "rl%d" % i], writes=["oT"])
                return scores, finish, len(chunks)

            def run_items(items, hooks=()):
                prev = None
                prev_n = 0
                hook_at = {}
                for hi_, h in enumerate(hooks):
                    hook_at[(hi_ + 1) * len(items) // (len(hooks) + 1)] = h
                for ii_, (sc, fin, n) in enumerate(items):
                    if ii_ in hook_at:
                        hook_at[ii_]()
                    if prev is not None and prev_n + n > NPT:
                        prev()
                        prev = None
                    sc()
                    if prev is not None:
                        prev()
                    prev = fin
                    prev_n = n
                if prev is not None:
                    prev()

            def load_ctx(k_d, v_d, c0, ncols, nheads):
                P.dma("pool", kc_S[:, :, 0:ncols], k_d[:, c0:c0 + ncols].rearrange("(c p) n -> p c n", p=128), sem="kcS", writes=["kcS"])
                P.dma("pool", vc_S[:, :, 0:ncols], v_d[:, c0:c0 + ncols].rearrange("(c p) n -> p c n", p=128), sem="vcS", writes=["vcS"])
                for h in range(nheads):
                    b = rot_a2.next()
                    for c in range(2):
                        P.op("pe", I("transpose", PSB(b, 128, c * 128), kc_S[:, c, h * 128:(h + 1) * 128], identb),
                             reads=["kcS", "identb"], writes=[bk(b)], signal=(c == 1))
                    P.op("dve", I("tensor_copy", out=kc_T[:, h, :], in_=PSB(b, 256)), reads=[bk(b)], writes=["kcT"])

            Wq = small[:, 0:1]
            Wk = small[:, 1:2]
            Wqb = small[:, 2:3]
            Wkb = small[:, 3:4]
            set_mode("proj")
            load_ctx(cak_d, cav_d, 0, 256, 2)
            for h in range(2):
                proj_k(tkv, h, h, Wk, True, nak_o, h)
            proj_v(tkv, 256, 256, 0, nav_o, 0)
            ada_some(1)
            for rnd in range(2):
                set_mode("proj")
                tq = load_w16(w_in, rnd * 512)
                for cc in range(4):
                    proj_q(tq, cc, cc, Wq, True)
                drain()
                ada_some(1)
                set_mode("att")
                items = []
                for pr in range(2):
                    h0 = rnd * 4 + pr * 2
                    kvh = h0 // 4
                    for grp in range(3):
                        items.append(att_item([h0, h0 + 1], [kvh, kvh], [kvh * 128, kvh * 128], [pr * 2, pr * 2 + 1], 0,
                                              [h0, h0 + 1], lambda c: (maskA[:, c, :], "maskA"), grp))
                run_items(items, hooks=[lambda: ada_some(1), lambda: ada_some(1)])
            stop_at("oA", oT[:, 0], ["oT"])

            P.alias(["biasB0", "biasB1"], ["maskA", "cosT", "sinT"])
            bB_t = biasB_d.rearrange("(c p) n -> p c n", p=128)
            for rnd in range(2):
                set_mode("proj")
                tq = load_w16(w_in, 1536 + rnd * 512)
                load_ctx(cbk_d, cbv_d, rnd * 512, 512, 4)
                for cc in range(4):
                    proj_q(tq, cc, cc, Wqb, False)
                ada_some(1)
                tk = load_w16(w_in, 2560 + rnd * 512)
                for cc in range(4):
                    proj_k(tk, cc, cc, Wkb, False, nbk_o, rnd * 4 + cc)
                ada_some(1)
                tv = load_w16(w_in, 3584 + rnd * 512)
                proj_v(tv, 0, 512, 0, nbv_o, rnd * 512)
                drain()
                ada_some(1)
                set_mode("att")
                items = []
                for pr in range(2):
                    gp = rnd * 2 + pr
                    bi = gp % 2
                    P.dma("pool", biasB[bi], bB_t[:, :, gp * 512:(gp + 1) * 512], sem="biasB%d" % bi, writes=["biasB%d" % bi])
                    h0 = rnd * 4 + pr * 2
                    for grp in range(3):
                        items.append(att_item([h0, h0 + 1], [pr * 2, pr * 2 + 1], [pr * 256, pr * 256 + 128], [pr * 2, pr * 2 + 1], 1,
                                              None, lambda c, bi=bi: (biasB[bi][:, c, :], "biasB%d" % bi), grp))
                run_items(items, hooks=[lambda: ada_some(1), lambda: ada_some(1)])
            stop_at("oB", oT[:, 1], ["oT"])

            xkeys = ["x%d" % t for t in range(6)]
            newk = xkeys + ["sg0", "sg1", "tm0", "tm1", "t512_0", "t512_1", "onesf", "diag0", "diag1", "tA"]
            P.alias(newk, R1_ATT_KEYS)
            for t in range(6):
                P.dma("sp", xres[:, t, :], xm_t[t], sem="x%d" % t, writes=["x%d" % t])

            HALF = [(0, 384), (384, 384)]
            for cg in range(4):
                for mix in range(2):
                    wg = load_w16(w_in, (4608 if mix == 0 else 6656) + cg * 512)
                    wb = load_w8(w_bra if mix == 0 else w_brb, cg * 512)
                    for cc in range(4):
                        ch = cg * 4 + cc
                        for hi, (s0, sz) in enumerate(HALF):
                            bg = rot_main.next()
                            for k in range(KC):
                                P.op("pe", I("matmul", PS(bg, sz), lhsT=wg.k(k)[:, cc * 128:(cc + 1) * 128], rhs=hT[:, k, s0:s0 + sz],
                                             start=(k == 0), stop=(k == KC - 1)),
                                     reads=[wg.key(k), "hT%d" % k], writes=[bk(bg)], signal=(k == KC - 1 or k == 7))
                            by = (rot_a1 if hi == 0 else rot_a2).next()
                            for k in range(8):
                                P.op("pe", I("matmul", PS(by, sz), lhsT=wb.k(k)[:, cc * 128:(cc + 1) * 128], rhs=oT[:, mix, k, s0:s0 + sz],
                                             start=(k == 0), stop=(k == 7)),
                                     reads=[wb.key(k), "oT"], writes=[bk(by)], signal=(k == 7))
                            sg = sgs[hi][:, 0:sz]
                            P.op("act", I("activation", out=sg, in_=PS(bg, sz), func=AF.Sigmoid), reads=[bk(bg)], writes=["sg%d" % hi])
                            if mix == 0:
                                P.op("dve", I("tensor_tensor", out=tA[:, cc, s0:s0 + sz], in0=PS(by, sz), in1=sg, op=ALU.mult),
                                     reads=[bk(by), "sg%d" % hi], writes=["tA"])
                            else:
                                tm = tms[hi][:, 0:sz]
                                P.op("dve", I("tensor_tensor", out=tm, in0=PS(by, sz), in1=sg, op=ALU.mult),
                                     reads=[bk(by), "sg%d" % hi], writes=["tm%d" % hi])
                                P.op("dve", I("tensor_tensor", out=mTv[:, ch, s0:s0 + sz], in0=tm, in1=tA[:, cc, s0:s0 + sz], op=ALU.add),
                                     reads=["tm%d" % hi, "tA"], writes=["mT"])
            ada_finish()
            stop_at("mT", mTv, ["mT"])

            P.op("dve", I("memset", onesf, 1.0), writes=["onesf"])

            diag4 = [view(pb_scr + 11776 + i * 2048, [4, 128], F32) for i in range(2)]

            def build_G(mod_base, alias_from):
                P.alias(["G"], alias_from)
                di = 0
                for v in range(2):
                    for g4 in range(4):
                        b = rot_a1.next()
                        d4 = diag4[di % 2]
                        dk = "dg4_%d" % (di % 2)
                        di += 1
                        c0 = mod_base + g4 * 4
                        for j in range(4):
                            P.op("dve", I("tensor_scalar", out=d4[:, j, :], in0=identf, scalar1=modT[:, c0 + j, v:v + 1], scalar2=None, op0=ALU.mult),
                                 reads=["identf", "modT"], writes=[dk + "_%d" % j])
                        for j in range(4):
                            P.op("pe", I("matmul", PS(b, 128, j * 128), lhsT=onesf, rhs=d4[:, j, :], start=True, stop=True),
                                 reads=["onesf", dk + "_%d" % j], writes=[bk(b)], signal=(j == 3))
                        P.op("act", I("activation", out=Gt[:, v, g4 * 512:(g4 + 1) * 512], in_=PS(b, 512), func=AF.Copy),
                             reads=[bk(b)], writes=["G"])

            P.alias(["dg4_%d_%d" % (i, j) for i in range(2) for j in range(4)], ["tA"])
            build_G(32, ["oT"])

            for cg in range(4):
                wt = load_w16(w_out, cg * 512)
                for t in range(6):
                    v = 0 if t < 4 else 1
                    b = rot_main.next()
                    for k in range(KC):
                        P.op("pe", I("matmul", PS(b, 512), lhsT=mTv[:, k, t * 128:(t + 1) * 128], rhs=wt.k(k),
                                     start=(k == 0), stop=(k == KC - 1)),
                             reads=[wt.key(k), "mT"], writes=[bk(b)], signal=(k == KC - 1 or k == 7))
                    i = (cg * 6 + t) % 2
                    tp = tmp512[i]
                    P.op("dve", I("tensor_tensor", out=tp, in0=PS(b, 512), in1=Gt[:, v, cg * 512:(cg + 1) * 512], op=ALU.mult),
                         reads=[bk(b), "G"], writes=["t512_%d" % i])
                    P.op("dve", I("tensor_tensor", out=xres[:, t, cg * 512:(cg + 1) * 512], in0=xres[:, t, cg * 512:(cg + 1) * 512], in1=tp, op=ALU.add),
                         reads=["t512_%d" % i, "x%d" % t], writes=["x%d" % t])
            stop_at("x1", xres, ["x%d" % t for t in range(6)])

            xnk2 = ["xnn%d" % i for i in range(6)]
            P.alias(["h2T%d" % i for i in range(16)] + xnk2, ["hT%d" % i for i in range(16)] + ["oT"])
            xni = 0
            for tl, vec in [([0, 1, 2, 3], 0), ([4, 5], 1)]:
                tiles, xb, xk = [], [], []
                for ti in tl:
                    tiles.append((xres[:, ti, :], "x%d" % ti, None))
                    xb.append(xnn[xni % 6])
                    xk.append(xnk2[xni % 6])
                    xni += 1
                norm_group(tiles, None, vec, 2, 3, h2T, "h2T", xb, xk, tl[0] * 128, tl[0])
            stop_at("h2T", h2T, ["h2T%d" % i for i in range(16)])
            build_G(80, ["G"])

            ym_t = y_main.rearrange("(t p) d -> t p d", p=128)
            P.alias(["uT"], xnk2)
            P.alias(["rs0", "rs1", "tb0", "tb1"], ["sg0", "sg1", "tm0", "tm1", "t512_0", "t512_1"])
            ri = 0
            for fg in range(4):
                for t4 in range(4):
                    wt = load_w16(w_up, fg * 2048 + t4 * 512)
                    for cc in range(4):
                        fc = t4 * 4 + cc
                        for (s0, sz) in HALF:
                            b = rot_main.next()
                            for k in range(KC):
                                P.op("pe", I("matmul", PS(b, sz), lhsT=wt.k(k)[:, cc * 128:(cc + 1) * 128], rhs=h2T[:, k, s0:s0 + sz],
                                             start=(k == 0), stop=(k == KC - 1)),
                                     reads=[wt.key(k), "h2T%d" % k], writes=[bk(b)], signal=(k == KC - 1 or k == 7))
                            i = ri % 2
                            ri += 1
                            rs = rsb[i][:, 0:sz]
                            P.op("act", I("activation", out=rs, in_=PS(b, sz), func=AF.Relu), reads=[bk(b)], writes=["rs%d" % i])
                            P.op("dve", I("tensor_tensor", out=uT[:, fc, s0:s0 + sz], in0=rs, in1=rs, op=ALU.mult), reads=["rs%d" % i], writes=["uT"])
                for cg in range(4):
                    wt = load_w16(w_down, cg * 512, r0=fg * 2048)
                    for t in range(6):
                        v = 0 if t < 4 else 1
                        b = rot_a1.next() if (t % 2 == 0) else rot_a2.next()
                        for k in range(KC):
                            P.op("pe", I("matmul", PS(b, 512), lhsT=uT[:, k, t * 128:(t + 1) * 128], rhs=wt.k(k),
                                         start=(k == 0), stop=(k == KC - 1)),
                                 reads=[wt.key(k), "uT"], writes=[bk(b)], signal=(k == KC - 1 or k == 7))
                        i = (cg * 6 + t) % 2
                        tp = tmp512b[i]
                        P.op("dve", I("tensor_tensor", out=tp, in0=PS(b, 512), in1=Gt[:, v, cg * 512:(cg + 1) * 512], op=ALU.mult),
                             reads=[bk(b), "G"], writes=["tb%d" % i])
                        xo = "xo%d_%d" % (t, cg)
                        P.op("dve", I("tensor_tensor", out=xres[:, t, cg * 512:(cg + 1) * 512], in0=xres[:, t, cg * 512:(cg + 1) * 512], in1=tp, op=ALU.add),
                             reads=["tb%d" % i, "x%d" % t], writes=(["x%d" % t, xo] if fg == 3 else ["x%d" % t]))
                        if fg == 3:
                            P.dma("sp", ym_t[t][:, cg * 512:(cg + 1) * 512], xres[:, t, cg * 512:(cg + 1) * 512], sem="x%d" % t, reads=[xo])

            P.wait_all("sp", ["x%d" % t for t in range(6)] + ["xo%d_%d" % (t, cg) for t in range(6) for cg in range(4)] + ["ost0", "ost1"])

        try:
            body()
        except _Stop:
            pass
        fin = []
        for e_ in Prog.CE:
            if P.cnt[e_] > 0:
                fin.append((P.esem[e_], P.cnt[e_]))
        P.q["sp"].append((fin, None, None))

        with nc.Block() as block:
            @block.sync
            def _(e):
                P.emit("sp", e)

            @block.gpsimd
            def _(e):
                P.emit("pool", e)

            @block.tensor
            def _(e):
                P.emit("pe", e)

            @block.scalar
            def _(e):
                P.emit("act", e)

            @block.vector
            def _(e):
                P.emit("dve", e)
        build_program.stats = dict(n={k: len(v) for k, v in P.q.items()}, waits=P.n_wait, arena=A.off, cnt=dict(P.cnt))
    return nc


GRID_W = 64
ROWS = 16


def _core_geometry(j):
    b = j // 4
    qq = j % 4
    ws = 0 if qq < 2 else 4
    own_rows = list(range(4 * qq, 4 * qq + 4))
    halo_rows = [r for r in range(ws, ws + 12) if r not in own_rows]
    rows = own_rows + halo_rows
    pos = np.concatenate([np.arange(r * GRID_W, (r + 1) * GRID_W) for r in rows])
    return b, qq, pos


def _static_tables(j, rpb):
    b, qq, pos = _core_geometry(j)
    row = (pos // GRID_W).astype(np.int64)
    col = (pos % GRID_W).astype(np.int64)
    n_freq = 32
    inv = (10000.0 ** (-np.arange(n_freq, dtype=np.float32) / n_freq)).astype(np.float32)
    ang_r = row[:, None].astype(np.float32) * inv[None, :]
    ang_c = col[:, None].astype(np.float32) * inv[None, :]
    cosT = np.zeros((128, 768), np.float32)
    sinT = np.zeros((128, 768), np.float32)
    cosT[0:32] = np.cos(ang_r).T
    cosT[32:64] = np.cos(ang_r).T
    cosT[64:96] = np.cos(ang_c).T
    cosT[96:128] = np.cos(ang_c).T
    sinT[0:32] = -np.sin(ang_r).T
    sinT[32:64] = np.sin(ang_r).T
    sinT[64:96] = -np.sin(ang_c).T
    sinT[96:128] = np.sin(ang_c).T
    qpos = pos[:256]
    valid = np.abs(qpos[None, :] - pos[:, None]) <= 128
    mA = np.where(valid, 0.0, NEGM).astype(np.float32)
    maskA = np.concatenate([mA, mA], axis=1)
    qr = row[:256]
    qc = col[:256]
    rstart = np.clip(qr - 4, 0, ROWS - 8)
    cstart = np.clip(qc - 8, 0, GRID_W - 16)
    kr = row[:, None]
    kcc = col[:, None]
    vr = (kr >= rstart[None, :]) & (kr < rstart[None, :] + 8)
    vc = (kcc >= cstart[None, :]) & (kcc < cstart[None, :] + 16)
    valid = vr & vc
    dr = np.clip(kr - qr[None, :] + 7, 0, 14)
    dc = np.clip(kcc - qc[None, :] + 15, 0, 30)
    bias = rpb[:, dr, dc]
    bias = np.where(valid[None], bias, np.float32(NEGM)).astype(np.float32)
    biasB = np.ascontiguousarray(np.transpose(bias, (1, 0, 2))).reshape(768, 2048)
    return cosT, sinT, maskA, biasB


def _perm():
    p = np.zeros((128, 128), np.float32)
    for d in range(128):
        s = d + 32 if (d % 64) < 32 else d - 32
        p[s, d] = 1.0
    return p


_NC_CACHE = {}


def kernel(x_prompt, x_sample, cache_a_k, cache_a_v, cache_b_k, cache_b_v, c, c_ctx,
           norm1_w, norm2_w, w_ada, b_ada, w_in, q_norm_a, k_norm_a, q_norm_b, k_norm_b,
           sink_a, rpb_b, w_br_a, w_br_b, w_out, w_up, w_down, _debug=None, _cores=None):
    f = lambda a: np.ascontiguousarray(np.asarray(a, dtype=np.float32))
    x_prompt, x_sample = f(x_prompt), f(x_sample)
    c, c_ctx = f(c), f(c_ctx)
    cores = list(range(NCORES)) if _cores is None else _cores
    key = "dbg" if _debug is not None else "main"
    if key not in _NC_CACHE:
        _NC_CACHE[key] = build_program(_debug)
    nc = _NC_CACHE[key]

    ident = np.eye(128, dtype=np.float32)
    perm = _perm()
    shared = dict(
        ident=ident, perm=perm,
        w_ada=f(w_ada)[0], w_in=f(w_in)[0], w_br_a=f(w_br_a)[0], w_br_b=f(w_br_b)[0],
        w_out=f(w_out)[0], w_up=f(w_up)[0], w_down=f(w_down)[0],
    )
    rpb = f(rpb_b)[0]
    in_maps = []
    for idx_core, j in enumerate(cores):
        b, qq, pos = _core_geometry(j)
        xm = np.concatenate([x_prompt[2 * j], x_prompt[2 * j + 1], x_sample[b][pos[:256]]], axis=0)
        xh = x_sample[b][pos[256:]]
        cvec = np.stack([c_ctx, c[b]], axis=0)
        cT = np.ascontiguousarray(cvec.reshape(2, 16, 128).transpose(2, 1, 0)).reshape(128, 32)
        pp = np.zeros((128, PP_W), np.float32)
        pp[:, PP_N1:PP_N1 + 16] = f(norm1_w)[0].reshape(16, 128).T
        pp[:, PP_N2:PP_N2 + 16] = f(norm2_w)[0].reshape(16, 128).T
        pp[:, PP_BADA:PP_BADA + 96] = f(b_ada)[0].reshape(96, 128).T
        pp[:, PP_QNA] = f(q_norm_a)[0]
        pp[:, PP_KNA] = f(k_norm_a)[0]
        pp[:, PP_QNB] = f(q_norm_b)[0]
        pp[:, PP_KNB] = f(k_norm_b)[0]
        pp[:, PP_SINK:PP_SINK + 8] = f(sink_a)[0][None, :]
        cosT, sinT, maskA, biasB = _static_tables(j, rpb)
        m = dict(shared)
        m.update(
            x_main=np.ascontiguousarray(xm), x_halo=np.ascontiguousarray(xh), cT=cT, pp=pp,
            cosT=cosT, sinT=sinT, maskA=maskA, biasB=biasB,
            cak=f(cache_a_k)[b, 0].reshape(256, 256), cav=f(cache_a_v)[b, 0].reshape(256, 256),
            cbk=f(cache_b_k)[b, 0].reshape(256, 1024), cbv=f(cache_b_v)[b, 0].reshape(256, 1024),
        )
        in_maps.append(m)

    res = run_bass_kernel_spmd(nc, in_maps, core_ids=list(range(len(cores))))
    if _debug is not None:
        return res

    y_prompt = np.zeros((16, 256, D), np.float32)
    y_sample = np.zeros((2, 1024, D), np.float32)
    nak = np.zeros((16, 1, 256, 2, 128), np.float32)
    nav = np.zeros((16, 1, 256, 2, 128), np.float32)
    nbk = np.zeros((16, 1, 256, 8, 128), np.float32)
    nbv = np.zeros((16, 1, 256, 8, 128), np.float32)
    for idx, j in enumerate(cores):
        r = res.results[idx]
        b, qq, pos = _core_geometry(j)
        ym = r["y_main"]
        y_prompt[2 * j] = ym[0:256]
        y_prompt[2 * j + 1] = ym[256:512]
        y_sample[b][pos[:256]] = ym[512:768]
        for s in range(2):
            nak[2 * j + s, 0] = r["nak"][s * 256:(s + 1) * 256].reshape(256, 2, 128)
            nav[2 * j + s, 0] = r["nav"][s * 256:(s + 1) * 256].reshape(256, 2, 128)
            nbk[2 * j + s, 0] = r["nbk"][s * 256:(s + 1) * 256].reshape(256, 8, 128)
            nbv[2 * j + s, 0] = r["nbv"][s * 256:(s + 1) * 256].reshape(256, 8, 128)
    return (y_prompt, y_sample, nak, nav, nbk, nbv)
```

```python
import os
import numpy as np
import concourse.bass as bass
import concourse.mybir as mybir
from concourse.bass_utils import run_bass_kernel_spmd

F32 = mybir.dt.float32
BF16 = mybir.dt.bfloat16
ALU = mybir.AluOpType
AF = mybir.ActivationFunctionType
AX = mybir.AxisListType

NCORES = 8
D = 2048
KC = 16
NMAIN = 768
NHALO = 512
NTOK = NMAIN + NHALO
EPS = 1e-6
NEGM = -30000.0
IN_W = 8704
DFF = 8192
CH_P = (0, 512)
CH_S = (512, 256)
CH_H = (768, 512)

PP_N1 = 0
PP_N2 = 16
PP_BADA = 32
PP_QNA = 128
PP_KNA = 129
PP_QNB = 130
PP_KNB = 131
PP_SINK = 132
PP_SEL = 140
PP_W = 142


class Prog:
    CE = ("pe", "act", "dve", "pool")
    ALL = ("pe", "act", "dve", "pool", "sp")

    def __init__(self, nc, esems, dma_sems):
        self.nc = nc
        self.q = {e: [] for e in self.ALL}
        self.cnt = {e: 0 for e in self.CE}
        self.esem = esems
        self.free_dsems = list(dma_sems)
        self.dsem = {}
        self.seen = {e: {} for e in self.ALL}
        self.lastw = {}
        self.readers = {}
        self.n_wait = 0

    def _need(self, eng, ev, waits):
        if ev is None:
            return
        semkey, handle, val, _ = ev
        if self.seen[eng].get(semkey, 0) >= val:
            return
        self.seen[eng][semkey] = val
        waits.append((handle, val))
        self.n_wait += 1

    def _deps(self, eng, reads, writes, waits):
        for k in reads:
            ev = self.lastw.get(k)
            if ev is not None and not (ev[3] == "pe" and eng == "pe"):
                self._need(eng, ev, waits)
        for k in writes:
            ev = self.lastw.get(k)
            if ev is not None and not (ev[3] == "pe" and eng == "pe"):
                self._need(eng, ev, waits)
            for ev in self.readers.get(k, ()):
                if not (ev[3] == "pe" and eng == "pe"):
                    self._need(eng, ev, waits)

    def _commit(self, ev, reads, writes):
        for k in reads:
            lst = self.readers.setdefault(k, [])
            lst[:] = [e for e in lst if e[0] != ev[0]]
            lst.append(ev)
        for k in writes:
            self.lastw[k] = ev
            self.readers[k] = []

    def op(self, eng, fn, reads=(), writes=(), signal=True):
        waits = []
        self._deps(eng, reads, writes, waits)
        if signal:
            self.cnt[eng] += 1
            val = self.cnt[eng]
        else:
            val = self.cnt[eng] + 1
        ev = ("E" + eng, self.esem[eng], val, eng)
        self._commit(ev, reads, writes)
        self.q[eng].append((waits, fn, (self.esem[eng], 1) if signal else None))

    def dma(self, eng, out, in_, sem, reads=(), writes=(), **kw):
        waits = []
        self._deps(eng, reads, writes, waits)
        if sem not in self.dsem:
            self.dsem[sem] = [self.free_dsems.pop(), 0]
        rec = self.dsem[sem]
        rec[1] += 16
        ev = ("D" + sem, rec[0], rec[1], "dma")
        self._commit(ev, reads, writes)
        self.q[eng].append((waits, (lambda e, o=out, i=in_, k=kw: e.dma_start(out=o, in_=i, **k)), (rec[0], 16)))

    def custom(self, eng, fn, sem, inc, reads=(), writes=()):
        waits = []
        self._deps(eng, reads, writes, waits)
        if sem not in self.dsem:
            self.dsem[sem] = [self.free_dsems.pop(), 0]
        rec = self.dsem[sem]
        rec[1] += inc
        ev = ("D" + sem, rec[0], rec[1], "dma")
        self._commit(ev, reads, writes)
        self.q[eng].append((waits, fn, (rec[0], inc)))

    def wait_all(self, eng, keys):
        waits = []
        for k in keys:
            self._need(eng, self.lastw.get(k), waits)
            for ev in self.readers.get(k, ()):
                self._need(eng, ev, waits)
        if waits:
            self.q[eng].append((waits, None, None))

    def alias(self, new_keys, old_keys):
        evs = []
        for k in old_keys:
            if self.lastw.get(k) is not None:
                evs.append(self.lastw[k])
            evs.extend(self.readers.get(k, ()))
        for k in new_keys:
            self.lastw[k] = None
            self.readers[k] = list(evs)

    def emit(self, eng_name, eng):
        for waits, fn, inc in self.q[eng_name]:
            for h, v in waits:
                eng.wait_ge(h, v)
            if fn is None:
                continue
            ins = fn(eng)
            if inc is not None:
                ins.then_inc(inc[0], inc[1])


def I(method, *a, **k):
    return lambda e: getattr(e, method)(*a, **k)


class Rot:
    def __init__(self, items):
        self.items = list(items)
        self.i = 0

    def next(self):
        v = self.items[self.i % len(self.items)]
        self.i += 1
        return v


def build_program(debug=None):
    nc = bass.Bass("TRN2", target_bir_lowering=False)

    def din(name, shape, dt=F32):
        return nc.dram_tensor(name, list(shape), dt, kind="ExternalInput").ap()

    def dout(name, shape, dt=F32):
        return nc.dram_tensor(name, list(shape), dt, kind="ExternalOutput").ap()

    x_main = din("x_main", [NMAIN, D])
    x_halo = din("x_halo", [NHALO, D])
    cT_d = din("cT", [128, 32])
    pp_d = din("pp", [128, PP_W])
    ident_d = din("ident", [128, 128])
    perm_d = din("perm", [128, 128])
    cos_d = din("cosT", [128, NMAIN])
    sin_d = din("sinT", [128, NMAIN])
    maskA_d = din("maskA", [NMAIN, 512])
    biasB_d = din("biasB", [NMAIN, 2048])
    cak_d = din("cak", [256, 256])
    cav_d = din("cav", [256, 256])
    cbk_d = din("cbk", [256, 1024])
    cbv_d = din("cbv", [256, 1024])
    w_ada = din("w_ada", [D, 6 * D])
    w_in = din("w_in", [D, IN_W])
    w_bra = din("w_br_a", [1024, D])
    w_brb = din("w_br_b", [1024, D])
    w_out = din("w_out", [D, D])
    w_up = din("w_up", [D, DFF])
    w_down = din("w_down", [DFF, D])

    y_main = dout("y_main", [NMAIN, D])
    nak_o = dout("nak", [512, 256])
    nav_o = dout("nav", [512, 256])
    nbk_o = dout("nbk", [512, 1024])
    nbv_o = dout("nbv", [512, 1024])
    dbg_o = None
    if debug is not None:
        dbg_o = dout("dbg", debug["shape"], BF16 if debug.get("dtype") == "bf16" else F32)

    import contextlib
    es = contextlib.ExitStack()
    with es:
        ARENA_W = 53200
        arena = es.enter_context(nc.sbuf_tensor("arena", [128, ARENA_W], F32))
        ps = es.enter_context(nc.psum_tensor("ps", [128, 8, 512], F32))
        sem_names = ["pe", "act", "dve", "pool"]
        esems = {n: es.enter_context(nc.semaphore("s_" + n)) for n in sem_names}
        dsems = [es.enter_context(nc.semaphore("d%d" % i)) for i in range(70)]
        P = Prog(nc, esems, dsems)

        class Arena:
            def __init__(self):
                self.off = 0

            def take(self, nbytes):
                o = self.off
                self.off += (nbytes + 63) // 64 * 64
                assert self.off <= ARENA_W * 4, ("arena overflow", self.off)
                return o

        def view(off_b, shape, dt):
            esz = 2 if dt == BF16 else 4
            n = int(np.prod(shape))
            assert off_b % 4 == 0
            w0 = off_b // 4
            nw = (n * esz + 3) // 4
            ap = arena[:, w0:w0 + nw]
            if dt == BF16:
                ap = ap.bitcast(BF16)
            if len(shape) == 2:
                ap = ap.rearrange("p (a b) -> p a b", a=shape[0])
            elif len(shape) == 3:
                ap = ap.rearrange("p (a b c) -> p a b c", a=shape[0], b=shape[1])
            return ap

        A = Arena()
        RING_N = 6
        ring_off = [A.take(8192) for _ in range(RING_N)]
        o_pp = A.take(PP_W * 4)
        o_identf = A.take(512)
        o_identb = A.take(256)
        o_permf = A.take(512)
        o_onesb = A.take(256)
        o_modT = A.take(96 * 2 * 4)
        o_ab = A.take(4 * 2 * 16 * 4)
        o_small = A.take(256)
        o_esink = A.take(32)
        o_cT = A.take(128)
        o_scT = A.take(64)

        pp = view(o_pp, [PP_W], F32)
        identf = view(o_identf, [128], F32)
        identb = view(o_identb, [128], BF16)
        permf = view(o_permf, [128], F32)
        onesb = view(o_onesb, [128], BF16)
        modT = view(o_modT, [96, 2], F32)
        abv = view(o_ab, [4, 2, 16], F32)
        small = view(o_small, [64], F32)
        esink = view(o_esink, [8], F32)
        epsc = small[:, 20:21]
        epsc128 = small[:, 21:22]
        zeroc = small[:, 22:23]

        R1 = A.take(69632)
        R2 = A.take(65536)
        mT_off = A.take(KC * NMAIN * 2)
        mTv = view(mT_off, [KC, NMAIN], BF16)
        hT = view(R2, [KC, NTOK], BF16)
        oT = view(R2 + 40960, [2, 8, NMAIN], BF16)
        Gt = view(R2 + 65536 - 16384, [2, 2048], F32)
        h2T = view(R2, [KC, NMAIN], BF16)
        uT = view(R2 + 24576, [KC, NMAIN], BF16)
        xnn = [view(R2 + 24576 + i * 4096, [2048], BF16) for i in range(6)]
        xs_ = [view(R1 + i * 8192, [2048], F32) for i in range(3)]
        xn_ = [view(R1 + 24576 + i * 4096, [2048], BF16) for i in range(10)]
        o = R1
        qT = view(o, [4, NMAIN], BF16); o += 4 * NMAIN * 2
        kT = view(o, [4, NTOK], BF16); o += 4 * NTOK * 2
        vS = view(o, [10, 512], BF16); o += 10 * 512 * 2
        NPT = 10
        pT = [view(o + i * 1024, [512], BF16) for i in range(NPT)]; o += NPT * 1024
        m = o
        maskA = view(m, [6, 512], BF16)
        cosT = view(m + 6144, [NMAIN], F32)
        sinT = view(m + 9216, [NMAIN], F32)
        biasB = [view(m + i * 6144, [6, 512], BF16) for i in range(2)]
        kc_T = view(m + 12288, [4, 256], BF16)
        vc_S = view(m + 14336, [2, 512], BF16)
        kc_S = view(m + 16384, [2, 512], BF16)
        f32s = [view(m + 18432 + i * 2048, [512], F32) for i in range(3)]
        o += 24576
        ost = [view(o + i * 2048, [512], F32) for i in range(2)]; o += 2 * 2048
        sqb = [view(o + i * 1024, [512], BF16) for i in range(2)]; o += 2 * 1024
        rl_ = [view(o + i * 1024, [256], F32) for i in range(2)]; o += 2 * 1024
        assert o - R1 <= 69632, o - R1
        R1_ATT_KEYS = (["qT%d" % i for i in range(4)] + ["kT%d" % i for i in range(4)] + ["vS", "kcT", "vcS", "kcS", "f32s0", "f32s1", "f32s2", "ost0", "ost1", "sqb0", "sqb1",
                        "rl0", "rl1", "biasB0", "biasB1", "maskA", "cosT", "sinT"] + ["pT%d" % i for i in range(NPT)])
        xres = view(R1, [6, 2048], F32)
        pb_scr = R1 + 49152
        sgs = [view(pb_scr + i * 1536, [384], F32) for i in range(2)]
        tms = [view(pb_scr + 3072 + i * 1536, [384], F32) for i in range(2)]
        tmp512 = [view(pb_scr + 6144 + i * 2048, [512], F32) for i in range(2)]
        onesf = view(pb_scr + 10240, [128], F32)
        diag = [view(pb_scr + 10752 + i * 512, [128], F32) for i in range(2)]
        tA = view(pb_scr + 11776, [4, NMAIN], BF16)
        rsb = [view(pb_scr + i * 1024, [384], BF16) for i in range(2)]
        tmp512b = [view(pb_scr + 4096 + i * 2048, [512], F32) for i in range(2)]

        def PS(bank, n=512, off=0):
            return ps[:, bank, off:off + n]

        def PSB(bank, n, off=0):
            return ps[:, bank, :].bitcast(BF16)[:, off:off + n]

        rot_main = Rot([0, 1, 2, 3])
        rot_a1 = Rot([4, 5])
        rot_a2 = Rot([6, 7])

        def bk(b):
            return "ps%d" % b

        def set_mode(m):
            if m == "proj":
                rot_main.items, rot_a1.items, rot_a2.items = [0, 1, 2, 3, 4, 5], [6], [7]
            else:
                rot_main.items, rot_a1.items, rot_a2.items = [0, 1, 2, 3], [4, 5], [6, 7]

        ring_state = {"n": 0}

        class WT:
            def __init__(self, slots):
                self.slots = slots

            def k(self, k):
                return view(ring_off[self.slots[k // 8]], [8, 512], BF16)[:, k % 8, :]

            def key(self, k):
                return "ring%d" % self.slots[k // 8]

        def load_entry(src):
            n = ring_state["n"]
            ring_state["n"] += 1
            slot = n % RING_N
            key = "ring%d" % slot
            P.dma("pool", view(ring_off[slot], [8, 512], BF16), src, sem=key, writes=[key])
            return slot

        def wsrc(w, r0, nr, c0, ncol):
            return w[r0:r0 + nr, c0:c0 + ncol].rearrange("(k p) n -> p k n", p=128)

        def load_w16(w, c0, r0=0):
            n = ring_state["n"]
            slot = n % RING_N
            if slot % 2 == 0:
                ring_state["n"] += 2
                k0, k1 = "ring%d" % slot, "ring%d" % (slot + 1)
                dst = view(ring_off[slot], [16, 512], BF16)
                P.dma("pool", dst, wsrc(w, r0, 2048, c0, 512), sem=k0, writes=[k0, k1])
                return WT([slot, slot + 1])
            return WT([load_entry(wsrc(w, r0, 1024, c0, 512)), load_entry(wsrc(w, r0 + 1024, 1024, c0, 512))])

        def load_w8(w, c0):
            return WT([load_entry(wsrc(w, 0, 1024, c0, 512))])

        class _Stop(Exception):
            pass

        def stop_at(name, ap, keys):
            if debug is not None and debug["at"] == name:
                P.dma("sp", dbg_o, ap, sem="dbg", reads=keys)
                P.wait_all("sp", keys)
                raise _Stop()

        chains = []

        def tick():
            for ch in list(chains):
                step = ch.pop(0)
                step()
                if not ch:
                    chains.remove(ch)

        def drain():
            while chains:
                tick()

        def body():
            P.dma("sp", pp, pp_d, sem="pp", writes=["pp"])
            P.dma("sp", identf, ident_d, sem="identf", writes=["identf"])
            P.dma("sp", permf, perm_d, sem="permf", writes=["permf"])
            P.dma("pool", identb, ident_d, sem="identb", writes=["identb"])
            P.op("dve", I("memset", onesb, 1.0), writes=["onesb"])
            P.op("dve", I("memset", epsc, EPS), writes=["epsc"])
            P.op("dve", I("memset", epsc128, 128.0 * EPS), writes=["epsc"])
            P.op("dve", I("memset", zeroc, 0.0), writes=["epsc"])
            cTf = view(o_cT, [32], F32)
            scTflat = view(o_scT, [32], BF16)
            scTb = scTflat.rearrange("p (k v) -> p k v", v=2)
            P.dma("sp", cTf, cT_d, sem="cT", writes=["cTf"])
            P.op("act", I("activation", out=scTflat, in_=cTf, func=AF.Silu), reads=["cTf"], writes=["scT"])
            P.op("act", I("activation", out=esink, in_=pp[:, PP_SINK:PP_SINK + 8], func=AF.Exp), reads=["pp"], writes=["esink"])

            def norm_group(tiles, src_fn, vec, kindA, kindB, dstT, dst_key, xbufs, xkeys, tok0, stat0):
                norm_stats(tiles, xbufs, xkeys, stat0)
                norm_tr(len(tiles), vec, kindA, kindB, dstT, dst_key, xbufs, xkeys, tok0)

            def norm_stats(tiles, xbufs, xkeys, stat0):
                for i, (xin, xkey, loader) in enumerate(tiles):
                    if loader is not None:
                        loader()
                    xnb, xnk = xbufs[i], xkeys[i]
                    si = (stat0 + i) % 10
                    ssq = small[:, 24 + si:25 + si]
                    rst = small[:, 36 + si:37 + si]
                    sk = "ssq%d" % si
                    P.op("act", I("activation", out=xnb, in_=xin, func=AF.Square, accum_out=ssq), reads=[xkey], writes=[xnk, sk])
                    P.op("act", I("activation", out=rst, in_=ssq, func=AF.Sqrt, scale=1.0 / D, bias=epsc), reads=[sk, "epsc"], writes=[sk + "r"])
                    P.op("dve", I("reciprocal", out=rst, in_=rst), reads=[sk + "r"], writes=[sk + "r"])
                    P.op("dve", I("tensor_scalar", out=xnb, in0=xin, scalar1=rst, scalar2=None, op0=ALU.mult), reads=[xkey, sk + "r"], writes=[xnk])

            def norm_tr(nt, vec, kindA, kindB, dstT, dst_key, xbufs, xkeys, tok0, chunks=None):
                for c in (range(KC) if chunks is None else chunks):
                    b = rot_main.next()
                    for i in range(nt):
                        P.op("pe", I("transpose", PSB(b, 128, i * 128), xbufs[i][:, c * 128:(c + 1) * 128], identb),
                             reads=[xkeys[i], "identb"], writes=[bk(b)], signal=(i == nt - 1))
                    dst = dstT[:, c, tok0:tok0 + nt * 128]
                    if c % 2 == 0:
                        P.op("act", I("activation", out=dst, in_=PSB(b, nt * 128), func=AF.Identity,
                                      scale=abv[:, kindA, vec, c:c + 1], bias=abv[:, kindB, vec, c:c + 1]),
                             reads=[bk(b), "ab", "ab2"], writes=[dst_key + str(c)])
                    else:
                        P.op("dve", I("tensor_scalar", out=dst, in0=PSB(b, nt * 128), scalar1=abv[:, kindA, vec, c:c + 1],
                                      scalar2=abv[:, kindB, vec, c:c + 1], op0=ALU.mult, op1=ALU.add),
                             reads=[bk(b), "ab", "ab2"], writes=[dst_key + str(c)])

            xm_t = x_main.rearrange("(t p) d -> t p d", p=128)
            xh_t = x_halo.rearrange("(t p) d -> t p d", p=128)
            norm1_groups = []
            xi = 0
            for tl, vec in [([0, 1, 2, 3], 0), ([4, 5], 1), ([6, 7, 8, 9], 1)]:
                tiles, xb, xk = [], [], []
                for ti in tl:
                    bi = xi % 3
                    xi += 1
                    src = xm_t[ti] if ti < 6 else xh_t[ti - 6]
                    ld = (lambda bi=bi, src=src: P.dma("sp", xs_[bi], src, sem="xs%d" % bi, writes=["xs%d" % bi]))
                    tiles.append((xs_[bi], "xs%d" % bi, ld))
                    xb.append(xn_[ti])
                    xk.append("xn%d" % ti)
                norm_stats(tiles, xb, xk, tl[0])
                norm1_groups.append((tl, vec, xb, xk))

            ada_pending = [0, 4, 1, 5, 2, 6, 3, 7] + list(range(8, 24))

            def ada_some(n):
                for _ in range(min(n, len(ada_pending))):
                    t = ada_pending.pop(0)
                    wt = load_w16(w_ada, t * 512)
                    b = rot_a2.next()
                    for cc in range(4):
                        for k in range(KC):
                            P.op("pe", I("matmul", ps[:, b, cc * 2:cc * 2 + 2], lhsT=wt.k(k)[:, cc * 128:(cc + 1) * 128],
                                         rhs=scTb[:, k, :], start=(k == 0), stop=(k == KC - 1)),
                                 reads=[wt.key(k), "scT"], writes=[bk(b)], signal=(k == KC - 1 or k == 7))
                    for v in range(2):
                        P.op("dve", I("tensor_tensor", out=modT[:, t * 4:(t + 1) * 4, v],
                                      in0=ps[:, b, 0:8].rearrange("p (c v) -> p c v", v=2)[:, :, v],
                                      in1=pp[:, PP_BADA + t * 4:PP_BADA + t * 4 + 4], op=ALU.add),
                             reads=[bk(b), "pp"], writes=["modT"])

            for qi in range(4):
                ada_some(2)
                c4 = slice(4 * qi, 4 * qi + 4)
                for v in range(2):
                    P.op("dve", I("scalar_tensor_tensor", out=abv[:, 0, v, c4], in0=modT[:, 16 + 4 * qi:20 + 4 * qi, v], scalar=1.0,
                                  in1=pp[:, PP_N1 + 4 * qi:PP_N1 + 4 * qi + 4], op0=ALU.add, op1=ALU.mult), reads=["modT", "pp"], writes=["ab"])
                    P.op("dve", I("tensor_copy", out=abv[:, 1, v, c4], in_=modT[:, 4 * qi:4 * qi + 4, v]), reads=["modT"], writes=["ab"])
                for (tl, vec, xb, xk) in norm1_groups:
                    norm_tr(len(tl), vec, 0, 1, hT, "hT", xb, xk, tl[0] * 128, chunks=range(4 * qi, 4 * qi + 4))
            SQ128 = float(np.sqrt(128.0))
            P.op("dve", I("tensor_copy", out=small[:, 0:1], in_=pp[:, PP_QNA:PP_QNA + 1]), reads=["pp"], writes=["small"])
            P.op("dve", I("tensor_scalar", out=small[:, 1:2], in0=pp[:, PP_KNA:PP_KNA + 1], scalar1=SQ128, scalar2=None, op0=ALU.mult), reads=["pp"], writes=["small"])
            P.op("dve", I("tensor_copy", out=small[:, 2:3], in_=pp[:, PP_QNB:PP_QNB + 1]), reads=["pp"], writes=["small"])
            P.op("dve", I("tensor_scalar", out=small[:, 3:4], in0=pp[:, PP_KNB:PP_KNB + 1], scalar1=SQ128, scalar2=None, op0=ALU.mult), reads=["pp"], writes=["small"])

            def ada_finish():
                ada_some(len(ada_pending))
                for v in range(2):
                    P.op("dve", I("scalar_tensor_tensor", out=abv[:, 2, v, :], in0=modT[:, 64:80, v], scalar=1.0,
                                  in1=pp[:, PP_N2:PP_N2 + 16], op0=ALU.add, op1=ALU.mult), reads=["modT", "pp"], writes=["ab2"])
                    P.op("dve", I("tensor_copy", out=abv[:, 3, v, :], in_=modT[:, 48:64, v]), reads=["modT"], writes=["ab2"])
            stop_at("ada", modT, ["modT", "ab", "small"])

            tkv = load_w16(w_in, 1024)
            stop_at("hT", hT, ["hT%d" % i for i in range(16)])

            P.alias(R1_ATT_KEYS, ["xs0", "xs1", "xs2"] + ["xn%d" % i for i in range(10)])
            P.dma("pool", maskA, maskA_d.rearrange("(c p) n -> p c n", p=128), sem="maskA", writes=["maskA"])
            P.dma("sp", cosT, cos_d, sem="cosT", writes=["cosT"])
            P.dma("sp", sinT, sin_d, sem="sinT", writes=["sinT"])

            cstate = {"i": 0, "ost": 0}

            def proj_fm(wt, cc, chunks, make_chain):
                for (s0, sz) in chunks:
                    b = rot_main.next()
                    for k in range(KC):
                        P.op("pe", I("matmul", PS(b, sz), lhsT=wt.k(k)[:, cc * 128:(cc + 1) * 128], rhs=hT[:, k, s0:s0 + sz],
                                     start=(k == 0), stop=(k == KC - 1)),
                             reads=[wt.key(k), "hT%d" % k], writes=[bk(b)], signal=(k == KC - 1 or k == 7))
                    tick()
                    make_chain(b, s0, sz)

            def norm_steps(b, sz, wcol, out_ap, out_key):
                i = cstate["i"] % 2
                cstate["i"] += 1
                sq = sqb[i][:, 0:sz]
                sqk = "sqb%d" % i
                P.op("act", I("activation", out=sq, in_=PS(b, sz), func=AF.Square), reads=[bk(b)], writes=[sqk])

                def step1():
                    b2 = rot_a1.next()
                    P.op("pe", I("matmul", PS(b2, sz), lhsT=onesb, rhs=sq, start=True, stop=True), reads=[sqk, "onesb"], writes=[bk(b2)])
                    rr = f32s[2][:, 0:sz]
                    P.op("act", I("activation", out=rr, in_=PS(b2, sz), func=AF.Ln, scale=1.0, bias=epsc128), reads=[bk(b2), "epsc"], writes=["f32s2"])
                    P.op("act", I("activation", out=rr, in_=rr, func=AF.Exp, scale=-0.5), reads=["f32s2"], writes=["f32s2"])
                    P.op("dve", I("scalar_tensor_tensor", out=out_ap, in0=PS(b, sz), scalar=wcol, in1=rr, op0=ALU.mult, op1=ALU.mult),
                         reads=[bk(b), "f32s2", "small"], writes=[out_key])
                return step1

            def rope_step(src_ap, src_key, e0, sz, out_ap, out_key):
                def step():
                    b3 = rot_a2.next()
                    P.op("pe", I("matmul", PS(b3, sz), lhsT=permf, rhs=src_ap, start=True, stop=True), reads=[src_key, "permf"], writes=[bk(b3)])
                    t1 = f32s[2][:, 0:sz]
                    P.op("dve", I("tensor_tensor", out=t1, in0=PS(b3, sz), in1=sinT[:, e0:e0 + sz], op=ALU.mult), reads=[bk(b3), "sinT"], writes=["f32s2"])
                    P.op("dve", I("tensor_tensor", out=src_ap, in0=src_ap, in1=cosT[:, e0:e0 + sz], op=ALU.mult), reads=[src_key, "cosT"], writes=[src_key])
                    P.op("dve", I("tensor_tensor", out=out_ap, in0=src_ap, in1=t1, op=ALU.add), reads=[src_key, "f32s2"], writes=[out_key])
                return step

            def kout_step(kn_ap, kn_key, dst_dram, head_col, kt_dst, kt_key):
                def step():
                    P.op("act", I("activation", out=kt_dst, in_=kn_ap, func=AF.Copy), reads=[kn_key], writes=[kt_key])
                    i = cstate["ost"] % 2
                    cstate["ost"] += 1
                    lo = pT[8 + i]
                    lok = "pT%d" % (8 + i)
                    P.op("dve", I("tensor_tensor", out=lo, in0=kn_ap, in1=kt_dst, op=ALU.subtract), reads=[kn_key, kt_key], writes=[lok])
                    b4 = rot_a2.next()
                    for t in range(4):
                        P.op("pe", I("transpose", PSB(b4, 128, t * 128), kt_dst[:, t * 128:(t + 1) * 128], identb),
                             reads=[kt_key, "identb"], writes=[bk(b4)], signal=False)
                    for t in range(4):
                        P.op("pe", I("transpose", PSB(b4, 128, 512 + t * 128), lo[:, t * 128:(t + 1) * 128], identb),
                             reads=[lok, "identb"], writes=[bk(b4)], signal=(t == 3))
                    st = ost[i]
                    P.op("act", I("activation", out=st, in_=PSB(b4, 512, 0), func=AF.Copy), reads=[bk(b4)], writes=["ost%d" % i])
                    P.op("dve", I("tensor_tensor", out=st, in0=PSB(b4, 512, 512), in1=st, op=ALU.add), reads=[bk(b4), "ost%d" % i], writes=["ost%d" % i])
                    P.dma("sp", dst_dram.rearrange("(t p) n -> p t n", p=128)[:, :, head_col * 128:(head_col + 1) * 128],
                          st.rearrange("p (t d) -> p t d", t=4), sem="ost%d" % i, reads=["ost%d" % i])
                return step

            tmp_i = [0]

            def _nop():
                pass

            def proj_q(wt, cc, hslot, wcol, do_rope):
                def mk(b, s0, sz):
                    dst = qT[:, hslot, s0:s0 + sz]
                    if do_rope and s0 == CH_S[0]:
                        fi = tmp_i[0] % 2
                        tmp_i[0] += 1
                        tmp = f32s[fi][:, 0:sz]
                        chains.append([_nop, norm_steps(b, sz, wcol, tmp, "f32s%d" % fi), _nop, rope_step(tmp, "f32s%d" % fi, 0, sz, dst, "qT%d" % hslot)])
                    else:
                        chains.append([_nop, norm_steps(b, sz, wcol, dst, "qT%d" % hslot)])
                proj_fm(wt, cc, [CH_P, CH_S], mk)

            def proj_k(wt, cc, hslot, wcol, do_rope, dst_dram, head_col):
                def mk(b, s0, sz):
                    fi = tmp_i[0] % 2
                    tmp_i[0] += 1
                    tmp = f32s[fi][:, 0:sz]
                    tk = "f32s%d" % fi
                    dst = kT[:, hslot, s0:s0 + sz]
                    st1 = norm_steps(b, sz, wcol, tmp, tk)
                    if s0 == 0:
                        chains.append([_nop, st1, _nop, kout_step(tmp, tk, dst_dram, head_col, dst, "kT%d" % hslot)])
                    elif do_rope:
                        chains.append([_nop, st1, _nop, rope_step(tmp, tk, s0 - 512, sz, dst, "kT%d" % hslot)])
                    else:
                        def cp(tmp=tmp, tk=tk, dst=dst):
                            P.op("act", I("activation", out=dst, in_=tmp, func=AF.Copy), reads=[tk], writes=["kT%d" % hslot])
                        chains.append([_nop, st1, _nop, cp])
                proj_fm(wt, cc, [CH_P, CH_S, CH_H], mk)

            def proj_v(wt, c0, ncols, vcol0, dst_dram, dcol0):
                for ti in range(10):
                    b = rot_main.next()
                    for k in range(KC):
                        P.op("pe", I("matmul", PS(b, ncols), lhsT=hT[:, k, ti * 128:(ti + 1) * 128], rhs=wt.k(k)[:, c0:c0 + ncols],
                                     start=(k == 0), stop=(k == KC - 1)),
                             reads=[wt.key(k), "hT%d" % k], writes=[bk(b)], signal=(k == KC - 1 or k == 7))
                    tick()
                    if ti < 4:
                        i = cstate["ost"] % 2
                        cstate["ost"] += 1
                        st = ost[i][:, 0:ncols]
                        P.op("act", I("activation", out=st, in_=PS(b, ncols), func=AF.Copy), reads=[bk(b)], writes=["ost%d" % i])
                        P.op("dve", I("tensor_copy", out=vS[:, ti, vcol0:vcol0 + ncols], in_=st), reads=["ost%d" % i], writes=["vS"])
                        P.dma("sp", dst_dram[ti * 128:(ti + 1) * 128, dcol0:dcol0 + ncols], st, sem="ost%d" % i, reads=["ost%d" % i])
                    else:
                        P.op("dve", I("tensor_copy", out=vS[:, ti, vcol0:vcol0 + ncols], in_=PS(b, ncols)), reads=[bk(b)], writes=["vS"])

            pt_i = [0]

            def att_item(pair_heads, kslots, vcols, hs_slots, mixer, sink_cols, sample_bias, grp):
                same_kv = (kslots[0] == kslots[1]) and (vcols[0] == vcols[1])
                q0 = grp * 256
                if grp < 2:
                    chunks = [("tok", grp * 256 + c * 128, grp * 2 + c, None) for c in range(2)]
                else:
                    chunks = [("tok", 512 + c * 128, 4 + c, c) for c in range(6)] + [("ctx", c * 128, c, None) for c in range(2)]
                pts = []

                def scores():
                    for (kind, k0, vt, bc) in chunks:
                        b = rot_main.next()
                        first = True
                        if bc is not None:
                            bap, bkey = sample_bias(bc)
                            P.op("pe", I("matmul", PS(b, 512), lhsT=identb, rhs=bap, start=True, stop=False),
                                 reads=[bkey, "identb"], writes=[bk(b)], signal=False)
                            first = False
                        if same_kv:
                            lk, lkey = (kT[:, kslots[0], k0:k0 + 128], "kT%d" % kslots[0]) if kind == "tok" else (kc_T[:, kslots[0], k0:k0 + 128], "kcT")
                            P.op("pe", I("matmul", PS(b, 512).rearrange("p (a q) -> p a q", a=2), lhsT=lk,
                                         rhs=qT[:, hs_slots[0]:hs_slots[0] + 2, q0:q0 + 256], start=first, stop=True),
                                 reads=[lkey, "qT%d" % hs_slots[0], "qT%d" % (hs_slots[0] + 1)], writes=[bk(b)])
                        else:
                            for i in range(2):
                                lk, lkey = (kT[:, kslots[i], k0:k0 + 128], "kT%d" % kslots[i]) if kind == "tok" else (kc_T[:, kslots[i], k0:k0 + 128], "kcT")
                                P.op("pe", I("matmul", PS(b, 256, i * 256), lhsT=lk, rhs=qT[:, hs_slots[i], q0:q0 + 256],
                                             start=first, stop=(first or i == 1)),
                                     reads=[lkey, "qT%d" % hs_slots[i]], writes=[bk(b)], signal=(i == 1))
                        pi = pt_i[0] % NPT
                        pt_i[0] += 1
                        P.op("act", I("activation", out=pT[pi], in_=PS(b, 512), func=AF.Exp), reads=[bk(b)], writes=["pT%d" % pi])
                        pts.append((pi, kind, vt))

                def finish():
                    bo = rot_a1.next()
                    bl = rot_a2.next()
                    n = len(pts)
                    for i in range(2):
                        if same_kv and i == 1:
                            break
                        for ci, (pi, kind, vt) in enumerate(pts):
                            lv, vkey = (vS[:, vt, vcols[i]:vcols[i] + 128], "vS") if kind == "tok" else (vc_S[:, vt, vcols[i]:vcols[i] + 128], "vcS")
                            if same_kv:
                                P.op("pe", I("matmul", PS(bo, 512), lhsT=lv, rhs=pT[pi], start=(ci == 0), stop=(ci == n - 1)),
                                     reads=[vkey, "pT%d" % pi], writes=[bk(bo)], signal=(ci == n - 1))
                            else:
                                P.op("pe", I("matmul", PS(bo, 256, i * 256), lhsT=lv, rhs=pT[pi][:, i * 256:(i + 1) * 256],
                                             start=(ci == 0), stop=(ci == n - 1)),
                                     reads=[vkey, "pT%d" % pi], writes=[bk(bo)], signal=(ci == n - 1))
                    for ci, (pi, kind, vt) in enumerate(pts):
                        P.op("pe", I("matmul", PS(bl, 512), lhsT=onesb, rhs=pT[pi], start=(ci == 0), stop=(ci == n - 1)),
                             reads=["onesb", "pT%d" % pi], writes=[bk(bl)], signal=(ci == n - 1))
                    for i in range(2):
                        rl = rl_[i]
                        sc = esink[:, sink_cols[i]:sink_cols[i] + 1] if sink_cols is not None else zeroc
                        P.op("act", I("activation", out=rl, in_=PS(bl, 256, i * 256), func=AF.Ln, scale=1.0, bias=sc),
                             reads=[bk(bl), "esink", "epsc"], writes=["rl%d" % i])
                        P.op("act", I("activation", out=rl, in_=rl, func=AF.Exp, scale=-1.0), reads=["rl%d" % i], writes=["rl%d" % i])
                        P.op("dve", I("tensor_tensor", out=oT[:, mixer, pair_heads[i], q0:q0 + 256], in0=PS(bo, 256, i * 256), in1=rl, op=ALU.mult),
                             reads=[bk(bo), "rl%d" % i], writes=["oT"])
                return scores, finish, len(chunks)

            def run_items(items, hooks=()):
                prev = None
                prev_n = 0
                hook_at = {}
                for hi_, h in enumerate(hooks):
                    hook_at[(hi_ + 1) * len(items) // (len(hooks) + 1)] = h
                for ii_, (sc, fin, n) in enumerate(items):
                    if ii_ in hook_at:
                        hook_at[ii_]()
                    if prev is not None and prev_n + n > NPT:
                        prev()
                        prev = None
                    sc()
                    if prev is not None:
                        prev()
                    prev = fin
                    prev_n = n
                if prev is not None:
                    prev()

            def load_ctx(k_d, v_d, c0, ncols, nheads):
                P.dma("pool", kc_S[:, :, 0:ncols], k_d[:, c0:c0 + ncols].rearrange("(c p) n -> p c n", p=128), sem="kcS", writes=["kcS"])
                P.dma("pool", vc_S[:, :, 0:ncols], v_d[:, c0:c0 + ncols].rearrange("(c p) n -> p c n", p=128), sem="vcS", writes=["vcS"])
                for h in range(nheads):
                    b = rot_a2.next()
                    for c in range(2):
                        P.op("pe", I("transpose", PSB(b, 128, c * 128), kc_S[:, c, h * 128:(h + 1) * 128], identb),
                             reads=["kcS", "identb"], writes=[bk(b)], signal=(c == 1))
                    P.op("dve", I("tensor_copy", out=kc_T[:, h, :], in_=PSB(b, 256)), reads=[bk(b)], writes=["kcT"])

            Wq = small[:, 0:1]
            Wk = small[:, 1:2]
            Wqb = small[:, 2:3]
            Wkb = small[:, 3:4]
            set_mode("proj")
            load_ctx(cak_d, cav_d, 0, 256, 2)
            for h in range(2):
                proj_k(tkv, h, h, Wk, True, nak_o, h)
            proj_v(tkv, 256, 256, 0, nav_o, 0)
            ada_some(1)
            for rnd in range(2):
                set_mode("proj")
                tq = load_w16(w_in, rnd * 512)
                for cc in range(4):
                    proj_q(tq, cc, cc, Wq, True)
                drain()
                ada_some(1)
                set_mode("att")
                items = []
                for pr in range(2):
                    h0 = rnd * 4 + pr * 2
                    kvh = h0 // 4
                    for grp in range(3):
                        items.append(att_item([h0, h0 + 1], [kvh, kvh], [kvh * 128, kvh * 128], [pr * 2, pr * 2 + 1], 0,
                                              [h0, h0 + 1], lambda c: (maskA[:, c, :], "maskA"), grp))
                run_items(items, hooks=[lambda: ada_some(1), lambda: ada_some(1)])
            stop_at("oA", oT[:, 0], ["oT"])

            P.alias(["biasB0", "biasB1"], ["maskA", "cosT", "sinT"])
            bB_t = biasB_d.rearrange("(c p) n -> p c n", p=128)
            for rnd in range(2):
                set_mode("proj")
                tq = load_w16(w_in, 1536 + rnd * 512)
                load_ctx(cbk_d, cbv_d, rnd * 512, 512, 4)
                for cc in range(4):
                    proj_q(tq, cc, cc, Wqb, False)
                ada_some(1)
                tk = load_w16(w_in, 2560 + rnd * 512)
                for cc in range(4):
                    proj_k(tk, cc, cc, Wkb, False, nbk_o, rnd * 4 + cc)
                ada_some(1)
                tv = load_w16(w_in, 3584 + rnd * 512)
                proj_v(tv, 0, 512, 0, nbv_o, rnd * 512)
                drain()
                ada_some(1)
                set_mode("att")
                items = []
                for pr in range(2):
                    gp = rnd * 2 + pr
                    bi = gp % 2
                    P.dma("pool", biasB[bi], bB_t[:, :, gp * 512:(gp + 1) * 512], sem="biasB%d" % bi, writes=["biasB%d" % bi])
                    h0 = rnd * 4 + pr * 2
                    for grp in range(3):
                        items.append(att_item([h0, h0 + 1], [pr * 2, pr * 2 + 1], [pr * 256, pr * 256 + 128], [pr * 2, pr * 2 + 1], 1,
                                              None, lambda c, bi=bi: (biasB[bi][:, c, :], "biasB%d" % bi), grp))
                run_items(items, hooks=[lambda: ada_some(1), lambda: ada_some(1)])
            stop_at("oB", oT[:, 1], ["oT"])

            xkeys = ["x%d" % t for t in range(6)]
            newk = xkeys + ["sg0", "sg1", "tm0", "tm1", "t512_0", "t512_1", "onesf", "diag0", "diag1", "tA"]
            P.alias(newk, R1_ATT_KEYS)
            for t in range(6):
                P.dma("sp", xres[:, t, :], xm_t[t], sem="x%d" % t, writes=["x%d" % t])

            HALF = [(0, 384), (384, 384)]
            for cg in range(4):
                for mix in range(2):
                    wg = load_w16(w_in, (4608 if mix == 0 else 6656) + cg * 512)
                    wb = load_w8(w_bra if mix == 0 else w_brb, cg * 512)
                    for cc in range(4):
                        ch = cg * 4 + cc
                        for hi, (s0, sz) in enumerate(HALF):
                            bg = rot_main.next()
                            for k in range(KC):
                                P.op("pe", I("matmul", PS(bg, sz), lhsT=wg.k(k)[:, cc * 128:(cc + 1) * 128], rhs=hT[:, k, s0:s0 + sz],
                                             start=(k == 0), stop=(k == KC - 1)),
                                     reads=[wg.key(k), "hT%d" % k], writes=[bk(bg)], signal=(k == KC - 1 or k == 7))
                            by = (rot_a1 if hi == 0 else rot_a2).next()
                            for k in range(8):
                                P.op("pe", I("matmul", PS(by, sz), lhsT=wb.k(k)[:, cc * 128:(cc + 1) * 128], rhs=oT[:, mix, k, s0:s0 + sz],
                                             start=(k == 0), stop=(k == 7)),
                                     reads=[wb.key(k), "oT"], writes=[bk(by)], signal=(k == 7))
                            sg = sgs[hi][:, 0:sz]
                            P.op("act", I("activation", out=sg, in_=PS(bg, sz), func=AF.Sigmoid), reads=[bk(bg)], writes=["sg%d" % hi])
                            if mix == 0:
                                P.op("dve", I("tensor_tensor", out=tA[:, cc, s0:s0 + sz], in0=PS(by, sz), in1=sg, op=ALU.mult),
                                     reads=[bk(by), "sg%d" % hi], writes=["tA"])
                            else:
                                tm = tms[hi][:, 0:sz]
                                P.op("dve", I("tensor_tensor", out=tm, in0=PS(by, sz), in1=sg, op=ALU.mult),
                                     reads=[bk(by), "sg%d" % hi], writes=["tm%d" % hi])
                                P.op("dve", I("tensor_tensor", out=mTv[:, ch, s0:s0 + sz], in0=tm, in1=tA[:, cc, s0:s0 + sz], op=ALU.add),
                                     reads=["tm%d" % hi, "tA"], writes=["mT"])
            ada_finish()
            stop_at("mT", mTv, ["mT"])

            P.op("dve", I("memset", onesf, 1.0), writes=["onesf"])

            diag4 = [view(pb_scr + 11776 + i * 2048, [4, 128], F32) for i in range(2)]

            def build_G(mod_base, alias_from):
                P.alias(["G"], alias_from)
                di = 0
                for v in range(2):
                    for g4 in range(4):
                        b = rot_a1.next()
                        d4 = diag4[di % 2]
                        dk = "dg4_%d" % (di % 2)
                        di += 1
                        c0 = mod_base + g4 * 4
                        for j in range(4):
                            P.op("dve", I("tensor_scalar", out=d4[:, j, :], in0=identf, scalar1=modT[:, c0 + j, v:v + 1], scalar2=None, op0=ALU.mult),
                                 reads=["identf", "modT"], writes=[dk + "_%d" % j])
                        for j in range(4):
                            P.op("pe", I("matmul", PS(b, 128, j * 128), lhsT=onesf, rhs=d4[:, j, :], start=True, stop=True),
                                 reads=["onesf", dk + "_%d" % j], writes=[bk(b)], signal=(j == 3))
                        P.op("act", I("activation", out=Gt[:, v, g4 * 512:(g4 + 1) * 512], in_=PS(b, 512), func=AF.Copy),
                             reads=[bk(b)], writes=["G"])

            P.alias(["dg4_%d_%d" % (i, j) for i in range(2) for j in range(4)], ["tA"])
            build_G(32, ["oT"])

            for cg in range(4):
                wt = load_w16(w_out, cg * 512)
                for t in range(6):
                    v = 0 if t < 4 else 1
                    b = rot_main.next()
                    for k in range(KC):
                        P.op("pe", I("matmul", PS(b, 512), lhsT=mTv[:, k, t * 128:(t + 1) * 128], rhs=wt.k(k),
                                     start=(k == 0), stop=(k == KC - 1)),
                             reads=[wt.key(k), "mT"], writes=[bk(b)], signal=(k == KC - 1 or k == 7))
                    i = (cg * 6 + t) % 2
                    tp = tmp512[i]
                    P.op("dve", I("tensor_tensor", out=tp, in0=PS(b, 512), in1=Gt[:, v, cg * 512:(cg + 1) * 512], op=ALU.mult),
                         reads=[bk(b), "G"], writes=["t512_%d" % i])
                    P.op("dve", I("tensor_tensor", out=xres[:, t, cg * 512:(cg + 1) * 512], in0=xres[:, t, cg * 512:(cg + 1) * 512], in1=tp, op=ALU.add),
                         reads=["t512_%d" % i, "x%d" % t], writes=["x%d" % t])
            stop_at("x1", xres, ["x%d" % t for t in range(6)])

            xnk2 = ["xnn%d" % i for i in range(6)]
            P.alias(["h2T%d" % i for i in range(16)] + xnk2, ["hT%d" % i for i in range(16)] + ["oT"])
            xni = 0
            for tl, vec in [([0, 1, 2, 3], 0), ([4, 5], 1)]:
                tiles, xb, xk = [], [], []
                for ti in tl:
                    tiles.append((xres[:, ti, :], "x%d" % ti, None))
                    xb.append(xnn[xni % 6])
                    xk.append(xnk2[xni % 6])
                    xni += 1
                norm_group(tiles, None, vec, 2, 3, h2T, "h2T", xb, xk, tl[0] * 128, tl[0])
            stop_at("h2T", h2T, ["h2T%d" % i for i in range(16)])
            build_G(80, ["G"])

            P.alias(["uT"], xnk2)
            P.alias(["rs0", "rs1", "tb0", "tb1"], ["sg0", "sg1", "tm0", "tm1", "t512_0", "t512_1"])
            ri = 0
            for fg in range(4):
                for t4 in range(4):
                    wt = load_w16(w_up, fg * 2048 + t4 * 512)
                    for cc in range(4):
                        fc = t4 * 4 + cc
                        for (s0, sz) in HALF:
                            b = rot_main.next()
                            for k in range(KC):
                                P.op("pe", I("matmul", PS(b, sz), lhsT=wt.k(k)[:, cc * 128:(cc + 1) * 128], rhs=h2T[:, k, s0:s0 + sz],
                                             start=(k == 0), stop=(k == KC - 1)),
                                     reads=[wt.key(k), "h2T%d" % k], writes=[bk(b)], signal=(k == KC - 1 or k == 7))
                            i = ri % 2
                            ri += 1
                            rs = rsb[i][:, 0:sz]
                            P.op("act", I("activation", out=rs, in_=PS(b, sz), func=AF.Relu), reads=[bk(b)], writes=["rs%d" % i])
                            P.op("dve", I("tensor_tensor", out=uT[:, fc, s0:s0 + sz], in0=rs, in1=rs, op=ALU.mult), reads=["rs%d" % i], writes=["uT"])
                for cg in range(4):
                    wt = load_w16(w_down, cg * 512, r0=fg * 2048)
                    for t in range(6):
                        v = 0 if t < 4 else 1
                        b = rot_a1.next() if (t % 2 == 0) else rot_a2.next()
                        for k in range(KC):
                            P.op("pe", I("matmul", PS(b, 512), lhsT=uT[:, k, t * 128:(t + 1) * 128], rhs=wt.k(k),
                                         start=(k == 0), stop=(k == KC - 1)),
                                 reads=[wt.key(k), "uT"], writes=[bk(b)], signal=(k == KC - 1 or k == 7))
                        i = (cg * 6 + t) % 2
                        tp = tmp512b[i]
                        P.op("dve", I("tensor_tensor", out=tp, in0=PS(b, 512), in1=Gt[:, v, cg * 512:(cg + 1) * 512], op=ALU.mult),
                             reads=[bk(b), "G"], writes=["tb%d" % i])
                        P.op("dve", I("tensor_tensor", out=xres[:, t, cg * 512:(cg + 1) * 512], in0=xres[:, t, cg * 512:(cg + 1) * 512], in1=tp, op=ALU.add),
                             reads=["tb%d" % i, "x%d" % t], writes=["x%d" % t])

            ym_t = y_main.rearrange("(t p) d -> t p d", p=128)
            for t in range(6):
                P.dma("sp", ym_t[t], xres[:, t, :], sem="x%d" % t, reads=["x%d" % t])
            P.wait_all("sp", ["x%d" % t for t in range(6)] + ["ost0", "ost1"])

        try:
            body()
        except _Stop:
            pass
        fin = []
        for e_ in Prog.CE:
            if P.cnt[e_] > 0:
                fin.append((P.esem[e_], P.cnt[e_]))
        P.q["sp"].append((fin, None, None))

        with nc.Block() as block:
            @block.sync
            def _(e):
                P.emit("sp", e)

            @block.gpsimd
            def _(e):
                P.emit("pool", e)

            @block.tensor
            def _(e):
                P.emit("pe", e)

            @block.scalar
            def _(e):
                P.emit("act", e)

            @block.vector
            def _(e):
                P.emit("dve", e)
        build_program.stats = dict(n={k: len(v) for k, v in P.q.items()}, waits=P.n_wait, arena=A.off, cnt=dict(P.cnt))
    return nc


GRID_W = 64
ROWS = 16


def _core_geometry(j):
    b = j // 4
    qq = j % 4
    ws = 0 if qq < 2 else 4
    own_rows = list(range(4 * qq, 4 * qq + 4))
    halo_rows = [r for r in range(ws, ws + 12) if r not in own_rows]
    rows = own_rows + halo_rows
    pos = np.concatenate([np.arange(r * GRID_W, (r + 1) * GRID_W) for r in rows])
    return b, qq, pos


def _static_tables(j, rpb):
    b, qq, pos = _core_geometry(j)
    row = (pos // GRID_W).astype(np.int64)
    col = (pos % GRID_W).astype(np.int64)
    n_freq = 32
    inv = (10000.0 ** (-np.arange(n_freq, dtype=np.float32) / n_freq)).astype(np.float32)
    ang_r = row[:, None].astype(np.float32) * inv[None, :]
    ang_c = col[:, None].astype(np.float32) * inv[None, :]
    cosT = np.zeros((128, 768), np.float32)
    sinT = np.zeros((128, 768), np.float32)
    cosT[0:32] = np.cos(ang_r).T
    cosT[32:64] = np.cos(ang_r).T
    cosT[64:96] = np.cos(ang_c).T
    cosT[96:128] = np.cos(ang_c).T
    sinT[0:32] = -np.sin(ang_r).T
    sinT[32:64] = np.sin(ang_r).T
    sinT[64:96] = -np.sin(ang_c).T
    sinT[96:128] = np.sin(ang_c).T
    qpos = pos[:256]
    valid = np.abs(qpos[None, :] - pos[:, None]) <= 128
    mA = np.where(valid, 0.0, NEGM).astype(np.float32)
    maskA = np.concatenate([mA, mA], axis=1)
    qr = row[:256]
    qc = col[:256]
    rstart = np.clip(qr - 4, 0, ROWS - 8)
    cstart = np.clip(qc - 8, 0, GRID_W - 16)
    kr = row[:, None]
    kcc = col[:, None]
    vr = (kr >= rstart[None, :]) & (kr < rstart[None, :] + 8)
    vc = (kcc >= cstart[None, :]) & (kcc < cstart[None, :] + 16)
    valid = vr & vc
    dr = np.clip(kr - qr[None, :] + 7, 0, 14)
    dc = np.clip(kcc - qc[None, :] + 15, 0, 30)
    bias = rpb[:, dr, dc]
    bias = np.where(valid[None], bias, np.float32(NEGM)).astype(np.float32)
    biasB = np.ascontiguousarray(np.transpose(bias, (1, 0, 2))).reshape(768, 2048)
    return cosT, sinT, maskA, biasB


def _perm():
    p = np.zeros((128, 128), np.float32)
    for d in range(128):
        s = d + 32 if (d % 64) < 32 else d - 32
        p[s, d] = 1.0
    return p


_NC_CACHE = {}


def kernel(x_prompt, x_sample, cache_a_k, cache_a_v, cache_b_k, cache_b_v, c, c_ctx,
           norm1_w, norm2_w, w_ada, b_ada, w_in, q_norm_a, k_norm_a, q_norm_b, k_norm_b,
           sink_a, rpb_b, w_br_a, w_br_b, w_out, w_up, w_down, _debug=None, _cores=None):
    f = lambda a: np.ascontiguousarray(np.asarray(a, dtype=np.float32))
    x_prompt, x_sample = f(x_prompt), f(x_sample)
    c, c_ctx = f(c), f(c_ctx)
    cores = list(range(NCORES)) if _cores is None else _cores
    key = "dbg" if _debug is not None else "main"
    if key not in _NC_CACHE:
        _NC_CACHE[key] = build_program(_debug)
    nc = _NC_CACHE[key]

    ident = np.eye(128, dtype=np.float32)
    perm = _perm()
    shared = dict(
        ident=ident, perm=perm,
        w_ada=f(w_ada)[0], w_in=f(w_in)[0], w_br_a=f(w_br_a)[0], w_br_b=f(w_br_b)[0],
        w_out=f(w_out)[0], w_up=f(w_up)[0], w_down=f(w_down)[0],
    )
    rpb = f(rpb_b)[0]
    in_maps = []
    for idx_core, j in enumerate(cores):
        b, qq, pos = _core_geometry(j)
        xm = np.concatenate([x_prompt[2 * j], x_prompt[2 * j + 1], x_sample[b][pos[:256]]], axis=0)
        xh = x_sample[b][pos[256:]]
        cvec = np.stack([c_ctx, c[b]], axis=0)
        cT = np.ascontiguousarray(cvec.reshape(2, 16, 128).transpose(2, 1, 0)).reshape(128, 32)
        pp = np.zeros((128, PP_W), np.float32)
        pp[:, PP_N1:PP_N1 + 16] = f(norm1_w)[0].reshape(16, 128).T
        pp[:, PP_N2:PP_N2 + 16] = f(norm2_w)[0].reshape(16, 128).T
        pp[:, PP_BADA:PP_BADA + 96] = f(b_ada)[0].reshape(96, 128).T
        pp[:, PP_QNA] = f(q_norm_a)[0]
        pp[:, PP_KNA] = f(k_norm_a)[0]
        pp[:, PP_QNB] = f(q_norm_b)[0]
        pp[:, PP_KNB] = f(k_norm_b)[0]
        pp[:, PP_SINK:PP_SINK + 8] = f(sink_a)[0][None, :]
        cosT, sinT, maskA, biasB = _static_tables(j, rpb)
        m = dict(shared)
        m.update(
            x_main=np.ascontiguousarray(xm), x_halo=np.ascontiguousarray(xh), cT=cT, pp=pp,
            cosT=cosT, sinT=sinT, maskA=maskA, biasB=biasB,
            cak=f(cache_a_k)[b, 0].reshape(256, 256), cav=f(cache_a_v)[b, 0].reshape(256, 256),
            cbk=f(cache_b_k)[b, 0].reshape(256, 1024), cbv=f(cache_b_v)[b, 0].reshape(256, 1024),
        )
        in_maps.append(m)

    res = run_bass_kernel_spmd(nc, in_maps, core_ids=list(range(len(cores))))
    if _debug is not None:
        return res

    y_prompt = np.zeros((16, 256, D), np.float32)
    y_sample = np.zeros((2, 1024, D), np.float32)
    nak = np.zeros((16, 1, 256, 2, 128), np.float32)
    nav = np.zeros((16, 1, 256, 2, 128), np.float32)
    nbk = np.zeros((16, 1, 256, 8, 128), np.float32)
    nbv = np.zeros((16, 1, 256, 8, 128), np.float32)
    for idx, j in enumerate(cores):
        r = res.results[idx]
        b, qq, pos = _core_geometry(j)
        ym = r["y_main"]
        y_prompt[2 * j] = ym[0:256]
        y_prompt[2 * j + 1] = ym[256:512]
        y_sample[b][pos[:256]] = ym[512:768]
        for s in range(2):
            nak[2 * j + s, 0] = r["nak"][s * 256:(s + 1) * 256].reshape(256, 2, 128)
            nav[2 * j + s, 0] = r["nav"][s * 256:(s + 1) * 256].reshape(256, 2, 128)
            nbk[2 * j + s, 0] = r["nbk"][s * 256:(s + 1) * 256].reshape(256, 8, 128)
            nbv[2 * j + s, 0] = r["nbv"][s * 256:(s + 1) * 256].reshape(256, 8, 128)
    return (y_prompt, y_sample, nak, nav, nbk, nbv)
```

```python
import os
import numpy as np
import concourse.bass as bass
import concourse.mybir as mybir
from concourse.bass_utils import run_bass_kernel_spmd

F32 = mybir.dt.float32
BF16 = mybir.dt.bfloat16
ALU = mybir.AluOpType
AF = mybir.ActivationFunctionType
AX = mybir.AxisListType

NCORES = 8
D = 2048
KC = 16
NMAIN = 768
NHALO = 512
NTOK = NMAIN + NHALO
EPS = 1e-6
NEGM = -30000.0
IN_W = 8704
DFF = 8192
CH_P = (0, 512)
CH_S = (512, 256)
CH_H = (768, 512)

PP_N1 = 0
PP_N2 = 16
PP_BADA = 32
PP_QNA = 128
PP_KNA = 129
PP_QNB = 130
PP_KNB = 131
PP_SINK = 132
PP_SEL = 140
PP_W = 142


class Prog:
    CE = ("pe", "act", "dve", "pool")
    ALL = ("pe", "act", "dve", "pool", "sp")

    def __init__(self, nc, esems, dma_sems):
        self.nc = nc
        self.q = {e: [] for e in self.ALL}
        self.cnt = {e: 0 for e in self.CE}
        self.esem = esems
        self.free_dsems = list(dma_sems)
        self.dsem = {}
        self.seen = {e: {} for e in self.ALL}
        self.lastw = {}
        self.readers = {}
        self.n_wait = 0

    def _need(self, eng, ev, waits):
        if ev is None:
            return
        semkey, handle, val, _ = ev
        if self.seen[eng].get(semkey, 0) >= val:
            return
        self.seen[eng][semkey] = val
        waits.append((handle, val))
        self.n_wait += 1

    def _deps(self, eng, reads, writes, waits):
        for k in reads:
            ev = self.lastw.get(k)
            if ev is not None and not (ev[3] == "pe" and eng == "pe"):
                self._need(eng, ev, waits)
        for k in writes:
            ev = self.lastw.get(k)
            if ev is not None and not (ev[3] == "pe" and eng == "pe"):
                self._need(eng, ev, waits)
            for ev in self.readers.get(k, ()):
                if not (ev[3] == "pe" and eng == "pe"):
                    self._need(eng, ev, waits)

    def _commit(self, ev, reads, writes):
        for k in reads:
            lst = self.readers.setdefault(k, [])
            lst[:] = [e for e in lst if e[0] != ev[0]]
            lst.append(ev)
        for k in writes:
            self.lastw[k] = ev
            self.readers[k] = []

    def op(self, eng, fn, reads=(), writes=(), signal=True):
        waits = []
        self._deps(eng, reads, writes, waits)
        if signal:
            self.cnt[eng] += 1
            val = self.cnt[eng]
        else:
            val = self.cnt[eng] + 1
        ev = ("E" + eng, self.esem[eng], val, eng)
        self._commit(ev, reads, writes)
        self.q[eng].append((waits, fn, (self.esem[eng], 1) if signal else None))

    def dma(self, eng, out, in_, sem, reads=(), writes=(), **kw):
        waits = []
        self._deps(eng, reads, writes, waits)
        if sem not in self.dsem:
            self.dsem[sem] = [self.free_dsems.pop(), 0]
        rec = self.dsem[sem]
        rec[1] += 16
        ev = ("D" + sem, rec[0], rec[1], "dma")
        self._commit(ev, reads, writes)
        self.q[eng].append((waits, (lambda e, o=out, i=in_, k=kw: e.dma_start(out=o, in_=i, **k)), (rec[0], 16)))

    def custom(self, eng, fn, sem, inc, reads=(), writes=()):
        waits = []
        self._deps(eng, reads, writes, waits)
        if sem not in self.dsem:
            self.dsem[sem] = [self.free_dsems.pop(), 0]
        rec = self.dsem[sem]
        rec[1] += inc
        ev = ("D" + sem, rec[0], rec[1], "dma")
        self._commit(ev, reads, writes)
        self.q[eng].append((waits, fn, (rec[0], inc)))

    def wait_all(self, eng, keys):
        waits = []
        for k in keys:
            self._need(eng, self.lastw.get(k), waits)
            for ev in self.readers.get(k, ()):
                self._need(eng, ev, waits)
        if waits:
            self.q[eng].append((waits, None, None))

    def alias(self, new_keys, old_keys):
        evs = []
        for k in old_keys:
            if self.lastw.get(k) is not None:
                evs.append(self.lastw[k])
            evs.extend(self.readers.get(k, ()))
        for k in new_keys:
            self.lastw[k] = None
            self.readers[k] = list(evs)

    def emit(self, eng_name, eng):
        for waits, fn, inc in self.q[eng_name]:
            for h, v in waits:
                eng.wait_ge(h, v)
            if fn is None:
                continue
            ins = fn(eng)
            if inc is not None:
                ins.then_inc(inc[0], inc[1])


def I(method, *a, **k):
    return lambda e: getattr(e, method)(*a, **k)


class Rot:
    def __init__(self, items):
        self.items = list(items)
        self.i = 0

    def next(self):
        v = self.items[self.i % len(self.items)]
        self.i += 1
        return v


def build_program(debug=None):
    nc = bass.Bass("TRN2", target_bir_lowering=False)

    def din(name, shape, dt=F32):
        return nc.dram_tensor(name, list(shape), dt, kind="ExternalInput").ap()

    def dout(name, shape, dt=F32):
        return nc.dram_tensor(name, list(shape), dt, kind="ExternalOutput").ap()

    x_main = din("x_main", [NMAIN, D])
    x_halo = din("x_halo", [NHALO, D])
    cT_d = din("cT", [128, 32])
    pp_d = din("pp", [128, PP_W])
    ident_d = din("ident", [128, 128])
    perm_d = din("perm", [128, 128])
    cos_d = din("cosT", [128, NMAIN])
    sin_d = din("sinT", [128, NMAIN])
    maskA_d = din("maskA", [NMAIN, 512])
    biasB_d = din("biasB", [NMAIN, 2048])
    cak_d = din("cak", [256, 256])
    cav_d = din("cav", [256, 256])
    cbk_d = din("cbk", [256, 1024])
    cbv_d = din("cbv", [256, 1024])
    w_ada = din("w_ada", [D, 6 * D])
    w_in = din("w_in", [D, IN_W])
    w_bra = din("w_br_a", [1024, D])
    w_brb = din("w_br_b", [1024, D])
    w_out = din("w_out", [D, D])
    w_up = din("w_up", [D, DFF])
    w_down = din("w_down", [DFF, D])

    y_main = dout("y_main", [NMAIN, D])
    nak_o = dout("nak", [512, 256])
    nav_o = dout("nav", [512, 256])
    nbk_o = dout("nbk", [512, 1024])
    nbv_o = dout("nbv", [512, 1024])
    dbg_o = None
    if debug is not None:
        dbg_o = dout("dbg", debug["shape"], BF16 if debug.get("dtype") == "bf16" else F32)

    import contextlib
    es = contextlib.ExitStack()
    with es:
        ARENA_W = 53200
        arena = es.enter_context(nc.sbuf_tensor("arena", [128, ARENA_W], F32))
        ps = es.enter_context(nc.psum_tensor("ps", [128, 8, 512], F32))
        sem_names = ["pe", "act", "dve", "pool"]
        esems = {n: es.enter_context(nc.semaphore("s_" + n)) for n in sem_names}
        dsems = [es.enter_context(nc.semaphore("d%d" % i)) for i in range(70)]
        P = Prog(nc, esems, dsems)

        class Arena:
            def __init__(self):
                self.off = 0

            def take(self, nbytes):
                o = self.off
                self.off += (nbytes + 63) // 64 * 64
                assert self.off <= ARENA_W * 4, ("arena overflow", self.off)
                return o

        def view(off_b, shape, dt):
            esz = 2 if dt == BF16 else 4
            n = int(np.prod(shape))
            assert off_b % 4 == 0
            w0 = off_b // 4
            nw = (n * esz + 3) // 4
            ap = arena[:, w0:w0 + nw]
            if dt == BF16:
                ap = ap.bitcast(BF16)
            if len(shape) == 2:
                ap = ap.rearrange("p (a b) -> p a b", a=shape[0])
            elif len(shape) == 3:
                ap = ap.rearrange("p (a b c) -> p a b c", a=shape[0], b=shape[1])
            return ap

        A = Arena()
        RING_N = 6
        ring_off = [A.take(8192) for _ in range(RING_N)]
        o_pp = A.take(PP_W * 4)
        o_identf = A.take(512)
        o_identb = A.take(256)
        o_permf = A.take(512)
        o_onesb = A.take(256)
        o_modT = A.take(96 * 2 * 4)
        o_ab = A.take(4 * 2 * 16 * 4)
        o_small = A.take(256)
        o_esink = A.take(32)
        o_cT = A.take(128)
        o_scT = A.take(64)

        pp = view(o_pp, [PP_W], F32)
        identf = view(o_identf, [128], F32)
        identb = view(o_identb, [128], BF16)
        permf = view(o_permf, [128], F32)
        onesb = view(o_onesb, [128], BF16)
        modT = view(o_modT, [96, 2], F32)
        abv = view(o_ab, [4, 2, 16], F32)
        small = view(o_small, [64], F32)
        esink = view(o_esink, [8], F32)
        epsc = small[:, 20:21]
        epsc128 = small[:, 21:22]
        zeroc = small[:, 22:23]

        R1 = A.take(69632)
        R2 = A.take(65536)
        mT_off = A.take(KC * NMAIN * 2)
        mTv = view(mT_off, [KC, NMAIN], BF16)
        hT = view(R2, [KC, NTOK], BF16)
        oT = view(R2 + 40960, [2, 8, NMAIN], BF16)
        Gt = view(R2 + 65536 - 16384, [2, 2048], F32)
        h2T = view(R2, [KC, NMAIN], BF16)
        uT = view(R2 + 24576, [KC, NMAIN], BF16)
        xnn = [view(R2 + 24576 + i * 4096, [2048], BF16) for i in range(6)]
        xs_ = [view(R1 + i * 8192, [2048], F32) for i in range(3)]
        xn_ = [view(R1 + 24576 + i * 4096, [2048], BF16) for i in range(10)]
        o = R1
        qT = view(o, [4, NMAIN], BF16); o += 4 * NMAIN * 2
        kT = view(o, [4, NTOK], BF16); o += 4 * NTOK * 2
        vS = view(o, [10, 512], BF16); o += 10 * 512 * 2
        NPT = 10
        pT = [view(o + i * 1024, [512], BF16) for i in range(NPT)]; o += NPT * 1024
        m = o
        maskA = view(m, [6, 512], BF16)
        cosT = view(m + 6144, [NMAIN], F32)
        sinT = view(m + 9216, [NMAIN], F32)
        biasB = [view(m + i * 6144, [6, 512], BF16) for i in range(2)]
        kc_T = view(m + 12288, [4, 256], BF16)
        vc_S = view(m + 14336, [2, 512], BF16)
        kc_S = view(m + 16384, [2, 512], BF16)
        f32s = [view(m + 18432 + i * 2048, [512], F32) for i in range(3)]
        o += 24576
        ost = [view(o + i * 2048, [512], F32) for i in range(2)]; o += 2 * 2048
        sqb = [view(o + i * 1024, [512], BF16) for i in range(2)]; o += 2 * 1024
        rl_ = [view(o + i * 1024, [256], F32) for i in range(2)]; o += 2 * 1024
        assert o - R1 <= 69632, o - R1
        R1_ATT_KEYS = (["qT%d" % i for i in range(4)] + ["kT%d" % i for i in range(4)] + ["vS", "kcT", "vcS", "kcS", "f32s0", "f32s1", "f32s2", "ost0", "ost1", "sqb0", "sqb1",
                        "rl0", "rl1", "biasB0", "biasB1", "maskA", "cosT", "sinT"] + ["pT%d" % i for i in range(NPT)])
        xres = view(R1, [6, 2048], F32)
        pb_scr = R1 + 49152
        sgs = [view(pb_scr + i * 1536, [384], F32) for i in range(2)]
        tms = [view(pb_scr + 3072 + i * 1536, [384], F32) for i in range(2)]
        tmp512 = [view(pb_scr + 6144 + i * 2048, [512], F32) for i in range(2)]
        onesf = view(pb_scr + 10240, [128], F32)
        diag = [view(pb_scr + 10752 + i * 512, [128], F32) for i in range(2)]
        tA = view(pb_scr + 11776, [4, NMAIN], BF16)
        rsb = [view(pb_scr + i * 1024, [384], BF16) for i in range(2)]
        tmp512b = [view(pb_scr + 4096 + i * 2048, [512], F32) for i in range(2)]

        def PS(bank, n=512, off=0):
            return ps[:, bank, off:off + n]

        def PSB(bank, n, off=0):
            return ps[:, bank, :].bitcast(BF16)[:, off:off + n]

        rot_main = Rot([0, 1, 2, 3])
        rot_a1 = Rot([4, 5])
        rot_a2 = Rot([6, 7])

        def bk(b):
            return "ps%d" % b

        def set_mode(m):
            if m == "proj":
                rot_main.items, rot_a1.items, rot_a2.items = [0, 1, 2, 3, 4, 5], [6], [7]
            else:
                rot_main.items, rot_a1.items, rot_a2.items = [0, 1, 2, 3], [4, 5], [6, 7]

        ring_state = {"n": 0}

        class WT:
            def __init__(self, slots):
                self.slots = slots

            def k(self, k):
                return view(ring_off[self.slots[k // 8]], [8, 512], BF16)[:, k % 8, :]

            def key(self, k):
                return "ring%d" % self.slots[k // 8]

        def load_entry(src):
            n = ring_state["n"]
            ring_state["n"] += 1
            slot = n % RING_N
            key = "ring%d" % slot
            P.dma("pool", view(ring_off[slot], [8, 512], BF16), src, sem=key, writes=[key])
            return slot

        def wsrc(w, r0, nr, c0, ncol):
            return w[r0:r0 + nr, c0:c0 + ncol].rearrange("(k p) n -> p k n", p=128)

        def load_w16(w, c0, r0=0):
            n = ring_state["n"]
            slot = n % RING_N
            if slot % 2 == 0:
                ring_state["n"] += 2
                k0, k1 = "ring%d" % slot, "ring%d" % (slot + 1)
                dst = view(ring_off[slot], [16, 512], BF16)
                P.dma("pool", dst, wsrc(w, r0, 2048, c0, 512), sem=k0, writes=[k0, k1])
                return WT([slot, slot + 1])
            return WT([load_entry(wsrc(w, r0, 1024, c0, 512)), load_entry(wsrc(w, r0 + 1024, 1024, c0, 512))])

        def load_w8(w, c0):
            return WT([load_entry(wsrc(w, 0, 1024, c0, 512))])

        class _Stop(Exception):
            pass

        def stop_at(name, ap, keys):
            if debug is not None and debug["at"] == name:
                P.dma("sp", dbg_o, ap, sem="dbg", reads=keys)
                P.wait_all("sp", keys)
                raise _Stop()

        chains = []

        def tick():
            for ch in list(chains):
                step = ch.pop(0)
                step()
                if not ch:
                    chains.remove(ch)

        def drain():
            while chains:
                tick()

        def body():
            P.dma("sp", pp, pp_d, sem="pp", writes=["pp"])
            P.dma("sp", identf, ident_d, sem="identf", writes=["identf"])
            P.dma("sp", permf, perm_d, sem="permf", writes=["permf"])
            P.dma("pool", identb, ident_d, sem="identb", writes=["identb"])
            P.op("dve", I("memset", onesb, 1.0), writes=["onesb"])
            P.op("dve", I("memset", epsc, EPS), writes=["epsc"])
            P.op("dve", I("memset", epsc128, 128.0 * EPS), writes=["epsc"])
            P.op("dve", I("memset", zeroc, 0.0), writes=["epsc"])
            cTf = view(o_cT, [32], F32)
            scTflat = view(o_scT, [32], BF16)
            scTb = scTflat.rearrange("p (k v) -> p k v", v=2)
            P.dma("sp", cTf, cT_d, sem="cT", writes=["cTf"])
            P.op("act", I("activation", out=scTflat, in_=cTf, func=AF.Silu), reads=["cTf"], writes=["scT"])
            P.op("act", I("activation", out=esink, in_=pp[:, PP_SINK:PP_SINK + 8], func=AF.Exp), reads=["pp"], writes=["esink"])

            def norm_group(tiles, src_fn, vec, kindA, kindB, dstT, dst_key, xbufs, xkeys, tok0, stat0):
                norm_stats(tiles, xbufs, xkeys, stat0)
                norm_tr(len(tiles), vec, kindA, kindB, dstT, dst_key, xbufs, xkeys, tok0)

            def norm_stats(tiles, xbufs, xkeys, stat0):
                for i, (xin, xkey, loader) in enumerate(tiles):
                    if loader is not None:
                        loader()
                    xnb, xnk = xbufs[i], xkeys[i]
                    si = (stat0 + i) % 10
                    ssq = small[:, 24 + si:25 + si]
                    rst = small[:, 36 + si:37 + si]
                    sk = "ssq%d" % si
                    P.op("act", I("activation", out=xnb, in_=xin, func=AF.Square, accum_out=ssq), reads=[xkey], writes=[xnk, sk])
                    P.op("act", I("activation", out=rst, in_=ssq, func=AF.Sqrt, scale=1.0 / D, bias=epsc), reads=[sk, "epsc"], writes=[sk + "r"])
                    P.op("dve", I("reciprocal", out=rst, in_=rst), reads=[sk + "r"], writes=[sk + "r"])
                    P.op("dve", I("tensor_scalar", out=xnb, in0=xin, scalar1=rst, scalar2=None, op0=ALU.mult), reads=[xkey, sk + "r"], writes=[xnk])

            def norm_tr(nt, vec, kindA, kindB, dstT, dst_key, xbufs, xkeys, tok0, chunks=None):
                for c in (range(KC) if chunks is None else chunks):
                    b = rot_main.next()
                    for i in range(nt):
                        P.op("pe", I("transpose", PSB(b, 128, i * 128), xbufs[i][:, c * 128:(c + 1) * 128], identb),
                             reads=[xkeys[i], "identb"], writes=[bk(b)], signal=(i == nt - 1))
                    dst = dstT[:, c, tok0:tok0 + nt * 128]
                    if c % 2 == 0:
                        P.op("act", I("activation", out=dst, in_=PSB(b, nt * 128), func=AF.Identity,
                                      scale=abv[:, kindA, vec, c:c + 1], bias=abv[:, kindB, vec, c:c + 1]),
                             reads=[bk(b), "ab", "ab2"], writes=[dst_key + str(c)])
                    else:
                        P.op("dve", I("tensor_scalar", out=dst, in0=PSB(b, nt * 128), scalar1=abv[:, kindA, vec, c:c + 1],
                                      scalar2=abv[:, kindB, vec, c:c + 1], op0=ALU.mult, op1=ALU.add),
                             reads=[bk(b), "ab", "ab2"], writes=[dst_key + str(c)])

            xm_t = x_main.rearrange("(t p) d -> t p d", p=128)
            xh_t = x_halo.rearrange("(t p) d -> t p d", p=128)
            norm1_groups = []
            xi = 0
            for tl, vec in [([0, 1, 2, 3], 0), ([4, 5], 1), ([6, 7, 8, 9], 1)]:
                tiles, xb, xk = [], [], []
                for ti in tl:
                    bi = xi % 3
                    xi += 1
                    src = xm_t[ti] if ti < 6 else xh_t[ti - 6]
                    ld = (lambda bi=bi, src=src: P.dma("sp", xs_[bi], src, sem="xs%d" % bi, writes=["xs%d" % bi]))
                    tiles.append((xs_[bi], "xs%d" % bi, ld))
                    xb.append(xn_[ti])
                    xk.append("xn%d" % ti)
                norm_stats(tiles, xb, xk, tl[0])
                norm1_groups.append((tl, vec, xb, xk))

            ada_pending = [0, 4, 1, 5, 2, 6, 3, 7] + list(range(8, 24))

            def ada_some(n):
                for _ in range(min(n, len(ada_pending))):
                    t = ada_pending.pop(0)
                    wt = load_w16(w_ada, t * 512)
                    b = rot_a2.next()
                    for cc in range(4):
                        for k in range(KC):
                            P.op("pe", I("matmul", ps[:, b, cc * 2:cc * 2 + 2], lhsT=wt.k(k)[:, cc * 128:(cc + 1) * 128],
                                         rhs=scTb[:, k, :], start=(k == 0), stop=(k == KC - 1)),
                                 reads=[wt.key(k), "scT"], writes=[bk(b)], signal=(k == KC - 1 or k == 7))
                    for v in range(2):
                        P.op("dve", I("tensor_tensor", out=modT[:, t * 4:(t + 1) * 4, v],
                                      in0=ps[:, b, 0:8].rearrange("p (c v) -> p c v", v=2)[:, :, v],
                                      in1=pp[:, PP_BADA + t * 4:PP_BADA + t * 4 + 4], op=ALU.add),
                             reads=[bk(b), "pp"], writes=["modT"])

            for qi in range(4):
                ada_some(2)
                c4 = slice(4 * qi, 4 * qi + 4)
                for v in range(2):
                    P.op("dve", I("scalar_tensor_tensor", out=abv[:, 0, v, c4], in0=modT[:, 16 + 4 * qi:20 + 4 * qi, v], scalar=1.0,
                                  in1=pp[:, PP_N1 + 4 * qi:PP_N1 + 4 * qi + 4], op0=ALU.add, op1=ALU.mult), reads=["modT", "pp"], writes=["ab"])
                    P.op("dve", I("tensor_copy", out=abv[:, 1, v, c4], in_=modT[:, 4 * qi:4 * qi + 4, v]), reads=["modT"], writes=["ab"])
                for (tl, vec, xb, xk) in norm1_groups:
                    norm_tr(len(tl), vec, 0, 1, hT, "hT", xb, xk, tl[0] * 128, chunks=range(4 * qi, 4 * qi + 4))
            SQ128 = float(np.sqrt(128.0))
            P.op("dve", I("tensor_copy", out=small[:, 0:1], in_=pp[:, PP_QNA:PP_QNA + 1]), reads=["pp"], writes=["small"])
            P.op("dve", I("tensor_scalar", out=small[:, 1:2], in0=pp[:, PP_KNA:PP_KNA + 1], scalar1=SQ128, scalar2=None, op0=ALU.mult), reads=["pp"], writes=["small"])
            P.op("dve", I("tensor_copy", out=small[:, 2:3], in_=pp[:, PP_QNB:PP_QNB + 1]), reads=["pp"], writes=["small"])
            P.op("dve", I("tensor_scalar", out=small[:, 3:4], in0=pp[:, PP_KNB:PP_KNB + 1], scalar1=SQ128, scalar2=None, op0=ALU.mult), reads=["pp"], writes=["small"])

            def ada_finish():
                ada_some(len(ada_pending))
                for v in range(2):
                    P.op("dve", I("scalar_tensor_tensor", out=abv[:, 2, v, :], in0=modT[:, 64:80, v], scalar=1.0,
                                  in1=pp[:, PP_N2:PP_N2 + 16], op0=ALU.add, op1=ALU.mult), reads=["modT", "pp"], writes=["ab2"])
                    P.op("dve", I("tensor_copy", out=abv[:, 3, v, :], in_=modT[:, 48:64, v]), reads=["modT"], writes=["ab2"])
            stop_at("ada", modT, ["modT", "ab", "small"])

            tkv = load_w16(w_in, 1024)
            stop_at("hT", hT, ["hT%d" % i for i in range(16)])

            P.alias(R1_ATT_KEYS, ["xs0", "xs1", "xs2"] + ["xn%d" % i for i in range(10)])
            P.dma("pool", maskA, maskA_d.rearrange("(c p) n -> p c n", p=128), sem="maskA", writes=["maskA"])
            P.dma("sp", cosT, cos_d, sem="cosT", writes=["cosT"])
            P.dma("sp", sinT, sin_d, sem="sinT", writes=["sinT"])

            cstate = {"i": 0, "ost": 0}

            def proj_fm(wt, cc, chunks, make_chain):
                for (s0, sz) in chunks:
                    b = rot_main.next()
                    for k in range(KC):
                        P.op("pe", I("matmul", PS(b, sz), lhsT=wt.k(k)[:, cc * 128:(cc + 1) * 128], rhs=hT[:, k, s0:s0 + sz],
                                     start=(k == 0), stop=(k == KC - 1)),
                             reads=[wt.key(k), "hT%d" % k], writes=[bk(b)], signal=(k == KC - 1 or k == 7))
                    tick()
                    make_chain(b, s0, sz)

            def norm_steps(b, sz, wcol, out_ap, out_key):
                i = cstate["i"] % 2
                cstate["i"] += 1
                sq = sqb[i][:, 0:sz]
                sqk = "sqb%d" % i
                P.op("act", I("activation", out=sq, in_=PS(b, sz), func=AF.Square), reads=[bk(b)], writes=[sqk])

                def step1():
                    b2 = rot_a1.next()
                    P.op("pe", I("matmul", PS(b2, sz), lhsT=onesb, rhs=sq, start=True, stop=True), reads=[sqk, "onesb"], writes=[bk(b2)])
                    rr = f32s[2][:, 0:sz]
                    P.op("act", I("activation", out=rr, in_=PS(b2, sz), func=AF.Ln, scale=1.0, bias=epsc128), reads=[bk(b2), "epsc"], writes=["f32s2"])
                    P.op("act", I("activation", out=rr, in_=rr, func=AF.Exp, scale=-0.5), reads=["f32s2"], writes=["f32s2"])
                    P.op("dve", I("scalar_tensor_tensor", out=out_ap, in0=PS(b, sz), scalar=wcol, in1=rr, op0=ALU.mult, op1=ALU.mult),
                         reads=[bk(b), "f32s2", "small"], writes=[out_key])
                return step1

            def rope_step(src_ap, src_key, e0, sz, out_ap, out_key):
                def step():
                    b3 = rot_a2.next()
                    P.op("pe", I("matmul", PS(b3, sz), lhsT=permf, rhs=src_ap, start=True, stop=True), reads=[src_key, "permf"], writes=[bk(b3)])
                    t1 = f32s[2][:, 0:sz]
                    P.op("dve", I("tensor_tensor", out=t1, in0=PS(b3, sz), in1=sinT[:, e0:e0 + sz], op=ALU.mult), reads=[bk(b3), "sinT"], writes=["f32s2"])
                    P.op("dve", I("tensor_tensor", out=src_ap, in0=src_ap, in1=cosT[:, e0:e0 + sz], op=ALU.mult), reads=[src_key, "cosT"], writes=[src_key])
                    P.op("dve", I("tensor_tensor", out=out_ap, in0=src_ap, in1=t1, op=ALU.add), reads=[src_key, "f32s2"], writes=[out_key])
                return step

            def kout_step(kn_ap, kn_key, dst_dram, head_col, kt_dst, kt_key):
                def step():
                    b4 = rot_a2.next()
                    for t in range(4):
                        P.op("pe", I("transpose", PS(b4, 128, t * 128), kn_ap[:, t * 128:(t + 1) * 128], identf),
                             reads=[kn_key, "identf"], writes=[bk(b4)], signal=(t == 3))
                    i = cstate["ost"] % 2
                    cstate["ost"] += 1
                    st = ost[i]
                    P.op("act", I("activation", out=st, in_=PS(b4, 512), func=AF.Copy), reads=[bk(b4)], writes=["ost%d" % i])
                    P.dma("sp", dst_dram.rearrange("(t p) n -> p t n", p=128)[:, :, head_col * 128:(head_col + 1) * 128],
                          st.rearrange("p (t d) -> p t d", t=4), sem="ost%d" % i, reads=["ost%d" % i])
                    P.op("act", I("activation", out=kt_dst, in_=kn_ap, func=AF.Copy), reads=[kn_key], writes=[kt_key])
                return step

            tmp_i = [0]

            def _nop():
                pass

            def proj_q(wt, cc, hslot, wcol, do_rope):
                def mk(b, s0, sz):
                    dst = qT[:, hslot, s0:s0 + sz]
                    if do_rope and s0 == CH_S[0]:
                        fi = tmp_i[0] % 2
                        tmp_i[0] += 1
                        tmp = f32s[fi][:, 0:sz]
                        chains.append([_nop, norm_steps(b, sz, wcol, tmp, "f32s%d" % fi), _nop, rope_step(tmp, "f32s%d" % fi, 0, sz, dst, "qT%d" % hslot)])
                    else:
                        chains.append([_nop, norm_steps(b, sz, wcol, dst, "qT%d" % hslot)])
                proj_fm(wt, cc, [CH_P, CH_S], mk)

            def proj_k(wt, cc, hslot, wcol, do_rope, dst_dram, head_col):
                def mk(b, s0, sz):
                    fi = tmp_i[0] % 2
                    tmp_i[0] += 1
                    tmp = f32s[fi][:, 0:sz]
                    tk = "f32s%d" % fi
                    dst = kT[:, hslot, s0:s0 + sz]
                    st1 = norm_steps(b, sz, wcol, tmp, tk)
                    if s0 == 0:
                        chains.append([_nop, st1, _nop, kout_step(tmp, tk, dst_dram, head_col, dst, "kT%d" % hslot)])
                    elif do_rope:
                        chains.append([_nop, st1, _nop, rope_step(tmp, tk, s0 - 512, sz, dst, "kT%d" % hslot)])
                    else:
                        def cp(tmp=tmp, tk=tk, dst=dst):
                            P.op("act", I("activation", out=dst, in_=tmp, func=AF.Copy), reads=[tk], writes=["kT%d" % hslot])
                        chains.append([_nop, st1, _nop, cp])
                proj_fm(wt, cc, [CH_P, CH_S, CH_H], mk)

            def proj_v(wt, c0, ncols, vcol0, dst_dram, dcol0):
                for ti in range(10):
                    b = rot_main.next()
                    for k in range(KC):
                        P.op("pe", I("matmul", PS(b, ncols), lhsT=hT[:, k, ti * 128:(ti + 1) * 128], rhs=wt.k(k)[:, c0:c0 + ncols],
                                     start=(k == 0), stop=(k == KC - 1)),
                             reads=[wt.key(k), "hT%d" % k], writes=[bk(b)], signal=(k == KC - 1 or k == 7))
                    tick()
                    if ti < 4:
                        i = cstate["ost"] % 2
                        cstate["ost"] += 1
                        st = ost[i][:, 0:ncols]
                        P.op("act", I("activation", out=st, in_=PS(b, ncols), func=AF.Copy), reads=[bk(b)], writes=["ost%d" % i])
                        P.op("dve", I("tensor_copy", out=vS[:, ti, vcol0:vcol0 + ncols], in_=st), reads=["ost%d" % i], writes=["vS"])
                        P.dma("sp", dst_dram[ti * 128:(ti + 1) * 128, dcol0:dcol0 + ncols], st, sem="ost%d" % i, reads=["ost%d" % i])
                    else:
                        P.op("dve", I("tensor_copy", out=vS[:, ti, vcol0:vcol0 + ncols], in_=PS(b, ncols)), reads=[bk(b)], writes=["vS"])

            pt_i = [0]

            def att_item(pair_heads, kslots, vcols, hs_slots, mixer, sink_cols, sample_bias, grp):
                same_kv = (kslots[0] == kslots[1]) and (vcols[0] == vcols[1])
                q0 = grp * 256
                if grp < 2:
                    chunks = [("tok", grp * 256 + c * 128, grp * 2 + c, None) for c in range(2)]
                else:
                    chunks = [("tok", 512 + c * 128, 4 + c, c) for c in range(6)] + [("ctx", c * 128, c, None) for c in range(2)]
                pts = []

                def scores():
                    for (kind, k0, vt, bc) in chunks:
                        b = rot_main.next()
                        first = True
                        if bc is not None:
                            bap, bkey = sample_bias(bc)
                            P.op("pe", I("matmul", PS(b, 512), lhsT=identb, rhs=bap, start=True, stop=False),
                                 reads=[bkey, "identb"], writes=[bk(b)], signal=False)
                            first = False
                        if same_kv:
                            lk, lkey = (kT[:, kslots[0], k0:k0 + 128], "kT%d" % kslots[0]) if kind == "tok" else (kc_T[:, kslots[0], k0:k0 + 128], "kcT")
                            P.op("pe", I("matmul", PS(b, 512).rearrange("p (a q) -> p a q", a=2), lhsT=lk,
                                         rhs=qT[:, hs_slots[0]:hs_slots[0] + 2, q0:q0 + 256], start=first, stop=True),
                                 reads=[lkey, "qT%d" % hs_slots[0], "qT%d" % (hs_slots[0] + 1)], writes=[bk(b)])
                        else:
                            for i in range(2):
                                lk, lkey = (kT[:, kslots[i], k0:k0 + 128], "kT%d" % kslots[i]) if kind == "tok" else (kc_T[:, kslots[i], k0:k0 + 128], "kcT")
                                P.op("pe", I("matmul", PS(b, 256, i * 256), lhsT=lk, rhs=qT[:, hs_slots[i], q0:q0 + 256],
                                             start=first, stop=(first or i == 1)),
                                     reads=[lkey, "qT%d" % hs_slots[i]], writes=[bk(b)], signal=(i == 1))
                        pi = pt_i[0] % NPT
                        pt_i[0] += 1
                        P.op("act", I("activation", out=pT[pi], in_=PS(b, 512), func=AF.Exp), reads=[bk(b)], writes=["pT%d" % pi])
                        pts.append((pi, kind, vt))

                def finish():
                    bo = rot_a1.next()
                    bl = rot_a2.next()
                    n = len(pts)
                    for i in range(2):
                        if same_kv and i == 1:
                            break
                        for ci, (pi, kind, vt) in enumerate(pts):
                            lv, vkey = (vS[:, vt, vcols[i]:vcols[i] + 128], "vS") if kind == "tok" else (vc_S[:, vt, vcols[i]:vcols[i] + 128], "vcS")
                            if same_kv:
                                P.op("pe", I("matmul", PS(bo, 512), lhsT=lv, rhs=pT[pi], start=(ci == 0), stop=(ci == n - 1)),
                                     reads=[vkey, "pT%d" % pi], writes=[bk(bo)], signal=(ci == n - 1))
                            else:
                                P.op("pe", I("matmul", PS(bo, 256, i * 256), lhsT=lv, rhs=pT[pi][:, i * 256:(i + 1) * 256],
                                             start=(ci == 0), stop=(ci == n - 1)),
                                     reads=[vkey, "pT%d" % pi], writes=[bk(bo)], signal=(ci == n - 1))
                    for ci, (pi, kind, vt) in enumerate(pts):
                        P.op("pe", I("matmul", PS(bl, 512), lhsT=onesb, rhs=pT[pi], start=(ci == 0), stop=(ci == n - 1)),
                             reads=["onesb", "pT%d" % pi], writes=[bk(bl)], signal=(ci == n - 1))
                    for i in range(2):
                        rl = rl_[i]
                        sc = esink[:, sink_cols[i]:sink_cols[i] + 1] if sink_cols is not None else zeroc
                        P.op("act", I("activation", out=rl, in_=PS(bl, 256, i * 256), func=AF.Ln, scale=1.0, bias=sc),
                             reads=[bk(bl), "esink", "epsc"], writes=["rl%d" % i])
                        P.op("act", I("activation", out=rl, in_=rl, func=AF.Exp, scale=-1.0), reads=["rl%d" % i], writes=["rl%d" % i])
                        P.op("dve", I("tensor_tensor", out=oT[:, mixer, pair_heads[i], q0:q0 + 256], in0=PS(bo, 256, i * 256), in1=rl, op=ALU.mult),
                             reads=[bk(bo), "rl%d" % i], writes=["oT"])
                return scores, finish, len(chunks)

            def run_items(items, hooks=(), depth=2):
                pend = []
                hook_at = {}
                for hi_, h in enumerate(hooks):
                    hook_at[(hi_ + 1) * len(items) // (len(hooks) + 1)] = h
                for ii_, (sc, fin, n) in enumerate(items):
                    if ii_ in hook_at:
                        hook_at[ii_]()
                    while pend and sum(p[1] for p in pend) + n > NPT:
                        pend.pop(0)[0]()
                    sc()
                    pend.append((fin, n))
                    while len(pend) > depth:
                        pend.pop(0)[0]()
                while pend:
                    pend.pop(0)[0]()

            def load_ctx(k_d, v_d, c0, ncols, nheads):
                P.dma("pool", kc_S[:, :, 0:ncols], k_d[:, c0:c0 + ncols].rearrange("(c p) n -> p c n", p=128), sem="kcS", writes=["kcS"])
                P.dma("pool", vc_S[:, :, 0:ncols], v_d[:, c0:c0 + ncols].rearrange("(c p) n -> p c n", p=128), sem="vcS", writes=["vcS"])
                for h in range(nheads):
                    b = rot_a2.next()
                    for c in range(2):
                        P.op("pe", I("transpose", PSB(b, 128, c * 128), kc_S[:, c, h * 128:(h + 1) * 128], identb),
                             reads=["kcS", "identb"], writes=[bk(b)], signal=(c == 1))
                    P.op("dve", I("tensor_copy", out=kc_T[:, h, :], in_=PSB(b, 256)), reads=[bk(b)], writes=["kcT"])

            Wq = small[:, 0:1]
            Wk = small[:, 1:2]
            Wqb = small[:, 2:3]
            Wkb = small[:, 3:4]
            set_mode("proj")
            load_ctx(cak_d, cav_d, 0, 256, 2)
            for h in range(2):
                proj_k(tkv, h, h, Wk, True, nak_o, h)
            proj_v(tkv, 256, 256, 0, nav_o, 0)
            ada_some(1)
            for rnd in range(2):
                set_mode("proj")
                tq = load_w16(w_in, rnd * 512)
                for cc in range(4):
                    proj_q(tq, cc, cc, Wq, True)
                drain()
                ada_some(1)
                set_mode("att")
                items = []
                for pr in range(2):
                    h0 = rnd * 4 + pr * 2
                    kvh = h0 // 4
                    for grp in range(3):
                        items.append(att_item([h0, h0 + 1], [kvh, kvh], [kvh * 128, kvh * 128], [pr * 2, pr * 2 + 1], 0,
                                              [h0, h0 + 1], lambda c: (maskA[:, c, :], "maskA"), grp))
                run_items(items, hooks=[lambda: ada_some(1), lambda: ada_some(1)])
            stop_at("oA", oT[:, 0], ["oT"])

            P.alias(["biasB0", "biasB1"], ["maskA", "cosT", "sinT"])
            bB_t = biasB_d.rearrange("(c p) n -> p c n", p=128)
            for rnd in range(2):
                set_mode("proj")
                tq = load_w16(w_in, 1536 + rnd * 512)
                load_ctx(cbk_d, cbv_d, rnd * 512, 512, 4)
                for cc in range(4):
                    proj_q(tq, cc, cc, Wqb, False)
                ada_some(1)
                tk = load_w16(w_in, 2560 + rnd * 512)
                for cc in range(4):
                    proj_k(tk, cc, cc, Wkb, False, nbk_o, rnd * 4 + cc)
                ada_some(1)
                tv = load_w16(w_in, 3584 + rnd * 512)
                proj_v(tv, 0, 512, 0, nbv_o, rnd * 512)
                drain()
                ada_some(1)
                set_mode("att")
                items = []
                for pr in range(2):
                    gp = rnd * 2 + pr
                    bi = gp % 2
                    P.dma("pool", biasB[bi], bB_t[:, :, gp * 512:(gp + 1) * 512], sem="biasB%d" % bi, writes=["biasB%d" % bi])
                    h0 = rnd * 4 + pr * 2
                    for grp in range(3):
                        items.append(att_item([h0, h0 + 1], [pr * 2, pr * 2 + 1], [pr * 256, pr * 256 + 128], [pr * 2, pr * 2 + 1], 1,
                                              None, lambda c, bi=bi: (biasB[bi][:, c, :], "biasB%d" % bi), grp))
                run_items(items, hooks=[lambda: ada_some(1), lambda: ada_some(1)])
            stop_at("oB", oT[:, 1], ["oT"])

            xkeys = ["x%d" % t for t in range(6)]
            newk = xkeys + ["sg0", "sg1", "tm0", "tm1", "t512_0", "t512_1", "onesf", "diag0", "diag1", "tA"]
            P.alias(newk, R1_ATT_KEYS)
            for t in range(6):
                P.dma("sp", xres[:, t, :], xm_t[t], sem="x%d" % t, writes=["x%d" % t])

            HALF = [(0, 384), (384, 384)]
            for cg in range(4):
                for mix in range(2):
                    wg = load_w16(w_in, (4608 if mix == 0 else 6656) + cg * 512)
                    wb = load_w8(w_bra if mix == 0 else w_brb, cg * 512)
                    for cc in range(4):
                        ch = cg * 4 + cc
                        for hi, (s0, sz) in enumerate(HALF):
                            bg = rot_main.next()
                            for k in range(KC):
                                P.op("pe", I("matmul", PS(bg, sz), lhsT=wg.k(k)[:, cc * 128:(cc + 1) * 128], rhs=hT[:, k, s0:s0 + sz],
                                             start=(k == 0), stop=(k == KC - 1)),
                                     reads=[wg.key(k), "hT%d" % k], writes=[bk(bg)], signal=(k == KC - 1 or k == 7))
                            by = (rot_a1 if hi == 0 else rot_a2).next()
                            for k in range(8):
                                P.op("pe", I("matmul", PS(by, sz), lhsT=wb.k(k)[:, cc * 128:(cc + 1) * 128], rhs=oT[:, mix, k, s0:s0 + sz],
                                             start=(k == 0), stop=(k == 7)),
                                     reads=[wb.key(k), "oT"], writes=[bk(by)], signal=(k == 7))
                            sg = sgs[hi][:, 0:sz]
                            P.op("act", I("activation", out=sg, in_=PS(bg, sz), func=AF.Sigmoid), reads=[bk(bg)], writes=["sg%d" % hi])
                            if mix == 0:
                                P.op("dve", I("tensor_tensor", out=tA[:, cc, s0:s0 + sz], in0=PS(by, sz), in1=sg, op=ALU.mult),
                                     reads=[bk(by), "sg%d" % hi], writes=["tA"])
                            else:
                                tm = tms[hi][:, 0:sz]
                                P.op("dve", I("tensor_tensor", out=tm, in0=PS(by, sz), in1=sg, op=ALU.mult),
                                     reads=[bk(by), "sg%d" % hi], writes=["tm%d" % hi])
                                P.op("dve", I("tensor_tensor", out=mTv[:, ch, s0:s0 + sz], in0=tm, in1=tA[:, cc, s0:s0 + sz], op=ALU.add),
                                     reads=["tm%d" % hi, "tA"], writes=["mT"])
            ada_finish()
            stop_at("mT", mTv, ["mT"])

            P.op("dve", I("memset", onesf, 1.0), writes=["onesf"])

            diag4 = [view(pb_scr + 11776 + i * 2048, [4, 128], F32) for i in range(2)]

            def build_G(mod_base, alias_from):
                P.alias(["G"], alias_from)
                di = 0
                for v in range(2):
                    for g4 in range(4):
                        b = rot_a1.next()
                        d4 = diag4[di % 2]
                        dk = "dg4_%d" % (di % 2)
                        di += 1
                        c0 = mod_base + g4 * 4
                        for j in range(4):
                            P.op("dve", I("tensor_scalar", out=d4[:, j, :], in0=identf, scalar1=modT[:, c0 + j, v:v + 1], scalar2=None, op0=ALU.mult),
                                 reads=["identf", "modT"], writes=[dk + "_%d" % j])
                        for j in range(4):
                            P.op("pe", I("matmul", PS(b, 128, j * 128), lhsT=onesf, rhs=d4[:, j, :], start=True, stop=True),
                                 reads=["onesf", dk + "_%d" % j], writes=[bk(b)], signal=(j == 3))
                        P.op("act", I("activation", out=Gt[:, v, g4 * 512:(g4 + 1) * 512], in_=PS(b, 512), func=AF.Copy),
                             reads=[bk(b)], writes=["G"])

            P.alias(["dg4_%d_%d" % (i, j) for i in range(2) for j in range(4)], ["tA"])
            build_G(32, ["oT"])

            for cg in range(4):
                wt = load_w16(w_out, cg * 512)
                for t in range(6):
                    v = 0 if t < 4 else 1
                    b = rot_main.next()
                    for k in range(KC):
                        P.op("pe", I("matmul", PS(b, 512), lhsT=mTv[:, k, t * 128:(t + 1) * 128], rhs=wt.k(k),
                                     start=(k == 0), stop=(k == KC - 1)),
                             reads=[wt.key(k), "mT"], writes=[bk(b)], signal=(k == KC - 1 or k == 7))
                    i = (cg * 6 + t) % 2
                    tp = tmp512[i]
                    P.op("dve", I("tensor_tensor", out=tp, in0=PS(b, 512), in1=Gt[:, v, cg * 512:(cg + 1) * 512], op=ALU.mult),
                         reads=[bk(b), "G"], writes=["t512_%d" % i])
                    P.op("dve", I("tensor_tensor", out=xres[:, t, cg * 512:(cg + 1) * 512], in0=xres[:, t, cg * 512:(cg + 1) * 512], in1=tp, op=ALU.add),
                         reads=["t512_%d" % i, "x%d" % t], writes=["x%d" % t])
            stop_at("x1", xres, ["x%d" % t for t in range(6)])

            xnk2 = ["xnn%d" % i for i in range(6)]
            P.alias(["h2T%d" % i for i in range(16)] + xnk2, ["hT%d" % i for i in range(16)] + ["oT"])
            xni = 0
            for tl, vec in [([0, 1, 2, 3], 0), ([4, 5], 1)]:
                tiles, xb, xk = [], [], []
                for ti in tl:
                    tiles.append((xres[:, ti, :], "x%d" % ti, None))
                    xb.append(xnn[xni % 6])
                    xk.append(xnk2[xni % 6])
                    xni += 1
                norm_group(tiles, None, vec, 2, 3, h2T, "h2T", xb, xk, tl[0] * 128, tl[0])
            stop_at("h2T", h2T, ["h2T%d" % i for i in range(16)])
            build_G(80, ["G"])

            P.alias(["uT"], xnk2)
            P.alias(["rs0", "rs1", "tb0", "tb1"], ["sg0", "sg1", "tm0", "tm1", "t512_0", "t512_1"])
            ri = 0
            for fg in range(4):
                for t4 in range(4):
                    wt = load_w16(w_up, fg * 2048 + t4 * 512)
                    for cc in range(4):
                        fc = t4 * 4 + cc
                        for (s0, sz) in HALF:
                            b = rot_main.next()
                            for k in range(KC):
                                P.op("pe", I("matmul", PS(b, sz), lhsT=wt.k(k)[:, cc * 128:(cc + 1) * 128], rhs=h2T[:, k, s0:s0 + sz],
                                             start=(k == 0), stop=(k == KC - 1)),
                                     reads=[wt.key(k), "h2T%d" % k], writes=[bk(b)], signal=(k == KC - 1 or k == 7))
                            i = ri % 2
                            ri += 1
                            rs = rsb[i][:, 0:sz]
                            P.op("act", I("activation", out=rs, in_=PS(b, sz), func=AF.Relu), reads=[bk(b)], writes=["rs%d" % i])
                            P.op("dve", I("tensor_tensor", out=uT[:, fc, s0:s0 + sz], in0=rs, in1=rs, op=ALU.mult), reads=["rs%d" % i], writes=["uT"])
                for cg in range(4):
                    wt = load_w16(w_down, cg * 512, r0=fg * 2048)
                    for t in range(6):
                        v = 0 if t < 4 else 1
                        b = rot_a1.next() if (t % 2 == 0) else rot_a2.next()
                        for k in range(KC):
                            P.op("pe", I("matmul", PS(b, 512), lhsT=uT[:, k, t * 128:(t + 1) * 128], rhs=wt.k(k),
                                         start=(k == 0), stop=(k == KC - 1)),
                                 reads=[wt.key(k), "uT"], writes=[bk(b)], signal=(k == KC - 1 or k == 7))
                        i = (cg * 6 + t) % 2
                        tp = tmp512b[i]
                        P.op("dve", I("tensor_tensor", out=tp, in0=PS(b, 512), in1=Gt[:, v, cg * 512:(cg + 1) * 512], op=ALU.mult),
                             reads=[bk(b), "G"], writes=["tb%d" % i])
                        P.op("dve", I("tensor_tensor", out=xres[:, t, cg * 512:(cg + 1) * 512], in0=xres[:, t, cg * 512:(cg + 1) * 512], in1=tp, op=ALU.add),
                             reads=["tb%d" % i, "x%d" % t], writes=["x%d" % t])

            ym_t = y_main.rearrange("(t p) d -> t p d", p=128)
            for t in range(6):
                P.dma("sp", ym_t[t], xres[:, t, :], sem="x%d" % t, reads=["x%d" % t])
            P.wait_all("sp", ["x%d" % t for t in range(6)] + ["ost0", "ost1"])

        try:
            body()
        except _Stop:
            pass
        fin = []
        for e_ in Prog.CE:
            if P.cnt[e_] > 0:
                fin.append((P.esem[e_], P.cnt[e_]))
        P.q["sp"].append((fin, None, None))

        with nc.Block() as block:
            @block.sync
            def _(e):
                P.emit("sp", e)

            @block.gpsimd
            def _(e):
                P.emit("pool", e)

            @block.tensor
            def _(e):
                P.emit("pe", e)

            @block.scalar
            def _(e):
                P.emit("act", e)

            @block.vector
            def _(e):
                P.emit("dve", e)
        build_program.stats = dict(n={k: len(v) for k, v in P.q.items()}, waits=P.n_wait, arena=A.off, cnt=dict(P.cnt))
    return nc


GRID_W = 64
ROWS = 16


def _core_geometry(j):
    b = j // 4
    qq = j % 4
    ws = 0 if qq < 2 else 4
    own_rows = list(range(4 * qq, 4 * qq + 4))
    halo_rows = [r for r in range(ws, ws + 12) if r not in own_rows]
    rows = own_rows + halo_rows
    pos = np.concatenate([np.arange(r * GRID_W, (r + 1) * GRID_W) for r in rows])
    return b, qq, pos


def _static_tables(j, rpb):
    b, qq, pos = _core_geometry(j)
    row = (pos // GRID_W).astype(np.int64)
    col = (pos % GRID_W).astype(np.int64)
    n_freq = 32
    inv = (10000.0 ** (-np.arange(n_freq, dtype=np.float32) / n_freq)).astype(np.float32)
    ang_r = row[:, None].astype(np.float32) * inv[None, :]
    ang_c = col[:, None].astype(np.float32) * inv[None, :]
    cosT = np.zeros((128, 768), np.float32)
    sinT = np.zeros((128, 768), np.float32)
    cosT[0:32] = np.cos(ang_r).T
    cosT[32:64] = np.cos(ang_r).T
    cosT[64:96] = np.cos(ang_c).T
    cosT[96:128] = np.cos(ang_c).T
    sinT[0:32] = -np.sin(ang_r).T
    sinT[32:64] = np.sin(ang_r).T
    sinT[64:96] = -np.sin(ang_c).T
    sinT[96:128] = np.sin(ang_c).T
    qpos = pos[:256]
    valid = np.abs(qpos[None, :] - pos[:, None]) <= 128
    mA = np.where(valid, 0.0, NEGM).astype(np.float32)
    maskA = np.concatenate([mA, mA], axis=1)
    qr = row[:256]
    qc = col[:256]
    rstart = np.clip(qr - 4, 0, ROWS - 8)
    cstart = np.clip(qc - 8, 0, GRID_W - 16)
    kr = row[:, None]
    kcc = col[:, None]
    vr = (kr >= rstart[None, :]) & (kr < rstart[None, :] + 8)
    vc = (kcc >= cstart[None, :]) & (kcc < cstart[None, :] + 16)
    valid = vr & vc
    dr = np.clip(kr - qr[None, :] + 7, 0, 14)
    dc = np.clip(kcc - qc[None, :] + 15, 0, 30)
    bias = rpb[:, dr, dc]
    bias = np.where(valid[None], bias, np.float32(NEGM)).astype(np.float32)
    biasB = np.ascontiguousarray(np.transpose(bias, (1, 0, 2))).reshape(768, 2048)
    return cosT, sinT, maskA, biasB


def _perm():
    p = np.zeros((128, 128), np.float32)
    for d in range(128):
        s = d + 32 if (d % 64) < 32 else d - 32
        p[s, d] = 1.0
    return p


_NC_CACHE = {}


def kernel(x_prompt, x_sample, cache_a_k, cache_a_v, cache_b_k, cache_b_v, c, c_ctx,
           norm1_w, norm2_w, w_ada, b_ada, w_in, q_norm_a, k_norm_a, q_norm_b, k_norm_b,
           sink_a, rpb_b, w_br_a, w_br_b, w_out, w_up, w_down, _debug=None, _cores=None):
    f = lambda a: np.ascontiguousarray(np.asarray(a, dtype=np.float32))
    x_prompt, x_sample = f(x_prompt), f(x_sample)
    c, c_ctx = f(c), f(c_ctx)
    cores = list(range(NCORES)) if _cores is None else _cores
    key = "dbg" if _debug is not None else "main"
    if key not in _NC_CACHE:
        _NC_CACHE[key] = build_program(_debug)
    nc = _NC_CACHE[key]

    ident = np.eye(128, dtype=np.float32)
    perm = _perm()
    shared = dict(
        ident=ident, perm=perm,
        w_ada=f(w_ada)[0], w_in=f(w_in)[0], w_br_a=f(w_br_a)[0], w_br_b=f(w_br_b)[0],
        w_out=f(w_out)[0], w_up=f(w_up)[0], w_down=f(w_down)[0],
    )
    rpb = f(rpb_b)[0]
    in_maps = []
    for idx_core, j in enumerate(cores):
        b, qq, pos = _core_geometry(j)
        xm = np.concatenate([x_prompt[2 * j], x_prompt[2 * j + 1], x_sample[b][pos[:256]]], axis=0)
        xh = x_sample[b][pos[256:]]
        cvec = np.stack([c_ctx, c[b]], axis=0)
        cT = np.ascontiguousarray(cvec.reshape(2, 16, 128).transpose(2, 1, 0)).reshape(128, 32)
        pp = np.zeros((128, PP_W), np.float32)
        pp[:, PP_N1:PP_N1 + 16] = f(norm1_w)[0].reshape(16, 128).T
        pp[:, PP_N2:PP_N2 + 16] = f(norm2_w)[0].reshape(16, 128).T
        pp[:, PP_BADA:PP_BADA + 96] = f(b_ada)[0].reshape(96, 128).T
        pp[:, PP_QNA] = f(q_norm_a)[0]
        pp[:, PP_KNA] = f(k_norm_a)[0]
        pp[:, PP_QNB] = f(q_norm_b)[0]
        pp[:, PP_KNB] = f(k_norm_b)[0]
        pp[:, PP_SINK:PP_SINK + 8] = f(sink_a)[0][None, :]
        cosT, sinT, maskA, biasB = _static_tables(j, rpb)
        m = dict(shared)
        m.update(
            x_main=np.ascontiguousarray(xm), x_halo=np.ascontiguousarray(xh), cT=cT, pp=pp,
            cosT=cosT, sinT=sinT, maskA=maskA, biasB=biasB,
            cak=f(cache_a_k)[b, 0].reshape(256, 256), cav=f(cache_a_v)[b, 0].reshape(256, 256),
            cbk=f(cache_b_k)[b, 0].reshape(256, 1024), cbv=f(cache_b_v)[b, 0].reshape(256, 1024),
        )
        in_maps.append(m)

    res = run_bass_kernel_spmd(nc, in_maps, core_ids=list(range(len(cores))))
    if _debug is not None:
        return res

    y_prompt = np.zeros((16, 256, D), np.float32)
    y_sample = np.zeros((2, 1024, D), np.float32)
    nak = np.zeros((16, 1, 256, 2, 128), np.float32)
    nav = np.zeros((16, 1, 256, 2, 128), np.float32)
    nbk = np.zeros((16, 1, 256, 8, 128), np.float32)
    nbv = np.zeros((16, 1, 256, 8, 128), np.float32)
    for idx, j in enumerate(cores):
        r = res.results[idx]
        b, qq, pos = _core_geometry(j)
        ym = r["y_main"]
        y_prompt[2 * j] = ym[0:256]
        y_prompt[2 * j + 1] = ym[256:512]
        y_sample[b][pos[:256]] = ym[512:768]
        for s in range(2):
            nak[2 * j + s, 0] = r["nak"][s * 256:(s + 1) * 256].reshape(256, 2, 128)
            nav[2 * j + s, 0] = r["nav"][s * 256:(s + 1) * 256].reshape(256, 2, 128)
            nbk[2 * j + s, 0] = r["nbk"][s * 256:(s + 1) * 256].reshape(256, 8, 128)
            nbv[2 * j + s, 0] = r["nbv"][s * 256:(s + 1) * 256].reshape(256, 8, 128)
    return (y_prompt, y_sample, nak, nav, nbk, nbv)
```

```python
import os
import numpy as np
import concourse.bass as bass
import concourse.mybir as mybir
from concourse.bass_utils import run_bass_kernel_spmd

F32 = mybir.dt.float32
BF16 = mybir.dt.bfloat16
ALU = mybir.AluOpType
AF = mybir.ActivationFunctionType
AX = mybir.AxisListType

NCORES = 8
D = 2048
KC = 16
NMAIN = 768
NHALO = 512
NTOK = NMAIN + NHALO
EPS = 1e-6
NEGM = -30000.0
IN_W = 8704
DFF = 8192
CH_P = (0, 512)
CH_S = (512, 256)
CH_H = (768, 512)

PP_N1 = 0
PP_N2 = 16
PP_BADA = 32
PP_QNA = 128
PP_KNA = 129
PP_QNB = 130
PP_KNB = 131
PP_SINK = 132
PP_SEL = 140
PP_W = 142


class Prog:
    CE = ("pe", "act", "dve", "pool")
    ALL = ("pe", "act", "dve", "pool", "sp")

    def __init__(self, nc, esems, dma_sems):
        self.nc = nc
        self.q = {e: [] for e in self.ALL}
        self.cnt = {e: 0 for e in self.CE}
        self.esem = esems
        self.free_dsems = list(dma_sems)
        self.dsem = {}
        self.seen = {e: {} for e in self.ALL}
        self.lastw = {}
        self.readers = {}
        self.n_wait = 0

    def _need(self, eng, ev, waits):
        if ev is None:
            return
        semkey, handle, val, _ = ev
        if self.seen[eng].get(semkey, 0) >= val:
            return
        self.seen[eng][semkey] = val
        waits.append((handle, val))
        self.n_wait += 1

    def _deps(self, eng, reads, writes, waits):
        for k in reads:
            ev = self.lastw.get(k)
            if ev is not None and not (ev[3] == "pe" and eng == "pe"):
                self._need(eng, ev, waits)
        for k in writes:
            ev = self.lastw.get(k)
            if ev is not None and not (ev[3] == "pe" and eng == "pe"):
                self._need(eng, ev, waits)
            for ev in self.readers.get(k, ()):
                if not (ev[3] == "pe" and eng == "pe"):
                    self._need(eng, ev, waits)

    def _commit(self, ev, reads, writes):
        for k in reads:
            lst = self.readers.setdefault(k, [])
            lst[:] = [e for e in lst if e[0] != ev[0]]
            lst.append(ev)
        for k in writes:
            self.lastw[k] = ev
            self.readers[k] = []

    def op(self, eng, fn, reads=(), writes=(), signal=True):
        waits = []
        self._deps(eng, reads, writes, waits)
        if signal:
            self.cnt[eng] += 1
            val = self.cnt[eng]
        else:
            val = self.cnt[eng] + 1
        ev = ("E" + eng, self.esem[eng], val, eng)
        self._commit(ev, reads, writes)
        self.q[eng].append((waits, fn, (self.esem[eng], 1) if signal else None))

    def dma(self, eng, out, in_, sem, reads=(), writes=(), **kw):
        waits = []
        self._deps(eng, reads, writes, waits)
        if sem not in self.dsem:
            self.dsem[sem] = [self.free_dsems.pop(), 0]
        rec = self.dsem[sem]
        rec[1] += 16
        ev = ("D" + sem, rec[0], rec[1], "dma")
        self._commit(ev, reads, writes)
        self.q[eng].append((waits, (lambda e, o=out, i=in_, k=kw: e.dma_start(out=o, in_=i, **k)), (rec[0], 16)))

    def custom(self, eng, fn, sem, inc, reads=(), writes=()):
        waits = []
        self._deps(eng, reads, writes, waits)
        if sem not in self.dsem:
            self.dsem[sem] = [self.free_dsems.pop(), 0]
        rec = self.dsem[sem]
        rec[1] += inc
        ev = ("D" + sem, rec[0], rec[1], "dma")
        self._commit(ev, reads, writes)
        self.q[eng].append((waits, fn, (rec[0], inc)))

    def wait_all(self, eng, keys):
        waits = []
        for k in keys:
            self._need(eng, self.lastw.get(k), waits)
            for ev in self.readers.get(k, ()):
                self._need(eng, ev, waits)
        if waits:
            self.q[eng].append((waits, None, None))

    def alias(self, new_keys, old_keys):
        evs = []
        for k in old_keys:
            if self.lastw.get(k) is not None:
                evs.append(self.lastw[k])
            evs.extend(self.readers.get(k, ()))
        for k in new_keys:
            self.lastw[k] = None
            self.readers[k] = list(evs)

    def emit(self, eng_name, eng):
        for waits, fn, inc in self.q[eng_name]:
            for h, v in waits:
                eng.wait_ge(h, v)
            if fn is None:
                continue
            ins = fn(eng)
            if inc is not None:
                ins.then_inc(inc[0], inc[1])


def I(method, *a, **k):
    return lambda e: getattr(e, method)(*a, **k)


class Rot:
    def __init__(self, items):
        self.items = list(items)
        self.i = 0

    def next(self):
        v = self.items[self.i % len(self.items)]
        self.i += 1
        return v


def build_program(debug=None):
    nc = bass.Bass("TRN2", target_bir_lowering=False)

    def din(name, shape, dt=F32):
        return nc.dram_tensor(name, list(shape), dt, kind="ExternalInput").ap()

    def dout(name, shape, dt=F32):
        return nc.dram_tensor(name, list(shape), dt, kind="ExternalOutput").ap()

    x_main = din("x_main", [NMAIN, D])
    x_halo = din("x_halo", [NHALO, D])
    cT_d = din("cT", [128, 32])
    pp_d = din("pp", [128, PP_W])
    ident_d = din("ident", [128, 128])
    perm_d = din("perm", [128, 128])
    cos_d = din("cosT", [128, NMAIN])
    sin_d = din("sinT", [128, NMAIN])
    maskA_d = din("maskA", [NMAIN, 512])
    biasB_d = din("biasB", [NMAIN, 2048])
    cak_d = din("cak", [256, 256])
    cav_d = din("cav", [256, 256])
    cbk_d = din("cbk", [256, 1024])
    cbv_d = din("cbv", [256, 1024])
    w_ada = din("w_ada", [D, 6 * D])
    w_in = din("w_in", [D, IN_W])
    w_bra = din("w_br_a", [1024, D])
    w_brb = din("w_br_b", [1024, D])
    w_out = din("w_out", [D, D])
    w_up = din("w_up", [D, DFF])
    w_down = din("w_down", [DFF, D])

    y_main = dout("y_main", [NMAIN, D])
    nak_o = dout("nak", [512, 256])
    nav_o = dout("nav", [512, 256])
    nbk_o = dout("nbk", [512, 1024])
    nbv_o = dout("nbv", [512, 1024])
    dbg_o = None
    if debug is not None:
        dbg_o = dout("dbg", debug["shape"], BF16 if debug.get("dtype") == "bf16" else F32)

    import contextlib
    es = contextlib.ExitStack()
    with es:
        ARENA_W = 53200
        arena = es.enter_context(nc.sbuf_tensor("arena", [128, ARENA_W], F32))
        ps = es.enter_context(nc.psum_tensor("ps", [128, 8, 512], F32))
        sem_names = ["pe", "act", "dve", "pool"]
        esems = {n: es.enter_context(nc.semaphore("s_" + n)) for n in sem_names}
        dsems = [es.enter_context(nc.semaphore("d%d" % i)) for i in range(70)]
        P = Prog(nc, esems, dsems)

        class Arena:
            def __init__(self):
                self.off = 0

            def take(self, nbytes):
                o = self.off
                self.off += (nbytes + 63) // 64 * 64
                assert self.off <= ARENA_W * 4, ("arena overflow", self.off)
                return o

        def view(off_b, shape, dt):
            esz = 2 if dt == BF16 else 4
            n = int(np.prod(shape))
            assert off_b % 4 == 0
            w0 = off_b // 4
            nw = (n * esz + 3) // 4
            ap = arena[:, w0:w0 + nw]
            if dt == BF16:
                ap = ap.bitcast(BF16)
            if len(shape) == 2:
                ap = ap.rearrange("p (a b) -> p a b", a=shape[0])
            elif len(shape) == 3:
                ap = ap.rearrange("p (a b c) -> p a b c", a=shape[0], b=shape[1])
            return ap

        A = Arena()
        RING_N = 6
        ring_off = [A.take(8192) for _ in range(RING_N)]
        o_pp = A.take(PP_W * 4)
        o_identf = A.take(512)
        o_identb = A.take(256)
        o_permf = A.take(512)
        o_onesb = A.take(256)
        o_modT = A.take(96 * 2 * 4)
        o_ab = A.take(4 * 2 * 16 * 4)
        o_small = A.take(256)
        o_esink = A.take(32)
        o_cT = A.take(128)
        o_scT = A.take(64)

        pp = view(o_pp, [PP_W], F32)
        identf = view(o_identf, [128], F32)
        identb = view(o_identb, [128], BF16)
        permf = view(o_permf, [128], F32)
        onesb = view(o_onesb, [128], BF16)
        modT = view(o_modT, [96, 2], F32)
        abv = view(o_ab, [4, 2, 16], F32)
        small = view(o_small, [64], F32)
        esink = view(o_esink, [8], F32)
        epsc = small[:, 20:21]
        epsc128 = small[:, 21:22]
        zeroc = small[:, 22:23]

        R1 = A.take(69632)
        R2 = A.take(65536)
        mT_off = A.take(KC * NMAIN * 2)
        mTv = view(mT_off, [KC, NMAIN], BF16)
        hT = view(R2, [KC, NTOK], BF16)
        oT = view(R2 + 40960, [2, 8, NMAIN], BF16)
        Gt = view(R2 + 65536 - 16384, [2, 2048], F32)
        h2T = view(R2, [KC, NMAIN], BF16)
        uT = view(R2 + 24576, [KC, NMAIN], BF16)
        xnn = [view(R2 + 24576 + i * 4096, [2048], BF16) for i in range(6)]
        xs_ = [view(R1 + i * 8192, [2048], F32) for i in range(3)]
        xn_ = [view(R1 + 24576 + i * 4096, [2048], BF16) for i in range(10)]
        o = R1
        qT = view(o, [4, NMAIN], BF16); o += 4 * NMAIN * 2
        kT = view(o, [4, NTOK], BF16); o += 4 * NTOK * 2
        vS = view(o, [10, 512], BF16); o += 10 * 512 * 2
        NPT = 10
        pT = [view(o + i * 1024, [512], BF16) for i in range(NPT)]; o += NPT * 1024
        m = o
        maskA = view(m, [6, 512], BF16)
        cosT = view(m + 6144, [NMAIN], F32)
        sinT = view(m + 9216, [NMAIN], F32)
        biasB = [view(m + i * 6144, [6, 512], BF16) for i in range(2)]
        kc_T = view(m + 12288, [4, 256], BF16)
        vc_S = view(m + 14336, [2, 512], BF16)
        kc_S = view(m + 16384, [2, 512], BF16)
        f32s = [view(m + 18432 + i * 2048, [512], F32) for i in range(3)]
        o += 24576
        ost = [view(o + i * 2048, [512], F32) for i in range(2)]; o += 2 * 2048
        sqb = [view(o + i * 1024, [512], BF16) for i in range(2)]; o += 2 * 1024
        rl_ = [view(o + i * 1024, [256], F32) for i in range(2)]; o += 2 * 1024
        assert o - R1 <= 69632, o - R1
        R1_ATT_KEYS = (["qT%d" % i for i in range(4)] + ["kT%d" % i for i in range(4)] + ["vS", "kcT", "vcS", "kcS", "f32s0", "f32s1", "f32s2", "ost0", "ost1", "sqb0", "sqb1",
                        "rl0", "rl1", "biasB0", "biasB1", "maskA", "cosT", "sinT"] + ["pT%d" % i for i in range(NPT)])
        xres = view(R1, [6, 2048], F32)
        pb_scr = R1 + 49152
        sgs = [view(pb_scr + i * 1536, [384], F32) for i in range(2)]
        tms = [view(pb_scr + 3072 + i * 1536, [384], F32) for i in range(2)]
        tmp512 = [view(pb_scr + 6144 + i * 2048, [512], F32) for i in range(2)]
        onesf = view(pb_scr + 10240, [128], F32)
        diag = [view(pb_scr + 10752 + i * 512, [128], F32) for i in range(2)]
        tA = view(pb_scr + 11776, [4, NMAIN], BF16)
        rsb = [view(pb_scr + i * 1024, [384], BF16) for i in range(2)]
        tmp512b = [view(pb_scr + 4096 + i * 2048, [512], F32) for i in range(2)]

        def PS(bank, n=512, off=0):
            return ps[:, bank, off:off + n]

        def PSB(bank, n, off=0):
            return ps[:, bank, :].bitcast(BF16)[:, off:off + n]

        rot_main = Rot([0, 1, 2, 3])
        rot_a1 = Rot([4, 5])
        rot_a2 = Rot([6, 7])

        def bk(b):
            return "ps%d" % b

        def set_mode(m):
            if m == "proj":
                rot_main.items, rot_a1.items, rot_a2.items = [0, 1, 2, 3, 4, 5], [6], [7]
            else:
                rot_main.items, rot_a1.items, rot_a2.items = [0, 1, 2, 3], [4, 5], [6, 7]

        ring_state = {"n": 0}

        class WT:
            def __init__(self, slots):
                self.slots = slots

            def k(self, k):
                return view(ring_off[self.slots[k // 8]], [8, 512], BF16)[:, k % 8, :]

            def key(self, k):
                return "ring%d" % self.slots[k // 8]

        def load_entry(src):
            n = ring_state["n"]
            ring_state["n"] += 1
            slot = n % RING_N
            key = "ring%d" % slot
            P.dma("pool", view(ring_off[slot], [8, 512], BF16), src, sem=key, writes=[key])
            return slot

        def wsrc(w, r0, nr, c0, ncol):
            return w[r0:r0 + nr, c0:c0 + ncol].rearrange("(k p) n -> p k n", p=128)

        def load_w16(w, c0, r0=0):
            n = ring_state["n"]
            slot = n % RING_N
            if slot % 2 == 0:
                ring_state["n"] += 2
                k0, k1 = "ring%d" % slot, "ring%d" % (slot + 1)
                dst = view(ring_off[slot], [16, 512], BF16)
                P.dma("pool", dst, wsrc(w, r0, 2048, c0, 512), sem=k0, writes=[k0, k1])
                return WT([slot, slot + 1])
            return WT([load_entry(wsrc(w, r0, 1024, c0, 512)), load_entry(wsrc(w, r0 + 1024, 1024, c0, 512))])

        def load_w8(w, c0):
            return WT([load_entry(wsrc(w, 0, 1024, c0, 512))])

        class _Stop(Exception):
            pass

        def stop_at(name, ap, keys):
            if debug is not None and debug["at"] == name:
                P.dma("sp", dbg_o, ap, sem="dbg", reads=keys)
                P.wait_all("sp", keys)
                raise _Stop()

        chains = []

        def tick():
            for ch in list(chains):
                step = ch.pop(0)
                step()
                if not ch:
                    chains.remove(ch)

        def drain():
            while chains:
                tick()

        def body():
            P.dma("sp", pp, pp_d, sem="pp", writes=["pp"])
            P.dma("sp", identf, ident_d, sem="identf", writes=["identf"])
            P.dma("sp", permf, perm_d, sem="permf", writes=["permf"])
            P.dma("pool", identb, ident_d, sem="identb", writes=["identb"])
            P.op("dve", I("memset", onesb, 1.0), writes=["onesb"])
            P.op("dve", I("memset", epsc, EPS), writes=["epsc"])
            P.op("dve", I("memset", epsc128, 128.0 * EPS), writes=["epsc"])
            P.op("dve", I("memset", zeroc, 0.0), writes=["epsc"])
            cTf = view(o_cT, [32], F32)
            scTflat = view(o_scT, [32], BF16)
            scTb = scTflat.rearrange("p (k v) -> p k v", v=2)
            P.dma("sp", cTf, cT_d, sem="cT", writes=["cTf"])
            P.op("act", I("activation", out=scTflat, in_=cTf, func=AF.Silu), reads=["cTf"], writes=["scT"])
            P.op("act", I("activation", out=esink, in_=pp[:, PP_SINK:PP_SINK + 8], func=AF.Exp), reads=["pp"], writes=["esink"])

            def norm_group(tiles, src_fn, vec, kindA, kindB, dstT, dst_key, xbufs, xkeys, tok0, stat0):
                norm_stats(tiles, xbufs, xkeys, stat0)
                norm_tr(len(tiles), vec, kindA, kindB, dstT, dst_key, xbufs, xkeys, tok0)

            def norm_stats(tiles, xbufs, xkeys, stat0):
                for i, (xin, xkey, loader) in enumerate(tiles):
                    if loader is not None:
                        loader()
                    xnb, xnk = xbufs[i], xkeys[i]
                    si = (stat0 + i) % 10
                    ssq = small[:, 24 + si:25 + si]
                    rst = small[:, 36 + si:37 + si]
                    sk = "ssq%d" % si
                    P.op("act", I("activation", out=xnb, in_=xin, func=AF.Square, accum_out=ssq), reads=[xkey], writes=[xnk, sk])
                    P.op("act", I("activation", out=rst, in_=ssq, func=AF.Sqrt, scale=1.0 / D, bias=epsc), reads=[sk, "epsc"], writes=[sk + "r"])
                    P.op("dve", I("reciprocal", out=rst, in_=rst), reads=[sk + "r"], writes=[sk + "r"])
                    P.op("dve", I("tensor_scalar", out=xnb, in0=xin, scalar1=rst, scalar2=None, op0=ALU.mult), reads=[xkey, sk + "r"], writes=[xnk])

            def norm_tr(nt, vec, kindA, kindB, dstT, dst_key, xbufs, xkeys, tok0, chunks=None):
                for c in (range(KC) if chunks is None else chunks):
                    b = rot_main.next()
                    for i in range(nt):
                        P.op("pe", I("transpose", PSB(b, 128, i * 128), xbufs[i][:, c * 128:(c + 1) * 128], identb),
                             reads=[xkeys[i], "identb"], writes=[bk(b)], signal=(i == nt - 1))
                    dst = dstT[:, c, tok0:tok0 + nt * 128]
                    if c % 2 == 0:
                        P.op("act", I("activation", out=dst, in_=PSB(b, nt * 128), func=AF.Identity,
                                      scale=abv[:, kindA, vec, c:c + 1], bias=abv[:, kindB, vec, c:c + 1]),
                             reads=[bk(b), "ab", "ab2"], writes=[dst_key + str(c)])
                    else:
                        P.op("dve", I("tensor_scalar", out=dst, in0=PSB(b, nt * 128), scalar1=abv[:, kindA, vec, c:c + 1],
                                      scalar2=abv[:, kindB, vec, c:c + 1], op0=ALU.mult, op1=ALU.add),
                             reads=[bk(b), "ab", "ab2"], writes=[dst_key + str(c)])

            xm_t = x_main.rearrange("(t p) d -> t p d", p=128)
            xh_t = x_halo.rearrange("(t p) d -> t p d", p=128)
            norm1_groups = []
            xi = 0
            for tl, vec in [([0, 1, 2, 3], 0), ([4, 5], 1), ([6, 7, 8, 9], 1)]:
                tiles, xb, xk = [], [], []
                for ti in tl:
                    bi = xi % 3
                    xi += 1
                    src = xm_t[ti] if ti < 6 else xh_t[ti - 6]
                    ld = (lambda bi=bi, src=src: P.dma("sp", xs_[bi], src, sem="xs%d" % bi, writes=["xs%d" % bi]))
                    tiles.append((xs_[bi], "xs%d" % bi, ld))
                    xb.append(xn_[ti])
                    xk.append("xn%d" % ti)
                norm_stats(tiles, xb, xk, tl[0])
                norm1_groups.append((tl, vec, xb, xk))

            ada_pending = [0, 4, 1, 5, 2, 6, 3, 7] + list(range(8, 24))

            def ada_some(n):
                for _ in range(min(n, len(ada_pending))):
                    t = ada_pending.pop(0)
                    wt = load_w16(w_ada, t * 512)
                    b = rot_a2.next()
                    for cc in range(4):
                        for k in range(KC):
                            P.op("pe", I("matmul", ps[:, b, cc * 2:cc * 2 + 2], lhsT=wt.k(k)[:, cc * 128:(cc + 1) * 128],
                                         rhs=scTb[:, k, :], start=(k == 0), stop=(k == KC - 1)),
                                 reads=[wt.key(k), "scT"], writes=[bk(b)], signal=(k == KC - 1 or k == 7))
                    for v in range(2):
                        P.op("dve", I("tensor_tensor", out=modT[:, t * 4:(t + 1) * 4, v],
                                      in0=ps[:, b, 0:8].rearrange("p (c v) -> p c v", v=2)[:, :, v],
                                      in1=pp[:, PP_BADA + t * 4:PP_BADA + t * 4 + 4], op=ALU.add),
                             reads=[bk(b), "pp"], writes=["modT"])

            for qi in range(4):
                ada_some(2)
                c4 = slice(4 * qi, 4 * qi + 4)
                for v in range(2):
                    P.op("dve", I("scalar_tensor_tensor", out=abv[:, 0, v, c4], in0=modT[:, 16 + 4 * qi:20 + 4 * qi, v], scalar=1.0,
                                  in1=pp[:, PP_N1 + 4 * qi:PP_N1 + 4 * qi + 4], op0=ALU.add, op1=ALU.mult), reads=["modT", "pp"], writes=["ab"])
                    P.op("dve", I("tensor_copy", out=abv[:, 1, v, c4], in_=modT[:, 4 * qi:4 * qi + 4, v]), reads=["modT"], writes=["ab"])
                for (tl, vec, xb, xk) in norm1_groups:
                    norm_tr(len(tl), vec, 0, 1, hT, "hT", xb, xk, tl[0] * 128, chunks=range(4 * qi, 4 * qi + 4))
            SQ128 = float(np.sqrt(128.0))
            P.op("dve", I("tensor_copy", out=small[:, 0:1], in_=pp[:, PP_QNA:PP_QNA + 1]), reads=["pp"], writes=["small"])
            P.op("dve", I("tensor_scalar", out=small[:, 1:2], in0=pp[:, PP_KNA:PP_KNA + 1], scalar1=SQ128, scalar2=None, op0=ALU.mult), reads=["pp"], writes=["small"])
            P.op("dve", I("tensor_copy", out=small[:, 2:3], in_=pp[:, PP_QNB:PP_QNB + 1]), reads=["pp"], writes=["small"])
            P.op("dve", I("tensor_scalar", out=small[:, 3:4], in0=pp[:, PP_KNB:PP_KNB + 1], scalar1=SQ128, scalar2=None, op0=ALU.mult), reads=["pp"], writes=["small"])

            def ada_finish():
                ada_some(len(ada_pending))
                for v in range(2):
                    P.op("dve", I("scalar_tensor_tensor", out=abv[:, 2, v, :], in0=modT[:, 64:80, v], scalar=1.0,
                                  in1=pp[:, PP_N2:PP_N2 + 16], op0=ALU.add, op1=ALU.mult), reads=["modT", "pp"], writes=["ab2"])
                    P.op("dve", I("tensor_copy", out=abv[:, 3, v, :], in_=modT[:, 48:64, v]), reads=["modT"], writes=["ab2"])
            stop_at("ada", modT, ["modT", "ab", "small"])

            tkv = load_w16(w_in, 1024)
            stop_at("hT", hT, ["hT%d" % i for i in range(16)])

            P.alias(R1_ATT_KEYS, ["xs0", "xs1", "xs2"] + ["xn%d" % i for i in range(10)])
            P.dma("pool", maskA, maskA_d.rearrange("(c p) n -> p c n", p=128), sem="maskA", writes=["maskA"])
            P.dma("sp", cosT, cos_d, sem="cosT", writes=["cosT"])
            P.dma("sp", sinT, sin_d, sem="sinT", writes=["sinT"])

            cstate = {"i": 0, "ost": 0}

            def proj_fm(wt, cc, chunks, make_chain):
                for (s0, sz) in chunks:
                    b = rot_main.next()
                    for k in range(KC):
                        P.op("pe", I("matmul", PS(b, sz), lhsT=wt.k(k)[:, cc * 128:(cc + 1) * 128], rhs=hT[:, k, s0:s0 + sz],
                                     start=(k == 0), stop=(k == KC - 1)),
                             reads=[wt.key(k), "hT%d" % k], writes=[bk(b)], signal=(k == KC - 1 or k == 7))
                    tick()
                    make_chain(b, s0, sz)

            def norm_steps(b, sz, wcol, out_ap, out_key):
                i = cstate["i"] % 2
                cstate["i"] += 1
                sq = sqb[i][:, 0:sz]
                sqk = "sqb%d" % i
                P.op("act", I("activation", out=sq, in_=PS(b, sz), func=AF.Square), reads=[bk(b)], writes=[sqk])

                def step1():
                    b2 = rot_a1.next()
                    P.op("pe", I("matmul", PS(b2, sz), lhsT=onesb, rhs=sq, start=True, stop=True), reads=[sqk, "onesb"], writes=[bk(b2)])
                    rr = f32s[2][:, 0:sz]
                    P.op("act", I("activation", out=rr, in_=PS(b2, sz), func=AF.Ln, scale=1.0, bias=epsc128), reads=[bk(b2), "epsc"], writes=["f32s2"])
                    P.op("act", I("activation", out=rr, in_=rr, func=AF.Exp, scale=-0.5), reads=["f32s2"], writes=["f32s2"])
                    P.op("dve", I("scalar_tensor_tensor", out=out_ap, in0=PS(b, sz), scalar=wcol, in1=rr, op0=ALU.mult, op1=ALU.mult),
                         reads=[bk(b), "f32s2", "small"], writes=[out_key])
                return step1

            def rope_step(src_ap, src_key, e0, sz, out_ap, out_key):
                def step():
                    b3 = rot_a2.next()
                    P.op("pe", I("matmul", PS(b3, sz), lhsT=permf, rhs=src_ap, start=True, stop=True), reads=[src_key, "permf"], writes=[bk(b3)])
                    t1 = f32s[2][:, 0:sz]
                    P.op("dve", I("tensor_tensor", out=t1, in0=PS(b3, sz), in1=sinT[:, e0:e0 + sz], op=ALU.mult), reads=[bk(b3), "sinT"], writes=["f32s2"])
                    P.op("dve", I("tensor_tensor", out=src_ap, in0=src_ap, in1=cosT[:, e0:e0 + sz], op=ALU.mult), reads=[src_key, "cosT"], writes=[src_key])
                    P.op("dve", I("tensor_tensor", out=out_ap, in0=src_ap, in1=t1, op=ALU.add), reads=[src_key, "f32s2"], writes=[out_key])
                return step

            def kout_step(kn_ap, kn_key, dst_dram, head_col, kt_dst, kt_key):
                def step():
                    b4 = rot_a2.next()
                    for t in range(4):
                        P.op("pe", I("transpose", PS(b4, 128, t * 128), kn_ap[:, t * 128:(t + 1) * 128], identf),
                             reads=[kn_key, "identf"], writes=[bk(b4)], signal=(t == 3))
                    i = cstate["ost"] % 2
                    cstate["ost"] += 1
                    st = ost[i]
                    P.op("act", I("activation", out=st, in_=PS(b4, 512), func=AF.Copy), reads=[bk(b4)], writes=["ost%d" % i])
                    P.dma("sp", dst_dram.rearrange("(t p) n -> p t n", p=128)[:, :, head_col * 128:(head_col + 1) * 128],
                          st.rearrange("p (t d) -> p t d", t=4), sem="ost%d" % i, reads=["ost%d" % i])
                    P.op("act", I("activation", out=kt_dst, in_=kn_ap, func=AF.Copy), reads=[kn_key], writes=[kt_key])
                return step

            tmp_i = [0]

            def _nop():
                pass

            def proj_q(wt, cc, hslot, wcol, do_rope):
                def mk(b, s0, sz):
                    dst = qT[:, hslot, s0:s0 + sz]
                    if do_rope and s0 == CH_S[0]:
                        fi = tmp_i[0] % 2
                        tmp_i[0] += 1
                        tmp = f32s[fi][:, 0:sz]
                        chains.append([_nop, norm_steps(b, sz, wcol, tmp, "f32s%d" % fi), _nop, rope_step(tmp, "f32s%d" % fi, 0, sz, dst, "qT%d" % hslot)])
                    else:
                        chains.append([_nop, norm_steps(b, sz, wcol, dst, "qT%d" % hslot)])
                proj_fm(wt, cc, [CH_P, CH_S], mk)

            def proj_k(wt, cc, hslot, wcol, do_rope, dst_dram, head_col):
                def mk(b, s0, sz):
                    fi = tmp_i[0] % 2
                    tmp_i[0] += 1
                    tmp = f32s[fi][:, 0:sz]
                    tk = "f32s%d" % fi
                    dst = kT[:, hslot, s0:s0 + sz]
                    st1 = norm_steps(b, sz, wcol, tmp, tk)
                    if s0 == 0:
                        chains.append([_nop, st1, _nop, kout_step(tmp, tk, dst_dram, head_col, dst, "kT%d" % hslot)])
                    elif do_rope:
                        chains.append([_nop, st1, _nop, rope_step(tmp, tk, s0 - 512, sz, dst, "kT%d" % hslot)])
                    else:
                        def cp(tmp=tmp, tk=tk, dst=dst):
                            P.op("act", I("activation", out=dst, in_=tmp, func=AF.Copy), reads=[tk], writes=["kT%d" % hslot])
                        chains.append([_nop, st1, _nop, cp])
                proj_fm(wt, cc, [CH_P, CH_S, CH_H], mk)

            def proj_v(wt, c0, ncols, vcol0, dst_dram, dcol0):
                for ti in range(10):
                    b = rot_main.next()
                    for k in range(KC):
                        P.op("pe", I("matmul", PS(b, ncols), lhsT=hT[:, k, ti * 128:(ti + 1) * 128], rhs=wt.k(k)[:, c0:c0 + ncols],
                                     start=(k == 0), stop=(k == KC - 1)),
                             reads=[wt.key(k), "hT%d" % k], writes=[bk(b)], signal=(k == KC - 1 or k == 7))
                    tick()
                    if ti < 4:
                        i = cstate["ost"] % 2
                        cstate["ost"] += 1
                        st = ost[i][:, 0:ncols]
                        P.op("act", I("activation", out=st, in_=PS(b, ncols), func=AF.Copy), reads=[bk(b)], writes=["ost%d" % i])
                        P.op("dve", I("tensor_copy", out=vS[:, ti, vcol0:vcol0 + ncols], in_=st), reads=["ost%d" % i], writes=["vS"])
                        P.dma("sp", dst_dram[ti * 128:(ti + 1) * 128, dcol0:dcol0 + ncols], st, sem="ost%d" % i, reads=["ost%d" % i])
                    else:
                        P.op("dve", I("tensor_copy", out=vS[:, ti, vcol0:vcol0 + ncols], in_=PS(b, ncols)), reads=[bk(b)], writes=["vS"])

            pt_i = [0]

            def att_item(pair_heads, kslots, vcols, hs_slots, mixer, sink_cols, sample_bias, grp):
                same_kv = (kslots[0] == kslots[1]) and (vcols[0] == vcols[1])
                q0 = grp * 256
                if grp < 2:
                    chunks = [("tok", grp * 256 + c * 128, grp * 2 + c, None) for c in range(2)]
                else:
                    chunks = [("tok", 512 + c * 128, 4 + c, c) for c in range(6)] + [("ctx", c * 128, c, None) for c in range(2)]
                pts = []

                def scores():
                    for (kind, k0, vt, bc) in chunks:
                        b = rot_main.next()
                        first = True
                        if bc is not None:
                            bap, bkey = sample_bias(bc)
                            P.op("pe", I("matmul", PS(b, 512), lhsT=identb, rhs=bap, start=True, stop=False),
                                 reads=[bkey, "identb"], writes=[bk(b)], signal=False)
                            first = False
                        if same_kv:
                            lk, lkey = (kT[:, kslots[0], k0:k0 + 128], "kT%d" % kslots[0]) if kind == "tok" else (kc_T[:, kslots[0], k0:k0 + 128], "kcT")
                            P.op("pe", I("matmul", PS(b, 512).rearrange("p (a q) -> p a q", a=2), lhsT=lk,
                                         rhs=qT[:, hs_slots[0]:hs_slots[0] + 2, q0:q0 + 256], start=first, stop=True),
                                 reads=[lkey, "qT%d" % hs_slots[0], "qT%d" % (hs_slots[0] + 1)], writes=[bk(b)])
                        else:
                            for i in range(2):
                                lk, lkey = (kT[:, kslots[i], k0:k0 + 128], "kT%d" % kslots[i]) if kind == "tok" else (kc_T[:, kslots[i], k0:k0 + 128], "kcT")
                                P.op("pe", I("matmul", PS(b, 256, i * 256), lhsT=lk, rhs=qT[:, hs_slots[i], q0:q0 + 256],
                                             start=first, stop=(first or i == 1)),
                                     reads=[lkey, "qT%d" % hs_slots[i]], writes=[bk(b)], signal=(i == 1))
                        pi = pt_i[0] % NPT
                        pt_i[0] += 1
                        P.op("act", I("activation", out=pT[pi], in_=PS(b, 512), func=AF.Exp), reads=[bk(b)], writes=["pT%d" % pi])
                        pts.append((pi, kind, vt))

                def finish():
                    bo = rot_a1.next()
                    bl = rot_a2.next()
                    n = len(pts)
                    for i in range(2):
                        if same_kv and i == 1:
                            break
                        for ci, (pi, kind, vt) in enumerate(pts):
                            lv, vkey = (vS[:, vt, vcols[i]:vcols[i] + 128], "vS") if kind == "tok" else (vc_S[:, vt, vcols[i]:vcols[i] + 128], "vcS")
                            if same_kv:
                                P.op("pe", I("matmul", PS(bo, 512), lhsT=lv, rhs=pT[pi], start=(ci == 0), stop=(ci == n - 1)),
                                     reads=[vkey, "pT%d" % pi], writes=[bk(bo)], signal=(ci == n - 1))
                            else:
                                P.op("pe", I("matmul", PS(bo, 256, i * 256), lhsT=lv, rhs=pT[pi][:, i * 256:(i + 1) * 256],
                                             start=(ci == 0), stop=(ci == n - 1)),
                                     reads=[vkey, "pT%d" % pi], writes=[bk(bo)], signal=(ci == n - 1))
                    for ci, (pi, kind, vt) in enumerate(pts):
                        P.op("pe", I("matmul", PS(bl, 512), lhsT=onesb, rhs=pT[pi], start=(ci == 0), stop=(ci == n - 1)),
                             reads=["onesb", "pT%d" % pi], writes=[bk(bl)], signal=(ci == n - 1))
                    for i in range(2):
                        rl = rl_[i]
                        sc = esink[:, sink_cols[i]:sink_cols[i] + 1] if sink_cols is not None else zeroc
                        P.op("act", I("activation", out=rl, in_=PS(bl, 256, i * 256), func=AF.Ln, scale=1.0, bias=sc),
                             reads=[bk(bl), "esink", "epsc"], writes=["rl%d" % i])
                        P.op("act", I("activation", out=rl, in_=rl, func=AF.Exp, scale=-1.0), reads=["rl%d" % i], writes=["rl%d" % i])
                        P.op("dve", I("tensor_tensor", out=oT[:, mixer, pair_heads[i], q0:q0 + 256], in0=PS(bo, 256, i * 256), in1=rl, op=ALU.mult),
                             reads=[bk(bo), "rl%d" % i], writes=["oT"])
                return scores, finish, len(chunks)

            def run_items(items, hooks=(), depth=3):
                pend = []
                hook_at = {}
                for hi_, h in enumerate(hooks):
                    hook_at[(hi_ + 1) * len(items) // (len(hooks) + 1)] = h
                for ii_, (sc, fin, n) in enumerate(items):
                    if ii_ in hook_at:
                        hook_at[ii_]()
                    while pend and sum(p[1] for p in pend) + n > NPT:
                        pend.pop(0)[0]()
                    sc()
                    pend.append((fin, n))
                    while len(pend) > depth:
                        pend.pop(0)[0]()
                while pend:
                    pend.pop(0)[0]()

            def load_ctx(k_d, v_d, c0, ncols, nheads):
                P.dma("pool", kc_S[:, :, 0:ncols], k_d[:, c0:c0 + ncols].rearrange("(c p) n -> p c n", p=128), sem="kcS", writes=["kcS"])
                P.dma("pool", vc_S[:, :, 0:ncols], v_d[:, c0:c0 + ncols].rearrange("(c p) n -> p c n", p=128), sem="vcS", writes=["vcS"])
                for h in range(nheads):
                    b = rot_a2.next()
                    for c in range(2):
                        P.op("pe", I("transpose", PSB(b, 128, c * 128), kc_S[:, c, h * 128:(h + 1) * 128], identb),
                             reads=["kcS", "identb"], writes=[bk(b)], signal=(c == 1))
                    P.op("dve", I("tensor_copy", out=kc_T[:, h, :], in_=PSB(b, 256)), reads=[bk(b)], writes=["kcT"])

            Wq = small[:, 0:1]
            Wk = small[:, 1:2]
            Wqb = small[:, 2:3]
            Wkb = small[:, 3:4]
            set_mode("proj")
            load_ctx(cak_d, cav_d, 0, 256, 2)
            for h in range(2):
                proj_k(tkv, h, h, Wk, True, nak_o, h)
            proj_v(tkv, 256, 256, 0, nav_o, 0)
            ada_some(1)
            for rnd in range(2):
                set_mode("proj")
                tq = load_w16(w_in, rnd * 512)
                for cc in range(4):
                    proj_q(tq, cc, cc, Wq, True)
                drain()
                ada_some(1)
                set_mode("att")
                items = []
                for pr in range(2):
                    h0 = rnd * 4 + pr * 2
                    kvh = h0 // 4
                    for grp in range(3):
                        items.append(att_item([h0, h0 + 1], [kvh, kvh], [kvh * 128, kvh * 128], [pr * 2, pr * 2 + 1], 0,
                                              [h0, h0 + 1], lambda c: (maskA[:, c, :], "maskA"), grp))
                run_items(items, hooks=[lambda: ada_some(1), lambda: ada_some(1)])
            stop_at("oA", oT[:, 0], ["oT"])

            P.alias(["biasB0", "biasB1"], ["maskA", "cosT", "sinT"])
            bB_t = biasB_d.rearrange("(c p) n -> p c n", p=128)
            for rnd in range(2):
                set_mode("proj")
                tq = load_w16(w_in, 1536 + rnd * 512)
                load_ctx(cbk_d, cbv_d, rnd * 512, 512, 4)
                for cc in range(4):
                    proj_q(tq, cc, cc, Wqb, False)
                ada_some(1)
                tk = load_w16(w_in, 2560 + rnd * 512)
                for cc in range(4):
                    proj_k(tk, cc, cc, Wkb, False, nbk_o, rnd * 4 + cc)
                ada_some(1)
                tv = load_w16(w_in, 3584 + rnd * 512)
                proj_v(tv, 0, 512, 0, nbv_o, rnd * 512)
                drain()
                ada_some(1)
                set_mode("att")
                items = []
                for pr in range(2):
                    gp = rnd * 2 + pr
                    bi = gp % 2
                    P.dma("pool", biasB[bi], bB_t[:, :, gp * 512:(gp + 1) * 512], sem="biasB%d" % bi, writes=["biasB%d" % bi])
                    h0 = rnd * 4 + pr * 2
                    for grp in range(3):
                        items.append(att_item([h0, h0 + 1], [pr * 2, pr * 2 + 1], [pr * 256, pr * 256 + 128], [pr * 2, pr * 2 + 1], 1,
                                              None, lambda c, bi=bi: (biasB[bi][:, c, :], "biasB%d" % bi), grp))
                run_items(items, hooks=[lambda: ada_some(1), lambda: ada_some(1)])
            stop_at("oB", oT[:, 1], ["oT"])

            xkeys = ["x%d" % t for t in range(6)]
            newk = xkeys + ["sg0", "sg1", "tm0", "tm1", "t512_0", "t512_1", "onesf", "diag0", "diag1", "tA"]
            P.alias(newk, R1_ATT_KEYS)
            for t in range(6):
                P.dma("sp", xres[:, t, :], xm_t[t], sem="x%d" % t, writes=["x%d" % t])

            HALF = [(0, 384), (384, 384)]
            for cg in range(4):
                for mix in range(2):
                    wg = load_w16(w_in, (4608 if mix == 0 else 6656) + cg * 512)
                    wb = load_w8(w_bra if mix == 0 else w_brb, cg * 512)
                    for cc in range(4):
                        ch = cg * 4 + cc
                        for hi, (s0, sz) in enumerate(HALF):
                            bg = rot_main.next()
                            for k in range(KC):
                                P.op("pe", I("matmul", PS(bg, sz), lhsT=wg.k(k)[:, cc * 128:(cc + 1) * 128], rhs=hT[:, k, s0:s0 + sz],
                                             start=(k == 0), stop=(k == KC - 1)),
                                     reads=[wg.key(k), "hT%d" % k], writes=[bk(bg)], signal=(k == KC - 1 or k == 7))
                            by = (rot_a1 if hi == 0 else rot_a2).next()
                            for k in range(8):
                                P.op("pe", I("matmul", PS(by, sz), lhsT=wb.k(k)[:, cc * 128:(cc + 1) * 128], rhs=oT[:, mix, k, s0:s0 + sz],
                                             start=(k == 0), stop=(k == 7)),
                                     reads=[wb.key(k), "oT"], writes=[bk(by)], signal=(k == 7))
                            sg = sgs[hi][:, 0:sz]
                            P.op("act", I("activation", out=sg, in_=PS(bg, sz), func=AF.Sigmoid), reads=[bk(bg)], writes=["sg%d" % hi])
                            if mix == 0:
                                P.op("dve", I("tensor_tensor", out=tA[:, cc, s0:s0 + sz], in0=PS(by, sz), in1=sg, op=ALU.mult),
                                     reads=[bk(by), "sg%d" % hi], writes=["tA"])
                            else:
                                tm = tms[hi][:, 0:sz]
                                P.op("dve", I("tensor_tensor", out=tm, in0=PS(by, sz), in1=sg, op=ALU.mult),
                                     reads=[bk(by), "sg%d" % hi], writes=["tm%d" % hi])
                                P.op("dve", I("tensor_tensor", out=mTv[:, ch, s0:s0 + sz], in0=tm, in1=tA[:, cc, s0:s0 + sz], op=ALU.add),
                                     reads=["tm%d" % hi, "tA"], writes=["mT"])
            ada_finish()
            stop_at("mT", mTv, ["mT"])

            P.op("dve", I("memset", onesf, 1.0), writes=["onesf"])

            diag4 = [view(pb_scr + 11776 + i * 2048, [4, 128], F32) for i in range(2)]

            def build_G(mod_base, alias_from):
                P.alias(["G"], alias_from)
                di = 0
                for v in range(2):
                    for g4 in range(4):
                        b = rot_a1.next()
                        d4 = diag4[di % 2]
                        dk = "dg4_%d" % (di % 2)
                        di += 1
                        c0 = mod_base + g4 * 4
                        for j in range(4):
                            P.op("dve", I("tensor_scalar", out=d4[:, j, :], in0=identf, scalar1=modT[:, c0 + j, v:v + 1], scalar2=None, op0=ALU.mult),
                                 reads=["identf", "modT"], writes=[dk + "_%d" % j])
                        for j in range(4):
                            P.op("pe", I("matmul", PS(b, 128, j * 128), lhsT=onesf, rhs=d4[:, j, :], start=True, stop=True),
                                 reads=["onesf", dk + "_%d" % j], writes=[bk(b)], signal=(j == 3))
                        P.op("act", I("activation", out=Gt[:, v, g4 * 512:(g4 + 1) * 512], in_=PS(b, 512), func=AF.Copy),
                             reads=[bk(b)], writes=["G"])

            P.alias(["dg4_%d_%d" % (i, j) for i in range(2) for j in range(4)], ["tA"])
            build_G(32, ["oT"])

            for cg in range(4):
                wt = load_w16(w_out, cg * 512)
                for t in range(6):
                    v = 0 if t < 4 else 1
                    b = rot_main.next()
                    for k in range(KC):
                        P.op("pe", I("matmul", PS(b, 512), lhsT=mTv[:, k, t * 128:(t + 1) * 128], rhs=wt.k(k),
                                     start=(k == 0), stop=(k == KC - 1)),
                             reads=[wt.key(k), "mT"], writes=[bk(b)], signal=(k == KC - 1 or k == 7))
                    i = (cg * 6 + t) % 2
                    tp = tmp512[i]
                    P.op("dve", I("tensor_tensor", out=tp, in0=PS(b, 512), in1=Gt[:, v, cg * 512:(cg + 1) * 512], op=ALU.mult),
                         reads=[bk(b), "G"], writes=["t512_%d" % i])
                    P.op("dve", I("tensor_tensor", out=xres[:, t, cg * 512:(cg + 1) * 512], in0=xres[:, t, cg * 512:(cg + 1) * 512], in1=tp, op=ALU.add),
                         reads=["t512_%d" % i, "x%d" % t], writes=["x%d" % t])
            stop_at("x1", xres, ["x%d" % t for t in range(6)])

            xnk2 = ["xnn%d" % i for i in range(6)]
            P.alias(["h2T%d" % i for i in range(16)] + xnk2, ["hT%d" % i for i in range(16)] + ["oT"])
            xni = 0
            for tl, vec in [([0, 1, 2, 3], 0), ([4, 5], 1)]:
                tiles, xb, xk = [], [], []
                for ti in tl:
                    tiles.append((xres[:, ti, :], "x%d" % ti, None))
                    xb.append(xnn[xni % 6])
                    xk.append(xnk2[xni % 6])
                    xni += 1
                norm_group(tiles, None, vec, 2, 3, h2T, "h2T", xb, xk, tl[0] * 128, tl[0])
            stop_at("h2T", h2T, ["h2T%d" % i for i in range(16)])
            build_G(80, ["G"])

            P.alias(["uT"], xnk2)
            P.alias(["rs0", "rs1", "tb0", "tb1"], ["sg0", "sg1", "tm0", "tm1", "t512_0", "t512_1"])
            ri = 0
            for fg in range(4):
                for t4 in range(4):
                    wt = load_w16(w_up, fg * 2048 + t4 * 512)
                    for cc in range(4):
                        fc = t4 * 4 + cc
                        for (s0, sz) in HALF:
                            b = rot_main.next()
                            for k in range(KC):
                                P.op("pe", I("matmul", PS(b, sz), lhsT=wt.k(k)[:, cc * 128:(cc + 1) * 128], rhs=h2T[:, k, s0:s0 + sz],
                                             start=(k == 0), stop=(k == KC - 1)),
                                     reads=[wt.key(k), "h2T%d" % k], writes=[bk(b)], signal=(k == KC - 1 or k == 7))
                            i = ri % 2
                            ri += 1
                            rs = rsb[i][:, 0:sz]
                            P.op("act", I("activation", out=rs, in_=PS(b, sz), func=AF.Relu), reads=[bk(b)], writes=["rs%d" % i])
                            P.op("dve", I("tensor_tensor", out=uT[:, fc, s0:s0 + sz], in0=rs, in1=rs, op=ALU.mult), reads=["rs%d" % i], writes=["uT"])
                for cg in range(4):
                    wt = load_w16(w_down, cg * 512, r0=fg * 2048)
                    for t in range(6):
                        v = 0 if t < 4 else 1
                        b = rot_a1.next() if (t % 2 == 0) else rot_a2.next()
                        for k in range(KC):
                            P.op("pe", I("matmul", PS(b, 512), lhsT=uT[:, k, t * 128:(t + 1) * 128], rhs=wt.k(k),
                                         start=(k == 0), stop=(k == KC - 1)),
                                 reads=[wt.key(k), "uT"], writes=[bk(b)], signal=(k == KC - 1 or k == 7))
                        i = (cg * 6 + t) % 2
                        tp = tmp512b[i]
                        P.op("dve", I("tensor_tensor", out=tp, in0=PS(b, 512), in1=Gt[:, v, cg * 512:(cg + 1) * 512], op=ALU.mult),
                             reads=[bk(b), "G"], writes=["tb%d" % i])
                        P.op("dve", I("tensor_tensor", out=xres[:, t, cg * 512:(cg + 1) * 512], in0=xres[:, t, cg * 512:(cg + 1) * 512], in1=tp, op=ALU.add),
                             reads=["tb%d" % i, "x%d" % t], writes=["x%d" % t])

            ym_t = y_main.rearrange("(t p) d -> t p d", p=128)
            for t in range(6):
                P.dma("sp", ym_t[t], xres[:, t, :], sem="x%d" % t, reads=["x%d" % t])
            P.wait_all("sp", ["x%d" % t for t in range(6)] + ["ost0", "ost1"])

        try:
            body()
        except _Stop:
            pass
        fin = []
        for e_ in Prog.CE:
            if P.cnt[e_] > 0:
                fin.append((P.esem[e_], P.cnt[e_]))
        P.q["sp"].append((fin, None, None))

        with nc.Block() as block:
            @block.sync
            def _(e):
                P.emit("sp", e)

            @block.gpsimd
            def _(e):
                P.emit("pool", e)

            @block.tensor
            def _(e):
                P.emit("pe", e)

            @block.scalar
            def _(e):
                P.emit("act", e)

            @block.vector
            def _(e):
                P.emit("dve", e)
        build_program.stats = dict(n={k: len(v) for k, v in P.q.items()}, waits=P.n_wait, arena=A.off, cnt=dict(P.cnt))
    return nc


GRID_W = 64
ROWS = 16


def _core_geometry(j):
    b = j // 4
    qq = j % 4
    ws = 0 if qq < 2 else 4
    own_rows = list(range(4 * qq, 4 * qq + 4))
    halo_rows = [r for r in range(ws, ws + 12) if r not in own_rows]
    rows = own_rows + halo_rows
    pos = np.concatenate([np.arange(r * GRID_W, (r + 1) * GRID_W) for r in rows])
    return b, qq, pos


def _static_tables(j, rpb):
    b, qq, pos = _core_geometry(j)
    row = (pos // GRID_W).astype(np.int64)
    col = (pos % GRID_W).astype(np.int64)
    n_freq = 32
    inv = (10000.0 ** (-np.arange(n_freq, dtype=np.float32) / n_freq)).astype(np.float32)
    ang_r = row[:, None].astype(np.float32) * inv[None, :]
    ang_c = col[:, None].astype(np.float32) * inv[None, :]
    cosT = np.zeros((128, 768), np.float32)
    sinT = np.zeros((128, 768), np.float32)
    cosT[0:32] = np.cos(ang_r).T
    cosT[32:64] = np.cos(ang_r).T
    cosT[64:96] = np.cos(ang_c).T
    cosT[96:128] = np.cos(ang_c).T
    sinT[0:32] = -np.sin(ang_r).T
    sinT[32:64] = np.sin(ang_r).T
    sinT[64:96] = -np.sin(ang_c).T
    sinT[96:128] = np.sin(ang_c).T
    qpos = pos[:256]
    valid = np.abs(qpos[None, :] - pos[:, None]) <= 128
    mA = np.where(valid, 0.0, NEGM).astype(np.float32)
    maskA = np.concatenate([mA, mA], axis=1)
    qr = row[:256]
    qc = col[:256]
    rstart = np.clip(qr - 4, 0, ROWS - 8)
    cstart = np.clip(qc - 8, 0, GRID_W - 16)
    kr = row[:, None]
    kcc = col[:, None]
    vr = (kr >= rstart[None, :]) & (kr < rstart[None, :] + 8)
    vc = (kcc >= cstart[None, :]) & (kcc < cstart[None, :] + 16)
    valid = vr & vc
    dr = np.clip(kr - qr[None, :] + 7, 0, 14)
    dc = np.clip(kcc - qc[None, :] + 15, 0, 30)
    bias = rpb[:, dr, dc]
    bias = np.where(valid[None], bias, np.float32(NEGM)).astype(np.float32)
    biasB = np.ascontiguousarray(np.transpose(bias, (1, 0, 2))).reshape(768, 2048)
    return cosT, sinT, maskA, biasB


def _perm():
    p = np.zeros((128, 128), np.float32)
    for d in range(128):
        s = d + 32 if (d % 64) < 32 else d - 32
        p[s, d] = 1.0
    return p


_NC_CACHE = {}


def kernel(x_prompt, x_sample, cache_a_k, cache_a_v, cache_b_k, cache_b_v, c, c_ctx,
           norm1_w, norm2_w, w_ada, b_ada, w_in, q_norm_a, k_norm_a, q_norm_b, k_norm_b,
           sink_a, rpb_b, w_br_a, w_br_b, w_out, w_up, w_down, _debug=None, _cores=None):
    f = lambda a: np.ascontiguousarray(np.asarray(a, dtype=np.float32))
    x_prompt, x_sample = f(x_prompt), f(x_sample)
    c, c_ctx = f(c), f(c_ctx)
    cores = list(range(NCORES)) if _cores is None else _cores
    key = "dbg" if _debug is not None else "main"
    if key not in _NC_CACHE:
        _NC_CACHE[key] = build_program(_debug)
    nc = _NC_CACHE[key]

    ident = np.eye(128, dtype=np.float32)
    perm = _perm()
    shared = dict(
        ident=ident, perm=perm,
        w_ada=f(w_ada)[0], w_in=f(w_in)[0], w_br_a=f(w_br_a)[0], w_br_b=f(w_br_b)[0],
        w_out=f(w_out)[0], w_up=f(w_up)[0], w_down=f(w_down)[0],
    )
    rpb = f(rpb_b)[0]
    in_maps = []
    for idx_core, j in enumerate(cores):
        b, qq, pos = _core_geometry(j)
        xm = np.concatenate([x_prompt[2 * j], x_prompt[2 * j + 1], x_sample[b][pos[:256]]], axis=0)
        xh = x_sample[b][pos[256:]]
        cvec = np.stack([c_ctx, c[b]], axis=0)
        cT = np.ascontiguousarray(cvec.reshape(2, 16, 128).transpose(2, 1, 0)).reshape(128, 32)
        pp = np.zeros((128, PP_W), np.float32)
        pp[:, PP_N1:PP_N1 + 16] = f(norm1_w)[0].reshape(16, 128).T
        pp[:, PP_N2:PP_N2 + 16] = f(norm2_w)[0].reshape(16, 128).T
        pp[:, PP_BADA:PP_BADA + 96] = f(b_ada)[0].reshape(96, 128).T
        pp[:, PP_QNA] = f(q_norm_a)[0]
        pp[:, PP_KNA] = f(k_norm_a)[0]
        pp[:, PP_QNB] = f(q_norm_b)[0]
        pp[:, PP_KNB] = f(k_norm_b)[0]
        pp[:, PP_SINK:PP_SINK + 8] = f(sink_a)[0][None, :]
        cosT, sinT, maskA, biasB = _static_tables(j, rpb)
        m = dict(shared)
        m.update(
            x_main=np.ascontiguousarray(xm), x_halo=np.ascontiguousarray(xh), cT=cT, pp=pp,
            cosT=cosT, sinT=sinT, maskA=maskA, biasB=biasB,
            cak=f(cache_a_k)[b, 0].reshape(256, 256), cav=f(cache_a_v)[b, 0].reshape(256, 256),
            cbk=f(cache_b_k)[b, 0].reshape(256, 1024), cbv=f(cache_b_v)[b, 0].reshape(256, 1024),
        )
        in_maps.append(m)

    res = run_bass_kernel_spmd(nc, in_maps, core_ids=list(range(len(cores))))
    if _debug is not None:
        return res

    y_prompt = np.zeros((16, 256, D), np.float32)
    y_sample = np.zeros((2, 1024, D), np.float32)
    nak = np.zeros((16, 1, 256, 2, 128), np.float32)
    nav = np.zeros((16, 1, 256, 2, 128), np.float32)
    nbk = np.zeros((16, 1, 256, 8, 128), np.float32)
    nbv = np.zeros((16, 1, 256, 8, 128), np.float32)
    for idx, j in enumerate(cores):
        r = res.results[idx]
        b, qq, pos = _core_geometry(j)
        ym = r["y_main"]
        y_prompt[2 * j] = ym[0:256]
        y_prompt[2 * j + 1] = ym[256:512]
        y_sample[b][pos[:256]] = ym[512:768]
        for s in range(2):
            nak[2 * j + s, 0] = r["nak"][s * 256:(s + 1) * 256].reshape(256, 2, 128)
            nav[2 * j + s, 0] = r["nav"][s * 256:(s + 1) * 256].reshape(256, 2, 128)
            nbk[2 * j + s, 0] = r["nbk"][s * 256:(s + 1) * 256].reshape(256, 8, 128)
            nbv[2 * j + s, 0] = r["nbv"][s * 256:(s + 1) * 256].reshape(256, 8, 128)
    return (y_prompt, y_sample, nak, nav, nbk, nbv)
```
